# Optimizing a Trainium2 kernel written in Bass

```python
import math
import jax
import jax.numpy as jnp
from jax import lax
import numpy as np

D_MODEL = 1024
BATCH = 8
SEQ = 2048
DEPTH = 4
DEC_BATCH = 128
DEC_SEQ = 1
PAST_LEN = 16384
PAGE_SIZE = 128

D_MIX = D_MODEL
D_BRANCH = D_MIX // 4
HEAD_DIM = 64
N_HEADS = D_BRANCH // HEAD_DIM
GLA_GATE_RANK = 16
GLA_TAU = 16.0
GLA_CHUNK = 16
S5_CH = 16
S5_GROUPS = D_BRANCH // S5_CH
S5_STATE = 64
GDN_CONV = 4
GDN_CHUNK = 64
RWKV_LORA_W = 64
RWKV_LORA_A = 64
RWKV_SHIFT_W = 3 * D_BRANCH + RWKV_LORA_W + RWKV_LORA_A
RMS_EPS = 1e-6
L2_EPS = 1e-6
RWKV_GN_EPS = 64e-5
IN_SIZES = (D_BRANCH, D_BRANCH, D_BRANCH, GLA_GATE_RANK, D_BRANCH,
            D_BRANCH, D_BRANCH,
            3 * D_BRANCH, N_HEADS, N_HEADS, D_BRANCH,
            RWKV_SHIFT_W, D_BRANCH)
D_IN = sum(IN_SIZES)
F32 = jnp.float32

kernel_name = 'hybrid_gla_s5_gdn_rwkv7_step'


def rms_norm(x, g):
    xf = x.astype(F32)
    y = xf * lax.rsqrt(jnp.mean(xf * xf, axis=-1, keepdims=True) + RMS_EPS) * g.astype(F32)
    return y.astype(x.dtype)


def head_rms(o, g):
    return o * lax.rsqrt(jnp.mean(o * o, axis=-1, keepdims=True) + RMS_EPS) * g.astype(F32)


def l2norm(t):
    return t * lax.rsqrt(jnp.sum(t * t, axis=-1, keepdims=True) + L2_EPS)


def heads(t):
    return t.reshape(t.shape[:-1] + (N_HEADS, HEAD_DIM))


def to_blocks(t, c):
    b, l = t.shape[:2]
    t = t.reshape((b, l // c, c) + t.shape[2:])
    return t.transpose((1, 0, 3, 2) + tuple(range(4, t.ndim)))


def from_blocks(t):
    n, b, h, c, d = t.shape
    return t.transpose(1, 0, 3, 2, 4).reshape(b, n * c, h, d)


def gla_mix(q, k, v, log_a, s0):
    seq = q.shape[1]
    c = math.gcd(seq, GLA_CHUNK)
    q = to_blocks(q * (HEAD_DIM ** -0.5), c)
    k, v, log_a = to_blocks(k, c), to_blocks(v, c), to_blocks(log_a, c)
    cum = jnp.cumsum(log_a, axis=-2)
    causal = jnp.tril(jnp.ones((c, c), bool))
    diff = cum[..., :, None, :] - cum[..., None, :, :]
    decay = jnp.exp(jnp.where(causal[:, :, None], diff, -jnp.inf))
    attn = jnp.einsum('nbhid,nbhjd,nbhijd->nbhij', q, k, decay)
    o_intra = jnp.einsum('nbhij,nbhjv->nbhiv', attn, v)
    q_dec = q * jnp.exp(cum)
    k_dec = k * jnp.exp(cum[..., -1:, :] - cum)
    a_tot = jnp.exp(cum[..., -1, :])

    def step(s, inp):
        qd, kd, vv, at = inp
        o = jnp.einsum('bhcd,bhdv->bhcv', qd, s)
        s = at[..., None] * s + jnp.einsum('bhcd,bhcv->bhdv', kd, vv)
        return s, o

    s_fin, o_inter = lax.scan(step, s0, (q_dec, k_dec, v, a_tot))
    return from_blocks(o_intra + o_inter), s_fin


def gdn_mix(q, k, v, g, beta, s0):
    seq = q.shape[1]
    c = math.gcd(seq, GDN_CHUNK)
    q = to_blocks(q * (HEAD_DIM ** -0.5), c)
    k, v = to_blocks(k, c), to_blocks(v, c)
    g, beta = to_blocks(g, c), to_blocks(beta, c)
    gc = jnp.cumsum(g, axis=-1)
    causal = jnp.tril(jnp.ones((c, c), bool))
    strict = jnp.tril(jnp.ones((c, c), bool), -1)
    decay = jnp.exp(jnp.where(causal, gc[..., :, None] - gc[..., None, :], -jnp.inf))
    kb = k * beta[..., None]
    vb = v * beta[..., None]
    low = jnp.where(strict, jnp.einsum('nbhid,nbhjd->nbhij', kb, k) * decay, 0.0)
    tmat = low + jnp.eye(c, dtype=low.dtype)
    rhs = jnp.concatenate([vb, kb * jnp.exp(gc)[..., None]], axis=-1)
    sol = lax.linalg.triangular_solve(tmat, rhs, left_side=True, lower=True, unit_diagonal=True)
    u, w = sol[..., :HEAD_DIM], sol[..., HEAD_DIM:]
    attn = jnp.einsum('nbhid,nbhjd->nbhij', q, k) * decay
    q_dec = q * jnp.exp(gc)[..., None]
    k_dec = k * jnp.exp(gc[..., -1:] - gc)[..., None]
    g_tot = jnp.exp(gc[..., -1])

    def step(s, inp):
        uu, ww, qd, kd, at, gt = inp
        v_new = uu - jnp.einsum('bhcd,bhdv->bhcv', ww, s)
        o = jnp.einsum('bhcd,bhdv->bhcv', qd, s) + jnp.einsum('bhij,bhjv->bhiv', at, v_new)
        s = gt[..., None, None] * s + jnp.einsum('bhcd,bhcv->bhdv', kd, v_new)
        return s, o

    s_fin, o = lax.scan(step, s0, (u, w, q_dec, k_dec, attn, g_tot))
    return from_blocks(o), s_fin


def s5_mix(u, lam_re, lam_im, log_step, b_re, b_im, c_re, c_im, d, h0_re, h0_im):
    bsz, seq, _ = u.shape
    lam_re, lam_im = lam_re.astype(F32), lam_im.astype(F32)
    step = jnp.exp(log_step.astype(F32))[:, None]
    mag = jnp.exp(lam_re * step)
    ab_re = mag * jnp.cos(lam_im * step)
    ab_im = mag * jnp.sin(lam_im * step)
    den = lam_re * lam_re + lam_im * lam_im
    z_re = ((ab_re - 1.0) * lam_re + ab_im * lam_im) / den
    z_im = (ab_im * lam_re - (ab_re - 1.0) * lam_im) / den
    b_re, b_im = b_re.astype(F32), b_im.astype(F32)
    bb_re = z_re[..., None] * b_re - z_im[..., None] * b_im
    bb_im = z_re[..., None] * b_im + z_im[..., None] * b_re
    ug = u.reshape(bsz, seq, S5_GROUPS, S5_CH)
    bu_re = jnp.einsum('blgc,gpc->blgp', ug, bb_re)
    bu_im = jnp.einsum('blgc,gpc->blgp', ug, bb_im)
    a_re = jnp.broadcast_to(ab_re, bu_re.shape)
    a_im = jnp.broadcast_to(ab_im, bu_im.shape)

    def combine(e1, e2):
        a1r, a1i, b1r, b1i = e1
        a2r, a2i, b2r, b2i = e2
        return (a1r * a2r - a1i * a2i, a1r * a2i + a1i * a2r,
                a2r * b1r - a2i * b1i + b2r, a2r * b1i + a2i * b1r + b2i)

    cum_re, cum_im, hs_re, hs_im = lax.associative_scan(combine, (a_re, a_im, bu_re, bu_im), axis=1)
    h0_re = h0_re.astype(F32)[:, None]
    h0_im = h0_im.astype(F32)[:, None]
    h_re = hs_re + cum_re * h0_re - cum_im * h0_im
    h_im = hs_im + cum_re * h0_im + cum_im * h0_re
    y = (jnp.einsum('blgp,gcp->blgc', h_re, c_re.astype(F32))
         - jnp.einsum('blgp,gcp->blgc', h_im, c_im.astype(F32)))
    y = y.reshape(bsz, seq, D_BRANCH) + d.astype(F32) * u
    return y, h_re[:, -1], h_im[:, -1]


def rwkv7_mix(r, log_w, k, v, kk, a, s0):
    def step(s, inp):
        r_t, lw_t, k_t, v_t, kk_t, a_t = inp
        s = (s * jnp.exp(lw_t)[:, :, None, :]
             - jnp.einsum('bhvi,bhi->bhv', s, kk_t)[..., None] * (kk_t * a_t)[:, :, None, :]
             + v_t[..., None] * k_t[:, :, None, :])
        return s, jnp.einsum('bhvk,bhk->bhv', s, r_t)

    xs = tuple(t.transpose(1, 0, 2, 3) for t in (r, log_w, k, v, kk, a))
    s_fin, o = lax.scan(step, s0, xs)
    return o.transpose(1, 0, 2, 3), s_fin


def hybrid_layer(x, st, lp):
    (norm_g, w_in, gla_wg2, gla_bg, gla_norm_g,
     s5_lam_re, s5_lam_im, s5_log_step, s5_b_re, s5_b_im, s5_c_re, s5_c_im, s5_d, s5_w_glu, s5_b_glu,
     gdn_conv_w, gdn_a_log, gdn_dt_bias, gdn_norm_g,
     rwkv_mu, rwkv_w0, rwkv_ww2, rwkv_a0, rwkv_wa2, rwkv_k_k, rwkv_k_a, rwkv_r_k, rwkv_ln_g, rwkv_ln_b,
     w_out) = lp
    st_gla, st_s5_re, st_s5_im, st_gdn, st_conv, st_rwkv, st_shift = st
    bsz, seq, _ = x.shape
    h = rms_norm(x, norm_g)
    proj = jnp.matmul(h, w_in).astype(F32)
    split_at = np.cumsum(IN_SIZES)[:-1].tolist()
    (gla_q, gla_k, gla_v, gla_glr, gla_gate, s5_u, s5_gate,
     gdn_qkv, gdn_a, gdn_b, gdn_gate, rwkv_in, rwkv_gate) = jnp.split(proj, split_at, axis=-1)

    log_a = jax.nn.log_sigmoid(gla_glr @ gla_wg2.astype(F32) + gla_bg) / GLA_TAU
    o, new_gla = gla_mix(heads(gla_q), heads(gla_k), heads(gla_v), heads(log_a), st_gla.astype(F32))
    o_gla = head_rms(o, gla_norm_g).reshape(bsz, seq, D_BRANCH) * jax.nn.silu(gla_gate)

    y, new_s5_re, new_s5_im = s5_mix(s5_u, s5_lam_re, s5_lam_im, s5_log_step, s5_b_re, s5_b_im,
                                     s5_c_re, s5_c_im, s5_d, st_s5_re, st_s5_im)
    y = jax.nn.gelu(y)
    y = y * jax.nn.sigmoid(y @ s5_w_glu.astype(F32) + s5_b_glu)
    o_s5 = y * jax.nn.silu(s5_gate)

    xp = jnp.concatenate([st_conv.astype(F32), gdn_qkv], axis=1)
    conv = xp[:, 0:seq] * gdn_conv_w[0]
    for j in range(1, GDN_CONV):
        conv = conv + xp[:, j:j + seq] * gdn_conv_w[j]
    new_conv = xp[:, seq:]
    gq, gk, gv = jnp.split(jax.nn.silu(conv), 3, axis=-1)
    g = -jnp.exp(gdn_a_log.astype(F32)) * jax.nn.softplus(gdn_a + gdn_dt_bias)
    beta = jax.nn.sigmoid(gdn_b)
    o, new_gdn = gdn_mix(l2norm(heads(gq)), l2norm(heads(gk)), heads(gv), g, beta, st_gdn.astype(F32))
    o_gdn = head_rms(o, gdn_norm_g).reshape(bsz, seq, D_BRANCH) * jax.nn.silu(gdn_gate)

    prev = jnp.concatenate([st_shift.astype(F32)[:, None], rwkv_in[:, :-1]], axis=1)
    xs = rwkv_in + (prev - rwkv_in) * rwkv_mu
    new_shift = rwkv_in[:, -1]
    rr, rk, rv, rwl, ral = jnp.split(xs, np.cumsum([D_BRANCH, D_BRANCH, D_BRANCH, RWKV_LORA_W]).tolist(), axis=-1)
    w = -jax.nn.softplus(-(rwkv_w0 + jnp.tanh(rwl) @ rwkv_ww2.astype(F32))) - 0.5
    log_w = -jnp.exp(w)
    a = jax.nn.sigmoid(rwkv_a0 + ral @ rwkv_wa2.astype(F32))
    kk = l2norm(heads(rk * rwkv_k_k))
    rk = rk * (1.0 + (a - 1.0) * rwkv_k_a)
    r_h, k_h, v_h = heads(rr), heads(rk), heads(rv)
    o, new_rwkv = rwkv7_mix(r_h, heads(log_w), k_h, v_h, kk, heads(a), st_rwkv.astype(F32))
    mu = jnp.mean(o, axis=-1, keepdims=True)
    var = jnp.mean(jnp.square(o - mu), axis=-1, keepdims=True)
    o = ((o - mu) * lax.rsqrt(var + RWKV_GN_EPS)).reshape(bsz, seq, D_BRANCH) * rwkv_ln_g + rwkv_ln_b
    bonus = jnp.sum(r_h * k_h * rwkv_r_k, axis=-1, keepdims=True) * v_h
    o_rwkv = (o + bonus.reshape(bsz, seq, D_BRANCH)) * jax.nn.silu(rwkv_gate)

    mixed = jnp.concatenate([o_gla, o_s5, o_gdn, o_rwkv], axis=-1)
    out = jnp.matmul(mixed, w_out.astype(F32))
    return x + out.astype(x.dtype), (new_gla, new_s5_re, new_s5_im, new_gdn, new_conv, new_rwkv, new_shift)


def run_trunk(x, init_state, layer_params, final_g):
    new = [[] for _ in init_state]
    for layer in range(DEPTH):
        x, st = hybrid_layer(x, tuple(s[layer] for s in init_state), tuple(w[layer] for w in layer_params))
        for acc, s in zip(new, st):
            acc.append(s)
    return rms_norm(x, final_g), tuple(jnp.stack(acc) for acc in new)


def setup_inputs(seed: int = 0) -> dict:
    key = jax.random.key(seed)
    ks = iter(jax.random.split(key, 48))

    def nrm(shape, scale):
        return jax.random.normal(next(ks), shape, F32) * scale

    def uni(shape, lo, hi):
        return jax.random.uniform(next(ks), shape, F32, lo, hi)

    L = DEPTH
    mat = (L, DEC_BATCH, N_HEADS, HEAD_DIM, HEAD_DIM)
    x_prompt = nrm((BATCH, SEQ, D_MODEL), 1.0)
    x_sample = nrm((DEC_BATCH, DEC_SEQ, D_MODEL), 1.0)
    state_gla = nrm(mat, 2.0)
    state_s5_re = nrm((L, DEC_BATCH, S5_GROUPS, S5_STATE), 0.1)
    state_s5_im = nrm((L, DEC_BATCH, S5_GROUPS, S5_STATE), 0.1)
    state_gdn = nrm(mat, 0.3)
    state_gdn_conv = nrm((L, DEC_BATCH, GDN_CONV - 1, 3 * D_BRANCH), 1.0)
    state_rwkv = nrm(mat, 0.5)
    state_rwkv_shift = nrm((L, DEC_BATCH, RWKV_SHIFT_W), 1.0)
    norm_g = 1.0 + nrm((L, D_MODEL), 0.02)
    w_in = nrm((L, D_MODEL, D_IN), D_MODEL ** -0.5)
    gla_wg2 = nrm((L, GLA_GATE_RANK, D_BRANCH), GLA_GATE_RANK ** -0.5)
    gla_bg = uni((L, D_BRANCH), 1.0, 3.0)
    gla_norm_g = 1.0 + nrm((L, HEAD_DIM), 0.02)
    s5_lam_re = -0.5 + nrm((L, S5_GROUPS, S5_STATE), 0.01)
    s5_lam_im = math.pi * jnp.arange(S5_STATE, dtype=F32) + nrm((L, S5_GROUPS, S5_STATE), 0.01)
    s5_log_step = uni((L, S5_GROUPS), math.log(1e-3), math.log(1e-1))
    s5_b_re = nrm((L, S5_GROUPS, S5_STATE, S5_CH), (2 * S5_CH) ** -0.5)
    s5_b_im = nrm((L, S5_GROUPS, S5_STATE, S5_CH), (2 * S5_CH) ** -0.5)
    s5_c_re = nrm((L, S5_GROUPS, S5_CH, S5_STATE), S5_STATE ** -0.5)
    s5_c_im = nrm((L, S5_GROUPS, S5_CH, S5_STATE), S5_STATE ** -0.5)
    s5_d = nrm((L, D_BRANCH), 1.0)
    s5_w_glu = nrm((L, D_BRANCH, D_BRANCH), D_BRANCH ** -0.5)
    s5_b_glu = nrm((L, D_BRANCH), 0.02)
    gdn_conv_w = nrm((L, GDN_CONV, 3 * D_BRANCH), GDN_CONV ** -0.5)
    gdn_a_log = jnp.log(uni((L, N_HEADS), 1.0, 16.0))
    dt = jnp.exp(uni((L, N_HEADS), math.log(1e-3), math.log(1e-1)))
    gdn_dt_bias = dt + jnp.log(-jnp.expm1(-dt))
    gdn_norm_g = 1.0 + nrm((L, HEAD_DIM), 0.02)
    rwkv_mu = uni((L, RWKV_SHIFT_W), 0.0, 1.0)
    rwkv_w0 = uni((L, D_BRANCH), -6.5, -1.5)
    rwkv_ww2 = nrm((L, RWKV_LORA_W, D_BRANCH), 0.1)
    rwkv_a0 = nrm((L, D_BRANCH), 0.1)
    rwkv_wa2 = nrm((L, RWKV_LORA_A, D_BRANCH), RWKV_LORA_A ** -0.5)
    rwkv_k_k = 0.85 + nrm((L, D_BRANCH), 0.02)
    rwkv_k_a = 1.0 + nrm((L, D_BRANCH), 0.02)
    rwkv_r_k = nrm((L, N_HEADS, HEAD_DIM), 0.1)
    rwkv_ln_g = 1.0 + nrm((L, D_BRANCH), 0.02)
    rwkv_ln_b = nrm((L, D_BRANCH), 0.02)
    w_out = nrm((L, D_MIX, D_MODEL), 0.5 * D_MIX ** -0.5)
    final_g = 1.0 + nrm((D_MODEL,), 0.02)
    return {'x_prompt': x_prompt, 'x_sample': x_sample,
            'state_gla': state_gla, 'state_s5_re': state_s5_re, 'state_s5_im': state_s5_im,
            'state_gdn': state_gdn, 'state_gdn_conv': state_gdn_conv,
            'state_rwkv': state_rwkv, 'state_rwkv_shift': state_rwkv_shift,
            'norm_g': norm_g, 'w_in': w_in, 'gla_wg2': gla_wg2, 'gla_bg': gla_bg, 'gla_norm_g': gla_norm_g,
            's5_lam_re': s5_lam_re, 's5_lam_im': s5_lam_im, 's5_log_step': s5_log_step,
            's5_b_re': s5_b_re, 's5_b_im': s5_b_im, 's5_c_re': s5_c_re, 's5_c_im': s5_c_im,
            's5_d': s5_d, 's5_w_glu': s5_w_glu, 's5_b_glu': s5_b_glu,
            'gdn_conv_w': gdn_conv_w, 'gdn_a_log': gdn_a_log, 'gdn_dt_bias': gdn_dt_bias, 'gdn_norm_g': gdn_norm_g,
            'rwkv_mu': rwkv_mu, 'rwkv_w0': rwkv_w0, 'rwkv_ww2': rwkv_ww2, 'rwkv_a0': rwkv_a0,
            'rwkv_wa2': rwkv_wa2, 'rwkv_k_k': rwkv_k_k, 'rwkv_k_a': rwkv_k_a, 'rwkv_r_k': rwkv_r_k,
            'rwkv_ln_g': rwkv_ln_g, 'rwkv_ln_b': rwkv_ln_b,
            'w_out': w_out, 'final_g': final_g}


def reference(x_prompt, x_sample, state_gla, state_s5_re, state_s5_im, state_gdn, state_gdn_conv,
              state_rwkv, state_rwkv_shift, norm_g, w_in, gla_wg2, gla_bg, gla_norm_g,
              s5_lam_re, s5_lam_im, s5_log_step, s5_b_re, s5_b_im, s5_c_re, s5_c_im, s5_d, s5_w_glu, s5_b_glu,
              gdn_conv_w, gdn_a_log, gdn_dt_bias, gdn_norm_g,
              rwkv_mu, rwkv_w0, rwkv_ww2, rwkv_a0, rwkv_wa2, rwkv_k_k, rwkv_k_a, rwkv_r_k, rwkv_ln_g, rwkv_ln_b,
              w_out, final_g):
    layer_params = (norm_g, w_in, gla_wg2, gla_bg, gla_norm_g,
                    s5_lam_re, s5_lam_im, s5_log_step, s5_b_re, s5_b_im, s5_c_re, s5_c_im, s5_d, s5_w_glu, s5_b_glu,
                    gdn_conv_w, gdn_a_log, gdn_dt_bias, gdn_norm_g,
                    rwkv_mu, rwkv_w0, rwkv_ww2, rwkv_a0, rwkv_wa2, rwkv_k_k, rwkv_k_a, rwkv_r_k, rwkv_ln_g, rwkv_ln_b,
                    w_out)
    sample_state = (state_gla, state_s5_re, state_s5_im, state_gdn, state_gdn_conv, state_rwkv, state_rwkv_shift)
    prompt_state = tuple(jnp.zeros((DEPTH, x_prompt.shape[0]) + s.shape[2:], F32) for s in sample_state)
    y_prompt, (gla_p, s5_re_p, s5_im_p, gdn_p, gdn_conv_p, rwkv_p, rwkv_shift_p) = run_trunk(
        x_prompt, prompt_state, layer_params, final_g)
    y_sample, (gla_s, s5_re_s, s5_im_s, gdn_s, gdn_conv_s, rwkv_s, rwkv_shift_s) = run_trunk(
        x_sample, sample_state, layer_params, final_g)
    return (y_prompt, y_sample,
            gla_p, s5_re_p, s5_im_p, gdn_p, gdn_conv_p, rwkv_p, rwkv_shift_p,
            gla_s, s5_re_s, s5_im_s, gdn_s, gdn_conv_s, rwkv_s, rwkv_shift_s)
```

```python
import contextlib
import numpy as np
import concourse.bass as bass
import concourse.mybir as mybir

F32 = mybir.dt.float32
BF16 = mybir.dt.bfloat16
I32 = mybir.dt.int32
AF = mybir.ActivationFunctionType
OP = mybir.AluOpType
AX = mybir.AxisListType

ENGS = ("pe", "act", "dve", "pool", "sp")
SKIP_SAME_ENGINE = False


class T:
    def __init__(self, name, handle):
        self.name = name
        self.h = handle
        self.is_psum = False
        self.w = None
        self.r = []

    def __getitem__(self, idx):
        return V(self, self.h[idx])

    def ap(self):
        return V(self, self.h[:])


class V:
    def __init__(self, t, ap):
        self.t = t
        self.a = ap

    def __getitem__(self, idx):
        return V(self.t, self.a[idx])

    def re(self, pat, **kw):
        return V(self.t, self.a.rearrange(pat, **kw))

    def bc(self, shape):
        return V(self.t, self.a.broadcast_to(shape))

    def bitcast(self, dt):
        return V(self.t, self.a.bitcast(dt))

    def ap(self):
        return self


class Prog:
    def __init__(self, nc, n_dma_sems=24):
        self.nc = nc
        self.st = contextlib.ExitStack()
        self.q = {e: [] for e in ENGS}
        self.cnt = {e: 0 for e in ENGS}
        self.sem = {e: self.st.enter_context(nc.semaphore("pg_" + e)) for e in ENGS}
        self.known = {e: {} for e in ENGS}
        self.dsem = [self.st.enter_context(nc.semaphore(f"dma{i}")) for i in range(n_dma_sems)]
        self.dcnt = [0] * n_dma_sems
        self.dnext = 0
        self.pool_sems = []
        self.pool_used = 0
        self.tiles = {}
        self.n_inst = 0

    def sb(self, name, shape, dt=F32):
        h = self.st.enter_context(self.nc.sbuf_tensor(name, list(shape), dt))
        t = T(name, h)
        self.tiles[name] = t
        return t

    def sbc(self, name, shape, dt=F32):
        if name in self.tiles:
            return self.tiles[name]
        return self.sb(name, shape, dt)

    def ps(self, name, shape, dt=F32):
        h = self.st.enter_context(self.nc.psum_tensor(name, list(shape), dt))
        t = T(name, h)
        t.is_psum = True
        self.tiles[name] = t
        return t

    def dram(self, name, shape, dt=F32, kind="Internal"):
        h = self.nc.dram_tensor(name, list(shape), dt, kind=kind)
        t = T(name, h.ap())
        self.tiles[name] = t
        return t

    def _need(self, eng, dep):
        if dep is None:
            return
        kind, i, c = dep
        if kind == "e" and i == eng and (SKIP_SAME_ENGINE or eng == "pe"):
            return
        key = (kind, i)
        if self.known[eng].get(key, 0) >= c:
            return
        self.known[eng][key] = c
        sem = self.sem[i] if kind == "e" else self.dsem[i]
        self.q[eng].append(lambda e, sem=sem, c=c: e.wait_ge(sem, c))

    def _deps(self, eng, reads, writes):
        for v in reads:
            if v is None or not isinstance(v, V):
                continue
            self._need(eng, v.t.w)
        for v in writes:
            self._need(eng, v.t.w)
            for r in v.t.r:
                self._need(eng, r)

    def _mark(self, token, reads, writes):
        for v in reads:
            if v is None or not isinstance(v, V):
                continue
            v.t.r.append(token)
            if len(v.t.r) > 64:
                best = {}
                for k, i, c in v.t.r:
                    best[(k, i)] = max(best.get((k, i), 0), c)
                v.t.r = [(k, i, c) for (k, i), c in best.items()]
        for v in writes:
            v.t.w = token
            v.t.r = []

    def op(self, eng, fn, reads, writes):
        if eng != "pe":
            writes = list(writes) + [v for v in reads if isinstance(v, V) and v.t.is_psum]
        self._deps(eng, reads, writes)
        self.cnt[eng] += 1
        c = self.cnt[eng]
        sem = self.sem[eng]
        self.q[eng].append(lambda e, fn=fn, sem=sem: fn(e).then_inc(sem, 1))
        self.known[eng][("e", eng)] = max(self.known[eng].get(("e", eng), 0), 0)
        self._mark(("e", eng, c), reads, writes)
        self.n_inst += 1

    def dma(self, out, in_, eng="sp", **kw):
        self._deps(eng, [in_], [out])
        if eng == "pool" and self.pool_used < 40:
            self.dsem.append(self.st.enter_context(self.nc.semaphore(f"pdma{self.pool_used}")))
            self.dcnt.append(0)
            self.pool_used += 1
            i = len(self.dsem) - 1
        else:
            i = self.dnext
            self.dnext = (self.dnext + 1) % 24
        self.dcnt[i] += 16
        c = self.dcnt[i]
        sem = self.dsem[i]
        oa, ia = out.a, in_.a
        self.q[eng].append(lambda e, oa=oa, ia=ia, sem=sem, kw=kw: e.dma_start(out=oa, in_=ia, **kw).then_inc(sem, 16))
        self._mark(("d", i, c), [in_], [out])
        self.n_inst += 1

    def barrier(self):
        for e in ENGS:
            for o in ENGS:
                if o != e and self.cnt[o]:
                    self._need(e, ("e", o, self.cnt[o]))
            for i, c in enumerate(self.dcnt):
                if c:
                    self._need(e, ("d", i, c))

    def wait_all_dma(self, eng="sp"):
        for i, c in enumerate(self.dcnt):
            if c:
                self._need(eng, ("d", i, c))

    def mm(self, out, lhsT, rhs, start=True, stop=True, **kw):
        reads = [lhsT, rhs] + ([] if start else [out])
        self.op("pe", lambda e: e.matmul(out.a, lhsT.a, rhs.a, start=start, stop=stop, **kw), reads, [out])

    def tr(self, out, in_, ident):
        if out.a.start_partition != 0 or in_.a.dtype != F32:
            return self.mm(out, in_, ident)
        self.op("pe", lambda e: e.transpose(out.a, in_.a, ident.a), [in_, ident], [out])

    def act(self, out, in_, func, bias=None, scale=None, accum=None, eng="act"):
        kw = {}
        reads = [in_]
        if bias is not None:
            kw["bias"] = bias.a if isinstance(bias, V) else bias
            reads.append(bias)
        if scale is not None:
            kw["scale"] = scale.a if isinstance(scale, V) else scale
            reads.append(scale)
        writes = [out]
        if accum is not None:
            kw["accum_out"] = accum.a
            writes.append(accum)
        self.op(eng, lambda e: e.activation(out.a, in_.a, func, **kw), reads, writes)

    def tt(self, out, a, b, op, eng="dve"):
        self.op(eng, lambda e: e.tensor_tensor(out.a, a.a, b.a, op), [a, b], [out])

    def ts(self, out, a, s1, op0, s2=None, op1=None, eng="dve", accum=None):
        reads = [a, s1, s2]
        x1 = s1.a if isinstance(s1, V) else s1
        x2 = s2.a if isinstance(s2, V) else s2
        kw = {}
        if op1 is not None:
            kw["op1"] = op1
        writes = [out]
        if accum is not None:
            kw["accum_out"] = accum.a
            writes.append(accum)
        self.op(eng, lambda e: e.tensor_scalar(out.a, a.a, x1, x2, op0, **kw), reads, writes)

    def stt(self, out, a, s, b, op0, op1, eng="dve"):
        x = s.a if isinstance(s, V) else s
        self.op(eng, lambda e: e.scalar_tensor_tensor(out.a, a.a, x, b.a, op0, op1), [a, s, b], [out])

    def cp(self, out, in_, eng="dve"):
        if eng == "act":
            self.op(eng, lambda e: e.copy(out.a, in_.a), [in_], [out])
        else:
            self.op(eng, lambda e: e.tensor_copy(out.a, in_.a), [in_], [out])

    def memset(self, out, val, eng="dve"):
        self.op(eng, lambda e: e.memset(out.a, val), [], [out])

    def scan(self, out, d0, d1, init, op0, op1):
        x = init.a if isinstance(init, V) else init
        self.op("dve", lambda e: e.tensor_tensor_scan(out.a, d0.a, d1.a, x, op0, op1), [d0, d1, init], [out])

    def reduce(self, out, in_, op, axis=AX.X):
        self.op("dve", lambda e: e.tensor_reduce(out.a, in_.a, axis, op), [in_], [out])

    def recip(self, out, in_):
        self.op("dve", lambda e: e.reciprocal(out.a, in_.a), [in_], [out])

    def iota(self, out, pattern, base=0, cm=0, **kw):
        self.op("pool", lambda e: e.iota(out.a, pattern, base=base, channel_multiplier=cm, **kw), [], [out])

    def affsel(self, out, in_, pattern, cmp, fill, base=0, cm=0):
        self.op("pool", lambda e: e.affine_select(out.a, in_.a, pattern, cmp, fill, base=base, channel_multiplier=cm),
                [in_], [out])

    def finish(self):
        self.wait_all_dma("sp")
        for e in ENGS:
            if e != "sp" and self.cnt[e]:
                self._need("sp", ("e", e, self.cnt[e]))
        nc = self.nc
        q = self.q
        with nc.Block() as block:
            @block.tensor
            def _(e):
                for f in q["pe"]:
                    f(e)

            @block.scalar
            def _(e):
                for f in q["act"]:
                    f(e)

            @block.vector
            def _(e):
                for f in q["dve"]:
                    f(e)

            @block.gpsimd
            def _(e):
                for f in q["pool"]:
                    f(e)

            @block.sync
            def _(e):
                for f in q["sp"]:
                    f(e)
        self.st.close()


import math
ST = 128
C = 64
NCH = ST // C
OFF = dict(gla_q=0, gla_k=256, gla_v=512, glr=768, gla_gate=784, s5_u=1040, s5_gate=1296,
           gdn_qkv=1552, gdn_ab=2320, gdn_gate=2328, rwkv_in=2584, rwkv_gate=3480)
D_IN = 3736
SHAPES = dict(
    x_prompt=[2048, 1024], x_sample=[16, 1024],
    state_gla=[4, 16, 4, 64, 64], state_s5_re=[4, 16, 16, 64], state_s5_im=[4, 16, 16, 64],
    state_gdn=[4, 16, 4, 64, 64], state_gdn_conv=[4, 16, 3, 768], state_rwkv=[4, 16, 4, 64, 64],
    state_rwkv_shift=[4, 16, 896],
    norm_g=[4, 1024], w_in=[4, 1024, 3736], gla_wg2=[4, 16, 256], gla_bg=[4, 256], gla_norm_g=[4, 64],
    s5_lam_re=[4, 16, 64], s5_lam_im=[4, 16, 64], s5_log_step=[4, 16], s5_b_re=[4, 16, 64, 16],
    s5_b_im=[4, 16, 64, 16], s5_c_re=[4, 16, 16, 64], s5_c_im=[4, 16, 16, 64], s5_d=[4, 256],
    s5_w_glu=[4, 256, 256], s5_b_glu=[4, 256], gdn_conv_w=[4, 4, 768], gdn_a_log=[4, 4], gdn_dt_bias=[4, 4],
    gdn_norm_g=[4, 64], rwkv_mu=[4, 896], rwkv_w0=[4, 256], rwkv_ww2=[4, 64, 256], rwkv_a0=[4, 256],
    rwkv_wa2=[4, 64, 256], rwkv_k_k=[4, 256], rwkv_k_a=[4, 256], rwkv_r_k=[4, 4, 64], rwkv_ln_g=[4, 256],
    rwkv_ln_b=[4, 256], w_out=[4, 1024, 1024], final_g=[1024])
OUT_SHAPES = dict(
    y_p=[2048, 1024], y_s=[16, 1024],
    gla_p=[4, 4, 64, 64], s5re_p=[4, 16, 64], s5im_p=[4, 16, 64], gdn_p=[4, 4, 64, 64], conv_p=[4, 3, 768],
    rwkv_p=[4, 4, 64, 64], shift_p=[4, 896],
    gla_s=[4, 16, 4, 64, 64], s5re_s=[4, 16, 16, 64], s5im_s=[4, 16, 16, 64], gdn_s=[4, 16, 4, 64, 64],
    conv_s=[4, 16, 3, 768], rwkv_s=[4, 16, 4, 64, 64], shift_s=[4, 16, 896])
OUT_ORDER = ["y_p", "y_s", "gla_p", "s5re_p", "s5im_p", "gdn_p", "conv_p", "rwkv_p", "shift_p",
             "gla_s", "s5re_s", "s5im_s", "gdn_s", "conv_s", "rwkv_s", "shift_s"]


def build(DEPTH=4, NST=8, SAMPLE=True, MIX=("gla", "s5", "gdn", "rwkv"), STREAMS=2, PSMODE=0):
    nc = bass.Bass("TRN2", target_bir_lowering=False, dynamic_dma_scratch_size=4096)
    p = Prog(nc)
    din = {k: p.dram(k, v, F32, kind="ExternalInput") for k, v in SHAPES.items()}
    dout = {k: p.dram(k, v, F32, kind="ExternalOutput") for k, v in OUT_SHAPES.items()}
    xbuf = p.dram("xbuf", [2048, 1024], F32)
    NS = True

    def rows(hp):
        return slice(64 * hp, 64 * hp + 64)

    ident = p.sb("ident", [128, 128])
    p.memset(ident.ap(), 1.0, eng="pool")
    p.affsel(ident.ap(), ident.ap(), [[-1, 128]], OP.is_equal, 0.0, base=0, cm=1)
    identb = p.sb("identb", [128, 128], BF16)
    p.cp(identb.ap(), ident.ap())
    identP = p.sb("identP", [128, 2, 64])
    for tl in range(2):
        p.cp(identP[0:64, tl, :], ident[0:64, 0:64])
        p.cp(identP[64:128, tl, :], ident[64:128, 64:128])
    identPb = p.sb("identPb", [128, 2, 64], BF16)
    p.cp(identPb.ap(), identP.ap())
    mI = p.sb("mI", [128, 64])
    mS = p.sb("mS", [128, 64])
    for hp in range(2):
        p.memset(mI[rows(hp), :], 1.0, eng="pool")
        p.affsel(mI[rows(hp), :], mI[rows(hp), :], [[1, 64]], OP.is_ge, 0.0, base=0, cm=-1)
        p.memset(mS[rows(hp), :], 1.0, eng="pool")
        p.affsel(mS[rows(hp), :], mS[rows(hp), :], [[1, 64]], OP.is_ge, 0.0, base=-1, cm=-1)
    nmI = p.sb("nmI", [128, 64])
    p.ts(nmI.ap(), mI.ap(), -1.0, OP.mult)
    mS4 = p.sb("mS4", [128, 4, 64])
    mI4 = p.sb("mI4", [128, 4, 64])
    for i in range(4):
        p.ts(mS4[:, i, :], mS.ap(), -1.0 if i < 2 else 1.0, OP.mult)
        p.ts(mI4[:, i, :], mI.ap(), 1.0 if i < 2 else -1.0, OP.mult)
    bones = p.sb("bones", [128, 128])
    p.memset(bones.ap(), 0.0)
    p.memset(bones[0:64, 0:64], 1.0)
    p.memset(bones[64:128, 64:128], 1.0)
    Eg = p.sb("Eg", [8, 2, 128])
    Eb = p.sb("Eb", [8, 2, 128])
    for E, sh in ((Eg, 0), (Eb, 4)):
        p.memset(E.ap(), 1.0, eng="pool")
        p.affsel(E.ap(), E.ap(), [[128, 2], [1, 128]], OP.is_ge, 0.0, base=64 * sh, cm=-64)
        p.affsel(E.ap(), E.ap(), [[-128, 2], [-1, 128]], OP.is_ge, 0.0, base=63 - 64 * sh, cm=64)
    dmask = p.sb("dmask", [16, 16, 64])
    p.memset(dmask.ap(), 1.0, eng="pool")
    p.affsel(dmask.ap(), dmask.ap(), [[-1, 16], [0, 64]], OP.is_equal, 0.0, base=0, cm=1)
    tidx = p.sb("tidx", [128, ST])
    p.iota(tidx.ap(), [[1, ST]], base=0, cm=0, allow_small_or_imprecise_dtypes=True)
    ones = p.sb("ones", [128, ST])
    p.memset(ones.ap(), 1.0)
    fg = p.sb("fg", [128, 1024])
    p.dma(fg.ap(), din["final_g"].ap().re("(o n) -> o n", o=1).bc([128, 1024]))

    PJ = [p.ps("PJ0", [128, 512]), p.ps("PJ1", [128, 512])]
    BK = {nm: p.ps(nm, [128, 512]) for nm in ("PT_A", "PA_A", "PX_A", "PT_B", "PA_B", "PX_B")}

    def psset(sfx):
        b1, b2, b3 = BK["PT_" + sfx], BK["PA_" + sfx], BK["PX_" + sfx]
        return (b2, b1, b3, b2, b1, b1)
    PS_A, PS_B = psset("A"), psset("B")
    if PSMODE == 1:
        PS_A = PS_B = (BK["PA_A"], BK["PX_A"], BK["PT_B"], BK["PA_B"], BK["PX_B"], BK["PT_A"])
    PS_FULL = (BK["PA_A"], BK["PX_A"], BK["PT_B"], BK["PA_B"], BK["PX_B"], BK["PT_A"])
    PT = BK["PT_A"]
    PB = BK["PX_A"]

    def v3(ps, n):
        return ps[:, 0:n * 64].re("p (a b) -> p a b", b=64)

    WGRP = [(1552, 2584), (2584, 3736), (0, 1040), (1040, 1552)]
    Wins = [p.sb(f"Win{i}", [128, 8, c1 - c0], BF16) for i, (c0, c1) in enumerate(WGRP)]
    Wout = p.sb("Wout", [128, 8, 1024], BF16)
    xt = p.sb("xt", [128, 1, 1024])
    xn = p.sb("xn", [128, 1024])
    hT = p.sb("hT", [128, 8, ST], BF16)
    mixTs = [p.sb(f"mixT{i}", [128, 2, ST], BF16) for i in range(4)]
    ss = p.sb("ss", [128, 1])
    ng = p.sb("ng", [128, 8])
    cnt = [0]

    RAW = p.sb("RAW", [128, 10240])
    carve_off = [0]

    def carve(name, shape, dt=F32):
        n = 1
        for d in shape[1:]:
            n *= d
        n32 = n if dt != BF16 else (n + 1) // 2
        a = RAW.h[0:shape[0], carve_off[0]:carve_off[0] + n32]
        carve_off[0] += n32
        assert carve_off[0] <= 10240, carve_off[0]
        if dt != F32:
            a = a.bitcast(dt)
        if len(shape) > 2:
            names = " ".join(f"d{i}" for i in range(1, len(shape)))
            a = a.rearrange(f"p ({names}) -> p {names}", **{f"d{i}": shape[i] for i in range(1, len(shape) - 1)})
        t = type(ones)(name, a)
        p.tiles[name] = t
        return t

    def make_set(sfx, alloc):
        W = {}
        for nm in ["qT", "kT", "vT", "aT", "bT", "ldT", "gate", "oT", "t0", "t1", "t2", "t3", "bon"]:
            W[nm] = alloc(f"w{sfx}_" + nm, [128, 2, ST])
        W["ones"] = ones
        W["g0"] = alloc(f"w{sfx}_g0", [128, 4])
        W["g0a"] = alloc(f"w{sfx}_g0a", [128, 4, 8])
        K = {}
        for nm in ["cum", "cumx", "E", "qs", "as", "ks", "bs", "kd", "bd", "X", "R2", "Ut", "WtT", "U", "araw", "qraw", "vb", "Hb"]:
            K[nm] = alloc(f"k{sfx}_" + nm, [128, 2, 64], F32 if nm in ("cum", "cumx", "E", "Ut") else BF16)
        K["identb"] = identb
        K["gam"] = alloc(f"k{sfx}_gam4", [128, 4, 64])
        K["tok"] = alloc(f"k{sfx}_tok", [128, 8, 64], BF16)
        K["gtok"] = alloc(f"k{sfx}_gtok", [128, 2, 64])
        K["amat"] = alloc(f"k{sfx}_amat", [128, 8, 64], BF16)
        K["BB"] = [alloc(f"k{sfx}_BB0", [128, 4, 64], BF16), alloc(f"k{sfx}_BB1", [128, 4, 64], BF16)]
        K["PC"] = alloc(f"k{sfx}_PC", [128, 2])
        K["BBd"] = [alloc(f"k{sfx}_BBd0", [128, 4, 128], BF16), alloc(f"k{sfx}_BBd1", [128, 4, 128], BF16)]
        K["identPb"] = identPb
        for nm in ("ksd", "bsd", "WtTd", "Hbd", "Udd"):
            K[nm] = alloc(f"k{sfx}_" + nm, [128, 2, 128], BF16)
        K["tokd"] = alloc(f"k{sfx}_tokd", [128, 6, 128], BF16)
        K["BDS"] = K["BBd"] + [K[nm] for nm in ("ksd", "bsd", "WtTd", "Hbd", "Udd", "tokd")]
        return W, K
    W, K = make_set("A", p.sb)
    carve_off[0] = 0
    W2, K2 = make_set("B", carve)
    W2["rin"] = carve("wB_rin", [128, 7, ST + 1])
    W2["xs7"] = carve("wB_xs7", [128, 7, ST])
    W2["PH"] = BK["PA_B"]
    W["PH"] = BK["PA_B"]
    set_b_end = carve_off[0]
    W["xp"] = p.sb("wA_xp", [128, 6, ST + 3])
    W["cv"] = p.sb("wA_cv", [128, 6, ST])
    W["abT"] = p.sb("w_abT", [8, ST])
    W["gf"] = p.sb("w_gf", [8, ST])
    W["bf"] = p.sb("w_bf", [8, ST])
    W["rin"] = W["xp"]
    carve_off[0] = 0
    xs_s = p.sb("xs_s", [16, 1024])
    big1 = carve("big1", [128, 2304])
    W["Snat"] = big1[:, 0:2048].re("p (a b c) -> p a b c", a=2, b=16)
    W["s5nat"] = big1[0:16, 0:2048].re("p (a b) -> p a b", a=2)
    W["cnat"] = big1[0:16, 0:2304].re("p (a b) -> p a b", a=3)
    W["snat"] = big1[0:16, 0:896]
    Hs1 = carve("Hs1", [128, 2, 16, 64])
    W["Dd"] = carve("w_Dd", [128, 2, 16])
    W["tokS"] = carve("w_tokS", [16, 3, 256])
    W["Ud"] = carve("w_Ud", [16, 16, 64])
    W["Vd"] = carve("w_Vd", [16, 16, 64])
    W["tmpd"] = W["Ud"]
    W["oTok"] = carve("w_oTok", [16, 256])
    W["hS"] = carve("w_hS", [128, 2, 8, 16])
    W["xsS"] = carve("w_xsS", [128, 6, 4, 16])
    W["prevS"] = carve("w_prevS", [128, 7, 16])
    W["rinS"] = carve("w_rinS", [128, 7, 16])
    W["xs7S"] = carve("w_xs7S", [128, 7, 16])
    H = {m: p.sb("H_" + m, [128, 2, 64]) for m in ("gla", "gdn", "rwkv")}
    Hs = {m: Hs1 for m in ("gla", "gdn", "rwkv")}

    def colvec(name, src, n):
        t = p.sb(name, [128, n // 128])
        p.dma(t.ap(), src.re("(k p) -> p k", p=128), allow_slow_non_contiguous=NS)
        return t

    def proj(dst, col0, n, T, scale=1.0):
        pj = PJ[cnt[0] % 2]
        cnt[0] += 1
        gi = [i for i, (c0, c1) in enumerate(WGRP) if c0 <= col0 < c1][0]
        Wg, cb = Wins[gi], col0 - WGRP[gi][0]
        for k in range(8):
            p.mm(pj[0:n, 0:T], Wg[:, k, cb:cb + n], hT[:, k, 0:T], start=(k == 0), stop=(k == 7))
        p.act(dst, pj[0:n, 0:T], AF.Identity, scale=scale)

    def rstd_inplace(t, T, mult, eps):
        p.ts(t, t, mult, OP.mult, eps, OP.add)
        p.act(t, t, AF.Sqrt)
        p.recip(t, t)

    def headsum(dst, src, T):
        for tl in range(2):
            pj = PJ[cnt[0] % 2]
            cnt[0] += 1
            p.mm(pj[:, 0:T], bones.ap(), src[:, tl, 0:T])
            p.cp(dst[:, tl, 0:T], pj[:, 0:T], eng="act")

    def silu_(dst, src):
        p.act(dst, src, AF.Silu)

    for l in range(DEPTH):
        for i, (c0, c1) in enumerate(WGRP):
            p.dma(Wins[i].ap(), din["w_in"][l, :, c0:c1].re("(k q) c -> q k c", q=128), eng="pool")
        p.dma(Wout.ap(), din["w_out"][l].re("(k q) c -> q k c", q=128), eng="pool")
        p.dma(ng.ap(), din["norm_g"][l].re("(k p) -> p k", p=128), allow_slow_non_contiguous=NS)
        L = {}
        wg2 = p.sbc(f"wg2", [32, 256])
        p.memset(wg2.ap(), 0.0)
        p.dma(wg2[0:16, :], din["gla_wg2"][l])
        p.dma(wg2[16:17, :], din["gla_bg"][l].re("(o n) -> o n", o=1))
        gla_ng = p.sbc(f"gla_ng", [128, 1])
        gdn_ng = p.sbc(f"gdn_ng", [128, 1])
        for hp in range(2):
            p.dma(gla_ng[rows(hp), :], din["gla_norm_g"][l].re("(n o) -> n o", o=1), allow_slow_non_contiguous=NS)
            p.dma(gdn_ng[rows(hp), :], din["gdn_norm_g"][l].re("(n o) -> n o", o=1), allow_slow_non_contiguous=NS)
        s5d = colvec(f"s5d_{l}", din["s5_d"][l], 256)
        s5bg = colvec(f"s5bg_{l}", din["s5_b_glu"][l], 256)
        wglu = p.sbc(f"wglu", [128, 2, 256])
        p.dma(wglu.ap(), din["s5_w_glu"][l].re("(k p) n -> p k n", p=128))
        convw = p.sbc(f"convw", [128, 6, 4])
        for i in range(4):
            p.dma(convw[:, :, i], din["gdn_conv_w"][l, i].re("(j p) -> p j", p=128), allow_slow_non_contiguous=NS)
        gab = p.sbc(f"gab", [8, 2])
        p.memset(gab.ap(), 0.0)
        p.dma(gab[0:4, 0:1], din["gdn_dt_bias"][l].re("(n o) -> n o", o=1), allow_slow_non_contiguous=NS)
        p.dma(gab[0:4, 1:2], din["gdn_a_log"][l].re("(n o) -> n o", o=1), allow_slow_non_contiguous=NS)
        p.act(gab[:, 1:2], gab[:, 1:2], AF.Exp)
        p.ts(gab[:, 1:2], gab[:, 1:2], -1.0, OP.mult)
        mu = colvec(f"mu_{l}", din["rwkv_mu"][l], 896)
        w0 = colvec(f"w0_{l}", din["rwkv_w0"][l], 256)
        a0 = colvec(f"a0_{l}", din["rwkv_a0"][l], 256)
        k_k = colvec(f"kk_{l}", din["rwkv_k_k"][l], 256)
        k_a = colvec(f"ka_{l}", din["rwkv_k_a"][l], 256)
        r_k = colvec(f"rk_{l}", din["rwkv_r_k"][l].re("h n -> (h n)"), 256)
        ln_g = colvec(f"lng_{l}", din["rwkv_ln_g"][l], 256)
        ln_b = colvec(f"lnb_{l}", din["rwkv_ln_b"][l], 256)
        wlo = p.sbc(f"wlo", [128, 256])
        p.dma(wlo[0:64, :], din["rwkv_ww2"][l])
        p.dma(wlo[64:128, :], din["rwkv_wa2"][l])

        S5 = s5_setup(p, nc, din, l, ident, tidx, PT, PB, rows) if "s5" in MIX else None

        for m in H:
            p.memset(H[m].ap(), 0.0)
        for KK in (K, K2):
            for t_ in KK["BDS"]:
                p.memset(t_.ap(), 0.0)
        hist_gdn = p.sbc(f"hist_gdn", [128, 6, 3])
        hist_rwkv = p.sbc(f"hist_rwkv", [128, 7, 1])
        p.memset(hist_gdn.ap(), 0.0)
        p.memset(hist_rwkv.ap(), 0.0)
        if S5 is not None:
            p.memset(S5["hre"].ap(), 0.0)
            p.memset(S5["him"].ap(), 0.0)

        common = dict(p=p, proj=proj, OFF=OFF, PJ=PJ, cnt=cnt, ident=ident, identP=identP, mI=mI, mS=mS, nmI=nmI,
                      mI4=mI4, mS4=mS4, dmask=dmask, rows=rows, v3=v3, headsum=headsum, rstd_inplace=rstd_inplace,
                      din=din, dout=dout, l=l, NST=NST, H=H, Hs=Hs, mixTs=mixTs)
        LW = dict(wg2=wg2, gla_ng=gla_ng, gdn_ng=gdn_ng, s5d=s5d, s5bg=s5bg, wglu=wglu, convw=convw, gab=gab, Eg=Eg,
                  Eb=Eb, mu=mu, w0=w0, a0=a0, k_k=k_k, k_a=k_a, r_k=r_k, ln_g=ln_g, ln_b=ln_b, wlo=wlo,
                  hist_gdn=hist_gdn, hist_rwkv=hist_rwkv, S5=S5)

        def run_streams(gens):
            gens = [g for g in gens if g is not None]
            while gens:
                for g in list(gens):
                    try:
                        next(g)
                    except StopIteration:
                        gens.remove(g)

        def chain(*gs):
            for g in gs:
                if g is not None:
                    yield from g

        groups = [("p", st) for st in range(NST)] + ([("s", 0)] if SAMPLE else [])
        for kind, st in groups:
            T = ST if kind == "p" else 16
            last_layer = (l == DEPTH - 1)
            if kind == "s":
                p.barrier()
            if kind == "p":
                r0 = st * ST
                src = din["x_prompt"] if l == 0 else xbuf
                xv = xt[:, 0, :]
                p.dma(xv, src[r0:r0 + 128, :])
                np_ = 128
            else:
                xv = xs_s[0:16, :]
                if l == 0:
                    p.dma(xv, din["x_sample"].ap())
                np_ = 16
            p.act(xn[0:np_, :], xv, AF.Square, accum=ss[0:np_, :])
            rstd_inplace(ss[0:np_, :], 1, 1.0 / 1024, 1e-6)
            p.ts(xn[0:np_, :], xv, ss[0:np_, :], OP.mult)
            for kk in range(2):
                for j in range(4):
                    k = kk * 4 + j
                    p.tr(PT[:, j * 128:j * 128 + np_], xn[0:np_, k * 128:(k + 1) * 128], ident[0:np_, 0:np_])
                p.tt(hT[:, kk * 4:kk * 4 + 4, 0:np_],
                     PT[:, 0:512].re("p (a b) -> p a b", b=128)[:, :, 0:np_],
                     ng[:, kk * 4:kk * 4 + 4, None].bc([128, 4, np_]), OP.mult)

            if kind == "p":
                ga = chain(gdn_block(kind, st, T, W, K, PS_A, LW, **common) if "gdn" in MIX else None,
                           gla_block(kind, st, T, W, K, PS_A, LW, **common) if "gla" in MIX else None)
                gb = chain(rwkv_block(kind, st, T, W2, K2, PS_B, LW, **common) if "rwkv" in MIX else None,
                           s5_block(kind, st, T, W2, PS_B, LW, **common) if "s5" in MIX else None)
                if STREAMS == 2:
                    run_streams([ga, gb])
                else:
                    run_streams([chain(ga, gb)])
            else:
                W["rin"], W["xs7"] = W["rinS"], W["xs7S"]
                run_streams([chain(gla_block(kind, st, T, W, K, PS_FULL, LW, **common) if "gla" in MIX else None,
                                   s5_block(kind, st, T, W, PS_FULL, LW, **common) if "s5" in MIX else None,
                                   gdn_block(kind, st, T, W, K, PS_FULL, LW, **common) if "gdn" in MIX else None,
                                   rwkv_block(kind, st, T, W, K, PS_FULL, LW, **common) if "rwkv" in MIX else None)])

            for i, m in enumerate(("gla", "s5", "gdn", "rwkv")):
                if m not in MIX:
                    p.memset(mixTs[i][:, :, 0:T], 0.0)

            for half in range(2):
                pj = PJ[cnt[0] % 2]
                cnt[0] += 1
                for k in range(8):
                    p.mm(pj[0:np_, :], mixTs[k // 2][:, k % 2, 0:np_], Wout[:, k, half * 512:(half + 1) * 512],
                         start=(k == 0), stop=(k == 7))
                p.tt(xv[:, half * 512:(half + 1) * 512], xv[:, half * 512:(half + 1) * 512], pj[0:np_, :], OP.add)
            if not last_layer:
                if kind == "p":
                    p.dma(xbuf[r0:r0 + 128, :], xv)
            else:
                p.act(xn[0:np_, :], xv, AF.Square, accum=ss[0:np_, :])
                rstd_inplace(ss[0:np_, :], 1, 1.0 / 1024, 1e-6)
                p.stt(xn[0:np_, :], xv, ss[0:np_, :], fg[0:np_, :], OP.mult, OP.mult)
                if kind == "p":
                    p.dma(dout["y_p"][r0:r0 + 128, :], xn[0:np_, :])
                else:
                    p.dma(dout["y_s"].ap(), xn[0:np_, :])
            if kind == "s":
                p.barrier()
    p.finish()
    return nc, p


def gla_block(kind, st, T, W, K, PS, LW, *, p, proj, OFF, PJ, cnt, ident, identP, mI, mS, nmI, mI4, mS4, dmask, rows, v3,
              headsum, rstd_inplace, din, dout, l, NST, H, Hs, mixTs):
    qT, kT, vT, ldT, gate, oT = (W[n] for n in ("qT", "kT", "vT", "ldT", "gate", "oT"))
    wg2, gla_ng = LW["wg2"], LW["gla_ng"]
    for tl in range(2):
        proj(qT[:, tl, 0:T], OFF["gla_q"] + 128 * tl, 128, T, scale=0.125)
        yield
        proj(kT[:, tl, 0:T], OFF["gla_k"] + 128 * tl, 128, T)
        yield
        proj(vT[:, tl, 0:T], OFF["gla_v"] + 128 * tl, 128, T)
        yield
        proj(gate[:, tl, 0:T], OFF["gla_gate"] + 128 * tl, 128, T)
        yield
    glr = W["t0"]
    p.memset(glr[0:32, 0, 0:T], 1.0)
    proj(glr[0:16, 0, 0:T], OFF["glr"], 16, T)
    yield
    for tl in range(2):
        pj = PJ[cnt[0] % 2]
        cnt[0] += 1
        p.mm(pj[:, 0:T], wg2[0:17, tl * 128:(tl + 1) * 128], glr[0:17, 0, 0:T])
        p.act(ldT[:, tl, 0:T], pj[:, 0:T], AF.Exp, scale=-1.0)
        p.act(ldT[:, tl, 0:T], ldT[:, tl, 0:T], AF.Ln, bias=1.0)
        yield
    p.ts(ldT[:, :, 0:T], ldT[:, :, 0:T], -1.0 / 16, OP.mult)
    yield from mixer_core(p, "gla", kind, T, W, K, H["gla"], Hs["gla"], dict(ab=False, scalar=False),
                          PS, ident, identP, mI, mS, nmI, mI4, mS4, dmask, rows, v3,
                          din["state_gla"], dout["gla_s"], l, transposed_state=False)
    yield from out_norm_rms(p, oT, gate, gla_ng, T, W, headsum, rstd_inplace, mixTs[0])
    if kind == "p" and st == NST - 1:
        p.dma(dout["gla_p"][l].re("(t hp) d v -> (hp d) t v", hp=2), H["gla"].ap())


def mixer_core(p, name, kind, T, W, K, Hst, Hsamp, fl, PS, ident, identP, mI, mS, nmI, mI4, mS4, dmask, rows, v3,
               state_in, state_out, l, transposed_state):
    PA, PB, PU, PO, PH, PT = PS
    qT, aT, kT, bT, vT, ldT, oT = (W[n] for n in ("qT", "aT", "kT", "bT", "vT", "ldT", "oT"))
    ab, scalar = fl["ab"], fl["scalar"]
    if kind == "s":
        sample_core(p, name, W, Hsamp, fl, PS, ident, dmask, rows, v3, state_in, state_out, l, transposed_state)
        yield
        return
    cum, cumx, E, qs, as_, ks, bs, kd, bd, X, R2, Ut, WtT, U = (K[n] for n in (
        "cum", "cumx", "E", "qs", "as", "ks", "bs", "kd", "bd", "X", "R2", "Ut", "WtT", "U"))
    tok, gtok, amat, BB, PC, gam = K["tok"], K["gtok"], K["amat"], K["BB"], K["PC"], K["gam"]
    araw, qraw, vb, Hb, identb = K["araw"], K["qraw"], K["vb"], K["Hb"], K["identb"]
    ksd, bsd, WtTd, Hbd, Udd, tokd = (K[n] for n in ("ksd", "bsd", "WtTd", "Hbd", "Udd", "tokd"))

    def bd_copy(dst, src, eng):
        for hp in range(2):
            p.cp(dst[rows(hp), :, 64 * hp:64 * hp + 64], src[rows(hp), :, :], eng=eng)
    p.cp(Hb.ap(), Hst.ap(), eng="act")
    bd_copy(Hbd, Hb, "pool")
    yield
    ones64 = None
    for c in range(T // C):
        sl = slice(c * C, (c + 1) * C)
        for tl in range(2):
            p.scan(cum[:, tl, :], W["ones"][:, 0:64], ldT[:, tl, sl], 0.0, OP.mult, OP.add)
            yield
        p.act(E.ap(), cum.ap(), AF.Exp)
        yield
        p.tt(qs.ap(), qT[:, :, sl], E.ap(), OP.mult)
        yield
        for tl in range(2):
            p.cp(PC[:, tl:tl + 1], E[:, tl, 63:64])
            yield
        if ab:
            p.tt(cumx.ap(), cum.ap(), ldT[:, :, sl], OP.subtract)
            yield
            p.act(E.ap(), cumx.ap(), AF.Exp)
            yield
            p.tt(as_.ap(), aT[:, :, sl], E.ap(), OP.mult)
            yield
        for tl in range(2):
            p.act(E[:, tl, :], cum[:, tl, :], AF.Exp, scale=-1.0, bias=cum[:, tl, 63:64])
            yield
        p.tt(kd.ap(), kT[:, :, sl], E.ap(), OP.mult)
        yield
        if ab:
            p.stt(bd.ap(), bT[:, :, sl], -1.0, E.ap(), OP.mult, OP.mult)
            yield
        if not scalar:
            p.act(E.ap(), cum.ap(), AF.Exp, scale=-1.0)
            yield
            for hp in range(2):
                p.tt(ksd[rows(hp), :, 64 * hp:64 * hp + 64], kT[rows(hp), :, sl], E[rows(hp), :, :], OP.mult)
            yield
            if ab:
                for hp in range(2):
                    p.tt(bsd[rows(hp), :, 64 * hp:64 * hp + 64], bT[rows(hp), :, sl], E[rows(hp), :, :], OP.mult)
                yield
            Yk, Yb, Xa, Xq = ksd, bsd, as_, qs
        else:
            for hp in range(2):
                p.cp(ksd[rows(hp), :, 64 * hp:64 * hp + 64], kT[rows(hp), :, sl], eng="act")
            yield
            for hp in range(2):
                p.cp(bsd[rows(hp), :, 64 * hp:64 * hp + 64], bT[rows(hp), :, sl], eng="act")
            yield
            p.cp(araw.ap(), aT[:, :, sl], eng="act")
            yield
            p.cp(qraw.ap(), qT[:, :, sl], eng="act")
            yield
            Yk, Yb, Xa, Xq = ksd, bsd, araw, qraw
        p.cp(vb.ap(), vT[:, :, sl], eng="act")
        yield
        tq = [("as", as_), ("kd", kd), ("bd", bd), ("v", None)]
        ptv = v3(PT, 8)
        for qi, (nm, src) in enumerate(tq):
            if nm in ("as", "bd") and not ab:
                continue
            for tl in range(2):
                for hp in range(2):
                    s_ = vb[rows(hp), tl, :] if nm == "v" else src[rows(hp), tl, :]
                    p.tr(ptv[rows(hp), qi * 2 + tl, :], s_, identb[rows(hp), rows(hp)])
        if ab:
            p.cp(tok.ap(), ptv, eng="act")
            yield
            bd_copy(tokd, tok[:, 2:8, :], "pool")
        else:
            p.cp(tok[:, 2:4, :], ptv[:, 2:4, :], eng="act")
            yield
            p.cp(tok[:, 6:8, :], ptv[:, 6:8, :], eng="act")
            yield
            bd_copy(tokd[:, 0:2, :], tok[:, 2:4, :], "pool")
            bd_copy(tokd[:, 4:6, :], tok[:, 6:8, :], "pool")
        aTok, kdTok, bdTok, vTok = tok[:, 0:2, :], tok[:, 2:4, :], tok[:, 4:6, :], tok[:, 6:8, :]
        kdTokd, bdTokd, vTokd = tokd[:, 0:2, :], tokd[:, 2:4, :], tokd[:, 4:6, :]
        pav = v3(PA, 8)
        pairs = [(0, Yb, Xa), (1, Yk, Xa), (2, Yk, Xq), (3, Yb, Xq)] if ab else [(2, Yk, Xq)]
        for ty, Y, Xx in pairs:
            for tl in range(2):
                p.mm(pav[:, ty * 2 + tl, :], Y[:, tl, :], Xx[:, tl, :])
        if scalar:
            puv = v3(PU, 2)
            for tl in range(2):
                for hp in range(2):
                    p.tr(puv[rows(hp), tl, :], ldT[rows(hp), tl, sl], ident[rows(hp), rows(hp)])
            p.cp(gtok.ap(), puv)
            yield
            pbv = v3(PB, 4)
            for tl in range(2):
                for hp in range(2):
                    for ei, msk in ((0, mS), (1, mI)):
                        p.mm(pbv[rows(hp), ei * 2 + tl, :], gtok[rows(hp), tl, :], msk[rows(hp), :], start=True, stop=False)
                        p.mm(pbv[rows(hp), ei * 2 + tl, :], nmI[rows(hp), :], gtok[rows(hp), tl, :], start=False, stop=True)
            p.ts(gam.ap(), pbv, 0.0, OP.min)
            yield
            p.act(gam.ap(), gam.ap(), AF.Exp)
            yield
            for ty in range(4):
                gsel = gam[:, 0:2, :] if ty < 2 else gam[:, 2:4, :]
                p.tt(amat[:, 2 * ty:2 * ty + 2, :], pav[:, 2 * ty:2 * ty + 2, :], gsel, OP.mult)
                yield
            p.tt(amat[:, 0:4, :], amat[:, 0:4, :], mS4.ap(), OP.mult)
            yield
            p.tt(amat[:, 4:8, :], amat[:, 4:8, :], mI4.ap(), OP.mult)
            yield
        elif ab:
            p.tt(amat[:, 0:4, :], pav[:, 0:4, :], mS4.ap(), OP.mult)
            yield
            p.tt(amat[:, 4:8, :], pav[:, 4:8, :], mI4.ap(), OP.mult)
            yield
        else:
            p.tt(amat[:, 4:6, :], pav[:, 4:6, :], mI4[:, 0:2, :], OP.mult)
            yield
        nLt, Akt, Qkt, nQbt = amat[:, 0:2, :], amat[:, 2:4, :], amat[:, 4:6, :], amat[:, 6:8, :]
        if ab:
            BBd, identPb = K["BBd"], K["identPb"]
            b0, b0d = BB[0], BBd[0]
            p.cp(b0[:, 0:2, :], nLt, eng="act")
            yield
            for hp in range(2):
                p.cp(b0d[rows(hp), 0:2, 64 * hp:64 * hp + 64], nLt[rows(hp), :, :])
            yield
            pbv = v3(PB, 4)
            for tl in range(2):
                p.mm(pbv[:, tl, :], b0d[:, tl, :], identPb[:, 0, :])
            p.cp(b0[:, 2:4, :], pbv[:, 0:2, :], eng="act")
            for hp in range(2):
                p.cp(b0d[rows(hp), 2:4, 64 * hp:64 * hp + 64], pbv[rows(hp), 0:2, :])
            yield
            p.tt(X.ap(), identP.ap(), nLt, OP.add)
            yield
            for k in range(1, 6):
                prev, cur = BB[(k - 1) % 2], BB[k % 2]
                prevd, curd = BBd[(k - 1) % 2], BBd[k % 2]
                for tl in range(2):
                    p.mm(pbv[:, tl, :], prevd[:, 2 + tl, :], prev[:, tl, :])
                    p.mm(pbv[:, 2 + tl, :], prevd[:, tl, :], prev[:, 2 + tl, :])
                p.cp(cur.ap(), pbv, eng="act")
                for hp in range(2):
                    p.cp(curd[rows(hp), :, 64 * hp:64 * hp + 64], pbv[rows(hp), :, :])
                yield
                puv = v3(PU, 2)
                for tl in range(2):
                    p.mm(puv[:, tl, :], curd[:, 2 + tl, :], X[:, tl, :])
                p.tt(X.ap(), X.ap(), puv, OP.add)
                yield
            pov = v3(PO, 2)
            for tl in range(2):
                for hp in range(2):
                    r = rows(hp)
                    p.mm(pov[r, tl, :], Akt[r, tl, :], vTok[r, tl, :])
            p.cp(R2.ap(), pov, eng="act")
            yield
            puv = v3(PU, 2)
            phv = v3(PH, 2)
            for tl in range(2):
                for hp in range(2):
                    r = rows(hp)
                    p.mm(puv[r, tl, :], X[r, tl, :], R2[r, tl, :])
                    p.mm(phv[r, tl, :], aTok[r, tl, :], X[r, tl, :])
            p.cp(Ut.ap(), puv)
            yield
            for hp in range(2):
                p.cp(WtTd[rows(hp), :, 64 * hp:64 * hp + 64], phv[rows(hp), :, :], eng="act")
            yield
            for tl in range(2):
                p.mm(puv[:, tl, :], WtTd[:, tl, :], Hb[:, tl, :])
            p.tt(U.ap(), puv, Ut.ap(), OP.add)
            yield
            bd_copy(Udd, U, "pool")
        pov = v3(PO, 2)
        for tl in range(2):
            p.mm(pov[:, tl, :], Hbd[:, tl, :], qs[:, tl, :], start=True, stop=False)
            p.mm(pov[:, tl, :], vTokd[:, tl, :], Qkt[:, tl, :], start=False, stop=not ab)
            if ab:
                p.mm(pov[:, tl, :], Udd[:, tl, :], nQbt[:, tl, :], start=False, stop=True)
        p.cp(oT[:, :, sl], pov, eng="act")
        yield
        phv = v3(PH, 2)
        for tl in range(2):
            p.mm(phv[:, tl, :], kdTokd[:, tl, :], vTok[:, tl, :], start=True, stop=not ab)
            if ab:
                p.mm(phv[:, tl, :], bdTokd[:, tl, :], U[:, tl, :], start=False, stop=True)
        for tl in range(2):
            p.stt(Hst[:, tl, :], Hst[:, tl, :], PC[:, tl:tl + 1], phv[:, tl, :], OP.mult, OP.add)
            yield
        p.cp(Hb.ap(), Hst.ap(), eng="act")
        yield
        bd_copy(Hbd, Hb, "pool")


def out_norm_rms(p, oT, gate, gcol, T, W, headsum, rstd_inplace, mixTm):
    t0, t1 = W["t0"], W["t1"]
    p.act(t0[:, :, 0:T], oT[:, :, 0:T], AF.Square)
    yield
    headsum(t1, t0, T)
    yield
    rstd_inplace(t1[:, :, 0:T], T, 1.0 / 64, 1e-6)
    yield
    p.tt(t0[:, :, 0:T], oT[:, :, 0:T], t1[:, :, 0:T], OP.mult)
    yield
    p.act(t1[:, :, 0:T], gate[:, :, 0:T], AF.Silu)
    yield
    p.stt(mixTm[:, :, 0:T], t0[:, :, 0:T], gcol[:, 0:1], t1[:, :, 0:T], OP.mult, OP.mult)
    yield


def sample_core(p, name, W, Hs, fl, PS, ident, dmask, rows, v3, state_in, state_out, l, transposed_state):
    PA, PB, PU, PO, PH, PT = PS
    ab = fl["ab"]
    qT, aT, kT, bT, vT, ldT, oT = (W[n] for n in ("qT", "aT", "kT", "bT", "vT", "ldT", "oT"))
    Snat = W["Snat"]
    Dd, tokS, Ud, Vd, oTok, tmpd = (W[n] for n in ("Dd", "tokS", "Ud", "Vd", "oTok", "tmpd"))
    ptv = v3(PT, 8)
    for tl in range(2):
        for hp in range(2):
            h = 2 * tl + hp
            if not transposed_state:
                p.dma(Hs[rows(hp), tl, :, :], state_in[l, :, h].re("b d v -> d b v"))
            else:
                p.dma(Snat[rows(hp), tl, :, :], state_in[l, :, h].re("b v d -> v b d"))
    if transposed_state:
        for tl in range(2):
            for g in range(2):
                for j in range(8):
                    for hp in range(2):
                        p.tr(ptv[rows(hp), j, :], Snat[rows(hp), tl, 8 * g + j, :], ident[rows(hp), rows(hp)])
                p.cp(Hs[:, tl, 8 * g:8 * g + 8, :], ptv)
    p.act(Dd.ap(), ldT[:, :, 0:16], AF.Exp)
    pt2 = PT[0:16, 0:512].re("p (a b) -> p a b", b=128)
    srcs = [kT, bT, vT] if ab else [kT, vT]
    for qi, src in enumerate(srcs):
        for tl in range(2):
            p.tr(pt2[:, tl, :], src[:, tl, 0:16], ident.ap())
        if ab and qi == 1:
            p.ts(tokS[:, 1, :], pt2[:, 0:2, :].re("p a b -> p (a b)"), -1.0, OP.mult)
        else:
            p.cp(tokS[:, (qi if ab else 2 * qi), :], pt2[:, 0:2, :].re("p a b -> p (a b)"))
    for tl in range(2):
        for hp in range(2):
            h = 2 * tl + hp
            r = rows(hp)
            hc = slice(64 * h, 64 * h + 64)
            p.tt(Vd.ap(), tokS[:, 2, hc][:, None, :].bc([16, 16, 64]), dmask.ap(), OP.mult)
            if ab:
                for g, ps in enumerate((PU, PB)):
                    p.mm(ps[0:16, :], aT[r, tl, 0:16], Hs[r, tl, 8 * g:8 * g + 8, :].re("p b v -> p (b v)"))
                    p.tt(Ud[:, 8 * g:8 * g + 8, :], ps[0:16, :].re("p (b v) -> p b v", v=64),
                         dmask[:, 8 * g:8 * g + 8, :], OP.mult)
            for g, ps in enumerate((PH, PO)):
                p.mm(ps[r, :], tokS[:, 0, hc], Vd[:, 8 * g:8 * g + 8, :].re("p b v -> p (b v)"), start=True, stop=not ab)
                if ab:
                    p.mm(ps[r, :], tokS[:, 1, hc], Ud[:, 8 * g:8 * g + 8, :].re("p b v -> p (b v)"), start=False, stop=True)
        p.tt(Hs[:, tl, :, :], Hs[:, tl, :, :], Dd[:, tl, :][:, :, None].bc([128, 16, 64]), OP.mult)
        for g, ps in enumerate((PH, PO)):
            p.tt(Hs[:, tl, 8 * g:8 * g + 8, :], Hs[:, tl, 8 * g:8 * g + 8, :], ps[:, :].re("p (b v) -> p b v", v=64), OP.add)
        for hp in range(2):
            h = 2 * tl + hp
            r = rows(hp)
            hc = slice(64 * h, 64 * h + 64)
            for g, ps in enumerate((PU, PB)):
                p.mm(ps[0:16, :], qT[r, tl, 0:16], Hs[r, tl, 8 * g:8 * g + 8, :].re("p b v -> p (b v)"))
                p.tt(tmpd[:, 8 * g:8 * g + 8, :], ps[0:16, :].re("p (b v) -> p b v", v=64),
                     dmask[:, 8 * g:8 * g + 8, :], OP.mult)
            p.reduce(oTok[:, hc], tmpd.ap().re("p b v -> p v b"), OP.add)
    for tl in range(2):
        p.tr(PT[:, tl * 16:tl * 16 + 16], oTok[:, tl * 128:(tl + 1) * 128], ident[0:16, 0:16])
    p.cp(oT[:, :, 0:16], PT[:, 0:32].re("p (a b) -> p a b", b=16))
    if transposed_state:
        for tl in range(2):
            for g in range(2):
                for j in range(8):
                    for hp in range(2):
                        p.tr(ptv[rows(hp), j, :], Hs[rows(hp), tl, 8 * g + j, :], ident[rows(hp), rows(hp)])
                p.cp(Snat[:, tl, 8 * g:8 * g + 8, :], ptv)
    for tl in range(2):
        for hp in range(2):
            h = 2 * tl + hp
            if not transposed_state:
                p.dma(state_out[l, :, h].re("b d v -> d b v"), Hs[rows(hp), tl, :, :])
            else:
                p.dma(state_out[l, :, h].re("b v d -> v b d"), Snat[rows(hp), tl, :, :])


TWO_PI = 2.0 * math.pi


def sincos(p, dst_s, dst_c, ang, fr, ii):
    for dst, sh in ((dst_s, 0.0), (dst_c, 0.25)):
        p.ts(dst, ang, sh, OP.add)
        p.cp(ii, dst)
        p.cp(fr, ii)
        p.tt(fr, dst, fr, OP.subtract)
        p.ts(fr, fr, 0.4999995, OP.min, -0.4999995, OP.max)
        p.act(dst, fr, AF.Sin, scale=TWO_PI)


def s5_setup(p, nc, din, l, ident, tidx, PT, PB, rows):
    S = {}
    NS = True

    def ld(name, src):
        t = p.sbc(f"s5{name}", [128, 8])
        for gp in range(2):
            p.dma(t[rows(gp), :], src.re("(pr gp) q -> gp q pr", gp=2)[gp], allow_slow_non_contiguous=NS)
        return t
    lre = ld("lre", din["s5_lam_re"][l])
    lim = ld("lim", din["s5_lam_im"][l])
    stp = p.sbc(f"s5stp", [128, 8])
    for gp in range(2):
        p.dma(stp[rows(gp), :], din["s5_log_step"][l].re("(pr gp) -> gp pr", gp=2)[gp][None, :].bc([64, 8]),
              allow_slow_non_contiguous=NS)
    p.act(stp.ap(), stp.ap(), AF.Exp)
    names = ["lr", "li", "mag", "cs", "sn", "abre", "abim", "nabim", "den", "am1", "zre", "zim", "t", "fr", "ang"]
    c = {n: p.sbc(f"s5{n}", [128, 8]) for n in names}
    ii = p.sbc(f"s5ii", [128, 8], I32)
    p.tt(c["lr"].ap(), lre.ap(), stp.ap(), OP.mult)
    p.tt(c["li"].ap(), lim.ap(), stp.ap(), OP.mult)
    p.act(c["mag"].ap(), c["lr"].ap(), AF.Exp)
    p.ts(c["ang"].ap(), c["li"].ap(), 1.0 / TWO_PI, OP.mult)
    sincos(p, c["sn"].ap(), c["cs"].ap(), c["ang"].ap(), c["fr"].ap(), ii.ap())
    p.tt(c["abre"].ap(), c["mag"].ap(), c["cs"].ap(), OP.mult)
    p.tt(c["abim"].ap(), c["mag"].ap(), c["sn"].ap(), OP.mult)
    p.ts(c["nabim"].ap(), c["abim"].ap(), -1.0, OP.mult)
    p.tt(c["den"].ap(), lre.ap(), lre.ap(), OP.mult)
    p.tt(c["t"].ap(), lim.ap(), lim.ap(), OP.mult)
    p.tt(c["den"].ap(), c["den"].ap(), c["t"].ap(), OP.add)
    p.recip(c["den"].ap(), c["den"].ap())
    p.ts(c["am1"].ap(), c["abre"].ap(), -1.0, OP.add)
    p.tt(c["zre"].ap(), c["am1"].ap(), lre.ap(), OP.mult)
    p.tt(c["t"].ap(), c["abim"].ap(), lim.ap(), OP.mult)
    p.tt(c["zre"].ap(), c["zre"].ap(), c["t"].ap(), OP.add)
    p.tt(c["zre"].ap(), c["zre"].ap(), c["den"].ap(), OP.mult)
    p.tt(c["zim"].ap(), c["abim"].ap(), lre.ap(), OP.mult)
    p.tt(c["t"].ap(), c["am1"].ap(), lim.ap(), OP.mult)
    p.tt(c["zim"].ap(), c["zim"].ap(), c["t"].ap(), OP.subtract)
    p.tt(c["zim"].ap(), c["zim"].ap(), c["den"].ap(), OP.mult)
    bre = p.sbc(f"s5bre", [128, 8, 16])
    bim = p.sbc(f"s5bim", [128, 8, 16])
    for gp in range(2):
        p.dma(bre[rows(gp)], din["s5_b_re"][l].re("(pr gp) q c -> gp q pr c", gp=2)[gp], allow_slow_non_contiguous=NS)
        p.dma(bim[rows(gp)], din["s5_b_im"][l].re("(pr gp) q c -> gp q pr c", gp=2)[gp], allow_slow_non_contiguous=NS)
    BD = [p.sbc(f"s5BD{r}", [128, 8, 64]) for r in range(2)]
    tmp = p.sbc(f"s5tmp", [128, 8, 16])
    tmp2 = p.sbc(f"s5tmp2", [128, 8, 16])
    zre_b = c["zre"].ap()[:, :, None].bc([128, 8, 16])
    zim_b = c["zim"].ap()[:, :, None].bc([128, 8, 16])
    for r in range(2):
        p.memset(BD[r].ap(), 0.0)

    def scatter(r):
        for par in range(2):
            for q in range(4):
                pr = 2 * q + par
                p.cp(BD[r][0:64, pr, par * 32:par * 32 + 16], tmp[0:64, pr, :])
                p.cp(BD[r][64:128, pr, par * 32 + 16:par * 32 + 32], tmp[64:128, pr, :])
    p.tt(tmp.ap(), bre.ap(), zre_b, OP.mult)
    p.tt(tmp2.ap(), bim.ap(), zim_b, OP.mult)
    p.tt(tmp.ap(), tmp.ap(), tmp2.ap(), OP.subtract)
    scatter(0)
    p.tt(tmp.ap(), bim.ap(), zre_b, OP.mult)
    p.tt(tmp2.ap(), bre.ap(), zim_b, OP.mult)
    p.tt(tmp.ap(), tmp.ap(), tmp2.ap(), OP.add)
    scatter(1)
    BT = p.sbc(f"s5BT", [128, 8, 128])
    ptv = PT[:, 0:512].re("p (a b) -> p a b", b=128)
    for r in range(2):
        for pr in range(8):
            hf = (pr % 4) // 2
            p.tr(ptv[64 * hf:64 * hf + 64, (pr // 4) * 2 + pr % 2, :], BD[r][:, pr, :], ident.ap())
        p.cp(BT[:, r * 4:r * 4 + 4, :], ptv)
    par = p.sbc(f"s5par", [128, 2])
    pii = p.sbc(f"s5pii", [128, 1], I32)
    p.iota(pii.ap(), [[0, 1]], base=0, cm=1)
    p.op("dve", lambda e: e.tensor_scalar(pii.h[:], pii.h[:], 4, 1, OP.arith_shift_right, op1=OP.bitwise_and),
         [pii.ap()], [pii.ap()])
    p.cp(par[:, 1:2], pii.ap())
    p.ts(par[:, 0:1], par[:, 1:2], -1.0, OP.mult, 1.0, OP.add)
    CT = p.sbc(f"s5CT", [128, 4, 128])
    Cn = p.sbc(f"s5Cn", [128, 2, 64])
    Cexp = p.sbc(f"s5Cexp", [128, 2, 128])
    pbv = PB[:, 0:512].re("p (a b) -> p a b", b=128)
    for r, nm in enumerate(("s5_c_re", "s5_c_im")):
        p.dma(Cn.ap(), din[nm][l].re("(s g) c q -> (g c) s q", s=2))
        sgn = 1.0 if r == 0 else -1.0
        p.ts(Cexp[:, :, 0:64], Cn.ap(), par[:, 0:1], OP.mult, sgn, OP.mult)
        p.ts(Cexp[:, :, 64:128], Cn.ap(), par[:, 1:2], OP.mult, sgn, OP.mult)
        for s in range(2):
            p.tr(pbv[:, r * 2 + s, :], Cexp[:, s, :], ident.ap())
    p.cp(CT.ap(), pbv)
    CTe = p.sbc("s5CTe", [128, 4, 128])
    CTo = p.sbc("s5CTo", [128, 4, 128])
    p.memset(CTe.ap(), 0.0)
    p.memset(CTo.ap(), 0.0)
    vw = "p s (q pp c) -> p s q pp c"
    p.cp(CTe.ap().re(vw, pp=2, c=32)[:, :, :, 0, :], CT.ap().re(vw, pp=2, c=32)[:, :, :, 0, :])
    p.cp(CTo.ap().re(vw, pp=2, c=32)[:, :, :, 1, :], CT.ap().re(vw, pp=2, c=32)[:, :, :, 1, :])
    cosT = p.sbc(f"s5cosT", [128, 8, ST])
    sinT = p.sbc(f"s5sinT", [128, 8, ST])
    frT = p.sbc(f"s5frT", [128, ST])
    angT = p.sbc(f"s5angT", [128, ST])
    iiT = p.sbc(f"s5iiT", [128, ST], I32)
    for pr in range(8):
        p.ts(angT.ap(), tidx.ap(), c["ang"][:, pr:pr + 1], OP.mult)
        sincos(p, sinT[:, pr, :], cosT[:, pr, :], angT.ap(), frT.ap(), iiT.ap())
    S.update(c)
    S.update(BT=BT, CT=CT, CTe=CTe, CTo=CTo, cosT=cosT, sinT=sinT)
    S["hre"] = p.sbc(f"s5hre", [128, 8])
    S["him"] = p.sbc(f"s5him", [128, 8])
    return S


def s5_block(kind, st, T, W, PS, LW, *, p, proj, OFF, PJ, cnt, ident, identP, mI, mS, nmI, mI4, mS4, dmask, rows, v3,
             headsum, rstd_inplace, din, dout, l, NST, H, Hs, mixTs):
    PA_, PB, PU, PO_, PH_, PT = PS
    S, s5d, s5bg, wglu = LW["S5"], LW["s5d"], LW["s5bg"], LW["wglu"]
    uT, gate, yT = W["qT"], W["gate"], W["oT"]
    PH = W["PH"] if kind == "p" else PO_
    for tl in range(2):
        proj(uT[:, tl, 0:T], OFF["s5_u"] + 128 * tl, 128, T)
        yield
        proj(gate[:, tl, 0:T], OFF["s5_gate"] + 128 * tl, 128, T)
        yield
    BT, CT, cosT, sinT = S["BT"], S["CT"], S["cosT"], S["sinT"]
    xre, xim, gre, gim, hre_t, him_t, ta, tb = (W[n][:, 0, 0:T] for n in ("t0", "t1", "t2", "t3", "kT", "vT", "aT", "bT"))
    g0 = W["g0"]
    if kind == "s":
        hS = W["hS"]
        nat = W["s5nat"]
        for r, nm in enumerate(("state_s5_re", "state_s5_im")):
            p.dma(nat[:, r, :], din[nm][l].re("b g q -> b (g q)"))
            for pr in range(8):
                p.tr(PT[:, pr * 16:pr * 16 + 16], nat[:, r, pr * 128:(pr + 1) * 128], ident[0:16, 0:16])
            p.cp(hS[:, r, :, :], PT[:, 0:128].re("p (a b) -> p a b", b=16))
            yield
    if kind == "p":
        g0a = W["g0a"]
        cs8, sn8, hr8, hi8 = S["cs"].ap(), S["sn"].ap(), S["hre"].ap(), S["him"].ap()
        p.tt(g0a[:, 0, :], cs8, hr8, OP.mult)
        p.tt(g0a[:, 1, :], sn8, hi8, OP.mult)
        p.tt(g0a[:, 2, :], g0a[:, 0, :], g0a[:, 1, :], OP.subtract)
        yield
        p.tt(g0a[:, 0, :], cs8, hi8, OP.mult)
        p.tt(g0a[:, 1, :], sn8, hr8, OP.mult)
        p.tt(g0a[:, 3, :], g0a[:, 0, :], g0a[:, 1, :], OP.add)
        yield
        Xre, Xim, Gre, Gim, Hre, Him, Ta, Tb = (W[n][:, :, 0:T] for n in ("t0", "t1", "t2", "t3", "kT", "vT", "aT", "bT"))
        PUv = PU[:, 0:256].re("p (a b) -> p a b", b=128)[:, :, 0:T]
        PBv = PB[:, 0:256].re("p (a b) -> p a b", b=128)[:, :, 0:T]
        for gi in range(4):
            prs = (2 * gi, 2 * gi + 1)
            for j, pr in enumerate(prs):
                q4, sl = pr % 4, pr // 4
                hf = q4 // 2
                rr = slice(64 * hf, 64 * hf + 64)
                bslot = sl * 2 + pr % 2
                p.mm(PUv[:, j, :], BT[rr, 0 + bslot, :], uT[rr, sl, 0:T])
                p.mm(PBv[:, j, :], BT[rr, 4 + bslot, :], uT[rr, sl, 0:T])
            cs, sn = cosT[:, 2 * gi:2 * gi + 2, 0:T], sinT[:, 2 * gi:2 * gi + 2, 0:T]
            p.tt(Ta, PUv, cs, OP.mult)
            yield
            p.tt(Tb, PBv, sn, OP.mult)
            yield
            p.tt(Xre, Ta, Tb, OP.add, eng="pool")
            yield
            p.tt(Ta, PBv, cs, OP.mult)
            yield
            p.tt(Tb, PUv, sn, OP.mult)
            yield
            p.tt(Xim, Ta, Tb, OP.subtract, eng="pool")
            yield
            for j, pr in enumerate(prs):
                magb = S["mag"][:, pr:pr + 1].bc([128, T])
                p.scan(Gre[:, j, :], magb, Xre[:, j, :], g0a[:, 2, pr:pr + 1], OP.mult, OP.add)
                yield
                p.scan(Gim[:, j, :], magb, Xim[:, j, :], g0a[:, 3, pr:pr + 1], OP.mult, OP.add)
                yield
            p.tt(Ta, Gre, cs, OP.mult)
            yield
            p.tt(Tb, Gim, sn, OP.mult, eng="pool")
            yield
            p.tt(Hre, Ta, Tb, OP.subtract)
            yield
            p.tt(Ta, Gim, cs, OP.mult, eng="pool")
            yield
            p.tt(Tb, Gre, sn, OP.mult)
            yield
            p.tt(Him, Ta, Tb, OP.add)
            yield
            p.cp(S["hre"][:, 2 * gi:2 * gi + 2], Hre[:, :, T - 1])
            p.cp(S["him"][:, 2 * gi:2 * gi + 2], Him[:, :, T - 1])
            yield
            for j, pr in enumerate(prs):
                q4, sl = pr % 4, pr // 4
                hf = q4 // 2
                rr = slice(64 * hf, 64 * hf + 64)
                CTx = S["CTe"] if pr % 2 == 0 else S["CTo"]
                p.mm(PH[rr, sl * ST:sl * ST + T], CTx[:, 0 + sl, rr], Hre[:, j, :], start=(pr % 2 == 0), stop=False)
                p.mm(PH[rr, sl * ST:sl * ST + T], CTx[:, 2 + sl, rr], Him[:, j, :], start=False, stop=(pr % 2 == 1))
    for pr in (range(8) if kind == "s" else ()):
        q4, sl = pr % 4, pr // 4
        hf = q4 // 2
        rr = slice(64 * hf, 64 * hf + 64)
        bslot = sl * 2 + pr % 2
        p.mm(PU[:, 0:T], BT[rr, 0 + bslot, :], uT[rr, sl, 0:T])
        p.mm(PB[:, 0:T], BT[rr, 4 + bslot, :], uT[rr, sl, 0:T])
        if kind == "p":
            cs, sn = cosT[:, pr, 0:T], sinT[:, pr, 0:T]
            p.tt(ta, PU[:, 0:T], cs, OP.mult)
            yield
            p.tt(tb, PB[:, 0:T], sn, OP.mult, eng="dve")
            yield
            p.tt(xre, ta, tb, OP.add, eng="pool")
            yield
            p.tt(ta, PB[:, 0:T], cs, OP.mult)
            yield
            p.tt(tb, PU[:, 0:T], sn, OP.mult)
            yield
            p.tt(xim, ta, tb, OP.subtract, eng="pool")
            yield
            c1, s1 = S["cs"][:, pr:pr + 1], S["sn"][:, pr:pr + 1]
            hr, hi = S["hre"][:, pr:pr + 1], S["him"][:, pr:pr + 1]
            p.tt(g0[:, 0:1], c1, hr, OP.mult)
            yield
            p.tt(g0[:, 1:2], s1, hi, OP.mult)
            yield
            p.tt(g0[:, 2:3], g0[:, 0:1], g0[:, 1:2], OP.subtract)
            yield
            p.tt(g0[:, 0:1], c1, hi, OP.mult)
            yield
            p.tt(g0[:, 1:2], s1, hr, OP.mult)
            yield
            p.tt(g0[:, 3:4], g0[:, 0:1], g0[:, 1:2], OP.add)
            yield
            magb = S["mag"][:, pr:pr + 1].bc([128, T])
            p.scan(gre, magb, xre, g0[:, 2:3], OP.mult, OP.add)
            yield
            p.scan(gim, magb, xim, g0[:, 3:4], OP.mult, OP.add)
            yield
            p.tt(ta, gre, cs, OP.mult)
            yield
            p.tt(tb, gim, sn, OP.mult, eng="pool")
            yield
            p.tt(hre_t, ta, tb, OP.subtract)
            yield
            p.tt(ta, gim, cs, OP.mult, eng="pool")
            yield
            p.tt(tb, gre, sn, OP.mult)
            yield
            p.tt(him_t, ta, tb, OP.add)
            yield
            p.cp(S["hre"][:, pr:pr + 1], hre_t[:, T - 1:T])
            yield
            p.cp(S["him"][:, pr:pr + 1], him_t[:, T - 1:T])
            yield
        else:
            hS = W["hS"]
            hr, hi = hS[:, 0, pr, :], hS[:, 1, pr, :]
            p.ts(ta, hr, S["abre"][:, pr:pr + 1], OP.mult)
            yield
            p.stt(ta, hi, S["nabim"][:, pr:pr + 1], ta, OP.mult, OP.add)
            yield
            p.tt(hre_t, ta, PU[:, 0:T], OP.add)
            yield
            p.ts(tb, hr, S["abim"][:, pr:pr + 1], OP.mult)
            yield
            p.stt(tb, hi, S["abre"][:, pr:pr + 1], tb, OP.mult, OP.add)
            yield
            p.tt(him_t, tb, PB[:, 0:T], OP.add)
            yield
            p.cp(hr, hre_t)
            yield
            p.cp(hi, him_t)
            yield
        CTx = S["CTe"] if pr % 2 == 0 else S["CTo"]
        p.mm(PH[rr, sl * ST:sl * ST + T], CTx[:, 0 + sl, rr], hre_t, start=(pr % 2 == 0), stop=False)
        p.mm(PH[rr, sl * ST:sl * ST + T], CTx[:, 2 + sl, rr], him_t, start=False, stop=(pr % 2 == 1))
    for tl in range(2):
        p.stt(yT[:, tl, 0:T], uT[:, tl, 0:T], s5d[:, tl:tl + 1], PH[:, tl * ST:tl * ST + T], OP.mult, OP.add)
        yield
    if kind == "p" and st == NST - 1:
        for nm, src in (("s5re_p", S["hre"]), ("s5im_p", S["him"])):
            for gp in range(2):
                p.dma(dout[nm][l].re("(pr gp) q -> gp q pr", gp=2)[gp], src[rows(gp), :], allow_slow_non_contiguous=True)
    if kind == "s":
        hS, nat = W["hS"], W["s5nat"]
        for r, nm in enumerate(("s5re_s", "s5im_s")):
            for pr in range(8):
                p.tr(PT[0:16, pr * 128:(pr + 1) * 128] if pr < 4 else PB[0:16, (pr - 4) * 128:(pr - 3) * 128],
                     hS[:, r, pr, :], ident.ap())
            p.cp(nat[:, r, 0:512], PT[0:16, 0:512])
            yield
            p.cp(nat[:, r, 512:1024], PB[0:16, 0:512])
            yield
            p.dma(dout[nm][l].re("b g q -> b (g q)"), nat[:, r, :])
    a, b, gsb = W["t0"][:, :, 0:T], W["t1"][:, :, 0:T], W["t2"][:, :, 0:T]
    y = yT[:, :, 0:T]
    p.tt(a, y, y, OP.mult)
    yield
    p.ts(a, a, 0.044715, OP.mult, 1.0, OP.add)
    yield
    p.tt(a, a, y, OP.mult)
    yield
    p.act(a, a, AF.Tanh, scale=math.sqrt(2.0 / math.pi))
    yield
    p.ts(a, a, 1.0, OP.add, 0.5, OP.mult)
    yield
    p.tt(b, a, y, OP.mult)
    yield
    for tl in range(2):
        pj = PJ[cnt[0] % 2]
        cnt[0] += 1
        for k in range(2):
            p.mm(pj[:, 0:T], wglu[:, k, tl * 128:(tl + 1) * 128], W["t1"][:, k, 0:T], start=(k == 0), stop=(k == 1))
        p.act(W["t2"][:, tl, 0:T], pj[:, 0:T], AF.Sigmoid, bias=s5bg[:, tl:tl + 1])
        yield
    p.tt(gsb, gsb, b, OP.mult)
    yield
    p.act(a, gate[:, :, 0:T], AF.Silu)
    yield
    p.tt(mixTs[1][:, :, 0:T], gsb, a, OP.mult)
    yield


def gdn_block(kind, st, T, W, K, PS, LW, *, p, proj, OFF, PJ, cnt, ident, identP, mI, mS, nmI, mI4, mS4, dmask, rows, v3,
              headsum, rstd_inplace, din, dout, l, NST, H, Hs, mixTs):
    PA, PB, PU, PO, PH, PT = PS
    convw, gab, Eg, Eb, gdn_ng, hist = (LW[n] for n in ("convw", "gab", "Eg", "Eb", "gdn_ng", "hist_gdn"))
    xp = W["xp"]
    cv = W["cv"]
    gate = W["gate"]
    if kind == "p":
        p.cp(xp[:, :, 0:3], hist.ap())
        yield
        for j in range(6):
            proj(xp[:, j, 3:3 + T], OFF["gdn_qkv"] + 128 * j, 128, T)
            yield
        p.cp(hist.ap(), xp[:, :, T:T + 3])
        yield
        taps = [xp[:, :, i:i + T] for i in range(4)]
        if st == NST - 1:
            for r in range(3):
                p.dma(dout["conv_p"][l, r].re("(j q) -> q j", q=128), xp[:, :, T + r], allow_slow_non_contiguous=True)
    else:
        xsS = W["xsS"]
        nat = W["cnat"]
        p.dma(nat.ap(), din["state_gdn_conv"][l])
        for r in range(3):
            for j in range(6):
                p.tr(PT[:, j * 16:j * 16 + 16], nat[:, r, j * 128:(j + 1) * 128], ident[0:16, 0:16])
            p.cp(xsS[:, :, r, :], PT[:, 0:96].re("p (a b) -> p a b", b=16))
            yield
        for j in range(6):
            proj(xsS[:, j, 3, :], OFF["gdn_qkv"] + 128 * j, 128, T)
            yield
        taps = [xsS[:, :, i, :] for i in range(4)]
        p.dma(dout["conv_s"][l, :, 0:2, :], din["state_gdn_conv"][l, :, 1:3, :])
        for j in range(6):
            p.tr(PB[0:16, j * 128:(j + 1) * 128] if j < 4 else PU[0:16, (j - 4) * 128:(j - 3) * 128], xsS[:, j, 3, :],
                 ident.ap())
        p.cp(nat[:, 0, 0:512], PB[0:16, 0:512])
        yield
        p.cp(nat[:, 0, 512:768], PU[0:16, 0:256])
        yield
        p.dma(dout["conv_s"][l, :, 2, :], nat[:, 0, :])
    c = cv[:, :, 0:T]
    for j in range(6):
        cj = cv[:, j, 0:T]
        p.ts(cj, taps[0][:, j, :], convw[:, j, 0:1], OP.mult)
        yield
        for i in range(1, 4):
            p.stt(cj, taps[i][:, j, :], convw[:, j, i:i + 1], cj, OP.mult, OP.add)
            yield
    p.act(c, c, AF.Silu)
    yield
    qT, kT, vT, aT, bT, ldT, oT = (W[n] for n in ("qT", "kT", "vT", "aT", "bT", "ldT", "oT"))
    t0, t1 = W["t0"], W["t1"]
    for src, dst, sc in ((cv[:, 0:2, 0:T], qT, 0.125), (cv[:, 2:4, 0:T], aT, 1.0)):
        p.act(t0[:, :, 0:T], src, AF.Square)
        yield
        headsum(t1, t0, T)
        yield
        rstd_inplace(t1[:, :, 0:T], T, 1.0, 1e-6)
        yield
        p.stt(dst[:, :, 0:T], src, sc, t1[:, :, 0:T], OP.mult, OP.mult)
        yield
    p.cp(vT[:, :, 0:T], cv[:, 4:6, 0:T])
    yield
    abT, gf, bf = W["abT"], W["gf"], W["bf"]
    proj(abT[0:8, 0:T], OFF["gdn_ab"], 8, T)
    yield
    p.act(gf[0:8, 0:T], abT[0:8, 0:T], AF.Exp, bias=gab[0:8, 0:1])
    yield
    p.act(gf[0:8, 0:T], gf[0:8, 0:T], AF.Ln, bias=1.0)
    yield
    p.ts(gf[0:8, 0:T], gf[0:8, 0:T], gab[0:8, 1:2], OP.mult)
    yield
    p.act(bf[0:8, 0:T], abT[0:8, 0:T], AF.Sigmoid)
    yield
    for tl in range(2):
        pj = PJ[cnt[0] % 2]
        cnt[0] += 1
        p.mm(pj[:, 0:T], Eg[0:8, tl, :], gf[0:8, 0:T])
        p.cp(ldT[:, tl, 0:T], pj[:, 0:T], eng="act")
        yield
        pj = PJ[cnt[0] % 2]
        cnt[0] += 1
        p.mm(pj[:, 0:T], Eb[0:8, tl, :], bf[0:8, 0:T])
        p.tt(kT[:, tl, 0:T], aT[:, tl, 0:T], pj[:, 0:T], OP.mult)
        yield
    p.act(t0[:, :, 0:T], ldT[:, :, 0:T], AF.Exp)
    yield
    p.tt(bT[:, :, 0:T], kT[:, :, 0:T], t0[:, :, 0:T], OP.mult)
    yield
    for tl in range(2):
        proj(gate[:, tl, 0:T], OFF["gdn_gate"] + 128 * tl, 128, T)
        yield
    yield from mixer_core(p, "gdn", kind, T, W, K, H["gdn"], Hs["gdn"], dict(ab=True, scalar=True), PS, ident, identP, mI, mS, nmI,
                          mI4, mS4, dmask, rows, v3, din["state_gdn"], dout["gdn_s"], l, transposed_state=False)
    yield from out_norm_rms(p, oT, gate, gdn_ng, T, W, headsum, rstd_inplace, mixTs[2])
    if kind == "p" and st == NST - 1:
        p.dma(dout["gdn_p"][l].re("(t hp) d v -> (hp d) t v", hp=2), H["gdn"].ap())


def rwkv_block(kind, st, T, W, K, PS, LW, *, p, proj, OFF, PJ, cnt, ident, identP, mI, mS, nmI, mI4, mS4, dmask, rows, v3,
               headsum, rstd_inplace, din, dout, l, NST, H, Hs, mixTs):
    PA, PB, PU, PO, PH, PT = PS
    mu, w0, a0, k_k, k_a, r_k, ln_g, ln_b, wlo, hist = (LW[n] for n in (
        "mu", "w0", "a0", "k_k", "k_a", "r_k", "ln_g", "ln_b", "wlo", "hist_rwkv"))
    rin = W["rin"]
    xs = W["xs7"]
    gate = W["gate"]
    if kind == "p":
        p.cp(rin[:, :, 0:1], hist.ap())
        yield
        for j in range(7):
            proj(rin[:, j, 1:1 + T], OFF["rwkv_in"] + 128 * j, 128, T)
            yield
        p.cp(hist.ap(), rin[:, :, T:T + 1])
        yield
        prev, cur = rin[:, :, 0:T], rin[:, :, 1:1 + T]
        if st == NST - 1:
            p.dma(dout["shift_p"][l].re("(j q o) -> q j o", q=128, o=1), rin[:, :, T:T + 1], allow_slow_non_contiguous=True)
    else:
        nat = W["snat"]
        prevS = W["prevS"]
        p.dma(nat.ap(), din["state_rwkv_shift"][l])
        for j in range(7):
            p.tr(PT[:, j * 16:j * 16 + 16], nat[:, j * 128:(j + 1) * 128], ident[0:16, 0:16])
        p.cp(prevS.ap(), PT[:, 0:112].re("p (a b) -> p a b", b=16))
        yield
        for j in range(7):
            proj(rin[:, j, 0:T], OFF["rwkv_in"] + 128 * j, 128, T)
            yield
        prev, cur = prevS.ap(), rin[:, :, 0:T]
        for j in range(7):
            p.tr(PB[0:16, j * 128:(j + 1) * 128] if j < 4 else PU[0:16, (j - 4) * 128:(j - 3) * 128], rin[:, j, 0:T],
                 ident.ap())
        p.cp(nat[:, 0:512], PB[0:16, 0:512])
        yield
        p.cp(nat[:, 512:896], PU[0:16, 0:384])
        yield
        p.dma(dout["shift_s"][l], nat.ap())
    x = xs[:, :, 0:T]
    p.tt(x, prev, cur, OP.subtract)
    yield
    p.tt(x, x, mu.ap()[:, :, None].bc([128, 7, T]), OP.mult)
    yield
    p.tt(x, x, cur, OP.add)
    yield
    qT, kT, vT, aT, bT, ldT, oT = (W[n] for n in ("qT", "kT", "vT", "aT", "bT", "ldT", "oT"))
    t0, t1, t2 = W["t0"], W["t1"], W["t2"]
    p.cp(qT[:, :, 0:T], xs[:, 0:2, 0:T])
    yield
    p.cp(vT[:, :, 0:T], xs[:, 4:6, 0:T])
    yield
    rk = xs[:, 2:4, 0:T]
    p.act(t0[0:64, 0, 0:T], xs[0:64, 6, 0:T], AF.Tanh)
    yield
    asg = t2
    for tl in range(2):
        pj = PJ[cnt[0] % 2]
        cnt[0] += 1
        p.mm(pj[:, 0:T], wlo[0:64, tl * 128:(tl + 1) * 128], t0[0:64, 0, 0:T])
        p.act(ldT[:, tl, 0:T], pj[:, 0:T], AF.Sigmoid, bias=w0[:, tl:tl + 1])
        yield
        pj = PJ[cnt[0] % 2]
        cnt[0] += 1
        p.mm(pj[:, 0:T], wlo[64:128, tl * 128:(tl + 1) * 128], xs[64:128, 6, 0:T])
        p.act(asg[:, tl, 0:T], pj[:, 0:T], AF.Sigmoid, bias=a0[:, tl:tl + 1])
        yield
    p.ts(ldT[:, :, 0:T], ldT[:, :, 0:T], -math.exp(-0.5), OP.mult)
    yield
    p.tt(aT[:, :, 0:T], rk, k_k.ap()[:, :, None].bc([128, 2, T]), OP.mult)
    yield
    p.act(t0[:, :, 0:T], aT[:, :, 0:T], AF.Square)
    yield
    headsum(t1, t0, T)
    yield
    rstd_inplace(t1[:, :, 0:T], T, 1.0, 1e-6)
    yield
    p.tt(aT[:, :, 0:T], aT[:, :, 0:T], t1[:, :, 0:T], OP.mult)
    yield
    p.tt(bT[:, :, 0:T], aT[:, :, 0:T], asg[:, :, 0:T], OP.mult)
    yield
    p.ts(t0[:, :, 0:T], asg[:, :, 0:T], -1.0, OP.add)
    yield
    p.tt(t0[:, :, 0:T], t0[:, :, 0:T], k_a.ap()[:, :, None].bc([128, 2, T]), OP.mult)
    yield
    p.ts(t0[:, :, 0:T], t0[:, :, 0:T], 1.0, OP.add)
    yield
    p.tt(kT[:, :, 0:T], rk, t0[:, :, 0:T], OP.mult)
    yield
    bon = W["bon"]
    p.tt(t0[:, :, 0:T], qT[:, :, 0:T], kT[:, :, 0:T], OP.mult)
    yield
    p.tt(t0[:, :, 0:T], t0[:, :, 0:T], r_k.ap()[:, :, None].bc([128, 2, T]), OP.mult)
    yield
    headsum(t1, t0, T)
    yield
    p.tt(bon[:, :, 0:T], t1[:, :, 0:T], vT[:, :, 0:T], OP.mult)
    yield
    for tl in range(2):
        proj(gate[:, tl, 0:T], OFF["rwkv_gate"] + 128 * tl, 128, T)
        yield
    yield from mixer_core(p, "rwkv", kind, T, W, K, H["rwkv"], Hs["rwkv"], dict(ab=True, scalar=False), PS, ident, identP, mI, mS,
                          nmI, mI4, mS4, dmask, rows, v3, din["state_rwkv"], dout["rwkv_s"], l, transposed_state=True)
    o = oT[:, :, 0:T]
    headsum(t1, oT, T)
    yield
    p.stt(t0[:, :, 0:T], t1[:, :, 0:T], -1.0 / 64, o, OP.mult, OP.add)
    yield
    p.act(t1[:, :, 0:T], t0[:, :, 0:T], AF.Square)
    yield
    headsum(t2, t1, T)
    yield
    rstd_inplace(t2[:, :, 0:T], T, 1.0 / 64, 64e-5)
    yield
    p.tt(t0[:, :, 0:T], t0[:, :, 0:T], t2[:, :, 0:T], OP.mult)
    yield
    p.tt(t0[:, :, 0:T], t0[:, :, 0:T], ln_g.ap()[:, :, None].bc([128, 2, T]), OP.mult)
    yield
    p.tt(t0[:, :, 0:T], t0[:, :, 0:T], ln_b.ap()[:, :, None].bc([128, 2, T]), OP.add)
    yield
    p.tt(t0[:, :, 0:T], t0[:, :, 0:T], bon[:, :, 0:T], OP.add)
    yield
    p.act(t1[:, :, 0:T], gate[:, :, 0:T], AF.Silu)
    yield
    p.tt(mixTs[3][:, :, 0:T], t0[:, :, 0:T], t1[:, :, 0:T], OP.mult)
    yield
    if kind == "p" and st == NST - 1:
        ptv = v3(PT, 2)
        for tl in range(2):
            for hp in range(2):
                p.tr(ptv[rows(hp), tl, :], H["rwkv"][rows(hp), tl, :], ident[rows(hp), rows(hp)])
        p.cp(K["Ut"].ap(), ptv)
        yield
        p.dma(dout["rwkv_p"][l].re("(t hp) v d -> (hp v) t d", hp=2), K["Ut"].ap())


from concourse.bass_utils import run_bass_kernel_spmd

_CACHE = {}


def kernel(**inputs):
    if "nc" not in _CACHE:
        _CACHE["nc"] = build(DEPTH=4, NST=2048 // ST, SAMPLE=True)[0]
    nc = _CACHE["nc"]
    in_maps = []
    for c in range(8):
        m = {}
        for k, shp in SHAPES.items():
            a = np.asarray(inputs[k])
            if k == "x_prompt":
                a = a[c]
            elif k == "x_sample":
                a = a[16 * c:16 * c + 16, 0]
            elif k.startswith("state_"):
                a = a[:, 16 * c:16 * c + 16]
            m[k] = np.ascontiguousarray(a, dtype=np.float32)
        in_maps.append(m)
    res = run_bass_kernel_spmd(nc, in_maps, core_ids=list(range(8)))
    rs = res.results
    outs = []
    for k in OUT_ORDER:
        if k == "y_p":
            o = np.stack([r[k] for r in rs], axis=0)
        elif k == "y_s":
            o = np.concatenate([r[k] for r in rs], axis=0)[:, None, :]
        elif k.endswith("_p"):
            o = np.stack([r[k] for r in rs], axis=1)
        else:
            o = np.concatenate([r[k] for r in rs], axis=1)
        outs.append(np.ascontiguousarray(o, dtype=np.float32))
    return tuple(outs)
```

```python
import contextlib
import numpy as np
import concourse.bass as bass
import concourse.mybir as mybir

F32 = mybir.dt.float32
BF16 = mybir.dt.bfloat16
I32 = mybir.dt.int32
AF = mybir.ActivationFunctionType
OP = mybir.AluOpType
AX = mybir.AxisListType

ENGS = ("pe", "act", "dve", "pool", "sp")
SKIP_SAME_ENGINE = False
SERIALIZE_PSUM_READERS = True


class T:
    def __init__(self, name, handle):
        self.name = name
        self.h = handle
        self.is_psum = False
        self.w = None
        self.r = []

    def __getitem__(self, idx):
        return V(self, self.h[idx])

    def ap(self):
        return V(self, self.h[:])


class V:
    def __init__(self, t, ap):
        self.t = t
        self.a = ap

    def __getitem__(self, idx):
        return V(self.t, self.a[idx])

    def re(self, pat, **kw):
        return V(self.t, self.a.rearrange(pat, **kw))

    def bc(self, shape):
        return V(self.t, self.a.broadcast_to(shape))

    def bitcast(self, dt):
        return V(self.t, self.a.bitcast(dt))

    def ap(self):
        return self


class Prog:
    def __init__(self, nc, n_dma_sems=24):
        self.nc = nc
        self.st = contextlib.ExitStack()
        self.q = {e: [] for e in ENGS}
        self.cnt = {e: 0 for e in ENGS}
        self.sem = {e: self.st.enter_context(nc.semaphore("pg_" + e)) for e in ENGS}
        self.known = {e: {} for e in ENGS}
        self.dsem = [self.st.enter_context(nc.semaphore(f"dma{i}")) for i in range(n_dma_sems)]
        self.dcnt = [0] * n_dma_sems
        self.dnext = 0
        self.pool_sems = []
        self.pool_used = 0
        self.tiles = {}
        self.n_inst = 0

    def sb(self, name, shape, dt=F32):
        h = self.st.enter_context(self.nc.sbuf_tensor(name, list(shape), dt))
        t = T(name, h)
        self.tiles[name] = t
        return t

    def sbc(self, name, shape, dt=F32):
        if name in self.tiles:
            return self.tiles[name]
        return self.sb(name, shape, dt)

    def ps(self, name, shape, dt=F32):
        h = self.st.enter_context(self.nc.psum_tensor(name, list(shape), dt))
        t = T(name, h)
        t.is_psum = True
        self.tiles[name] = t
        return t

    def dram(self, name, shape, dt=F32, kind="Internal"):
        h = self.nc.dram_tensor(name, list(shape), dt, kind=kind)
        t = T(name, h.ap())
        self.tiles[name] = t
        return t

    def _need(self, eng, dep):
        if dep is None:
            return
        kind, i, c = dep
        if kind == "e" and i == eng and (SKIP_SAME_ENGINE or eng == "pe"):
            return
        key = (kind, i)
        if self.known[eng].get(key, 0) >= c:
            return
        self.known[eng][key] = c
        sem = self.sem[i] if kind == "e" else self.dsem[i]
        self.q[eng].append(lambda e, sem=sem, c=c: e.wait_ge(sem, c))

    def _deps(self, eng, reads, writes):
        for v in reads:
            if v is None or not isinstance(v, V):
                continue
            self._need(eng, v.t.w)
        for v in writes:
            self._need(eng, v.t.w)
            for r in v.t.r:
                self._need(eng, r)

    def _mark(self, token, reads, writes):
        for v in reads:
            if v is None or not isinstance(v, V):
                continue
            v.t.r.append(token)
            if len(v.t.r) > 64:
                best = {}
                for k, i, c in v.t.r:
                    best[(k, i)] = max(best.get((k, i), 0), c)
                v.t.r = [(k, i, c) for (k, i), c in best.items()]
        for v in writes:
            v.t.w = token
            v.t.r = []

    def op(self, eng, fn, reads, writes):
        if eng != "pe" and SERIALIZE_PSUM_READERS:
            writes = list(writes) + [v for v in reads if isinstance(v, V) and v.t.is_psum]
        self._deps(eng, reads, writes)
        self.cnt[eng] += 1
        c = self.cnt[eng]
        sem = self.sem[eng]
        self.q[eng].append(lambda e, fn=fn, sem=sem: fn(e).then_inc(sem, 1))
        self.known[eng][("e", eng)] = max(self.known[eng].get(("e", eng), 0), 0)
        self._mark(("e", eng, c), reads, writes)
        self.n_inst += 1

    def dma(self, out, in_, eng="sp", **kw):
        self._deps(eng, [in_], [out])
        if eng == "pool" and self.pool_used < 40:
            self.dsem.append(self.st.enter_context(self.nc.semaphore(f"pdma{self.pool_used}")))
            self.dcnt.append(0)
            self.pool_used += 1
            i = len(self.dsem) - 1
        else:
            i = self.dnext
            self.dnext = (self.dnext + 1) % 24
        self.dcnt[i] += 16
        c = self.dcnt[i]
        sem = self.dsem[i]
        oa, ia = out.a, in_.a
        self.q[eng].append(lambda e, oa=oa, ia=ia, sem=sem, kw=kw: e.dma_start(out=oa, in_=ia, **kw).then_inc(sem, 16))
        self._mark(("d", i, c), [in_], [out])
        self.n_inst += 1

    def barrier(self):
        for e in ENGS:
            for o in ENGS:
                if o != e and self.cnt[o]:
                    self._need(e, ("e", o, self.cnt[o]))
            for i, c in enumerate(self.dcnt):
                if c:
                    self._need(e, ("d", i, c))

    def wait_all_dma(self, eng="sp"):
        for i, c in enumerate(self.dcnt):
            if c:
                self._need(eng, ("d", i, c))

    def mm(self, out, lhsT, rhs, start=True, stop=True, **kw):
        reads = [lhsT, rhs] + ([] if start else [out])
        self.op("pe", lambda e: e.matmul(out.a, lhsT.a, rhs.a, start=start, stop=stop, **kw), reads, [out])

    def tr(self, out, in_, ident):
        if out.a.start_partition != 0 or in_.a.dtype != F32:
            return self.mm(out, in_, ident)
        self.op("pe", lambda e: e.transpose(out.a, in_.a, ident.a), [in_, ident], [out])

    def act(self, out, in_, func, bias=None, scale=None, accum=None, eng="act"):
        kw = {}
        reads = [in_]
        if bias is not None:
            kw["bias"] = bias.a if isinstance(bias, V) else bias
            reads.append(bias)
        if scale is not None:
            kw["scale"] = scale.a if isinstance(scale, V) else scale
            reads.append(scale)
        writes = [out]
        if accum is not None:
            kw["accum_out"] = accum.a
            writes.append(accum)
        self.op(eng, lambda e: e.activation(out.a, in_.a, func, **kw), reads, writes)

    def tt(self, out, a, b, op, eng="dve"):
        self.op(eng, lambda e: e.tensor_tensor(out.a, a.a, b.a, op), [a, b], [out])

    def ts(self, out, a, s1, op0, s2=None, op1=None, eng="dve", accum=None):
        reads = [a, s1, s2]
        x1 = s1.a if isinstance(s1, V) else s1
        x2 = s2.a if isinstance(s2, V) else s2
        kw = {}
        if op1 is not None:
            kw["op1"] = op1
        writes = [out]
        if accum is not None:
            kw["accum_out"] = accum.a
            writes.append(accum)
        self.op(eng, lambda e: e.tensor_scalar(out.a, a.a, x1, x2, op0, **kw), reads, writes)

    def stt(self, out, a, s, b, op0, op1, eng="dve"):
        x = s.a if isinstance(s, V) else s
        self.op(eng, lambda e: e.scalar_tensor_tensor(out.a, a.a, x, b.a, op0, op1), [a, s, b], [out])

    def cp(self, out, in_, eng="dve"):
        if eng == "act":
            self.op(eng, lambda e: e.copy(out.a, in_.a), [in_], [out])
        else:
            self.op(eng, lambda e: e.tensor_copy(out.a, in_.a), [in_], [out])

    def memset(self, out, val, eng="dve"):
        self.op(eng, lambda e: e.memset(out.a, val), [], [out])

    def scan(self, out, d0, d1, init, op0, op1):
        x = init.a if isinstance(init, V) else init
        self.op("dve", lambda e: e.tensor_tensor_scan(out.a, d0.a, d1.a, x, op0, op1), [d0, d1, init], [out])

    def reduce(self, out, in_, op, axis=AX.X):
        self.op("dve", lambda e: e.tensor_reduce(out.a, in_.a, axis, op), [in_], [out])

    def recip(self, out, in_):
        self.op("dve", lambda e: e.reciprocal(out.a, in_.a), [in_], [out])

    def iota(self, out, pattern, base=0, cm=0, **kw):
        self.op("pool", lambda e: e.iota(out.a, pattern, base=base, channel_multiplier=cm, **kw), [], [out])

    def affsel(self, out, in_, pattern, cmp, fill, base=0, cm=0):
        self.op("pool", lambda e: e.affine_select(out.a, in_.a, pattern, cmp, fill, base=base, channel_multiplier=cm),
                [in_], [out])

    def finish(self):
        self.wait_all_dma("sp")
        for e in ENGS:
            if e != "sp" and self.cnt[e]:
                self._need("sp", ("e", e, self.cnt[e]))
        nc = self.nc
        q = self.q
        with nc.Block() as block:
            @block.tensor
            def _(e):
                for f in q["pe"]:
                    f(e)

            @block.scalar
            def _(e):
                for f in q["act"]:
                    f(e)

            @block.vector
            def _(e):
                for f in q["dve"]:
                    f(e)

            @block.gpsimd
            def _(e):
                for f in q["pool"]:
                    f(e)

            @block.sync
            def _(e):
                for f in q["sp"]:
                    f(e)
        self.st.close()


import math
ST = 128
C = 64
NCH = ST // C
OFF = dict(gla_q=0, gla_k=256, gla_v=512, glr=768, gla_gate=784, s5_u=1040, s5_gate=1296,
           gdn_qkv=1552, gdn_ab=2320, gdn_gate=2328, rwkv_in=2584, rwkv_gate=3480)
D_IN = 3736
SHAPES = dict(
    x_prompt=[2048, 1024], x_sample=[16, 1024],
    state_gla=[4, 16, 4, 64, 64], state_s5_re=[4, 16, 16, 64], state_s5_im=[4, 16, 16, 64],
    state_gdn=[4, 16, 4, 64, 64], state_gdn_conv=[4, 16, 3, 768], state_rwkv=[4, 16, 4, 64, 64],
    state_rwkv_shift=[4, 16, 896],
    norm_g=[4, 1024], w_in=[4, 1024, 3736], gla_wg2=[4, 16, 256], gla_bg=[4, 256], gla_norm_g=[4, 64],
    s5_lam_re=[4, 16, 64], s5_lam_im=[4, 16, 64], s5_log_step=[4, 16], s5_b_re=[4, 16, 64, 16],
    s5_b_im=[4, 16, 64, 16], s5_c_re=[4, 16, 16, 64], s5_c_im=[4, 16, 16, 64], s5_d=[4, 256],
    s5_w_glu=[4, 256, 256], s5_b_glu=[4, 256], gdn_conv_w=[4, 4, 768], gdn_a_log=[4, 4], gdn_dt_bias=[4, 4],
    gdn_norm_g=[4, 64], rwkv_mu=[4, 896], rwkv_w0=[4, 256], rwkv_ww2=[4, 64, 256], rwkv_a0=[4, 256],
    rwkv_wa2=[4, 64, 256], rwkv_k_k=[4, 256], rwkv_k_a=[4, 256], rwkv_r_k=[4, 4, 64], rwkv_ln_g=[4, 256],
    rwkv_ln_b=[4, 256], w_out=[4, 1024, 1024], final_g=[1024])
OUT_SHAPES = dict(
    y_p=[2048, 1024], y_s=[16, 1024],
    gla_p=[4, 4, 64, 64], s5re_p=[4, 16, 64], s5im_p=[4, 16, 64], gdn_p=[4, 4, 64, 64], conv_p=[4, 3, 768],
    rwkv_p=[4, 4, 64, 64], shift_p=[4, 896],
    gla_s=[4, 16, 4, 64, 64], s5re_s=[4, 16, 16, 64], s5im_s=[4, 16, 16, 64], gdn_s=[4, 16, 4, 64, 64],
    conv_s=[4, 16, 3, 768], rwkv_s=[4, 16, 4, 64, 64], shift_s=[4, 16, 896])
OUT_ORDER = ["y_p", "y_s", "gla_p", "s5re_p", "s5im_p", "gdn_p", "conv_p", "rwkv_p", "shift_p",
             "gla_s", "s5re_s", "s5im_s", "gdn_s", "conv_s", "rwkv_s", "shift_s"]


def build(DEPTH=4, NST=8, SAMPLE=True, MIX=("gla", "s5", "gdn", "rwkv"), STREAMS=2, PSMODE=0):
    nc = bass.Bass("TRN2", target_bir_lowering=False, dynamic_dma_scratch_size=4096)
    p = Prog(nc)
    din = {k: p.dram(k, v, F32, kind="ExternalInput") for k, v in SHAPES.items()}
    dout = {k: p.dram(k, v, F32, kind="ExternalOutput") for k, v in OUT_SHAPES.items()}
    xbuf = p.dram("xbuf", [2048, 1024], F32)
    NS = True

    def rows(hp):
        return slice(64 * hp, 64 * hp + 64)

    ident = p.sb("ident", [128, 128])
    p.memset(ident.ap(), 1.0, eng="pool")
    p.affsel(ident.ap(), ident.ap(), [[-1, 128]], OP.is_equal, 0.0, base=0, cm=1)
    identb = p.sb("identb", [128, 128], BF16)
    p.cp(identb.ap(), ident.ap())
    identP = p.sb("identP", [128, 2, 64])
    for tl in range(2):
        p.cp(identP[0:64, tl, :], ident[0:64, 0:64])
        p.cp(identP[64:128, tl, :], ident[64:128, 64:128])
    mI = p.sb("mI", [128, 64])
    mS = p.sb("mS", [128, 64])
    for hp in range(2):
        p.memset(mI[rows(hp), :], 1.0, eng="pool")
        p.affsel(mI[rows(hp), :], mI[rows(hp), :], [[1, 64]], OP.is_ge, 0.0, base=0, cm=-1)
        p.memset(mS[rows(hp), :], 1.0, eng="pool")
        p.affsel(mS[rows(hp), :], mS[rows(hp), :], [[1, 64]], OP.is_ge, 0.0, base=-1, cm=-1)
    nmI = p.sb("nmI", [128, 64])
    p.ts(nmI.ap(), mI.ap(), -1.0, OP.mult)
    mS4 = p.sb("mS4", [128, 4, 64])
    mI4 = p.sb("mI4", [128, 4, 64])
    for i in range(4):
        p.ts(mS4[:, i, :], mS.ap(), -1.0 if i < 2 else 1.0, OP.mult)
        p.ts(mI4[:, i, :], mI.ap(), 1.0 if i < 2 else -1.0, OP.mult)
    bones = p.sb("bones", [128, 128])
    p.memset(bones.ap(), 0.0)
    p.memset(bones[0:64, 0:64], 1.0)
    p.memset(bones[64:128, 64:128], 1.0)
    Eg = p.sb("Eg", [8, 2, 128])
    Eb = p.sb("Eb", [8, 2, 128])
    for E, sh in ((Eg, 0), (Eb, 4)):
        p.memset(E.ap(), 1.0, eng="pool")
        p.affsel(E.ap(), E.ap(), [[128, 2], [1, 128]], OP.is_ge, 0.0, base=64 * sh, cm=-64)
        p.affsel(E.ap(), E.ap(), [[-128, 2], [-1, 128]], OP.is_ge, 0.0, base=63 - 64 * sh, cm=64)
    dmask = p.sb("dmask", [16, 16, 64])
    p.memset(dmask.ap(), 1.0, eng="pool")
    p.affsel(dmask.ap(), dmask.ap(), [[-1, 16], [0, 64]], OP.is_equal, 0.0, base=0, cm=1)
    tidx = p.sb("tidx", [128, ST])
    p.iota(tidx.ap(), [[1, ST]], base=0, cm=0, allow_small_or_imprecise_dtypes=True)
    ones = p.sb("ones", [128, ST])
    p.memset(ones.ap(), 1.0)
    fg = p.sb("fg", [128, 1024])
    p.dma(fg.ap(), din["final_g"].ap().re("(o n) -> o n", o=1).bc([128, 1024]))

    PJ = [p.ps("PJ0", [128, 512]), p.ps("PJ1", [128, 512])]
    BK = {nm: p.ps(nm, [128, 512]) for nm in ("PT_A", "PA_A", "PX_A", "PT_B", "PA_B", "PX_B")}

    def psset(sfx):
        b1, b2, b3 = BK["PT_" + sfx], BK["PA_" + sfx], BK["PX_" + sfx]
        return (b2, b1, b3, b2, b1, b1)
    PS_A, PS_B = psset("A"), psset("B")
    if PSMODE == 1:
        PS_A = PS_B = (BK["PA_A"], BK["PX_A"], BK["PT_B"], BK["PA_B"], BK["PX_B"], BK["PT_A"])
    PS_FULL = (BK["PA_A"], BK["PX_A"], BK["PT_B"], BK["PA_B"], BK["PX_B"], BK["PT_A"])
    PT = BK["PT_A"]
    PB = BK["PX_A"]

    def v3(ps, n):
        return ps[:, 0:n * 64].re("p (a b) -> p a b", b=64)

    WGRP = [(1552, 2584), (2584, 3736), (0, 1040), (1040, 1552)]
    Wins = [p.sb(f"Win{i}", [128, 8, c1 - c0], BF16) for i, (c0, c1) in enumerate(WGRP)]
    Wout = p.sb("Wout", [128, 8, 1024], BF16)
    xts = [p.sb(f"xt{b}", [128, 1, 1024]) for b in range(2)]
    xn = p.sb("xn", [128, 1024])
    xn2 = p.sb("xn2", [128, 1024])
    hTs = [p.sb(f"hT{b}", [128, 8, ST], BF16) for b in range(2)]
    mixT2 = [[p.sb(f"mixT{b}_{i}", [128, 2, ST], BF16) for i in range(4)] for b in range(2)]
    ss = p.sb("ss", [128, 1])
    ss2 = p.sb("ss2", [128, 1])
    cur = {"hT": hTs[0]}
    ng = p.sb("ng", [128, 8])
    cnt = [0]

    RAW = p.sb("RAW", [128, 8704])
    carve_off = [0]

    def carve(name, shape, dt=F32):
        n = 1
        for d in shape[1:]:
            n *= d
        n32 = n if dt != BF16 else (n + 1) // 2
        a = RAW.h[0:shape[0], carve_off[0]:carve_off[0] + n32]
        carve_off[0] += n32
        assert carve_off[0] <= 8704, carve_off[0]
        if dt != F32:
            a = a.bitcast(dt)
        if len(shape) > 2:
            names = " ".join(f"d{i}" for i in range(1, len(shape)))
            a = a.rearrange(f"p ({names}) -> p {names}", **{f"d{i}": shape[i] for i in range(1, len(shape) - 1)})
        t = type(ones)(name, a)
        p.tiles[name] = t
        return t

    def make_set(sfx, alloc):
        W = {}
        for nm in ["qT", "kT", "vT", "aT", "bT", "ldT", "gate", "oT", "t0", "t1", "t2", "t3", "bon"]:
            W[nm] = alloc(f"w{sfx}_" + nm, [128, 2, ST])
        W["ones"] = ones
        W["g0"] = alloc(f"w{sfx}_g0", [128, 4])
        W["g0a"] = alloc(f"w{sfx}_g0a", [128, 4, 8])
        K = {}
        for nm in ["cum", "cumx", "E", "qs", "as", "ks", "bs", "kd", "bd", "X", "R2", "Ut", "WtT", "U", "araw", "qraw", "vb", "Hb"]:
            K[nm] = alloc(f"k{sfx}_" + nm, [128, 2, 64], F32 if nm in ("cum", "cumx", "E", "Ut") else BF16)
        K["identb"] = identb
        K["gam"] = alloc(f"k{sfx}_gam4", [128, 4, 64])
        K["tok"] = alloc(f"k{sfx}_tok", [128, 8, 64], BF16)
        K["gtok"] = alloc(f"k{sfx}_gtok", [128, 2, 64])
        K["amat"] = alloc(f"k{sfx}_amat", [128, 8, 64], BF16)
        K["BB"] = [alloc(f"k{sfx}_BB0", [128, 4, 64], BF16), alloc(f"k{sfx}_BB1", [128, 4, 64], BF16)]
        K["PC"] = alloc(f"k{sfx}_PC", [128, 2])
        return W, K
    W, K = make_set("A", p.sb)
    carve_off[0] = 0
    W2, K2 = make_set("B", carve)
    W2["rin"] = carve("wB_rin", [128, 7, ST + 1])
    W2["xs7"] = carve("wB_xs7", [128, 7, ST])
    W2["PH"] = BK["PA_B"]
    W["PH"] = BK["PA_B"]
    set_b_end = carve_off[0]
    W["xp"] = p.sb("wA_xp", [128, 6, ST + 3])
    W["cv"] = p.sb("wA_cv", [128, 6, ST])
    W["abT"] = p.sb("w_abT", [8, ST])
    W["gf"] = p.sb("w_gf", [8, ST])
    W["bf"] = p.sb("w_bf", [8, ST])
    W["rin"] = W["xp"]
    carve_off[0] = 0
    xs_s = p.sb("xs_s", [16, 1024])
    big1 = carve("big1", [128, 2304])
    W["Snat"] = big1[:, 0:2048].re("p (a b c) -> p a b c", a=2, b=16)
    W["s5nat"] = big1[0:16, 0:2048].re("p (a b) -> p a b", a=2)
    W["cnat"] = big1[0:16, 0:2304].re("p (a b) -> p a b", a=3)
    W["snat"] = big1[0:16, 0:896]
    Hs1 = carve("Hs1", [128, 2, 16, 64])
    W["Dd"] = carve("w_Dd", [128, 2, 16])
    W["tokS"] = carve("w_tokS", [16, 3, 256])
    W["Ud"] = carve("w_Ud", [16, 16, 64])
    W["Vd"] = carve("w_Vd", [16, 16, 64])
    W["tmpd"] = W["Ud"]
    W["oTok"] = carve("w_oTok", [16, 256])
    W["hS"] = carve("w_hS", [128, 2, 8, 16])
    W["xsS"] = carve("w_xsS", [128, 6, 4, 16])
    W["prevS"] = carve("w_prevS", [128, 7, 16])
    W["rinS"] = carve("w_rinS", [128, 7, 16])
    W["xs7S"] = carve("w_xs7S", [128, 7, 16])
    H = {m: p.sb("H_" + m, [128, 2, 64]) for m in ("gla", "gdn", "rwkv")}
    Hs = {m: Hs1 for m in ("gla", "gdn", "rwkv")}

    def colvec(name, src, n):
        t = p.sb(name, [128, n // 128])
        p.dma(t.ap(), src.re("(k p) -> p k", p=128), allow_slow_non_contiguous=NS)
        return t

    def proj(dst, col0, n, T, scale=1.0):
        pj = PJ[cnt[0] % 2]
        cnt[0] += 1
        gi = [i for i, (c0, c1) in enumerate(WGRP) if c0 <= col0 < c1][0]
        Wg, cb = Wins[gi], col0 - WGRP[gi][0]
        for k in range(8):
            p.mm(pj[0:n, 0:T], Wg[:, k, cb:cb + n], cur["hT"][:, k, 0:T], start=(k == 0), stop=(k == 7))
        p.act(dst, pj[0:n, 0:T], AF.Identity, scale=scale)

    def rstd_inplace(t, T, mult, eps):
        p.ts(t, t, mult, OP.mult, eps, OP.add)
        p.act(t, t, AF.Sqrt)
        p.recip(t, t)

    def headsum(dst, src, T):
        for tl in range(2):
            pj = PJ[cnt[0] % 2]
            cnt[0] += 1
            p.mm(pj[:, 0:T], bones.ap(), src[:, tl, 0:T])
            p.cp(dst[:, tl, 0:T], pj[:, 0:T], eng="act")

    def silu_(dst, src):
        p.act(dst, src, AF.Silu)

    for l in range(DEPTH):
        for i, (c0, c1) in enumerate(WGRP):
            p.dma(Wins[i].ap(), din["w_in"][l, :, c0:c1].re("(k q) c -> q k c", q=128), eng="pool")
        p.dma(Wout.ap(), din["w_out"][l].re("(k q) c -> q k c", q=128), eng="pool")
        p.dma(ng.ap(), din["norm_g"][l].re("(k p) -> p k", p=128), allow_slow_non_contiguous=NS)
        L = {}
        wg2 = p.sbc(f"wg2", [32, 256])
        p.memset(wg2.ap(), 0.0)
        p.dma(wg2[0:16, :], din["gla_wg2"][l])
        p.dma(wg2[16:17, :], din["gla_bg"][l].re("(o n) -> o n", o=1))
        gla_ng = p.sbc(f"gla_ng", [128, 1])
        gdn_ng = p.sbc(f"gdn_ng", [128, 1])
        for hp in range(2):
            p.dma(gla_ng[rows(hp), :], din["gla_norm_g"][l].re("(n o) -> n o", o=1), allow_slow_non_contiguous=NS)
            p.dma(gdn_ng[rows(hp), :], din["gdn_norm_g"][l].re("(n o) -> n o", o=1), allow_slow_non_contiguous=NS)
        s5d = colvec(f"s5d_{l}", din["s5_d"][l], 256)
        s5bg = colvec(f"s5bg_{l}", din["s5_b_glu"][l], 256)
        wglu = p.sbc(f"wglu", [128, 2, 256])
        p.dma(wglu.ap(), din["s5_w_glu"][l].re("(k p) n -> p k n", p=128))
        convw = p.sbc(f"convw", [128, 6, 4])
        for i in range(4):
            p.dma(convw[:, :, i], din["gdn_conv_w"][l, i].re("(j p) -> p j", p=128), allow_slow_non_contiguous=NS)
        gab = p.sbc(f"gab", [8, 2])
        p.memset(gab.ap(), 0.0)
        p.dma(gab[0:4, 0:1], din["gdn_dt_bias"][l].re("(n o) -> n o", o=1), allow_slow_non_contiguous=NS)
        p.dma(gab[0:4, 1:2], din["gdn_a_log"][l].re("(n o) -> n o", o=1), allow_slow_non_contiguous=NS)
        p.act(gab[:, 1:2], gab[:, 1:2], AF.Exp)
        p.ts(gab[:, 1:2], gab[:, 1:2], -1.0, OP.mult)
        mu = colvec(f"mu_{l}", din["rwkv_mu"][l], 896)
        w0 = colvec(f"w0_{l}", din["rwkv_w0"][l], 256)
        a0 = colvec(f"a0_{l}", din["rwkv_a0"][l], 256)
        k_k = colvec(f"kk_{l}", din["rwkv_k_k"][l], 256)
        k_a = colvec(f"ka_{l}", din["rwkv_k_a"][l], 256)
        r_k = colvec(f"rk_{l}", din["rwkv_r_k"][l].re("h n -> (h n)"), 256)
        ln_g = colvec(f"lng_{l}", din["rwkv_ln_g"][l], 256)
        ln_b = colvec(f"lnb_{l}", din["rwkv_ln_b"][l], 256)
        wlo = p.sbc(f"wlo", [128, 256])
        p.dma(wlo[0:64, :], din["rwkv_ww2"][l])
        p.dma(wlo[64:128, :], din["rwkv_wa2"][l])

        S5 = s5_setup(p, nc, din, l, ident, tidx, PT, PB, rows) if "s5" in MIX else None

        for m in H:
            p.memset(H[m].ap(), 0.0)
        hist_gdn = p.sbc(f"hist_gdn", [128, 6, 3])
        hist_rwkv = p.sbc(f"hist_rwkv", [128, 7, 1])
        p.memset(hist_gdn.ap(), 0.0)
        p.memset(hist_rwkv.ap(), 0.0)
        if S5 is not None:
            p.memset(S5["hre"].ap(), 0.0)
            p.memset(S5["him"].ap(), 0.0)

        common = dict(p=p, proj=proj, OFF=OFF, PJ=PJ, cnt=cnt, ident=ident, identP=identP, mI=mI, mS=mS, nmI=nmI,
                      mI4=mI4, mS4=mS4, dmask=dmask, rows=rows, v3=v3, headsum=headsum, rstd_inplace=rstd_inplace,
                      din=din, dout=dout, l=l, NST=NST, H=H, Hs=Hs)
        LW = dict(wg2=wg2, gla_ng=gla_ng, gdn_ng=gdn_ng, s5d=s5d, s5bg=s5bg, wglu=wglu, convw=convw, gab=gab, Eg=Eg,
                  Eb=Eb, mu=mu, w0=w0, a0=a0, k_k=k_k, k_a=k_a, r_k=r_k, ln_g=ln_g, ln_b=ln_b, wlo=wlo,
                  hist_gdn=hist_gdn, hist_rwkv=hist_rwkv, S5=S5)
        last_layer = (l == DEPTH - 1)

        def run_streams(gens):
            gens = [g for g in gens if g is not None]
            while gens:
                for g in list(gens):
                    try:
                        next(g)
                    except StopIteration:
                        gens.remove(g)

        def chain(*gs):
            for g in gs:
                if g is not None:
                    yield from g

        def head_gen(kind, st, bi):
            if kind == "p":
                r0 = st * ST
                src = din["x_prompt"] if l == 0 else xbuf
                xv = xts[bi][:, 0, :]
                p.dma(xv, src[r0:r0 + 128, :])
                np_ = 128
            else:
                xv = xs_s[0:16, :]
                if l == 0:
                    p.dma(xv, din["x_sample"].ap())
                np_ = 16
            p.act(xn[0:np_, :], xv, AF.Square, accum=ss[0:np_, :])
            yield
            rstd_inplace(ss[0:np_, :], 1, 1.0 / 1024, 1e-6)
            yield
            p.ts(xn[0:np_, :], xv, ss[0:np_, :], OP.mult)
            yield
            for kk in range(2):
                pj = PJ[cnt[0] % 2]
                cnt[0] += 1
                for j in range(4):
                    k = kk * 4 + j
                    p.tr(pj[:, j * 128:j * 128 + np_], xn[0:np_, k * 128:(k + 1) * 128], ident[0:np_, 0:np_])
                p.tt(hTs[bi][:, kk * 4:kk * 4 + 4, 0:np_],
                     pj[:, 0:512].re("p (a b) -> p a b", b=128)[:, :, 0:np_],
                     ng[:, kk * 4:kk * 4 + 4, None].bc([128, 4, np_]), OP.mult)
                yield

        def tail_gen(kind, st, bi):
            np_ = 128 if kind == "p" else 16
            xv = xts[bi][:, 0, :] if kind == "p" else xs_s[0:16, :]
            r0 = st * ST
            mt = mixT2[bi]
            for half in range(2):
                pj = PJ[cnt[0] % 2]
                cnt[0] += 1
                for k in range(8):
                    p.mm(pj[0:np_, :], mt[k // 2][:, k % 2, 0:np_], Wout[:, k, half * 512:(half + 1) * 512],
                         start=(k == 0), stop=(k == 7))
                p.tt(xv[:, half * 512:(half + 1) * 512], xv[:, half * 512:(half + 1) * 512], pj[0:np_, :], OP.add)
                yield
            if not last_layer:
                if kind == "p":
                    p.dma(xbuf[r0:r0 + 128, :], xv)
            else:
                p.act(xn2[0:np_, :], xv, AF.Square, accum=ss2[0:np_, :])
                yield
                rstd_inplace(ss2[0:np_, :], 1, 1.0 / 1024, 1e-6)
                yield
                p.stt(xn2[0:np_, :], xv, ss2[0:np_, :], fg[0:np_, :], OP.mult, OP.mult)
                yield
                if kind == "p":
                    p.dma(dout["y_p"][r0:r0 + 128, :], xn2[0:np_, :])
                else:
                    p.dma(dout["y_s"].ap(), xn2[0:np_, :])

        def mixers(kind, st, bi):
            T = ST if kind == "p" else 16
            cur["hT"] = hTs[bi]
            cm = dict(common, mixTs=mixT2[bi])
            for i, m in enumerate(("gla", "s5", "gdn", "rwkv")):
                if m not in MIX:
                    p.memset(mixT2[bi][i][:, :, 0:T], 0.0)
            if kind == "p":
                ga = chain(gdn_block(kind, st, T, W, K, PS_A, LW, **cm) if "gdn" in MIX else None,
                           gla_block(kind, st, T, W, K, PS_A, LW, **cm) if "gla" in MIX else None)
                gb = chain(rwkv_block(kind, st, T, W2, K2, PS_B, LW, **cm) if "rwkv" in MIX else None,
                           s5_block(kind, st, T, W2, PS_B, LW, **cm) if "s5" in MIX else None)
                return [ga, gb] if STREAMS == 2 else [chain(ga, gb)]
            W["rin"], W["xs7"] = W["rinS"], W["xs7S"]
            return [chain(gla_block(kind, st, T, W, K, PS_FULL, LW, **cm) if "gla" in MIX else None,
                          s5_block(kind, st, T, W, PS_FULL, LW, **cm) if "s5" in MIX else None,
                          gdn_block(kind, st, T, W, K, PS_FULL, LW, **cm) if "gdn" in MIX else None,
                          rwkv_block(kind, st, T, W, K, PS_FULL, LW, **cm) if "rwkv" in MIX else None)]

        run_streams([head_gen("p", 0, 0)])
        for step in range(NST + 1):
            gens = []
            if step < NST:
                gens += mixers("p", step, step % 2)
            gens.append(chain(tail_gen("p", step - 1, (step - 1) % 2) if step >= 1 else None,
                              head_gen("p", step + 1, (step + 1) % 2) if step + 1 < NST else None))
            run_streams(gens)
        if SAMPLE:
            p.barrier()
            run_streams([head_gen("s", 0, 0)])
            run_streams(mixers("s", 0, 0))
            run_streams([tail_gen("s", 0, 0)])
            p.barrier()
    p.finish()
    return nc, p


def gla_block(kind, st, T, W, K, PS, LW, *, p, proj, OFF, PJ, cnt, ident, identP, mI, mS, nmI, mI4, mS4, dmask, rows, v3,
              headsum, rstd_inplace, din, dout, l, NST, H, Hs, mixTs):
    qT, kT, vT, ldT, gate, oT = (W[n] for n in ("qT", "kT", "vT", "ldT", "gate", "oT"))
    wg2, gla_ng = LW["wg2"], LW["gla_ng"]
    for tl in range(2):
        proj(qT[:, tl, 0:T], OFF["gla_q"] + 128 * tl, 128, T, scale=0.125)
        yield
        proj(kT[:, tl, 0:T], OFF["gla_k"] + 128 * tl, 128, T)
        yield
        proj(vT[:, tl, 0:T], OFF["gla_v"] + 128 * tl, 128, T)
        yield
        proj(gate[:, tl, 0:T], OFF["gla_gate"] + 128 * tl, 128, T)
        yield
    glr = W["t0"]
    p.memset(glr[0:32, 0, 0:T], 1.0)
    proj(glr[0:16, 0, 0:T], OFF["glr"], 16, T)
    yield
    for tl in range(2):
        pj = PJ[cnt[0] % 2]
        cnt[0] += 1
        p.mm(pj[:, 0:T], wg2[0:17, tl * 128:(tl + 1) * 128], glr[0:17, 0, 0:T])
        p.act(ldT[:, tl, 0:T], pj[:, 0:T], AF.Exp, scale=-1.0)
        p.act(ldT[:, tl, 0:T], ldT[:, tl, 0:T], AF.Ln, bias=1.0)
        yield
    p.ts(ldT[:, :, 0:T], ldT[:, :, 0:T], -1.0 / 16, OP.mult)
    yield from mixer_core(p, "gla", kind, T, W, K, H["gla"], Hs["gla"], dict(ab=False, scalar=False),
                          PS, ident, identP, mI, mS, nmI, mI4, mS4, dmask, rows, v3,
                          din["state_gla"], dout["gla_s"], l, transposed_state=False)
    yield from out_norm_rms(p, oT, gate, gla_ng, T, W, headsum, rstd_inplace, mixTs[0])
    if kind == "p" and st == NST - 1:
        p.dma(dout["gla_p"][l].re("(t hp) d v -> (hp d) t v", hp=2), H["gla"].ap())


def mixer_core(p, name, kind, T, W, K, Hst, Hsamp, fl, PS, ident, identP, mI, mS, nmI, mI4, mS4, dmask, rows, v3,
               state_in, state_out, l, transposed_state):
    PA, PB, PU, PO, PH, PT = PS
    qT, aT, kT, bT, vT, ldT, oT = (W[n] for n in ("qT", "aT", "kT", "bT", "vT", "ldT", "oT"))
    ab, scalar = fl["ab"], fl["scalar"]
    if kind == "s":
        sample_core(p, name, W, Hsamp, fl, PS, ident, dmask, rows, v3, state_in, state_out, l, transposed_state)
        yield
        return
    cum, cumx, E, qs, as_, ks, bs, kd, bd, X, R2, Ut, WtT, U = (K[n] for n in (
        "cum", "cumx", "E", "qs", "as", "ks", "bs", "kd", "bd", "X", "R2", "Ut", "WtT", "U"))
    tok, gtok, amat, BB, PC, gam = K["tok"], K["gtok"], K["amat"], K["BB"], K["PC"], K["gam"]
    araw, qraw, vb, Hb, identb = K["araw"], K["qraw"], K["vb"], K["Hb"], K["identb"]
    p.cp(Hb.ap(), Hst.ap(), eng="act")
    yield
    ones64 = None
    for c in range(T // C):
        sl = slice(c * C, (c + 1) * C)
        for tl in range(2):
            p.scan(cum[:, tl, :], W["ones"][:, 0:64], ldT[:, tl, sl], 0.0, OP.mult, OP.add)
            yield
        p.act(E.ap(), cum.ap(), AF.Exp)
        yield
        p.tt(qs.ap(), qT[:, :, sl], E.ap(), OP.mult)
        yield
        for tl in range(2):
            p.cp(PC[:, tl:tl + 1], E[:, tl, 63:64])
            yield
        if ab:
            p.tt(cumx.ap(), cum.ap(), ldT[:, :, sl], OP.subtract)
            yield
            p.act(E.ap(), cumx.ap(), AF.Exp)
            yield
            p.tt(as_.ap(), aT[:, :, sl], E.ap(), OP.mult)
            yield
        for tl in range(2):
            p.act(E[:, tl, :], cum[:, tl, :], AF.Exp, scale=-1.0, bias=cum[:, tl, 63:64])
            yield
        p.tt(kd.ap(), kT[:, :, sl], E.ap(), OP.mult)
        yield
        if ab:
            p.stt(bd.ap(), bT[:, :, sl], -1.0, E.ap(), OP.mult, OP.mult)
            yield
        if not scalar:
            p.act(E.ap(), cum.ap(), AF.Exp, scale=-1.0)
            yield
            p.tt(ks.ap(), kT[:, :, sl], E.ap(), OP.mult)
            yield
            if ab:
                p.tt(bs.ap(), bT[:, :, sl], E.ap(), OP.mult)
                yield
            Yk, Yb, Xa, Xq = ks, bs, as_, qs
        else:
            p.cp(ks.ap(), kT[:, :, sl], eng="act")
            yield
            p.cp(bs.ap(), bT[:, :, sl], eng="act")
            yield
            p.cp(araw.ap(), aT[:, :, sl], eng="act")
            yield
            p.cp(qraw.ap(), qT[:, :, sl], eng="act")
            yield
            Yk, Yb, Xa, Xq = ks, bs, araw, qraw
        p.cp(vb.ap(), vT[:, :, sl], eng="act")
        yield
        tq = [("as", as_), ("kd", kd), ("bd", bd), ("v", None)]
        ptv = v3(PT, 8)
        for qi, (nm, src) in enumerate(tq):
            if nm in ("as", "bd") and not ab:
                continue
            for tl in range(2):
                for hp in range(2):
                    s_ = vb[rows(hp), tl, :] if nm == "v" else src[rows(hp), tl, :]
                    p.tr(ptv[rows(hp), qi * 2 + tl, :], s_, identb[rows(hp), rows(hp)])
        if ab:
            p.cp(tok.ap(), ptv, eng="act")
            yield
        else:
            p.cp(tok[:, 2:4, :], ptv[:, 2:4, :], eng="act")
            yield
            p.cp(tok[:, 6:8, :], ptv[:, 6:8, :], eng="act")
            yield
        aTok, kdTok, bdTok, vTok = tok[:, 0:2, :], tok[:, 2:4, :], tok[:, 4:6, :], tok[:, 6:8, :]
        pav = v3(PA, 8)
        pairs = [(0, Yb, Xa), (1, Yk, Xa), (2, Yk, Xq), (3, Yb, Xq)] if ab else [(2, Yk, Xq)]
        for ty, Y, Xx in pairs:
            for tl in range(2):
                for hp in range(2):
                    ysl = Y[rows(hp), tl, :]
                    xsl = Xx[rows(hp), tl, :]
                    p.mm(pav[rows(hp), ty * 2 + tl, :], ysl, xsl)
        if scalar:
            puv = v3(PU, 2)
            for tl in range(2):
                for hp in range(2):
                    p.tr(puv[rows(hp), tl, :], ldT[rows(hp), tl, sl], ident[rows(hp), rows(hp)])
            p.cp(gtok.ap(), puv)
            yield
            pbv = v3(PB, 4)
            for tl in range(2):
                for hp in range(2):
                    for ei, msk in ((0, mS), (1, mI)):
                        p.mm(pbv[rows(hp), ei * 2 + tl, :], gtok[rows(hp), tl, :], msk[rows(hp), :], start=True, stop=False)
                        p.mm(pbv[rows(hp), ei * 2 + tl, :], nmI[rows(hp), :], gtok[rows(hp), tl, :], start=False, stop=True)
            p.ts(gam.ap(), pbv, 0.0, OP.min)
            yield
            p.act(gam.ap(), gam.ap(), AF.Exp)
            yield
            for ty in range(4):
                gsel = gam[:, 0:2, :] if ty < 2 else gam[:, 2:4, :]
                p.tt(amat[:, 2 * ty:2 * ty + 2, :], pav[:, 2 * ty:2 * ty + 2, :], gsel, OP.mult)
                yield
            p.tt(amat[:, 0:4, :], amat[:, 0:4, :], mS4.ap(), OP.mult)
            yield
            p.tt(amat[:, 4:8, :], amat[:, 4:8, :], mI4.ap(), OP.mult)
            yield
        elif ab:
            p.tt(amat[:, 0:4, :], pav[:, 0:4, :], mS4.ap(), OP.mult)
            yield
            p.tt(amat[:, 4:8, :], pav[:, 4:8, :], mI4.ap(), OP.mult)
            yield
        else:
            p.tt(amat[:, 4:6, :], pav[:, 4:6, :], mI4[:, 0:2, :], OP.mult)
            yield
        nLt, Akt, Qkt, nQbt = amat[:, 0:2, :], amat[:, 2:4, :], amat[:, 4:6, :], amat[:, 6:8, :]
        if ab:
            b0 = BB[0]
            p.cp(b0[:, 0:2, :], nLt)
            yield
            pbv = v3(PB, 4)
            for tl in range(2):
                for hp in range(2):
                    p.tr(pbv[rows(hp), tl, :], nLt[rows(hp), tl, :], identb[rows(hp), rows(hp)])
            p.cp(b0[:, 2:4, :], pbv[:, 0:2, :], eng="act")
            yield
            p.tt(X.ap(), identP.ap(), nLt, OP.add)
            yield
            for k in range(1, 6):
                prev, cur = BB[(k - 1) % 2], BB[k % 2]
                for tl in range(2):
                    for hp in range(2):
                        r = rows(hp)
                        p.mm(pbv[r, tl, :], prev[r, 2 + tl, :], prev[r, tl, :])
                        p.mm(pbv[r, 2 + tl, :], prev[r, tl, :], prev[r, 2 + tl, :])
                p.cp(cur.ap(), pbv, eng="act")
                yield
                puv = v3(PU, 2)
                for tl in range(2):
                    for hp in range(2):
                        r = rows(hp)
                        p.mm(puv[r, tl, :], cur[r, 2 + tl, :], X[r, tl, :])
                p.tt(X.ap(), X.ap(), puv, OP.add)
                yield
            pov = v3(PO, 2)
            for tl in range(2):
                for hp in range(2):
                    r = rows(hp)
                    p.mm(pov[r, tl, :], Akt[r, tl, :], vTok[r, tl, :])
            p.cp(R2.ap(), pov, eng="act")
            yield
            puv = v3(PU, 2)
            phv = v3(PH, 2)
            for tl in range(2):
                for hp in range(2):
                    r = rows(hp)
                    p.mm(puv[r, tl, :], X[r, tl, :], R2[r, tl, :])
                    p.mm(phv[r, tl, :], aTok[r, tl, :], X[r, tl, :])
            p.cp(Ut.ap(), puv)
            yield
            p.cp(WtT.ap(), phv, eng="act")
            yield
            for tl in range(2):
                for hp in range(2):
                    r = rows(hp)
                    p.mm(puv[r, tl, :], WtT[r, tl, :], Hb[r, tl, :])
            p.tt(U.ap(), puv, Ut.ap(), OP.add)
            yield
        pov = v3(PO, 2)
        for tl in range(2):
            for hp in range(2):
                r = rows(hp)
                p.mm(pov[r, tl, :], Hb[r, tl, :], qs[r, tl, :], start=True, stop=False)
                p.mm(pov[r, tl, :], vTok[r, tl, :], Qkt[r, tl, :], start=False, stop=not ab)
                if ab:
                    p.mm(pov[r, tl, :], U[r, tl, :], nQbt[r, tl, :], start=False, stop=True)
        p.cp(oT[:, :, sl], pov, eng="act")
        yield
        phv = v3(PH, 2)
        for tl in range(2):
            for hp in range(2):
                r = rows(hp)
                p.mm(phv[r, tl, :], kdTok[r, tl, :], vTok[r, tl, :], start=True, stop=not ab)
                if ab:
                    p.mm(phv[r, tl, :], bdTok[r, tl, :], U[r, tl, :], start=False, stop=True)
        for tl in range(2):
            p.stt(Hst[:, tl, :], Hst[:, tl, :], PC[:, tl:tl + 1], phv[:, tl, :], OP.mult, OP.add)
            yield
        p.cp(Hb.ap(), Hst.ap(), eng="act")
        yield


def out_norm_rms(p, oT, gate, gcol, T, W, headsum, rstd_inplace, mixTm):
    t0, t1 = W["t0"], W["t1"]
    p.act(t0[:, :, 0:T], oT[:, :, 0:T], AF.Square)
    yield
    headsum(t1, t0, T)
    yield
    rstd_inplace(t1[:, :, 0:T], T, 1.0 / 64, 1e-6)
    yield
    p.tt(t0[:, :, 0:T], oT[:, :, 0:T], t1[:, :, 0:T], OP.mult)
    yield
    p.act(t1[:, :, 0:T], gate[:, :, 0:T], AF.Silu)
    yield
    p.stt(mixTm[:, :, 0:T], t0[:, :, 0:T], gcol[:, 0:1], t1[:, :, 0:T], OP.mult, OP.mult)
    yield


def sample_core(p, name, W, Hs, fl, PS, ident, dmask, rows, v3, state_in, state_out, l, transposed_state):
    PA, PB, PU, PO, PH, PT = PS
    ab = fl["ab"]
    qT, aT, kT, bT, vT, ldT, oT = (W[n] for n in ("qT", "aT", "kT", "bT", "vT", "ldT", "oT"))
    Snat = W["Snat"]
    Dd, tokS, Ud, Vd, oTok, tmpd = (W[n] for n in ("Dd", "tokS", "Ud", "Vd", "oTok", "tmpd"))
    ptv = v3(PT, 8)
    for tl in range(2):
        for hp in range(2):
            h = 2 * tl + hp
            if not transposed_state:
                p.dma(Hs[rows(hp), tl, :, :], state_in[l, :, h].re("b d v -> d b v"))
            else:
                p.dma(Snat[rows(hp), tl, :, :], state_in[l, :, h].re("b v d -> v b d"))
    if transposed_state:
        for tl in range(2):
            for g in range(2):
                for j in range(8):
                    for hp in range(2):
                        p.tr(ptv[rows(hp), j, :], Snat[rows(hp), tl, 8 * g + j, :], ident[rows(hp), rows(hp)])
                p.cp(Hs[:, tl, 8 * g:8 * g + 8, :], ptv)
    p.act(Dd.ap(), ldT[:, :, 0:16], AF.Exp)
    pt2 = PT[0:16, 0:512].re("p (a b) -> p a b", b=128)
    srcs = [kT, bT, vT] if ab else [kT, vT]
    for qi, src in enumerate(srcs):
        for tl in range(2):
            p.tr(pt2[:, tl, :], src[:, tl, 0:16], ident.ap())
        if ab and qi == 1:
            p.ts(tokS[:, 1, :], pt2[:, 0:2, :].re("p a b -> p (a b)"), -1.0, OP.mult)
        else:
            p.cp(tokS[:, (qi if ab else 2 * qi), :], pt2[:, 0:2, :].re("p a b -> p (a b)"))
    for tl in range(2):
        for hp in range(2):
            h = 2 * tl + hp
            r = rows(hp)
            hc = slice(64 * h, 64 * h + 64)
            p.tt(Vd.ap(), tokS[:, 2, hc][:, None, :].bc([16, 16, 64]), dmask.ap(), OP.mult)
            if ab:
                for g, ps in enumerate((PU, PB)):
                    p.mm(ps[0:16, :], aT[r, tl, 0:16], Hs[r, tl, 8 * g:8 * g + 8, :].re("p b v -> p (b v)"))
                    p.tt(Ud[:, 8 * g:8 * g + 8, :], ps[0:16, :].re("p (b v) -> p b v", v=64),
                         dmask[:, 8 * g:8 * g + 8, :], OP.mult)
            for g, ps in enumerate((PH, PO)):
                p.mm(ps[r, :], tokS[:, 0, hc], Vd[:, 8 * g:8 * g + 8, :].re("p b v -> p (b v)"), start=True, stop=not ab)
                if ab:
                    p.mm(ps[r, :], tokS[:, 1, hc], Ud[:, 8 * g:8 * g + 8, :].re("p b v -> p (b v)"), start=False, stop=True)
        p.tt(Hs[:, tl, :, :], Hs[:, tl, :, :], Dd[:, tl, :][:, :, None].bc([128, 16, 64]), OP.mult)
        for g, ps in enumerate((PH, PO)):
            p.tt(Hs[:, tl, 8 * g:8 * g + 8, :], Hs[:, tl, 8 * g:8 * g + 8, :], ps[:, :].re("p (b v) -> p b v", v=64), OP.add)
        for hp in range(2):
            h = 2 * tl + hp
            r = rows(hp)
            hc = slice(64 * h, 64 * h + 64)
            for g, ps in enumerate((PU, PB)):
                p.mm(ps[0:16, :], qT[r, tl, 0:16], Hs[r, tl, 8 * g:8 * g + 8, :].re("p b v -> p (b v)"))
                p.tt(tmpd[:, 8 * g:8 * g + 8, :], ps[0:16, :].re("p (b v) -> p b v", v=64),
                     dmask[:, 8 * g:8 * g + 8, :], OP.mult)
            p.reduce(oTok[:, hc], tmpd.ap().re("p b v -> p v b"), OP.add)
    for tl in range(2):
        p.tr(PT[:, tl * 16:tl * 16 + 16], oTok[:, tl * 128:(tl + 1) * 128], ident[0:16, 0:16])
    p.cp(oT[:, :, 0:16], PT[:, 0:32].re("p (a b) -> p a b", b=16))
    if transposed_state:
        for tl in range(2):
            for g in range(2):
                for j in range(8):
                    for hp in range(2):
                        p.tr(ptv[rows(hp), j, :], Hs[rows(hp), tl, 8 * g + j, :], ident[rows(hp), rows(hp)])
                p.cp(Snat[:, tl, 8 * g:8 * g + 8, :], ptv)
    for tl in range(2):
        for hp in range(2):
            h = 2 * tl + hp
            if not transposed_state:
                p.dma(state_out[l, :, h].re("b d v -> d b v"), Hs[rows(hp), tl, :, :])
            else:
                p.dma(state_out[l, :, h].re("b v d -> v b d"), Snat[rows(hp), tl, :, :])


TWO_PI = 2.0 * math.pi


def sincos(p, dst_s, dst_c, ang, fr, ii):
    for dst, sh in ((dst_s, 0.0), (dst_c, 0.25)):
        p.ts(dst, ang, sh, OP.add)
        p.cp(ii, dst)
        p.cp(fr, ii)
        p.tt(fr, dst, fr, OP.subtract)
        p.ts(fr, fr, 0.4999995, OP.min, -0.4999995, OP.max)
        p.act(dst, fr, AF.Sin, scale=TWO_PI)


def s5_setup(p, nc, din, l, ident, tidx, PT, PB, rows):
    S = {}
    NS = True

    def ld(name, src):
        t = p.sbc(f"s5{name}", [128, 8])
        for gp in range(2):
            p.dma(t[rows(gp), :], src.re("(pr gp) q -> gp q pr", gp=2)[gp], allow_slow_non_contiguous=NS)
        return t
    lre = ld("lre", din["s5_lam_re"][l])
    lim = ld("lim", din["s5_lam_im"][l])
    stp = p.sbc(f"s5stp", [128, 8])
    for gp in range(2):
        p.dma(stp[rows(gp), :], din["s5_log_step"][l].re("(pr gp) -> gp pr", gp=2)[gp][None, :].bc([64, 8]),
              allow_slow_non_contiguous=NS)
    p.act(stp.ap(), stp.ap(), AF.Exp)
    names = ["lr", "li", "mag", "cs", "sn", "abre", "abim", "nabim", "den", "am1", "zre", "zim", "t", "fr", "ang"]
    c = {n: p.sbc(f"s5{n}", [128, 8]) for n in names}
    ii = p.sbc(f"s5ii", [128, 8], I32)
    p.tt(c["lr"].ap(), lre.ap(), stp.ap(), OP.mult)
    p.tt(c["li"].ap(), lim.ap(), stp.ap(), OP.mult)
    p.act(c["mag"].ap(), c["lr"].ap(), AF.Exp)
    p.ts(c["ang"].ap(), c["li"].ap(), 1.0 / TWO_PI, OP.mult)
    sincos(p, c["sn"].ap(), c["cs"].ap(), c["ang"].ap(), c["fr"].ap(), ii.ap())
    p.tt(c["abre"].ap(), c["mag"].ap(), c["cs"].ap(), OP.mult)
    p.tt(c["abim"].ap(), c["mag"].ap(), c["sn"].ap(), OP.mult)
    p.ts(c["nabim"].ap(), c["abim"].ap(), -1.0, OP.mult)
    p.tt(c["den"].ap(), lre.ap(), lre.ap(), OP.mult)
    p.tt(c["t"].ap(), lim.ap(), lim.ap(), OP.mult)
    p.tt(c["den"].ap(), c["den"].ap(), c["t"].ap(), OP.add)
    p.recip(c["den"].ap(), c["den"].ap())
    p.ts(c["am1"].ap(), c["abre"].ap(), -1.0, OP.add)
    p.tt(c["zre"].ap(), c["am1"].ap(), lre.ap(), OP.mult)
    p.tt(c["t"].ap(), c["abim"].ap(), lim.ap(), OP.mult)
    p.tt(c["zre"].ap(), c["zre"].ap(), c["t"].ap(), OP.add)
    p.tt(c["zre"].ap(), c["zre"].ap(), c["den"].ap(), OP.mult)
    p.tt(c["zim"].ap(), c["abim"].ap(), lre.ap(), OP.mult)
    p.tt(c["t"].ap(), c["am1"].ap(), lim.ap(), OP.mult)
    p.tt(c["zim"].ap(), c["zim"].ap(), c["t"].ap(), OP.subtract)
    p.tt(c["zim"].ap(), c["zim"].ap(), c["den"].ap(), OP.mult)
    bre = p.sbc(f"s5bre", [128, 8, 16])
    bim = p.sbc(f"s5bim", [128, 8, 16])
    for gp in range(2):
        p.dma(bre[rows(gp)], din["s5_b_re"][l].re("(pr gp) q c -> gp q pr c", gp=2)[gp], allow_slow_non_contiguous=NS)
        p.dma(bim[rows(gp)], din["s5_b_im"][l].re("(pr gp) q c -> gp q pr c", gp=2)[gp], allow_slow_non_contiguous=NS)
    BD = [p.sbc(f"s5BD{r}", [128, 8, 64]) for r in range(2)]
    tmp = p.sbc(f"s5tmp", [128, 8, 16])
    tmp2 = p.sbc(f"s5tmp2", [128, 8, 16])
    zre_b = c["zre"].ap()[:, :, None].bc([128, 8, 16])
    zim_b = c["zim"].ap()[:, :, None].bc([128, 8, 16])
    for r in range(2):
        p.memset(BD[r].ap(), 0.0)

    def scatter(r):
        for par in range(2):
            for q in range(4):
                pr = 2 * q + par
                p.cp(BD[r][0:64, pr, par * 32:par * 32 + 16], tmp[0:64, pr, :])
                p.cp(BD[r][64:128, pr, par * 32 + 16:par * 32 + 32], tmp[64:128, pr, :])
    p.tt(tmp.ap(), bre.ap(), zre_b, OP.mult)
    p.tt(tmp2.ap(), bim.ap(), zim_b, OP.mult)
    p.tt(tmp.ap(), tmp.ap(), tmp2.ap(), OP.subtract)
    scatter(0)
    p.tt(tmp.ap(), bim.ap(), zre_b, OP.mult)
    p.tt(tmp2.ap(), bre.ap(), zim_b, OP.mult)
    p.tt(tmp.ap(), tmp.ap(), tmp2.ap(), OP.add)
    scatter(1)
    BT = p.sbc(f"s5BT", [128, 8, 128])
    ptv = PT[:, 0:512].re("p (a b) -> p a b", b=128)
    for r in range(2):
        for pr in range(8):
            hf = (pr % 4) // 2
            p.tr(ptv[64 * hf:64 * hf + 64, (pr // 4) * 2 + pr % 2, :], BD[r][:, pr, :], ident.ap())
        p.cp(BT[:, r * 4:r * 4 + 4, :], ptv)
    par = p.sbc(f"s5par", [128, 2])
    pii = p.sbc(f"s5pii", [128, 1], I32)
    p.iota(pii.ap(), [[0, 1]], base=0, cm=1)
    p.op("dve", lambda e: e.tensor_scalar(pii.h[:], pii.h[:], 4, 1, OP.arith_shift_right, op1=OP.bitwise_and),
         [pii.ap()], [pii.ap()])
    p.cp(par[:, 1:2], pii.ap())
    p.ts(par[:, 0:1], par[:, 1:2], -1.0, OP.mult, 1.0, OP.add)
    CT = p.sbc(f"s5CT", [128, 4, 128])
    Cn = p.sbc(f"s5Cn", [128, 2, 64])
    Cexp = p.sbc(f"s5Cexp", [128, 2, 128])
    pbv = PB[:, 0:512].re("p (a b) -> p a b", b=128)
    for r, nm in enumerate(("s5_c_re", "s5_c_im")):
        p.dma(Cn.ap(), din[nm][l].re("(s g) c q -> (g c) s q", s=2))
        sgn = 1.0 if r == 0 else -1.0
        p.ts(Cexp[:, :, 0:64], Cn.ap(), par[:, 0:1], OP.mult, sgn, OP.mult)
        p.ts(Cexp[:, :, 64:128], Cn.ap(), par[:, 1:2], OP.mult, sgn, OP.mult)
        for s in range(2):
            p.tr(pbv[:, r * 2 + s, :], Cexp[:, s, :], ident.ap())
    p.cp(CT.ap(), pbv)
    CTe = p.sbc("s5CTe", [128, 4, 128])
    CTo = p.sbc("s5CTo", [128, 4, 128])
    p.memset(CTe.ap(), 0.0)
    p.memset(CTo.ap(), 0.0)
    vw = "p s (q pp c) -> p s q pp c"
    p.cp(CTe.ap().re(vw, pp=2, c=32)[:, :, :, 0, :], CT.ap().re(vw, pp=2, c=32)[:, :, :, 0, :])
    p.cp(CTo.ap().re(vw, pp=2, c=32)[:, :, :, 1, :], CT.ap().re(vw, pp=2, c=32)[:, :, :, 1, :])
    cosT = p.sbc(f"s5cosT", [128, 8, ST])
    sinT = p.sbc(f"s5sinT", [128, 8, ST])
    frT = p.sbc(f"s5frT", [128, ST])
    angT = p.sbc(f"s5angT", [128, ST])
    iiT = p.sbc(f"s5iiT", [128, ST], I32)
    for pr in range(8):
        p.ts(angT.ap(), tidx.ap(), c["ang"][:, pr:pr + 1], OP.mult)
        sincos(p, sinT[:, pr, :], cosT[:, pr, :], angT.ap(), frT.ap(), iiT.ap())
    S.update(c)
    S.update(BT=BT, CT=CT, CTe=CTe, CTo=CTo, cosT=cosT, sinT=sinT)
    S["hre"] = p.sbc(f"s5hre", [128, 8])
    S["him"] = p.sbc(f"s5him", [128, 8])
    return S


def s5_block(kind, st, T, W, PS, LW, *, p, proj, OFF, PJ, cnt, ident, identP, mI, mS, nmI, mI4, mS4, dmask, rows, v3,
             headsum, rstd_inplace, din, dout, l, NST, H, Hs, mixTs):
    PA_, PB, PU, PO_, PH_, PT = PS
    S, s5d, s5bg, wglu = LW["S5"], LW["s5d"], LW["s5bg"], LW["wglu"]
    uT, gate, yT = W["qT"], W["gate"], W["oT"]
    PH = W["PH"] if kind == "p" else PO_
    for tl in range(2):
        proj(uT[:, tl, 0:T], OFF["s5_u"] + 128 * tl, 128, T)
        yield
        proj(gate[:, tl, 0:T], OFF["s5_gate"] + 128 * tl, 128, T)
        yield
    BT, CT, cosT, sinT = S["BT"], S["CT"], S["cosT"], S["sinT"]
    xre, xim, gre, gim, hre_t, him_t, ta, tb = (W[n][:, 0, 0:T] for n in ("t0", "t1", "t2", "t3", "kT", "vT", "aT", "bT"))
    g0 = W["g0"]
    if kind == "s":
        hS = W["hS"]
        nat = W["s5nat"]
        for r, nm in enumerate(("state_s5_re", "state_s5_im")):
            p.dma(nat[:, r, :], din[nm][l].re("b g q -> b (g q)"))
            for pr in range(8):
                p.tr(PT[:, pr * 16:pr * 16 + 16], nat[:, r, pr * 128:(pr + 1) * 128], ident[0:16, 0:16])
            p.cp(hS[:, r, :, :], PT[:, 0:128].re("p (a b) -> p a b", b=16))
            yield
    if kind == "p":
        g0a = W["g0a"]
        cs8, sn8, hr8, hi8 = S["cs"].ap(), S["sn"].ap(), S["hre"].ap(), S["him"].ap()
        p.tt(g0a[:, 0, :], cs8, hr8, OP.mult)
        p.tt(g0a[:, 1, :], sn8, hi8, OP.mult)
        p.tt(g0a[:, 2, :], g0a[:, 0, :], g0a[:, 1, :], OP.subtract)
        yield
        p.tt(g0a[:, 0, :], cs8, hi8, OP.mult)
        p.tt(g0a[:, 1, :], sn8, hr8, OP.mult)
        p.tt(g0a[:, 3, :], g0a[:, 0, :], g0a[:, 1, :], OP.add)
        yield
        Xre, Xim, Gre, Gim, Hre, Him, Ta, Tb = (W[n][:, :, 0:T] for n in ("t0", "t1", "t2", "t3", "kT", "vT", "aT", "bT"))
        PUv = PU[:, 0:256].re("p (a b) -> p a b", b=128)[:, :, 0:T]
        PBv = PB[:, 0:256].re("p (a b) -> p a b", b=128)[:, :, 0:T]
        for gi in range(4):
            prs = (2 * gi, 2 * gi + 1)
            for j, pr in enumerate(prs):
                q4, sl = pr % 4, pr // 4
                hf = q4 // 2
                rr = slice(64 * hf, 64 * hf + 64)
                bslot = sl * 2 + pr % 2
                p.mm(PUv[:, j, :], BT[rr, 0 + bslot, :], uT[rr, sl, 0:T])
                p.mm(PBv[:, j, :], BT[rr, 4 + bslot, :], uT[rr, sl, 0:T])
            cs, sn = cosT[:, 2 * gi:2 * gi + 2, 0:T], sinT[:, 2 * gi:2 * gi + 2, 0:T]
            p.tt(Ta, PUv, cs, OP.mult)
            yield
            p.tt(Tb, PBv, sn, OP.mult)
            yield
            p.tt(Xre, Ta, Tb, OP.add, eng="pool")
            yield
            p.tt(Ta, PBv, cs, OP.mult)
            yield
            p.tt(Tb, PUv, sn, OP.mult)
            yield
            p.tt(Xim, Ta, Tb, OP.subtract, eng="pool")
            yield
            for j, pr in enumerate(prs):
                magb = S["mag"][:, pr:pr + 1].bc([128, T])
                p.scan(Gre[:, j, :], magb, Xre[:, j, :], g0a[:, 2, pr:pr + 1], OP.mult, OP.add)
                yield
                p.scan(Gim[:, j, :], magb, Xim[:, j, :], g0a[:, 3, pr:pr + 1], OP.mult, OP.add)
                yield
            p.tt(Ta, Gre, cs, OP.mult)
            yield
            p.tt(Tb, Gim, sn, OP.mult, eng="pool")
            yield
            p.tt(Hre, Ta, Tb, OP.subtract)
            yield
            p.tt(Ta, Gim, cs, OP.mult, eng="pool")
            yield
            p.tt(Tb, Gre, sn, OP.mult)
            yield
            p.tt(Him, Ta, Tb, OP.add)
            yield
            p.cp(S["hre"][:, 2 * gi:2 * gi + 2], Hre[:, :, T - 1])
            p.cp(S["him"][:, 2 * gi:2 * gi + 2], Him[:, :, T - 1])
            yield
            for j, pr in enumerate(prs):
                q4, sl = pr % 4, pr // 4
                hf = q4 // 2
                rr = slice(64 * hf, 64 * hf + 64)
                CTx = S["CTe"] if pr % 2 == 0 else S["CTo"]
                p.mm(PH[rr, sl * ST:sl * ST + T], CTx[:, 0 + sl, rr], Hre[:, j, :], start=(pr % 2 == 0), stop=False)
                p.mm(PH[rr, sl * ST:sl * ST + T], CTx[:, 2 + sl, rr], Him[:, j, :], start=False, stop=(pr % 2 == 1))
    for pr in (range(8) if kind == "s" else ()):
        q4, sl = pr % 4, pr // 4
        hf = q4 // 2
        rr = slice(64 * hf, 64 * hf + 64)
        bslot = sl * 2 + pr % 2
        p.mm(PU[:, 0:T], BT[rr, 0 + bslot, :], uT[rr, sl, 0:T])
        p.mm(PB[:, 0:T], BT[rr, 4 + bslot, :], uT[rr, sl, 0:T])
        if kind == "p":
            cs, sn = cosT[:, pr, 0:T], sinT[:, pr, 0:T]
            p.tt(ta, PU[:, 0:T], cs, OP.mult)
            yield
            p.tt(tb, PB[:, 0:T], sn, OP.mult, eng="dve")
            yield
            p.tt(xre, ta, tb, OP.add, eng="pool")
            yield
            p.tt(ta, PB[:, 0:T], cs, OP.mult)
            yield
            p.tt(tb, PU[:, 0:T], sn, OP.mult)
            yield
            p.tt(xim, ta, tb, OP.subtract, eng="pool")
            yield
            c1, s1 = S["cs"][:, pr:pr + 1], S["sn"][:, pr:pr + 1]
            hr, hi = S["hre"][:, pr:pr + 1], S["him"][:, pr:pr + 1]
            p.tt(g0[:, 0:1], c1, hr, OP.mult)
            yield
            p.tt(g0[:, 1:2], s1, hi, OP.mult)
            yield
            p.tt(g0[:, 2:3], g0[:, 0:1], g0[:, 1:2], OP.subtract)
            yield
            p.tt(g0[:, 0:1], c1, hi, OP.mult)
            yield
            p.tt(g0[:, 1:2], s1, hr, OP.mult)
            yield
            p.tt(g0[:, 3:4], g0[:, 0:1], g0[:, 1:2], OP.add)
            yield
            magb = S["mag"][:, pr:pr + 1].bc([128, T])
            p.scan(gre, magb, xre, g0[:, 2:3], OP.mult, OP.add)
            yield
            p.scan(gim, magb, xim, g0[:, 3:4], OP.mult, OP.add)
            yield
            p.tt(ta, gre, cs, OP.mult)
            yield
            p.tt(tb, gim, sn, OP.mult, eng="pool")
            yield
            p.tt(hre_t, ta, tb, OP.subtract)
            yield
            p.tt(ta, gim, cs, OP.mult, eng="pool")
            yield
            p.tt(tb, gre, sn, OP.mult)
            yield
            p.tt(him_t, ta, tb, OP.add)
            yield
            p.cp(S["hre"][:, pr:pr + 1], hre_t[:, T - 1:T])
            yield
            p.cp(S["him"][:, pr:pr + 1], him_t[:, T - 1:T])
            yield
        else:
            hS = W["hS"]
            hr, hi = hS[:, 0, pr, :], hS[:, 1, pr, :]
            p.ts(ta, hr, S["abre"][:, pr:pr + 1], OP.mult)
            yield
            p.stt(ta, hi, S["nabim"][:, pr:pr + 1], ta, OP.mult, OP.add)
            yield
            p.tt(hre_t, ta, PU[:, 0:T], OP.add)
            yield
            p.ts(tb, hr, S["abim"][:, pr:pr + 1], OP.mult)
            yield
            p.stt(tb, hi, S["abre"][:, pr:pr + 1], tb, OP.mult, OP.add)
            yield
            p.tt(him_t, tb, PB[:, 0:T], OP.add)
            yield
            p.cp(hr, hre_t)
            yield
            p.cp(hi, him_t)
            yield
        CTx = S["CTe"] if pr % 2 == 0 else S["CTo"]
        p.mm(PH[rr, sl * ST:sl * ST + T], CTx[:, 0 + sl, rr], hre_t, start=(pr % 2 == 0), stop=False)
        p.mm(PH[rr, sl * ST:sl * ST + T], CTx[:, 2 + sl, rr], him_t, start=False, stop=(pr % 2 == 1))
    for tl in range(2):
        p.stt(yT[:, tl, 0:T], uT[:, tl, 0:T], s5d[:, tl:tl + 1], PH[:, tl * ST:tl * ST + T], OP.mult, OP.add)
        yield
    if kind == "p" and st == NST - 1:
        for nm, src in (("s5re_p", S["hre"]), ("s5im_p", S["him"])):
            for gp in range(2):
                p.dma(dout[nm][l].re("(pr gp) q -> gp q pr", gp=2)[gp], src[rows(gp), :], allow_slow_non_contiguous=True)
    if kind == "s":
        hS, nat = W["hS"], W["s5nat"]
        for r, nm in enumerate(("s5re_s", "s5im_s")):
            for pr in range(8):
                p.tr(PT[0:16, pr * 128:(pr + 1) * 128] if pr < 4 else PB[0:16, (pr - 4) * 128:(pr - 3) * 128],
                     hS[:, r, pr, :], ident.ap())
            p.cp(nat[:, r, 0:512], PT[0:16, 0:512])
            yield
            p.cp(nat[:, r, 512:1024], PB[0:16, 0:512])
            yield
            p.dma(dout[nm][l].re("b g q -> b (g q)"), nat[:, r, :])
    a, b, gsb = W["t0"][:, :, 0:T], W["t1"][:, :, 0:T], W["t2"][:, :, 0:T]
    y = yT[:, :, 0:T]
    p.tt(a, y, y, OP.mult)
    yield
    p.ts(a, a, 0.044715, OP.mult, 1.0, OP.add)
    yield
    p.tt(a, a, y, OP.mult)
    yield
    p.act(a, a, AF.Tanh, scale=math.sqrt(2.0 / math.pi))
    yield
    p.ts(a, a, 1.0, OP.add, 0.5, OP.mult)
    yield
    p.tt(b, a, y, OP.mult)
    yield
    for tl in range(2):
        pj = PJ[cnt[0] % 2]
        cnt[0] += 1
        for k in range(2):
            p.mm(pj[:, 0:T], wglu[:, k, tl * 128:(tl + 1) * 128], W["t1"][:, k, 0:T], start=(k == 0), stop=(k == 1))
        p.act(W["t2"][:, tl, 0:T], pj[:, 0:T], AF.Sigmoid, bias=s5bg[:, tl:tl + 1])
        yield
    p.tt(gsb, gsb, b, OP.mult)
    yield
    p.act(a, gate[:, :, 0:T], AF.Silu)
    yield
    p.tt(mixTs[1][:, :, 0:T], gsb, a, OP.mult)
    yield


def gdn_block(kind, st, T, W, K, PS, LW, *, p, proj, OFF, PJ, cnt, ident, identP, mI, mS, nmI, mI4, mS4, dmask, rows, v3,
              headsum, rstd_inplace, din, dout, l, NST, H, Hs, mixTs):
    PA, PB, PU, PO, PH, PT = PS
    convw, gab, Eg, Eb, gdn_ng, hist = (LW[n] for n in ("convw", "gab", "Eg", "Eb", "gdn_ng", "hist_gdn"))
    xp = W["xp"]
    cv = W["cv"]
    gate = W["gate"]
    if kind == "p":
        p.cp(xp[:, :, 0:3], hist.ap())
        yield
        for j in range(6):
            proj(xp[:, j, 3:3 + T], OFF["gdn_qkv"] + 128 * j, 128, T)
            yield
        p.cp(hist.ap(), xp[:, :, T:T + 3])
        yield
        taps = [xp[:, :, i:i + T] for i in range(4)]
        if st == NST - 1:
            for r in range(3):
                p.dma(dout["conv_p"][l, r].re("(j q) -> q j", q=128), xp[:, :, T + r], allow_slow_non_contiguous=True)
    else:
        xsS = W["xsS"]
        nat = W["cnat"]
        p.dma(nat.ap(), din["state_gdn_conv"][l])
        for r in range(3):
            for j in range(6):
                p.tr(PT[:, j * 16:j * 16 + 16], nat[:, r, j * 128:(j + 1) * 128], ident[0:16, 0:16])
            p.cp(xsS[:, :, r, :], PT[:, 0:96].re("p (a b) -> p a b", b=16))
            yield
        for j in range(6):
            proj(xsS[:, j, 3, :], OFF["gdn_qkv"] + 128 * j, 128, T)
            yield
        taps = [xsS[:, :, i, :] for i in range(4)]
        p.dma(dout["conv_s"][l, :, 0:2, :], din["state_gdn_conv"][l, :, 1:3, :])
        for j in range(6):
            p.tr(PB[0:16, j * 128:(j + 1) * 128] if j < 4 else PU[0:16, (j - 4) * 128:(j - 3) * 128], xsS[:, j, 3, :],
                 ident.ap())
        p.cp(nat[:, 0, 0:512], PB[0:16, 0:512])
        yield
        p.cp(nat[:, 0, 512:768], PU[0:16, 0:256])
        yield
        p.dma(dout["conv_s"][l, :, 2, :], nat[:, 0, :])
    c = cv[:, :, 0:T]
    for j in range(6):
        cj = cv[:, j, 0:T]
        p.ts(cj, taps[0][:, j, :], convw[:, j, 0:1], OP.mult)
        yield
        for i in range(1, 4):
            p.stt(cj, taps[i][:, j, :], convw[:, j, i:i + 1], cj, OP.mult, OP.add)
            yield
    p.act(c, c, AF.Silu)
    yield
    qT, kT, vT, aT, bT, ldT, oT = (W[n] for n in ("qT", "kT", "vT", "aT", "bT", "ldT", "oT"))
    t0, t1 = W["t0"], W["t1"]
    for src, dst, sc in ((cv[:, 0:2, 0:T], qT, 0.125), (cv[:, 2:4, 0:T], aT, 1.0)):
        p.act(t0[:, :, 0:T], src, AF.Square)
        yield
        headsum(t1, t0, T)
        yield
        rstd_inplace(t1[:, :, 0:T], T, 1.0, 1e-6)
        yield
        p.stt(dst[:, :, 0:T], src, sc, t1[:, :, 0:T], OP.mult, OP.mult)
        yield
    p.cp(vT[:, :, 0:T], cv[:, 4:6, 0:T])
    yield
    abT, gf, bf = W["abT"], W["gf"], W["bf"]
    proj(abT[0:8, 0:T], OFF["gdn_ab"], 8, T)
    yield
    p.act(gf[0:8, 0:T], abT[0:8, 0:T], AF.Exp, bias=gab[0:8, 0:1])
    yield
    p.act(gf[0:8, 0:T], gf[0:8, 0:T], AF.Ln, bias=1.0)
    yield
    p.ts(gf[0:8, 0:T], gf[0:8, 0:T], gab[0:8, 1:2], OP.mult)
    yield
    p.act(bf[0:8, 0:T], abT[0:8, 0:T], AF.Sigmoid)
    yield
    for tl in range(2):
        pj = PJ[cnt[0] % 2]
        cnt[0] += 1
        p.mm(pj[:, 0:T], Eg[0:8, tl, :], gf[0:8, 0:T])
        p.cp(ldT[:, tl, 0:T], pj[:, 0:T], eng="act")
        yield
        pj = PJ[cnt[0] % 2]
        cnt[0] += 1
        p.mm(pj[:, 0:T], Eb[0:8, tl, :], bf[0:8, 0:T])
        p.tt(kT[:, tl, 0:T], aT[:, tl, 0:T], pj[:, 0:T], OP.mult)
        yield
    p.act(t0[:, :, 0:T], ldT[:, :, 0:T], AF.Exp)
    yield
    p.tt(bT[:, :, 0:T], kT[:, :, 0:T], t0[:, :, 0:T], OP.mult)
    yield
    for tl in range(2):
        proj(gate[:, tl, 0:T], OFF["gdn_gate"] + 128 * tl, 128, T)
        yield
    yield from mixer_core(p, "gdn", kind, T, W, K, H["gdn"], Hs["gdn"], dict(ab=True, scalar=True), PS, ident, identP, mI, mS, nmI,
                          mI4, mS4, dmask, rows, v3, din["state_gdn"], dout["gdn_s"], l, transposed_state=False)
    yield from out_norm_rms(p, oT, gate, gdn_ng, T, W, headsum, rstd_inplace, mixTs[2])
    if kind == "p" and st == NST - 1:
        p.dma(dout["gdn_p"][l].re("(t hp) d v -> (hp d) t v", hp=2), H["gdn"].ap())


def rwkv_block(kind, st, T, W, K, PS, LW, *, p, proj, OFF, PJ, cnt, ident, identP, mI, mS, nmI, mI4, mS4, dmask, rows, v3,
               headsum, rstd_inplace, din, dout, l, NST, H, Hs, mixTs):
    PA, PB, PU, PO, PH, PT = PS
    mu, w0, a0, k_k, k_a, r_k, ln_g, ln_b, wlo, hist = (LW[n] for n in (
        "mu", "w0", "a0", "k_k", "k_a", "r_k", "ln_g", "ln_b", "wlo", "hist_rwkv"))
    rin = W["rin"]
    xs = W["xs7"]
    gate = W["gate"]
    if kind == "p":
        p.cp(rin[:, :, 0:1], hist.ap())
        yield
        for j in range(7):
            proj(rin[:, j, 1:1 + T], OFF["rwkv_in"] + 128 * j, 128, T)
            yield
        p.cp(hist.ap(), rin[:, :, T:T + 1])
        yield
        prev, cur = rin[:, :, 0:T], rin[:, :, 1:1 + T]
        if st == NST - 1:
            p.dma(dout["shift_p"][l].re("(j q o) -> q j o", q=128, o=1), rin[:, :, T:T + 1], allow_slow_non_contiguous=True)
    else:
        nat = W["snat"]
        prevS = W["prevS"]
        p.dma(nat.ap(), din["state_rwkv_shift"][l])
        for j in range(7):
            p.tr(PT[:, j * 16:j * 16 + 16], nat[:, j * 128:(j + 1) * 128], ident[0:16, 0:16])
        p.cp(prevS.ap(), PT[:, 0:112].re("p (a b) -> p a b", b=16))
        yield
        for j in range(7):
            proj(rin[:, j, 0:T], OFF["rwkv_in"] + 128 * j, 128, T)
            yield
        prev, cur = prevS.ap(), rin[:, :, 0:T]
        for j in range(7):
            p.tr(PB[0:16, j * 128:(j + 1) * 128] if j < 4 else PU[0:16, (j - 4) * 128:(j - 3) * 128], rin[:, j, 0:T],
                 ident.ap())
        p.cp(nat[:, 0:512], PB[0:16, 0:512])
        yield
        p.cp(nat[:, 512:896], PU[0:16, 0:384])
        yield
        p.dma(dout["shift_s"][l], nat.ap())
    x = xs[:, :, 0:T]
    p.tt(x, prev, cur, OP.subtract)
    yield
    p.tt(x, x, mu.ap()[:, :, None].bc([128, 7, T]), OP.mult)
    yield
    p.tt(x, x, cur, OP.add)
    yield
    qT, kT, vT, aT, bT, ldT, oT = (W[n] for n in ("qT", "kT", "vT", "aT", "bT", "ldT", "oT"))
    t0, t1, t2 = W["t0"], W["t1"], W["t2"]
    p.cp(qT[:, :, 0:T], xs[:, 0:2, 0:T])
    yield
    p.cp(vT[:, :, 0:T], xs[:, 4:6, 0:T])
    yield
    rk = xs[:, 2:4, 0:T]
    p.act(t0[0:64, 0, 0:T], xs[0:64, 6, 0:T], AF.Tanh)
    yield
    asg = t2
    for tl in range(2):
        pj = PJ[cnt[0] % 2]
        cnt[0] += 1
        p.mm(pj[:, 0:T], wlo[0:64, tl * 128:(tl + 1) * 128], t0[0:64, 0, 0:T])
        p.act(ldT[:, tl, 0:T], pj[:, 0:T], AF.Sigmoid, bias=w0[:, tl:tl + 1])
        yield
        pj = PJ[cnt[0] % 2]
        cnt[0] += 1
        p.mm(pj[:, 0:T], wlo[64:128, tl * 128:(tl + 1) * 128], xs[64:128, 6, 0:T])
        p.act(asg[:, tl, 0:T], pj[:, 0:T], AF.Sigmoid, bias=a0[:, tl:tl + 1])
        yield
    p.ts(ldT[:, :, 0:T], ldT[:, :, 0:T], -math.exp(-0.5), OP.mult)
    yield
    p.tt(aT[:, :, 0:T], rk, k_k.ap()[:, :, None].bc([128, 2, T]), OP.mult)
    yield
    p.act(t0[:, :, 0:T], aT[:, :, 0:T], AF.Square)
    yield
    headsum(t1, t0, T)
    yield
    rstd_inplace(t1[:, :, 0:T], T, 1.0, 1e-6)
    yield
    p.tt(aT[:, :, 0:T], aT[:, :, 0:T], t1[:, :, 0:T], OP.mult)
    yield
    p.tt(bT[:, :, 0:T], aT[:, :, 0:T], asg[:, :, 0:T], OP.mult)
    yield
    p.ts(t0[:, :, 0:T], asg[:, :, 0:T], -1.0, OP.add)
    yield
    p.tt(t0[:, :, 0:T], t0[:, :, 0:T], k_a.ap()[:, :, None].bc([128, 2, T]), OP.mult)
    yield
    p.ts(t0[:, :, 0:T], t0[:, :, 0:T], 1.0, OP.add)
    yield
    p.tt(kT[:, :, 0:T], rk, t0[:, :, 0:T], OP.mult)
    yield
    bon = W["bon"]
    p.tt(t0[:, :, 0:T], qT[:, :, 0:T], kT[:, :, 0:T], OP.mult)
    yield
    p.tt(t0[:, :, 0:T], t0[:, :, 0:T], r_k.ap()[:, :, None].bc([128, 2, T]), OP.mult)
    yield
    headsum(t1, t0, T)
    yield
    p.tt(bon[:, :, 0:T], t1[:, :, 0:T], vT[:, :, 0:T], OP.mult)
    yield
    for tl in range(2):
        proj(gate[:, tl, 0:T], OFF["rwkv_gate"] + 128 * tl, 128, T)
        yield
    yield from mixer_core(p, "rwkv", kind, T, W, K, H["rwkv"], Hs["rwkv"], dict(ab=True, scalar=False), PS, ident, identP, mI, mS,
                          nmI, mI4, mS4, dmask, rows, v3, din["state_rwkv"], dout["rwkv_s"], l, transposed_state=True)
    o = oT[:, :, 0:T]
    headsum(t1, oT, T)
    yield
    p.stt(t0[:, :, 0:T], t1[:, :, 0:T], -1.0 / 64, o, OP.mult, OP.add)
    yield
    p.act(t1[:, :, 0:T], t0[:, :, 0:T], AF.Square)
    yield
    headsum(t2, t1, T)
    yield
    rstd_inplace(t2[:, :, 0:T], T, 1.0 / 64, 64e-5)
    yield
    p.tt(t0[:, :, 0:T], t0[:, :, 0:T], t2[:, :, 0:T], OP.mult)
    yield
    p.tt(t0[:, :, 0:T], t0[:, :, 0:T], ln_g.ap()[:, :, None].bc([128, 2, T]), OP.mult)
    yield
    p.tt(t0[:, :, 0:T], t0[:, :, 0:T], ln_b.ap()[:, :, None].bc([128, 2, T]), OP.add)
    yield
    p.tt(t0[:, :, 0:T], t0[:, :, 0:T], bon[:, :, 0:T], OP.add)
    yield
    p.act(t1[:, :, 0:T], gate[:, :, 0:T], AF.Silu)
    yield
    p.tt(mixTs[3][:, :, 0:T], t0[:, :, 0:T], t1[:, :, 0:T], OP.mult)
    yield
    if kind == "p" and st == NST - 1:
        ptv = v3(PT, 2)
        for tl in range(2):
            for hp in range(2):
                p.tr(ptv[rows(hp), tl, :], H["rwkv"][rows(hp), tl, :], ident[rows(hp), rows(hp)])
        p.cp(K["Ut"].ap(), ptv)
        yield
        p.dma(dout["rwkv_p"][l].re("(t hp) v d -> (hp v) t d", hp=2), K["Ut"].ap())


from concourse.bass_utils import run_bass_kernel_spmd

_CACHE = {}


def kernel(**inputs):
    if "nc" not in _CACHE:
        _CACHE["nc"] = build(DEPTH=4, NST=2048 // ST, SAMPLE=True)[0]
    nc = _CACHE["nc"]
    in_maps = []
    for c in range(8):
        m = {}
        for k, shp in SHAPES.items():
            a = np.asarray(inputs[k])
            if k == "x_prompt":
                a = a[c]
            elif k == "x_sample":
                a = a[16 * c:16 * c + 16, 0]
            elif k.startswith("state_"):
                a = a[:, 16 * c:16 * c + 16]
            m[k] = np.ascontiguousarray(a, dtype=np.float32)
        in_maps.append(m)
    res = run_bass_kernel_spmd(nc, in_maps, core_ids=list(range(8)))
    rs = res.results
    outs = []
    for k in OUT_ORDER:
        if k == "y_p":
            o = np.stack([r[k] for r in rs], axis=0)
        elif k == "y_s":
            o = np.concatenate([r[k] for r in rs], axis=0)[:, None, :]
        elif k.endswith("_p"):
            o = np.stack([r[k] for r in rs], axis=1)
        else:
            o = np.concatenate([r[k] for r in rs], axis=1)
        outs.append(np.ascontiguousarray(o, dtype=np.float32))
    return tuple(outs)
```

```python
import contextlib
import numpy as np
import concourse.bass as bass
import concourse.mybir as mybir

F32 = mybir.dt.float32
BF16 = mybir.dt.bfloat16
I32 = mybir.dt.int32
AF = mybir.ActivationFunctionType
OP = mybir.AluOpType
AX = mybir.AxisListType

ENGS = ("pe", "act", "dve", "pool", "sp")
SKIP_SAME_ENGINE = False
SERIALIZE_PSUM_READERS = True


class T:
    def __init__(self, name, handle):
        self.name = name
        self.h = handle
        self.is_psum = False
        self.w = None
        self.r = []

    def __getitem__(self, idx):
        return V(self, self.h[idx])

    def ap(self):
        return V(self, self.h[:])


class V:
    def __init__(self, t, ap):
        self.t = t
        self.a = ap

    def __getitem__(self, idx):
        return V(self.t, self.a[idx])

    def re(self, pat, **kw):
        return V(self.t, self.a.rearrange(pat, **kw))

    def bc(self, shape):
        return V(self.t, self.a.broadcast_to(shape))

    def bitcast(self, dt):
        return V(self.t, self.a.bitcast(dt))

    def ap(self):
        return self


class Prog:
    def __init__(self, nc, n_dma_sems=24):
        self.nc = nc
        self.st = contextlib.ExitStack()
        self.q = {e: [] for e in ENGS}
        self.cnt = {e: 0 for e in ENGS}
        self.sem = {e: self.st.enter_context(nc.semaphore("pg_" + e)) for e in ENGS}
        self.known = {e: {} for e in ENGS}
        self.dsem = [self.st.enter_context(nc.semaphore(f"dma{i}")) for i in range(n_dma_sems)]
        self.dcnt = [0] * n_dma_sems
        self.dnext = 0
        self.pool_sems = []
        self.pool_used = 0
        self.tiles = {}
        self.n_inst = 0

    def sb(self, name, shape, dt=F32):
        h = self.st.enter_context(self.nc.sbuf_tensor(name, list(shape), dt))
        t = T(name, h)
        self.tiles[name] = t
        return t

    def sbc(self, name, shape, dt=F32):
        if name in self.tiles:
            return self.tiles[name]
        return self.sb(name, shape, dt)

    def ps(self, name, shape, dt=F32):
        h = self.st.enter_context(self.nc.psum_tensor(name, list(shape), dt))
        t = T(name, h)
        t.is_psum = True
        self.tiles[name] = t
        return t

    def dram(self, name, shape, dt=F32, kind="Internal"):
        h = self.nc.dram_tensor(name, list(shape), dt, kind=kind)
        t = T(name, h.ap())
        self.tiles[name] = t
        return t

    def _need(self, eng, dep):
        if dep is None:
            return
        kind, i, c = dep
        if kind == "e" and i == eng and (SKIP_SAME_ENGINE or eng == "pe"):
            return
        key = (kind, i)
        if self.known[eng].get(key, 0) >= c:
            return
        self.known[eng][key] = c
        sem = self.sem[i] if kind == "e" else self.dsem[i]
        self.q[eng].append(lambda e, sem=sem, c=c: e.wait_ge(sem, c))

    def _deps(self, eng, reads, writes):
        for v in reads:
            if v is None or not isinstance(v, V):
                continue
            self._need(eng, v.t.w)
        for v in writes:
            self._need(eng, v.t.w)
            for r in v.t.r:
                self._need(eng, r)

    def _mark(self, token, reads, writes):
        for v in reads:
            if v is None or not isinstance(v, V):
                continue
            v.t.r.append(token)
            if len(v.t.r) > 64:
                best = {}
                for k, i, c in v.t.r:
                    best[(k, i)] = max(best.get((k, i), 0), c)
                v.t.r = [(k, i, c) for (k, i), c in best.items()]
        for v in writes:
            v.t.w = token
            v.t.r = []

    def op(self, eng, fn, reads, writes):
        if eng != "pe" and SERIALIZE_PSUM_READERS:
            writes = list(writes) + [v for v in reads if isinstance(v, V) and v.t.is_psum]
        self._deps(eng, reads, writes)
        self.cnt[eng] += 1
        c = self.cnt[eng]
        sem = self.sem[eng]
        self.q[eng].append(lambda e, fn=fn, sem=sem: fn(e).then_inc(sem, 1))
        self.known[eng][("e", eng)] = max(self.known[eng].get(("e", eng), 0), 0)
        self._mark(("e", eng, c), reads, writes)
        self.n_inst += 1

    def dma(self, out, in_, eng="sp", **kw):
        self._deps(eng, [in_], [out])
        if eng == "pool" and self.pool_used < 40:
            self.dsem.append(self.st.enter_context(self.nc.semaphore(f"pdma{self.pool_used}")))
            self.dcnt.append(0)
            self.pool_used += 1
            i = len(self.dsem) - 1
        else:
            i = self.dnext
            self.dnext = (self.dnext + 1) % 24
        self.dcnt[i] += 16
        c = self.dcnt[i]
        sem = self.dsem[i]
        oa, ia = out.a, in_.a
        self.q[eng].append(lambda e, oa=oa, ia=ia, sem=sem, kw=kw: e.dma_start(out=oa, in_=ia, **kw).then_inc(sem, 16))
        self._mark(("d", i, c), [in_], [out])
        self.n_inst += 1

    def barrier(self):
        for e in ENGS:
            for o in ENGS:
                if o != e and self.cnt[o]:
                    self._need(e, ("e", o, self.cnt[o]))
            for i, c in enumerate(self.dcnt):
                if c:
                    self._need(e, ("d", i, c))

    def wait_all_dma(self, eng="sp"):
        for i, c in enumerate(self.dcnt):
            if c:
                self._need(eng, ("d", i, c))

    def mm(self, out, lhsT, rhs, start=True, stop=True, **kw):
        reads = [lhsT, rhs] + ([] if start else [out])
        self.op("pe", lambda e: e.matmul(out.a, lhsT.a, rhs.a, start=start, stop=stop, **kw), reads, [out])

    def tr(self, out, in_, ident):
        if out.a.start_partition != 0 or in_.a.dtype != F32:
            return self.mm(out, in_, ident)
        self.op("pe", lambda e: e.transpose(out.a, in_.a, ident.a), [in_, ident], [out])

    def act(self, out, in_, func, bias=None, scale=None, accum=None, eng="act"):
        kw = {}
        reads = [in_]
        if bias is not None:
            kw["bias"] = bias.a if isinstance(bias, V) else bias
            reads.append(bias)
        if scale is not None:
            kw["scale"] = scale.a if isinstance(scale, V) else scale
            reads.append(scale)
        writes = [out]
        if accum is not None:
            kw["accum_out"] = accum.a
            writes.append(accum)
        self.op(eng, lambda e: e.activation(out.a, in_.a, func, **kw), reads, writes)

    def tt(self, out, a, b, op, eng="dve"):
        self.op(eng, lambda e: e.tensor_tensor(out.a, a.a, b.a, op), [a, b], [out])

    def ts(self, out, a, s1, op0, s2=None, op1=None, eng="dve", accum=None):
        reads = [a, s1, s2]
        x1 = s1.a if isinstance(s1, V) else s1
        x2 = s2.a if isinstance(s2, V) else s2
        kw = {}
        if op1 is not None:
            kw["op1"] = op1
        writes = [out]
        if accum is not None:
            kw["accum_out"] = accum.a
            writes.append(accum)
        self.op(eng, lambda e: e.tensor_scalar(out.a, a.a, x1, x2, op0, **kw), reads, writes)

    def stt(self, out, a, s, b, op0, op1, eng="dve"):
        x = s.a if isinstance(s, V) else s
        self.op(eng, lambda e: e.scalar_tensor_tensor(out.a, a.a, x, b.a, op0, op1), [a, s, b], [out])

    def cp(self, out, in_, eng="dve"):
        if eng == "act":
            self.op(eng, lambda e: e.copy(out.a, in_.a), [in_], [out])
        else:
            self.op(eng, lambda e: e.tensor_copy(out.a, in_.a), [in_], [out])

    def memset(self, out, val, eng="dve"):
        self.op(eng, lambda e: e.memset(out.a, val), [], [out])

    def scan(self, out, d0, d1, init, op0, op1):
        x = init.a if isinstance(init, V) else init
        self.op("dve", lambda e: e.tensor_tensor_scan(out.a, d0.a, d1.a, x, op0, op1), [d0, d1, init], [out])

    def reduce(self, out, in_, op, axis=AX.X):
        self.op("dve", lambda e: e.tensor_reduce(out.a, in_.a, axis, op), [in_], [out])

    def recip(self, out, in_):
        self.op("dve", lambda e: e.reciprocal(out.a, in_.a), [in_], [out])

    def iota(self, out, pattern, base=0, cm=0, **kw):
        self.op("pool", lambda e: e.iota(out.a, pattern, base=base, channel_multiplier=cm, **kw), [], [out])

    def affsel(self, out, in_, pattern, cmp, fill, base=0, cm=0):
        self.op("pool", lambda e: e.affine_select(out.a, in_.a, pattern, cmp, fill, base=base, channel_multiplier=cm),
                [in_], [out])

    def finish(self):
        self.wait_all_dma("sp")
        for e in ENGS:
            if e != "sp" and self.cnt[e]:
                self._need("sp", ("e", e, self.cnt[e]))
        nc = self.nc
        q = self.q
        with nc.Block() as block:
            @block.tensor
            def _(e):
                for f in q["pe"]:
                    f(e)

            @block.scalar
            def _(e):
                for f in q["act"]:
                    f(e)

            @block.vector
            def _(e):
                for f in q["dve"]:
                    f(e)

            @block.gpsimd
            def _(e):
                for f in q["pool"]:
                    f(e)

            @block.sync
            def _(e):
                for f in q["sp"]:
                    f(e)
        self.st.close()


import math
ST = 128
C = 64
NCH = ST // C
OFF = dict(gla_q=0, gla_k=256, gla_v=512, glr=768, gla_gate=784, s5_u=1040, s5_gate=1296,
           gdn_qkv=1552, gdn_ab=2320, gdn_gate=2328, rwkv_in=2584, rwkv_gate=3480)
D_IN = 3736
SHAPES = dict(
    x_prompt=[2048, 1024], x_sample=[16, 1024],
    state_gla=[4, 16, 4, 64, 64], state_s5_re=[4, 16, 16, 64], state_s5_im=[4, 16, 16, 64],
    state_gdn=[4, 16, 4, 64, 64], state_gdn_conv=[4, 16, 3, 768], state_rwkv=[4, 16, 4, 64, 64],
    state_rwkv_shift=[4, 16, 896],
    norm_g=[4, 1024], w_in=[4, 1024, 3736], gla_wg2=[4, 16, 256], gla_bg=[4, 256], gla_norm_g=[4, 64],
    s5_lam_re=[4, 16, 64], s5_lam_im=[4, 16, 64], s5_log_step=[4, 16], s5_b_re=[4, 16, 64, 16],
    s5_b_im=[4, 16, 64, 16], s5_c_re=[4, 16, 16, 64], s5_c_im=[4, 16, 16, 64], s5_d=[4, 256],
    s5_w_glu=[4, 256, 256], s5_b_glu=[4, 256], gdn_conv_w=[4, 4, 768], gdn_a_log=[4, 4], gdn_dt_bias=[4, 4],
    gdn_norm_g=[4, 64], rwkv_mu=[4, 896], rwkv_w0=[4, 256], rwkv_ww2=[4, 64, 256], rwkv_a0=[4, 256],
    rwkv_wa2=[4, 64, 256], rwkv_k_k=[4, 256], rwkv_k_a=[4, 256], rwkv_r_k=[4, 4, 64], rwkv_ln_g=[4, 256],
    rwkv_ln_b=[4, 256], w_out=[4, 1024, 1024], final_g=[1024])
OUT_SHAPES = dict(
    y_p=[2048, 1024], y_s=[16, 1024],
    gla_p=[4, 4, 64, 64], s5re_p=[4, 16, 64], s5im_p=[4, 16, 64], gdn_p=[4, 4, 64, 64], conv_p=[4, 3, 768],
    rwkv_p=[4, 4, 64, 64], shift_p=[4, 896],
    gla_s=[4, 16, 4, 64, 64], s5re_s=[4, 16, 16, 64], s5im_s=[4, 16, 16, 64], gdn_s=[4, 16, 4, 64, 64],
    conv_s=[4, 16, 3, 768], rwkv_s=[4, 16, 4, 64, 64], shift_s=[4, 16, 896])
OUT_ORDER = ["y_p", "y_s", "gla_p", "s5re_p", "s5im_p", "gdn_p", "conv_p", "rwkv_p", "shift_p",
             "gla_s", "s5re_s", "s5im_s", "gdn_s", "conv_s", "rwkv_s", "shift_s"]


def build(DEPTH=4, NST=8, SAMPLE=True, MIX=("gla", "s5", "gdn", "rwkv"), STREAMS=2, PSMODE=0):
    nc = bass.Bass("TRN2", target_bir_lowering=False, dynamic_dma_scratch_size=4096)
    p = Prog(nc)
    din = {k: p.dram(k, v, F32, kind="ExternalInput") for k, v in SHAPES.items()}
    dout = {k: p.dram(k, v, F32, kind="ExternalOutput") for k, v in OUT_SHAPES.items()}
    xbuf = p.dram("xbuf", [2048, 1024], F32)
    NS = True

    def rows(hp):
        return slice(64 * hp, 64 * hp + 64)

    ident = p.sb("ident", [128, 128])
    p.memset(ident.ap(), 1.0, eng="pool")
    p.affsel(ident.ap(), ident.ap(), [[-1, 128]], OP.is_equal, 0.0, base=0, cm=1)
    identb = p.sb("identb", [128, 128], BF16)
    p.cp(identb.ap(), ident.ap())
    identP = p.sb("identP", [128, 2, 64])
    for tl in range(2):
        p.cp(identP[0:64, tl, :], ident[0:64, 0:64])
        p.cp(identP[64:128, tl, :], ident[64:128, 64:128])
    mI = p.sb("mI", [128, 64])
    mS = p.sb("mS", [128, 64])
    for hp in range(2):
        p.memset(mI[rows(hp), :], 1.0, eng="pool")
        p.affsel(mI[rows(hp), :], mI[rows(hp), :], [[1, 64]], OP.is_ge, 0.0, base=0, cm=-1)
        p.memset(mS[rows(hp), :], 1.0, eng="pool")
        p.affsel(mS[rows(hp), :], mS[rows(hp), :], [[1, 64]], OP.is_ge, 0.0, base=-1, cm=-1)
    nmI = p.sb("nmI", [128, 64])
    p.ts(nmI.ap(), mI.ap(), -1.0, OP.mult)
    mS4 = p.sb("mS4", [128, 4, 64])
    mI4 = p.sb("mI4", [128, 4, 64])
    for i in range(4):
        p.ts(mS4[:, i, :], mS.ap(), -1.0 if i < 2 else 1.0, OP.mult)
        p.ts(mI4[:, i, :], mI.ap(), 1.0 if i < 2 else -1.0, OP.mult)
    bones = p.sb("bones", [128, 128])
    p.memset(bones.ap(), 0.0)
    p.memset(bones[0:64, 0:64], 1.0)
    p.memset(bones[64:128, 64:128], 1.0)
    Eg = p.sb("Eg", [8, 2, 128])
    Eb = p.sb("Eb", [8, 2, 128])
    for E, sh in ((Eg, 0), (Eb, 4)):
        p.memset(E.ap(), 1.0, eng="pool")
        p.affsel(E.ap(), E.ap(), [[128, 2], [1, 128]], OP.is_ge, 0.0, base=64 * sh, cm=-64)
        p.affsel(E.ap(), E.ap(), [[-128, 2], [-1, 128]], OP.is_ge, 0.0, base=63 - 64 * sh, cm=64)
    dmask = p.sb("dmask", [16, 16, 64])
    p.memset(dmask.ap(), 1.0, eng="pool")
    p.affsel(dmask.ap(), dmask.ap(), [[-1, 16], [0, 64]], OP.is_equal, 0.0, base=0, cm=1)
    tidx = p.sb("tidx", [128, ST])
    p.iota(tidx.ap(), [[1, ST]], base=0, cm=0, allow_small_or_imprecise_dtypes=True)
    ones = p.sb("ones", [128, ST])
    p.memset(ones.ap(), 1.0)
    fg = p.sb("fg", [128, 1024])
    p.dma(fg.ap(), din["final_g"].ap().re("(o n) -> o n", o=1).bc([128, 1024]))

    PJ = [p.ps("PJ0", [128, 512]), p.ps("PJ1", [128, 512])]
    BK = {nm: p.ps(nm, [128, 512]) for nm in ("PT_A", "PA_A", "PX_A", "PT_B", "PA_B", "PX_B")}

    def psset(sfx):
        b1, b2, b3 = BK["PT_" + sfx], BK["PA_" + sfx], BK["PX_" + sfx]
        return (b2, b1, b3, b2, b1, b1)
    PS_A, PS_B = psset("A"), psset("B")
    if PSMODE == 1:
        PS_A = PS_B = (BK["PA_A"], BK["PX_A"], BK["PT_B"], BK["PA_B"], BK["PX_B"], BK["PT_A"])
    PS_FULL = (BK["PA_A"], BK["PX_A"], BK["PT_B"], BK["PA_B"], BK["PX_B"], BK["PT_A"])
    PT = BK["PT_A"]
    PB = BK["PX_A"]

    def v3(ps, n):
        return ps[:, 0:n * 64].re("p (a b) -> p a b", b=64)

    WGRP = [(1552, 2584), (2584, 3736), (0, 1040), (1040, 1552)]
    Wins = [p.sb(f"Win{i}", [128, 8, c1 - c0], BF16) for i, (c0, c1) in enumerate(WGRP)]
    Wout = p.sb("Wout", [128, 8, 1024], BF16)
    xts = [p.sb(f"xt{b}", [128, 1, 1024]) for b in range(2)]
    xn = p.sb("xn", [128, 1024])
    xn2 = p.sb("xn2", [128, 1024])
    hTs = [p.sb(f"hT{b}", [128, 8, ST], BF16) for b in range(2)]
    mixT2 = [[p.sb(f"mixT{b}_{i}", [128, 2, ST], BF16) for i in range(4)] for b in range(2)]
    ss = p.sb("ss", [128, 1])
    ss2 = p.sb("ss2", [128, 1])
    cur = {"hT": hTs[0]}
    ng = p.sb("ng", [128, 8])
    cnt = [0]

    RAW = p.sb("RAW", [128, 8704])
    carve_off = [0]

    def carve(name, shape, dt=F32):
        n = 1
        for d in shape[1:]:
            n *= d
        n32 = n if dt != BF16 else (n + 1) // 2
        a = RAW.h[0:shape[0], carve_off[0]:carve_off[0] + n32]
        carve_off[0] += n32
        assert carve_off[0] <= 8704, carve_off[0]
        if dt != F32:
            a = a.bitcast(dt)
        if len(shape) > 2:
            names = " ".join(f"d{i}" for i in range(1, len(shape)))
            a = a.rearrange(f"p ({names}) -> p {names}", **{f"d{i}": shape[i] for i in range(1, len(shape) - 1)})
        t = type(ones)(name, a)
        p.tiles[name] = t
        return t

    def make_set(sfx, alloc):
        W = {}
        for nm in ["qT", "kT", "vT", "aT", "bT", "ldT", "gate", "oT", "t0", "t1", "t2", "t3", "bon"]:
            W[nm] = alloc(f"w{sfx}_" + nm, [128, 2, ST])
        W["ones"] = ones
        W["g0"] = alloc(f"w{sfx}_g0", [128, 4])
        W["g0a"] = alloc(f"w{sfx}_g0a", [128, 4, 8])
        for nm in ("ub", "hbr", "hbi"):
            W[nm] = alloc(f"w{sfx}_" + nm, [128, 2, ST], BF16)
        K = {}
        for nm in ["cum", "cumx", "E", "qs", "as", "ks", "bs", "kd", "bd", "X", "R2", "Ut", "WtT", "U", "araw", "qraw", "vb", "Hb"]:
            K[nm] = alloc(f"k{sfx}_" + nm, [128, 2, 64], F32 if nm in ("cum", "cumx", "E", "Ut") else BF16)
        K["identb"] = identb
        K["gam"] = alloc(f"k{sfx}_gam4", [128, 4, 64])
        K["tok"] = alloc(f"k{sfx}_tok", [128, 8, 64], BF16)
        K["gtok"] = alloc(f"k{sfx}_gtok", [128, 2, 64])
        K["amat"] = alloc(f"k{sfx}_amat", [128, 8, 64], BF16)
        K["BB"] = [alloc(f"k{sfx}_BB0", [128, 4, 64], BF16), alloc(f"k{sfx}_BB1", [128, 4, 64], BF16)]
        K["PC"] = alloc(f"k{sfx}_PC", [128, 2])
        return W, K
    W, K = make_set("A", p.sb)
    carve_off[0] = 0
    W2, K2 = make_set("B", carve)
    W2["rin"] = carve("wB_rin", [128, 7, ST + 1])
    W2["xs7"] = carve("wB_xs7", [128, 7, ST])
    W2["PH"] = BK["PA_B"]
    W["PH"] = BK["PA_B"]
    set_b_end = carve_off[0]
    W["xp"] = p.sb("wA_xp", [128, 6, ST + 3])
    W["cv"] = p.sb("wA_cv", [128, 6, ST])
    W["abT"] = p.sb("w_abT", [8, ST])
    W["gf"] = p.sb("w_gf", [8, ST])
    W["bf"] = p.sb("w_bf", [8, ST])
    W["rin"] = W["xp"]
    carve_off[0] = 0
    xs_s = p.sb("xs_s", [16, 1024])
    big1 = carve("big1", [128, 2304])
    W["Snat"] = big1[:, 0:2048].re("p (a b c) -> p a b c", a=2, b=16)
    W["s5nat"] = big1[0:16, 0:2048].re("p (a b) -> p a b", a=2)
    W["cnat"] = big1[0:16, 0:2304].re("p (a b) -> p a b", a=3)
    W["snat"] = big1[0:16, 0:896]
    Hs1 = carve("Hs1", [128, 2, 16, 64])
    W["Dd"] = carve("w_Dd", [128, 2, 16])
    W["tokS"] = carve("w_tokS", [16, 3, 256])
    W["Ud"] = carve("w_Ud", [16, 16, 64])
    W["Vd"] = carve("w_Vd", [16, 16, 64])
    W["tmpd"] = W["Ud"]
    W["oTok"] = carve("w_oTok", [16, 256])
    W["hS"] = carve("w_hS", [128, 2, 8, 16])
    W["xsS"] = carve("w_xsS", [128, 6, 4, 16])
    W["prevS"] = carve("w_prevS", [128, 7, 16])
    W["rinS"] = carve("w_rinS", [128, 7, 16])
    W["xs7S"] = carve("w_xs7S", [128, 7, 16])
    H = {m: p.sb("H_" + m, [128, 2, 64]) for m in ("gla", "gdn", "rwkv")}
    Hs = {m: Hs1 for m in ("gla", "gdn", "rwkv")}

    def colvec(name, src, n):
        t = p.sb(name, [128, n // 128])
        p.dma(t.ap(), src.re("(k p) -> p k", p=128), allow_slow_non_contiguous=NS)
        return t

    def proj(dst, col0, n, T, scale=1.0):
        pj = PJ[cnt[0] % 2]
        cnt[0] += 1
        gi = [i for i, (c0, c1) in enumerate(WGRP) if c0 <= col0 < c1][0]
        Wg, cb = Wins[gi], col0 - WGRP[gi][0]
        for k in range(8):
            p.mm(pj[0:n, 0:T], Wg[:, k, cb:cb + n], cur["hT"][:, k, 0:T], start=(k == 0), stop=(k == 7))
        p.act(dst, pj[0:n, 0:T], AF.Identity, scale=scale)

    def rstd_inplace(t, T, mult, eps):
        p.ts(t, t, mult, OP.mult, eps, OP.add)
        p.act(t, t, AF.Sqrt)
        p.recip(t, t)

    def headsum(dst, src, T):
        for tl in range(2):
            pj = PJ[cnt[0] % 2]
            cnt[0] += 1
            p.mm(pj[:, 0:T], bones.ap(), src[:, tl, 0:T])
            p.cp(dst[:, tl, 0:T], pj[:, 0:T], eng="act")

    def silu_(dst, src):
        p.act(dst, src, AF.Silu)

    for l in range(DEPTH):
        for i, (c0, c1) in enumerate(WGRP):
            p.dma(Wins[i].ap(), din["w_in"][l, :, c0:c1].re("(k q) c -> q k c", q=128), eng="pool")
        p.dma(Wout.ap(), din["w_out"][l].re("(k q) c -> q k c", q=128), eng="pool")
        p.dma(ng.ap(), din["norm_g"][l].re("(k p) -> p k", p=128), allow_slow_non_contiguous=NS)
        L = {}
        wg2 = p.sbc(f"wg2", [32, 256])
        p.memset(wg2.ap(), 0.0)
        p.dma(wg2[0:16, :], din["gla_wg2"][l])
        p.dma(wg2[16:17, :], din["gla_bg"][l].re("(o n) -> o n", o=1))
        gla_ng = p.sbc(f"gla_ng", [128, 1])
        gdn_ng = p.sbc(f"gdn_ng", [128, 1])
        for hp in range(2):
            p.dma(gla_ng[rows(hp), :], din["gla_norm_g"][l].re("(n o) -> n o", o=1), allow_slow_non_contiguous=NS)
            p.dma(gdn_ng[rows(hp), :], din["gdn_norm_g"][l].re("(n o) -> n o", o=1), allow_slow_non_contiguous=NS)
        s5d = colvec(f"s5d_{l}", din["s5_d"][l], 256)
        s5bg = colvec(f"s5bg_{l}", din["s5_b_glu"][l], 256)
        wglu = p.sbc(f"wglu", [128, 2, 256])
        p.dma(wglu.ap(), din["s5_w_glu"][l].re("(k p) n -> p k n", p=128))
        convw = p.sbc(f"convw", [128, 6, 4])
        for i in range(4):
            p.dma(convw[:, :, i], din["gdn_conv_w"][l, i].re("(j p) -> p j", p=128), allow_slow_non_contiguous=NS)
        gab = p.sbc(f"gab", [8, 2])
        p.memset(gab.ap(), 0.0)
        p.dma(gab[0:4, 0:1], din["gdn_dt_bias"][l].re("(n o) -> n o", o=1), allow_slow_non_contiguous=NS)
        p.dma(gab[0:4, 1:2], din["gdn_a_log"][l].re("(n o) -> n o", o=1), allow_slow_non_contiguous=NS)
        p.act(gab[:, 1:2], gab[:, 1:2], AF.Exp)
        p.ts(gab[:, 1:2], gab[:, 1:2], -1.0, OP.mult)
        mu = colvec(f"mu_{l}", din["rwkv_mu"][l], 896)
        w0 = colvec(f"w0_{l}", din["rwkv_w0"][l], 256)
        a0 = colvec(f"a0_{l}", din["rwkv_a0"][l], 256)
        k_k = colvec(f"kk_{l}", din["rwkv_k_k"][l], 256)
        k_a = colvec(f"ka_{l}", din["rwkv_k_a"][l], 256)
        r_k = colvec(f"rk_{l}", din["rwkv_r_k"][l].re("h n -> (h n)"), 256)
        ln_g = colvec(f"lng_{l}", din["rwkv_ln_g"][l], 256)
        ln_b = colvec(f"lnb_{l}", din["rwkv_ln_b"][l], 256)
        wlo = p.sbc(f"wlo", [128, 256])
        p.dma(wlo[0:64, :], din["rwkv_ww2"][l])
        p.dma(wlo[64:128, :], din["rwkv_wa2"][l])

        S5 = {}

        for m in H:
            p.memset(H[m].ap(), 0.0)
        hist_gdn = p.sbc(f"hist_gdn", [128, 6, 3])
        hist_rwkv = p.sbc(f"hist_rwkv", [128, 7, 1])
        p.memset(hist_gdn.ap(), 0.0)
        p.memset(hist_rwkv.ap(), 0.0)

        common = dict(p=p, proj=proj, OFF=OFF, PJ=PJ, cnt=cnt, ident=ident, identP=identP, mI=mI, mS=mS, nmI=nmI,
                      mI4=mI4, mS4=mS4, dmask=dmask, rows=rows, v3=v3, headsum=headsum, rstd_inplace=rstd_inplace,
                      din=din, dout=dout, l=l, NST=NST, H=H, Hs=Hs)
        LW = dict(wg2=wg2, gla_ng=gla_ng, gdn_ng=gdn_ng, s5d=s5d, s5bg=s5bg, wglu=wglu, convw=convw, gab=gab, Eg=Eg,
                  Eb=Eb, mu=mu, w0=w0, a0=a0, k_k=k_k, k_a=k_a, r_k=r_k, ln_g=ln_g, ln_b=ln_b, wlo=wlo,
                  hist_gdn=hist_gdn, hist_rwkv=hist_rwkv, S5=S5)
        last_layer = (l == DEPTH - 1)

        def run_streams(gens):
            gens = [g for g in gens if g is not None]
            while gens:
                for g in list(gens):
                    try:
                        next(g)
                    except StopIteration:
                        gens.remove(g)

        def chain(*gs):
            for g in gs:
                if g is not None:
                    yield from g

        def head_gen(kind, st, bi):
            if kind == "p":
                r0 = st * ST
                src = din["x_prompt"] if l == 0 else xbuf
                xv = xts[bi][:, 0, :]
                p.dma(xv, src[r0:r0 + 128, :])
                np_ = 128
            else:
                xv = xs_s[0:16, :]
                if l == 0:
                    p.dma(xv, din["x_sample"].ap())
                np_ = 16
            p.act(xn[0:np_, :], xv, AF.Square, accum=ss[0:np_, :])
            yield
            rstd_inplace(ss[0:np_, :], 1, 1.0 / 1024, 1e-6)
            yield
            p.ts(xn[0:np_, :], xv, ss[0:np_, :], OP.mult)
            yield
            for kk in range(2):
                pj = PJ[cnt[0] % 2]
                cnt[0] += 1
                for j in range(4):
                    k = kk * 4 + j
                    p.tr(pj[:, j * 128:j * 128 + np_], xn[0:np_, k * 128:(k + 1) * 128], ident[0:np_, 0:np_])
                p.tt(hTs[bi][:, kk * 4:kk * 4 + 4, 0:np_],
                     pj[:, 0:512].re("p (a b) -> p a b", b=128)[:, :, 0:np_],
                     ng[:, kk * 4:kk * 4 + 4, None].bc([128, 4, np_]), OP.mult)
                yield

        def tail_gen(kind, st, bi):
            np_ = 128 if kind == "p" else 16
            xv = xts[bi][:, 0, :] if kind == "p" else xs_s[0:16, :]
            r0 = st * ST
            mt = mixT2[bi]
            for half in range(2):
                pj = PJ[cnt[0] % 2]
                cnt[0] += 1
                for k in range(8):
                    p.mm(pj[0:np_, :], mt[k // 2][:, k % 2, 0:np_], Wout[:, k, half * 512:(half + 1) * 512],
                         start=(k == 0), stop=(k == 7))
                p.tt(xv[:, half * 512:(half + 1) * 512], xv[:, half * 512:(half + 1) * 512], pj[0:np_, :], OP.add)
                yield
            if not last_layer:
                if kind == "p":
                    p.dma(xbuf[r0:r0 + 128, :], xv)
            else:
                p.act(xn2[0:np_, :], xv, AF.Square, accum=ss2[0:np_, :])
                yield
                rstd_inplace(ss2[0:np_, :], 1, 1.0 / 1024, 1e-6)
                yield
                p.stt(xn2[0:np_, :], xv, ss2[0:np_, :], fg[0:np_, :], OP.mult, OP.mult)
                yield
                if kind == "p":
                    p.dma(dout["y_p"][r0:r0 + 128, :], xn2[0:np_, :])
                else:
                    p.dma(dout["y_s"].ap(), xn2[0:np_, :])

        def mixers(kind, st, bi):
            T = ST if kind == "p" else 16
            cur["hT"] = hTs[bi]
            cm = dict(common, mixTs=mixT2[bi])
            for i, m in enumerate(("gla", "s5", "gdn", "rwkv")):
                if m not in MIX:
                    p.memset(mixT2[bi][i][:, :, 0:T], 0.0)
            if kind == "p":
                ga = chain(gdn_block(kind, st, T, W, K, PS_A, LW, **cm) if "gdn" in MIX else None,
                           gla_block(kind, st, T, W, K, PS_A, LW, **cm) if "gla" in MIX else None)
                gb = chain(rwkv_block(kind, st, T, W2, K2, PS_B, LW, **cm) if "rwkv" in MIX else None,
                           s5_setup(S5, p, nc, din, l, ident, tidx, BK["PT_B"], BK["PX_B"], rows)
                           if ("s5" in MIX and st == 0) else None,
                           s5_block(kind, st, T, W2, PS_B, LW, **cm) if "s5" in MIX else None)
                return [ga, gb] if STREAMS == 2 else [chain(ga, gb)]
            W["rin"], W["xs7"] = W["rinS"], W["xs7S"]
            return [chain(gla_block(kind, st, T, W, K, PS_FULL, LW, **cm) if "gla" in MIX else None,
                          s5_block(kind, st, T, W, PS_FULL, LW, **cm) if "s5" in MIX else None,
                          gdn_block(kind, st, T, W, K, PS_FULL, LW, **cm) if "gdn" in MIX else None,
                          rwkv_block(kind, st, T, W, K, PS_FULL, LW, **cm) if "rwkv" in MIX else None)]

        run_streams([head_gen("p", 0, 0)])
        for step in range(NST + 1):
            gens = []
            if step < NST:
                gens += mixers("p", step, step % 2)
            gens.append(chain(tail_gen("p", step - 1, (step - 1) % 2) if step >= 1 else None,
                              head_gen("p", step + 1, (step + 1) % 2) if step + 1 < NST else None))
            run_streams(gens)
        if SAMPLE:
            p.barrier()
            run_streams([head_gen("s", 0, 0)])
            run_streams(mixers("s", 0, 0))
            run_streams([tail_gen("s", 0, 0)])
            p.barrier()
    p.finish()
    return nc, p


def gla_block(kind, st, T, W, K, PS, LW, *, p, proj, OFF, PJ, cnt, ident, identP, mI, mS, nmI, mI4, mS4, dmask, rows, v3,
              headsum, rstd_inplace, din, dout, l, NST, H, Hs, mixTs):
    qT, kT, vT, ldT, gate, oT = (W[n] for n in ("qT", "kT", "vT", "ldT", "gate", "oT"))
    wg2, gla_ng = LW["wg2"], LW["gla_ng"]
    for tl in range(2):
        proj(qT[:, tl, 0:T], OFF["gla_q"] + 128 * tl, 128, T, scale=0.125)
        yield
        proj(kT[:, tl, 0:T], OFF["gla_k"] + 128 * tl, 128, T)
        yield
        proj(vT[:, tl, 0:T], OFF["gla_v"] + 128 * tl, 128, T)
        yield
        proj(gate[:, tl, 0:T], OFF["gla_gate"] + 128 * tl, 128, T)
        yield
    glr = W["t0"]
    p.memset(glr[0:32, 0, 0:T], 1.0)
    proj(glr[0:16, 0, 0:T], OFF["glr"], 16, T)
    yield
    for tl in range(2):
        pj = PJ[cnt[0] % 2]
        cnt[0] += 1
        p.mm(pj[:, 0:T], wg2[0:17, tl * 128:(tl + 1) * 128], glr[0:17, 0, 0:T])
        p.act(ldT[:, tl, 0:T], pj[:, 0:T], AF.Exp, scale=-1.0)
        p.act(ldT[:, tl, 0:T], ldT[:, tl, 0:T], AF.Ln, bias=1.0)
        yield
    p.ts(ldT[:, :, 0:T], ldT[:, :, 0:T], -1.0 / 16, OP.mult)
    yield from mixer_core(p, "gla", kind, T, W, K, H["gla"], Hs["gla"], dict(ab=False, scalar=False),
                          PS, ident, identP, mI, mS, nmI, mI4, mS4, dmask, rows, v3,
                          din["state_gla"], dout["gla_s"], l, transposed_state=False)
    yield from out_norm_rms(p, oT, gate, gla_ng, T, W, headsum, rstd_inplace, mixTs[0])
    if kind == "p" and st == NST - 1:
        p.dma(dout["gla_p"][l].re("(t hp) d v -> (hp d) t v", hp=2), H["gla"].ap())


def mixer_core(p, name, kind, T, W, K, Hst, Hsamp, fl, PS, ident, identP, mI, mS, nmI, mI4, mS4, dmask, rows, v3,
               state_in, state_out, l, transposed_state):
    PA, PB, PU, PO, PH, PT = PS
    qT, aT, kT, bT, vT, ldT, oT = (W[n] for n in ("qT", "aT", "kT", "bT", "vT", "ldT", "oT"))
    ab, scalar = fl["ab"], fl["scalar"]
    if kind == "s":
        sample_core(p, name, W, Hsamp, fl, PS, ident, dmask, rows, v3, state_in, state_out, l, transposed_state)
        yield
        return
    cum, cumx, E, qs, as_, ks, bs, kd, bd, X, R2, Ut, WtT, U = (K[n] for n in (
        "cum", "cumx", "E", "qs", "as", "ks", "bs", "kd", "bd", "X", "R2", "Ut", "WtT", "U"))
    tok, gtok, amat, BB, PC, gam = K["tok"], K["gtok"], K["amat"], K["BB"], K["PC"], K["gam"]
    araw, qraw, vb, Hb, identb = K["araw"], K["qraw"], K["vb"], K["Hb"], K["identb"]
    p.cp(Hb.ap(), Hst.ap(), eng="act")
    yield
    ones64 = None
    for c in range(T // C):
        sl = slice(c * C, (c + 1) * C)
        for tl in range(2):
            p.scan(cum[:, tl, :], W["ones"][:, 0:64], ldT[:, tl, sl], 0.0, OP.mult, OP.add)
            yield
        p.act(E.ap(), cum.ap(), AF.Exp)
        yield
        p.tt(qs.ap(), qT[:, :, sl], E.ap(), OP.mult)
        yield
        for tl in range(2):
            p.cp(PC[:, tl:tl + 1], E[:, tl, 63:64])
            yield
        if ab:
            p.tt(cumx.ap(), cum.ap(), ldT[:, :, sl], OP.subtract)
            yield
            p.act(E.ap(), cumx.ap(), AF.Exp)
            yield
            p.tt(as_.ap(), aT[:, :, sl], E.ap(), OP.mult)
            yield
        for tl in range(2):
            p.act(E[:, tl, :], cum[:, tl, :], AF.Exp, scale=-1.0, bias=cum[:, tl, 63:64])
            yield
        p.tt(kd.ap(), kT[:, :, sl], E.ap(), OP.mult)
        yield
        if ab:
            p.stt(bd.ap(), bT[:, :, sl], -1.0, E.ap(), OP.mult, OP.mult)
            yield
        if not scalar:
            p.act(E.ap(), cum.ap(), AF.Exp, scale=-1.0)
            yield
            p.tt(ks.ap(), kT[:, :, sl], E.ap(), OP.mult)
            yield
            if ab:
                p.tt(bs.ap(), bT[:, :, sl], E.ap(), OP.mult)
                yield
            Yk, Yb, Xa, Xq = ks, bs, as_, qs
        else:
            p.cp(ks.ap(), kT[:, :, sl], eng="act")
            yield
            p.cp(bs.ap(), bT[:, :, sl], eng="act")
            yield
            p.cp(araw.ap(), aT[:, :, sl], eng="act")
            yield
            p.cp(qraw.ap(), qT[:, :, sl], eng="act")
            yield
            Yk, Yb, Xa, Xq = ks, bs, araw, qraw
        p.cp(vb.ap(), vT[:, :, sl], eng="act")
        yield
        tq = [("as", as_), ("kd", kd), ("bd", bd), ("v", None)]
        ptv = v3(PT, 8)
        for qi, (nm, src) in enumerate(tq):
            if nm in ("as", "bd") and not ab:
                continue
            for tl in range(2):
                for hp in range(2):
                    s_ = vb[rows(hp), tl, :] if nm == "v" else src[rows(hp), tl, :]
                    p.tr(ptv[rows(hp), qi * 2 + tl, :], s_, identb[rows(hp), rows(hp)])
        if ab:
            p.cp(tok.ap(), ptv, eng="act")
            yield
        else:
            p.cp(tok[:, 2:4, :], ptv[:, 2:4, :], eng="act")
            yield
            p.cp(tok[:, 6:8, :], ptv[:, 6:8, :], eng="act")
            yield
        aTok, kdTok, bdTok, vTok = tok[:, 0:2, :], tok[:, 2:4, :], tok[:, 4:6, :], tok[:, 6:8, :]
        pav = v3(PA, 8)
        pairs = [(0, Yb, Xa), (1, Yk, Xa), (2, Yk, Xq), (3, Yb, Xq)] if ab else [(2, Yk, Xq)]
        for ty, Y, Xx in pairs:
            for tl in range(2):
                for hp in range(2):
                    ysl = Y[rows(hp), tl, :]
                    xsl = Xx[rows(hp), tl, :]
                    p.mm(pav[rows(hp), ty * 2 + tl, :], ysl, xsl)
        if scalar:
            puv = v3(PU, 2)
            for tl in range(2):
                for hp in range(2):
                    p.tr(puv[rows(hp), tl, :], ldT[rows(hp), tl, sl], ident[rows(hp), rows(hp)])
            p.cp(gtok.ap(), puv)
            yield
            pbv = v3(PB, 4)
            for tl in range(2):
                for hp in range(2):
                    for ei, msk in ((0, mS), (1, mI)):
                        p.mm(pbv[rows(hp), ei * 2 + tl, :], gtok[rows(hp), tl, :], msk[rows(hp), :], start=True, stop=False)
                        p.mm(pbv[rows(hp), ei * 2 + tl, :], nmI[rows(hp), :], gtok[rows(hp), tl, :], start=False, stop=True)
            p.ts(gam.ap(), pbv, 0.0, OP.min)
            yield
            p.act(gam.ap(), gam.ap(), AF.Exp)
            yield
            for ty in range(4):
                gsel = gam[:, 0:2, :] if ty < 2 else gam[:, 2:4, :]
                p.tt(amat[:, 2 * ty:2 * ty + 2, :], pav[:, 2 * ty:2 * ty + 2, :], gsel, OP.mult)
                yield
            p.tt(amat[:, 0:4, :], amat[:, 0:4, :], mS4.ap(), OP.mult)
            yield
            p.tt(amat[:, 4:8, :], amat[:, 4:8, :], mI4.ap(), OP.mult)
            yield
        elif ab:
            p.tt(amat[:, 0:4, :], pav[:, 0:4, :], mS4.ap(), OP.mult)
            yield
            p.tt(amat[:, 4:8, :], pav[:, 4:8, :], mI4.ap(), OP.mult)
            yield
        else:
            p.tt(amat[:, 4:6, :], pav[:, 4:6, :], mI4[:, 0:2, :], OP.mult)
            yield
        nLt, Akt, Qkt, nQbt = amat[:, 0:2, :], amat[:, 2:4, :], amat[:, 4:6, :], amat[:, 6:8, :]
        if ab:
            b0 = BB[0]
            p.cp(b0[:, 0:2, :], nLt)
            yield
            pbv = v3(PB, 4)
            for tl in range(2):
                for hp in range(2):
                    p.tr(pbv[rows(hp), tl, :], nLt[rows(hp), tl, :], identb[rows(hp), rows(hp)])
            p.cp(b0[:, 2:4, :], pbv[:, 0:2, :], eng="act")
            yield
            p.tt(X.ap(), identP.ap(), nLt, OP.add)
            yield
            for k in range(1, 6):
                prev, cur = BB[(k - 1) % 2], BB[k % 2]
                for tl in range(2):
                    for hp in range(2):
                        r = rows(hp)
                        p.mm(pbv[r, tl, :], prev[r, 2 + tl, :], prev[r, tl, :])
                        p.mm(pbv[r, 2 + tl, :], prev[r, tl, :], prev[r, 2 + tl, :])
                p.cp(cur.ap(), pbv, eng="act")
                yield
                puv = v3(PU, 2)
                for tl in range(2):
                    for hp in range(2):
                        r = rows(hp)
                        p.mm(puv[r, tl, :], cur[r, 2 + tl, :], X[r, tl, :])
                p.tt(X.ap(), X.ap(), puv, OP.add)
                yield
            pov = v3(PO, 2)
            for tl in range(2):
                for hp in range(2):
                    r = rows(hp)
                    p.mm(pov[r, tl, :], Akt[r, tl, :], vTok[r, tl, :])
            p.cp(R2.ap(), pov, eng="act")
            yield
            puv = v3(PU, 2)
            phv = v3(PH, 2)
            for tl in range(2):
                for hp in range(2):
                    r = rows(hp)
                    p.mm(puv[r, tl, :], X[r, tl, :], R2[r, tl, :])
                    p.mm(phv[r, tl, :], aTok[r, tl, :], X[r, tl, :])
            p.cp(Ut.ap(), puv)
            yield
            p.cp(WtT.ap(), phv, eng="act")
            yield
            for tl in range(2):
                for hp in range(2):
                    r = rows(hp)
                    p.mm(puv[r, tl, :], WtT[r, tl, :], Hb[r, tl, :])
            p.tt(U.ap(), puv, Ut.ap(), OP.add)
            yield
        pov = v3(PO, 2)
        for tl in range(2):
            for hp in range(2):
                r = rows(hp)
                p.mm(pov[r, tl, :], Hb[r, tl, :], qs[r, tl, :], start=True, stop=False)
                p.mm(pov[r, tl, :], vTok[r, tl, :], Qkt[r, tl, :], start=False, stop=not ab)
                if ab:
                    p.mm(pov[r, tl, :], U[r, tl, :], nQbt[r, tl, :], start=False, stop=True)
        p.cp(oT[:, :, sl], pov, eng="act")
        yield
        phv = v3(PH, 2)
        for tl in range(2):
            for hp in range(2):
                r = rows(hp)
                p.mm(phv[r, tl, :], kdTok[r, tl, :], vTok[r, tl, :], start=True, stop=not ab)
                if ab:
                    p.mm(phv[r, tl, :], bdTok[r, tl, :], U[r, tl, :], start=False, stop=True)
        for tl in range(2):
            p.stt(Hst[:, tl, :], Hst[:, tl, :], PC[:, tl:tl + 1], phv[:, tl, :], OP.mult, OP.add)
            yield
        p.cp(Hb.ap(), Hst.ap(), eng="act")
        yield


def out_norm_rms(p, oT, gate, gcol, T, W, headsum, rstd_inplace, mixTm):
    t0, t1 = W["t0"], W["t1"]
    p.act(t0[:, :, 0:T], oT[:, :, 0:T], AF.Square)
    yield
    headsum(t1, t0, T)
    yield
    rstd_inplace(t1[:, :, 0:T], T, 1.0 / 64, 1e-6)
    yield
    p.tt(t0[:, :, 0:T], oT[:, :, 0:T], t1[:, :, 0:T], OP.mult)
    yield
    p.act(t1[:, :, 0:T], gate[:, :, 0:T], AF.Silu)
    yield
    p.stt(mixTm[:, :, 0:T], t0[:, :, 0:T], gcol[:, 0:1], t1[:, :, 0:T], OP.mult, OP.mult)
    yield


def sample_core(p, name, W, Hs, fl, PS, ident, dmask, rows, v3, state_in, state_out, l, transposed_state):
    PA, PB, PU, PO, PH, PT = PS
    ab = fl["ab"]
    qT, aT, kT, bT, vT, ldT, oT = (W[n] for n in ("qT", "aT", "kT", "bT", "vT", "ldT", "oT"))
    Snat = W["Snat"]
    Dd, tokS, Ud, Vd, oTok, tmpd = (W[n] for n in ("Dd", "tokS", "Ud", "Vd", "oTok", "tmpd"))
    ptv = v3(PT, 8)
    for tl in range(2):
        for hp in range(2):
            h = 2 * tl + hp
            if not transposed_state:
                p.dma(Hs[rows(hp), tl, :, :], state_in[l, :, h].re("b d v -> d b v"))
            else:
                p.dma(Snat[rows(hp), tl, :, :], state_in[l, :, h].re("b v d -> v b d"))
    if transposed_state:
        for tl in range(2):
            for g in range(2):
                for j in range(8):
                    for hp in range(2):
                        p.tr(ptv[rows(hp), j, :], Snat[rows(hp), tl, 8 * g + j, :], ident[rows(hp), rows(hp)])
                p.cp(Hs[:, tl, 8 * g:8 * g + 8, :], ptv)
    p.act(Dd.ap(), ldT[:, :, 0:16], AF.Exp)
    pt2 = PT[0:16, 0:512].re("p (a b) -> p a b", b=128)
    srcs = [kT, bT, vT] if ab else [kT, vT]
    for qi, src in enumerate(srcs):
        for tl in range(2):
            p.tr(pt2[:, tl, :], src[:, tl, 0:16], ident.ap())
        if ab and qi == 1:
            p.ts(tokS[:, 1, :], pt2[:, 0:2, :].re("p a b -> p (a b)"), -1.0, OP.mult)
        else:
            p.cp(tokS[:, (qi if ab else 2 * qi), :], pt2[:, 0:2, :].re("p a b -> p (a b)"))
    for tl in range(2):
        for hp in range(2):
            h = 2 * tl + hp
            r = rows(hp)
            hc = slice(64 * h, 64 * h + 64)
            p.tt(Vd.ap(), tokS[:, 2, hc][:, None, :].bc([16, 16, 64]), dmask.ap(), OP.mult)
            if ab:
                for g, ps in enumerate((PU, PB)):
                    p.mm(ps[0:16, :], aT[r, tl, 0:16], Hs[r, tl, 8 * g:8 * g + 8, :].re("p b v -> p (b v)"))
                    p.tt(Ud[:, 8 * g:8 * g + 8, :], ps[0:16, :].re("p (b v) -> p b v", v=64),
                         dmask[:, 8 * g:8 * g + 8, :], OP.mult)
            for g, ps in enumerate((PH, PO)):
                p.mm(ps[r, :], tokS[:, 0, hc], Vd[:, 8 * g:8 * g + 8, :].re("p b v -> p (b v)"), start=True, stop=not ab)
                if ab:
                    p.mm(ps[r, :], tokS[:, 1, hc], Ud[:, 8 * g:8 * g + 8, :].re("p b v -> p (b v)"), start=False, stop=True)
        p.tt(Hs[:, tl, :, :], Hs[:, tl, :, :], Dd[:, tl, :][:, :, None].bc([128, 16, 64]), OP.mult)
        for g, ps in enumerate((PH, PO)):
            p.tt(Hs[:, tl, 8 * g:8 * g + 8, :], Hs[:, tl, 8 * g:8 * g + 8, :], ps[:, :].re("p (b v) -> p b v", v=64), OP.add)
        for hp in range(2):
            h = 2 * tl + hp
            r = rows(hp)
            hc = slice(64 * h, 64 * h + 64)
            for g, ps in enumerate((PU, PB)):
                p.mm(ps[0:16, :], qT[r, tl, 0:16], Hs[r, tl, 8 * g:8 * g + 8, :].re("p b v -> p (b v)"))
                p.tt(tmpd[:, 8 * g:8 * g + 8, :], ps[0:16, :].re("p (b v) -> p b v", v=64),
                     dmask[:, 8 * g:8 * g + 8, :], OP.mult)
            p.reduce(oTok[:, hc], tmpd.ap().re("p b v -> p v b"), OP.add)
    for tl in range(2):
        p.tr(PT[:, tl * 16:tl * 16 + 16], oTok[:, tl * 128:(tl + 1) * 128], ident[0:16, 0:16])
    p.cp(oT[:, :, 0:16], PT[:, 0:32].re("p (a b) -> p a b", b=16))
    if transposed_state:
        for tl in range(2):
            for g in range(2):
                for j in range(8):
                    for hp in range(2):
                        p.tr(ptv[rows(hp), j, :], Hs[rows(hp), tl, 8 * g + j, :], ident[rows(hp), rows(hp)])
                p.cp(Snat[:, tl, 8 * g:8 * g + 8, :], ptv)
    for tl in range(2):
        for hp in range(2):
            h = 2 * tl + hp
            if not transposed_state:
                p.dma(state_out[l, :, h].re("b d v -> d b v"), Hs[rows(hp), tl, :, :])
            else:
                p.dma(state_out[l, :, h].re("b v d -> v b d"), Snat[rows(hp), tl, :, :])


TWO_PI = 2.0 * math.pi


def sincos(p, dst_s, dst_c, ang, fr, ii):
    for dst, sh in ((dst_s, 0.0), (dst_c, 0.25)):
        p.ts(dst, ang, sh, OP.add)
        p.cp(ii, dst)
        p.cp(fr, ii)
        p.tt(fr, dst, fr, OP.subtract)
        p.ts(fr, fr, 0.4999995, OP.min, -0.4999995, OP.max)
        p.act(dst, fr, AF.Sin, scale=TWO_PI)


def s5_setup(S, p, nc, din, l, ident, tidx, PT, PB, rows):
    NS = True

    def ld(name, src):
        t = p.sbc(f"s5{name}", [128, 8])
        for gp in range(2):
            p.dma(t[rows(gp), :], src.re("(pr gp) q -> gp q pr", gp=2)[gp], allow_slow_non_contiguous=NS)
        return t
    lre = ld("lre", din["s5_lam_re"][l])
    lim = ld("lim", din["s5_lam_im"][l])
    stp = p.sbc(f"s5stp", [128, 8])
    for gp in range(2):
        p.dma(stp[rows(gp), :], din["s5_log_step"][l].re("(pr gp) -> gp pr", gp=2)[gp][None, :].bc([64, 8]),
              allow_slow_non_contiguous=NS)
    p.act(stp.ap(), stp.ap(), AF.Exp)
    names = ["lr", "li", "mag", "cs", "sn", "abre", "abim", "nabim", "den", "am1", "zre", "zim", "t", "fr", "ang"]
    c = {n: p.sbc(f"s5{n}", [128, 8]) for n in names}
    ii = p.sbc(f"s5ii", [128, 8], I32)
    p.tt(c["lr"].ap(), lre.ap(), stp.ap(), OP.mult)
    p.tt(c["li"].ap(), lim.ap(), stp.ap(), OP.mult)
    p.act(c["mag"].ap(), c["lr"].ap(), AF.Exp)
    p.ts(c["ang"].ap(), c["li"].ap(), 1.0 / TWO_PI, OP.mult)
    sincos(p, c["sn"].ap(), c["cs"].ap(), c["ang"].ap(), c["fr"].ap(), ii.ap())
    p.tt(c["abre"].ap(), c["mag"].ap(), c["cs"].ap(), OP.mult)
    p.tt(c["abim"].ap(), c["mag"].ap(), c["sn"].ap(), OP.mult)
    p.ts(c["nabim"].ap(), c["abim"].ap(), -1.0, OP.mult)
    p.tt(c["den"].ap(), lre.ap(), lre.ap(), OP.mult)
    p.tt(c["t"].ap(), lim.ap(), lim.ap(), OP.mult)
    p.tt(c["den"].ap(), c["den"].ap(), c["t"].ap(), OP.add)
    p.recip(c["den"].ap(), c["den"].ap())
    p.ts(c["am1"].ap(), c["abre"].ap(), -1.0, OP.add)
    p.tt(c["zre"].ap(), c["am1"].ap(), lre.ap(), OP.mult)
    p.tt(c["t"].ap(), c["abim"].ap(), lim.ap(), OP.mult)
    p.tt(c["zre"].ap(), c["zre"].ap(), c["t"].ap(), OP.add)
    p.tt(c["zre"].ap(), c["zre"].ap(), c["den"].ap(), OP.mult)
    p.tt(c["zim"].ap(), c["abim"].ap(), lre.ap(), OP.mult)
    p.tt(c["t"].ap(), c["am1"].ap(), lim.ap(), OP.mult)
    p.tt(c["zim"].ap(), c["zim"].ap(), c["t"].ap(), OP.subtract)
    p.tt(c["zim"].ap(), c["zim"].ap(), c["den"].ap(), OP.mult)
    yield
    bre = p.sbc(f"s5bre", [128, 8, 16])
    bim = p.sbc(f"s5bim", [128, 8, 16])
    for gp in range(2):
        p.dma(bre[rows(gp)], din["s5_b_re"][l].re("(pr gp) q c -> gp q pr c", gp=2)[gp], allow_slow_non_contiguous=NS)
        p.dma(bim[rows(gp)], din["s5_b_im"][l].re("(pr gp) q c -> gp q pr c", gp=2)[gp], allow_slow_non_contiguous=NS)
    BD = [p.sbc(f"s5BD{r}", [128, 8, 64]) for r in range(2)]
    tmp = p.sbc(f"s5tmp", [128, 8, 16])
    tmp2 = p.sbc(f"s5tmp2", [128, 8, 16])
    zre_b = c["zre"].ap()[:, :, None].bc([128, 8, 16])
    zim_b = c["zim"].ap()[:, :, None].bc([128, 8, 16])
    for r in range(2):
        p.memset(BD[r].ap(), 0.0)

    def scatter(r):
        for par in range(2):
            for q in range(4):
                pr = 2 * q + par
                p.cp(BD[r][0:64, pr, par * 32:par * 32 + 16], tmp[0:64, pr, :])
                p.cp(BD[r][64:128, pr, par * 32 + 16:par * 32 + 32], tmp[64:128, pr, :])
    p.tt(tmp.ap(), bre.ap(), zre_b, OP.mult)
    p.tt(tmp2.ap(), bim.ap(), zim_b, OP.mult)
    p.tt(tmp.ap(), tmp.ap(), tmp2.ap(), OP.subtract)
    scatter(0)
    p.tt(tmp.ap(), bim.ap(), zre_b, OP.mult)
    p.tt(tmp2.ap(), bre.ap(), zim_b, OP.mult)
    p.tt(tmp.ap(), tmp.ap(), tmp2.ap(), OP.add)
    scatter(1)
    BT = p.sbc(f"s5BT", [128, 8, 128])
    ptv = PT[:, 0:512].re("p (a b) -> p a b", b=128)
    for r in range(2):
        for pr in range(8):
            hf = (pr % 4) // 2
            p.tr(ptv[64 * hf:64 * hf + 64, (pr // 4) * 2 + pr % 2, :], BD[r][:, pr, :], ident.ap())
        p.cp(BT[:, r * 4:r * 4 + 4, :], ptv)
    yield
    par = p.sbc(f"s5par", [128, 2])
    pii = p.sbc(f"s5pii", [128, 1], I32)
    p.iota(pii.ap(), [[0, 1]], base=0, cm=1)
    p.op("dve", lambda e: e.tensor_scalar(pii.h[:], pii.h[:], 4, 1, OP.arith_shift_right, op1=OP.bitwise_and),
         [pii.ap()], [pii.ap()])
    p.cp(par[:, 1:2], pii.ap())
    p.ts(par[:, 0:1], par[:, 1:2], -1.0, OP.mult, 1.0, OP.add)
    Cn = p.sbc(f"s5Cn", [128, 2, 64])
    Cexp = p.sbc(f"s5Cexp", [128, 2, 128])
    pbv = PB[:, 0:512].re("p (a b) -> p a b", b=128)
    for r, nm in enumerate(("s5_c_re", "s5_c_im")):
        p.dma(Cn.ap(), din[nm][l].re("(s g) c q -> (g c) s q", s=2))
        sgn = 1.0 if r == 0 else -1.0
        p.ts(Cexp[:, :, 0:64], Cn.ap(), par[:, 0:1], OP.mult, sgn, OP.mult)
        p.ts(Cexp[:, :, 64:128], Cn.ap(), par[:, 1:2], OP.mult, sgn, OP.mult)
        for s in range(2):
            p.tr(pbv[:, r * 2 + s, :], Cexp[:, s, :], ident.ap())
    CTe = p.sbc("s5CTe", [128, 4, 128])
    CTo = p.sbc("s5CTo", [128, 4, 128])
    p.memset(CTe.ap(), 0.0)
    p.memset(CTo.ap(), 0.0)
    vw = "p s (q pp c) -> p s q pp c"
    p.cp(CTe.ap().re(vw, pp=2, c=32)[:, :, :, 0, :], pbv.re(vw, pp=2, c=32)[:, :, :, 0, :])
    p.cp(CTo.ap().re(vw, pp=2, c=32)[:, :, :, 1, :], pbv.re(vw, pp=2, c=32)[:, :, :, 1, :])
    yield
    cosT = p.sbc(f"s5cosT", [128, 8, ST])
    sinT = p.sbc(f"s5sinT", [128, 8, ST])
    frT = p.sbc(f"s5frT", [128, ST])
    angT = p.sbc(f"s5angT", [128, ST])
    iiT = p.sbc(f"s5iiT", [128, ST], I32)
    for pr in range(8):
        p.ts(angT.ap(), tidx.ap(), c["ang"][:, pr:pr + 1], OP.mult)
        sincos(p, sinT[:, pr, :], cosT[:, pr, :], angT.ap(), frT.ap(), iiT.ap())
        yield
    S.update(c)
    BTb = p.sbc("s5BTb", [128, 8, 128], BF16)
    CTeb = BD[0][:, 0:4, :].bitcast(BF16)
    CTob = BD[1][:, 0:4, :].bitcast(BF16)
    p.cp(BTb.ap(), BT.ap())
    p.cp(CTeb.ap(), CTe.ap())
    p.cp(CTob.ap(), CTo.ap())
    S.update(BT=BT, CTe=CTe, CTo=CTo, cosT=cosT, sinT=sinT, BTb=BTb, CTeb=CTeb, CTob=CTob)
    S["hre"] = p.sbc(f"s5hre", [128, 8])
    S["him"] = p.sbc(f"s5him", [128, 8])
    p.memset(S["hre"].ap(), 0.0)
    p.memset(S["him"].ap(), 0.0)
    yield


def s5_block(kind, st, T, W, PS, LW, *, p, proj, OFF, PJ, cnt, ident, identP, mI, mS, nmI, mI4, mS4, dmask, rows, v3,
             headsum, rstd_inplace, din, dout, l, NST, H, Hs, mixTs):
    PA_, PB, PU, PO_, PH_, PT = PS
    S, s5d, s5bg, wglu = LW["S5"], LW["s5d"], LW["s5bg"], LW["wglu"]
    uT, gate, yT = W["qT"], W["gate"], W["oT"]
    PH = W["PH"] if kind == "p" else PO_
    for tl in range(2):
        proj(uT[:, tl, 0:T], OFF["s5_u"] + 128 * tl, 128, T)
        yield
        proj(gate[:, tl, 0:T], OFF["s5_gate"] + 128 * tl, 128, T)
        yield
    BT, cosT, sinT = S["BT"], S["cosT"], S["sinT"]
    xre, xim, gre, gim, hre_t, him_t, ta, tb = (W[n][:, 0, 0:T] for n in ("t0", "t1", "t2", "t3", "kT", "vT", "aT", "bT"))
    g0 = W["g0"]
    if kind == "s":
        hS = W["hS"]
        nat = W["s5nat"]
        for r, nm in enumerate(("state_s5_re", "state_s5_im")):
            p.dma(nat[:, r, :], din[nm][l].re("b g q -> b (g q)"))
            for pr in range(8):
                p.tr(PT[:, pr * 16:pr * 16 + 16], nat[:, r, pr * 128:(pr + 1) * 128], ident[0:16, 0:16])
            p.cp(hS[:, r, :, :], PT[:, 0:128].re("p (a b) -> p a b", b=16))
            yield
    if kind == "p":
        g0a = W["g0a"]
        cs8, sn8, hr8, hi8 = S["cs"].ap(), S["sn"].ap(), S["hre"].ap(), S["him"].ap()
        p.tt(g0a[:, 0, :], cs8, hr8, OP.mult)
        p.tt(g0a[:, 1, :], sn8, hi8, OP.mult)
        p.tt(g0a[:, 2, :], g0a[:, 0, :], g0a[:, 1, :], OP.subtract)
        yield
        p.tt(g0a[:, 0, :], cs8, hi8, OP.mult)
        p.tt(g0a[:, 1, :], sn8, hr8, OP.mult)
        p.tt(g0a[:, 3, :], g0a[:, 0, :], g0a[:, 1, :], OP.add)
        yield
        Xre, Xim, Gre, Gim, Hre, Him, Ta, Tb = (W[n][:, :, 0:T] for n in ("t0", "t1", "t2", "t3", "kT", "vT", "aT", "bT"))
        ub, hbr, hbi = W["ub"], W["hbr"], W["hbi"]
        BTb = S["BTb"]
        p.cp(ub[:, :, 0:T], uT[:, :, 0:T], eng="act")
        yield
        PUv = PU[:, 0:256].re("p (a b) -> p a b", b=128)[:, :, 0:T]
        PBv = PB[:, 0:256].re("p (a b) -> p a b", b=128)[:, :, 0:T]
        for gi in range(4):
            prs = (2 * gi, 2 * gi + 1)
            for j, pr in enumerate(prs):
                q4, sl = pr % 4, pr // 4
                hf = q4 // 2
                rr = slice(64 * hf, 64 * hf + 64)
                bslot = sl * 2 + pr % 2
                p.mm(PUv[:, j, :], BTb[rr, 0 + bslot, :], ub[rr, sl, 0:T])
                p.mm(PBv[:, j, :], BTb[rr, 4 + bslot, :], ub[rr, sl, 0:T])
            cs, sn = cosT[:, 2 * gi:2 * gi + 2, 0:T], sinT[:, 2 * gi:2 * gi + 2, 0:T]
            p.tt(Ta, PUv, cs, OP.mult)
            yield
            p.tt(Tb, PBv, sn, OP.mult)
            yield
            p.tt(Xre, Ta, Tb, OP.add, eng="pool")
            yield
            p.tt(Ta, PBv, cs, OP.mult)
            yield
            p.tt(Tb, PUv, sn, OP.mult)
            yield
            p.tt(Xim, Ta, Tb, OP.subtract, eng="pool")
            yield
            for j, pr in enumerate(prs):
                magb = S["mag"][:, pr:pr + 1].bc([128, T])
                p.scan(Gre[:, j, :], magb, Xre[:, j, :], g0a[:, 2, pr:pr + 1], OP.mult, OP.add)
                yield
                p.scan(Gim[:, j, :], magb, Xim[:, j, :], g0a[:, 3, pr:pr + 1], OP.mult, OP.add)
                yield
            p.tt(Ta, Gre, cs, OP.mult)
            yield
            p.tt(Tb, Gim, sn, OP.mult, eng="pool")
            yield
            p.tt(Hre, Ta, Tb, OP.subtract)
            yield
            p.tt(Ta, Gim, cs, OP.mult, eng="pool")
            yield
            p.tt(Tb, Gre, sn, OP.mult)
            yield
            p.tt(Him, Ta, Tb, OP.add)
            yield
            p.cp(S["hre"][:, 2 * gi:2 * gi + 2], Hre[:, :, T - 1])
            p.cp(S["him"][:, 2 * gi:2 * gi + 2], Him[:, :, T - 1])
            p.cp(hbr[:, :, 0:T], Hre, eng="act")
            p.cp(hbi[:, :, 0:T], Him, eng="act")
            yield
            for j, pr in enumerate(prs):
                q4, sl = pr % 4, pr // 4
                hf = q4 // 2
                rr = slice(64 * hf, 64 * hf + 64)
                CTx = S["CTeb"] if pr % 2 == 0 else S["CTob"]
                p.mm(PH[rr, sl * ST:sl * ST + T], CTx[:, 0 + sl, rr], hbr[:, j, 0:T], start=(pr % 2 == 0), stop=False)
                p.mm(PH[rr, sl * ST:sl * ST + T], CTx[:, 2 + sl, rr], hbi[:, j, 0:T], start=False, stop=(pr % 2 == 1))
    for pr in (range(8) if kind == "s" else ()):
        q4, sl = pr % 4, pr // 4
        hf = q4 // 2
        rr = slice(64 * hf, 64 * hf + 64)
        bslot = sl * 2 + pr % 2
        p.mm(PU[:, 0:T], BT[rr, 0 + bslot, :], uT[rr, sl, 0:T])
        p.mm(PB[:, 0:T], BT[rr, 4 + bslot, :], uT[rr, sl, 0:T])
        if kind == "p":
            cs, sn = cosT[:, pr, 0:T], sinT[:, pr, 0:T]
            p.tt(ta, PU[:, 0:T], cs, OP.mult)
            yield
            p.tt(tb, PB[:, 0:T], sn, OP.mult, eng="dve")
            yield
            p.tt(xre, ta, tb, OP.add, eng="pool")
            yield
            p.tt(ta, PB[:, 0:T], cs, OP.mult)
            yield
            p.tt(tb, PU[:, 0:T], sn, OP.mult)
            yield
            p.tt(xim, ta, tb, OP.subtract, eng="pool")
            yield
            c1, s1 = S["cs"][:, pr:pr + 1], S["sn"][:, pr:pr + 1]
            hr, hi = S["hre"][:, pr:pr + 1], S["him"][:, pr:pr + 1]
            p.tt(g0[:, 0:1], c1, hr, OP.mult)
            yield
            p.tt(g0[:, 1:2], s1, hi, OP.mult)
            yield
            p.tt(g0[:, 2:3], g0[:, 0:1], g0[:, 1:2], OP.subtract)
            yield
            p.tt(g0[:, 0:1], c1, hi, OP.mult)
            yield
            p.tt(g0[:, 1:2], s1, hr, OP.mult)
            yield
            p.tt(g0[:, 3:4], g0[:, 0:1], g0[:, 1:2], OP.add)
            yield
            magb = S["mag"][:, pr:pr + 1].bc([128, T])
            p.scan(gre, magb, xre, g0[:, 2:3], OP.mult, OP.add)
            yield
            p.scan(gim, magb, xim, g0[:, 3:4], OP.mult, OP.add)
            yield
            p.tt(ta, gre, cs, OP.mult)
            yield
            p.tt(tb, gim, sn, OP.mult, eng="pool")
            yield
            p.tt(hre_t, ta, tb, OP.subtract)
            yield
            p.tt(ta, gim, cs, OP.mult, eng="pool")
            yield
            p.tt(tb, gre, sn, OP.mult)
            yield
            p.tt(him_t, ta, tb, OP.add)
            yield
            p.cp(S["hre"][:, pr:pr + 1], hre_t[:, T - 1:T])
            yield
            p.cp(S["him"][:, pr:pr + 1], him_t[:, T - 1:T])
            yield
        else:
            hS = W["hS"]
            hr, hi = hS[:, 0, pr, :], hS[:, 1, pr, :]
            p.ts(ta, hr, S["abre"][:, pr:pr + 1], OP.mult)
            yield
            p.stt(ta, hi, S["nabim"][:, pr:pr + 1], ta, OP.mult, OP.add)
            yield
            p.tt(hre_t, ta, PU[:, 0:T], OP.add)
            yield
            p.ts(tb, hr, S["abim"][:, pr:pr + 1], OP.mult)
            yield
            p.stt(tb, hi, S["abre"][:, pr:pr + 1], tb, OP.mult, OP.add)
            yield
            p.tt(him_t, tb, PB[:, 0:T], OP.add)
            yield
            p.cp(hr, hre_t)
            yield
            p.cp(hi, him_t)
            yield
        CTx = S["CTe"] if pr % 2 == 0 else S["CTo"]
        p.mm(PH[rr, sl * ST:sl * ST + T], CTx[:, 0 + sl, rr], hre_t, start=(pr % 2 == 0), stop=False)
        p.mm(PH[rr, sl * ST:sl * ST + T], CTx[:, 2 + sl, rr], him_t, start=False, stop=(pr % 2 == 1))
    for tl in range(2):
        p.stt(yT[:, tl, 0:T], uT[:, tl, 0:T], s5d[:, tl:tl + 1], PH[:, tl * ST:tl * ST + T], OP.mult, OP.add)
        yield
    if kind == "p" and st == NST - 1:
        for nm, src in (("s5re_p", S["hre"]), ("s5im_p", S["him"])):
            for gp in range(2):
                p.dma(dout[nm][l].re("(pr gp) q -> gp q pr", gp=2)[gp], src[rows(gp), :], allow_slow_non_contiguous=True)
    if kind == "s":
        hS, nat = W["hS"], W["s5nat"]
        for r, nm in enumerate(("s5re_s", "s5im_s")):
            for pr in range(8):
                p.tr(PT[0:16, pr * 128:(pr + 1) * 128] if pr < 4 else PB[0:16, (pr - 4) * 128:(pr - 3) * 128],
                     hS[:, r, pr, :], ident.ap())
            p.cp(nat[:, r, 0:512], PT[0:16, 0:512])
            yield
            p.cp(nat[:, r, 512:1024], PB[0:16, 0:512])
            yield
            p.dma(dout[nm][l].re("b g q -> b (g q)"), nat[:, r, :])
    a, b, gsb = W["t0"][:, :, 0:T], W["t1"][:, :, 0:T], W["t2"][:, :, 0:T]
    y = yT[:, :, 0:T]
    p.tt(a, y, y, OP.mult)
    yield
    p.ts(a, a, 0.044715, OP.mult, 1.0, OP.add)
    yield
    p.tt(a, a, y, OP.mult)
    yield
    p.act(a, a, AF.Tanh, scale=math.sqrt(2.0 / math.pi))
    yield
    p.ts(a, a, 1.0, OP.add, 0.5, OP.mult)
    yield
    p.tt(b, a, y, OP.mult)
    yield
    for tl in range(2):
        pj = PJ[cnt[0] % 2]
        cnt[0] += 1
        for k in range(2):
            p.mm(pj[:, 0:T], wglu[:, k, tl * 128:(tl + 1) * 128], W["t1"][:, k, 0:T], start=(k == 0), stop=(k == 1))
        p.act(W["t2"][:, tl, 0:T], pj[:, 0:T], AF.Sigmoid, bias=s5bg[:, tl:tl + 1])
        yield
    p.tt(gsb, gsb, b, OP.mult)
    yield
    p.act(a, gate[:, :, 0:T], AF.Silu)
    yield
    p.tt(mixTs[1][:, :, 0:T], gsb, a, OP.mult)
    yield


def gdn_block(kind, st, T, W, K, PS, LW, *, p, proj, OFF, PJ, cnt, ident, identP, mI, mS, nmI, mI4, mS4, dmask, rows, v3,
              headsum, rstd_inplace, din, dout, l, NST, H, Hs, mixTs):
    PA, PB, PU, PO, PH, PT = PS
    convw, gab, Eg, Eb, gdn_ng, hist = (LW[n] for n in ("convw", "gab", "Eg", "Eb", "gdn_ng", "hist_gdn"))
    xp = W["xp"]
    cv = W["cv"]
    gate = W["gate"]
    if kind == "p":
        p.cp(xp[:, :, 0:3], hist.ap())
        yield
        for j in range(6):
            proj(xp[:, j, 3:3 + T], OFF["gdn_qkv"] + 128 * j, 128, T)
            yield
        p.cp(hist.ap(), xp[:, :, T:T + 3])
        yield
        taps = [xp[:, :, i:i + T] for i in range(4)]
        if st == NST - 1:
            for r in range(3):
                p.dma(dout["conv_p"][l, r].re("(j q) -> q j", q=128), xp[:, :, T + r], allow_slow_non_contiguous=True)
    else:
        xsS = W["xsS"]
        nat = W["cnat"]
        p.dma(nat.ap(), din["state_gdn_conv"][l])
        for r in range(3):
            for j in range(6):
                p.tr(PT[:, j * 16:j * 16 + 16], nat[:, r, j * 128:(j + 1) * 128], ident[0:16, 0:16])
            p.cp(xsS[:, :, r, :], PT[:, 0:96].re("p (a b) -> p a b", b=16))
            yield
        for j in range(6):
            proj(xsS[:, j, 3, :], OFF["gdn_qkv"] + 128 * j, 128, T)
            yield
        taps = [xsS[:, :, i, :] for i in range(4)]
        p.dma(dout["conv_s"][l, :, 0:2, :], din["state_gdn_conv"][l, :, 1:3, :])
        for j in range(6):
            p.tr(PB[0:16, j * 128:(j + 1) * 128] if j < 4 else PU[0:16, (j - 4) * 128:(j - 3) * 128], xsS[:, j, 3, :],
                 ident.ap())
        p.cp(nat[:, 0, 0:512], PB[0:16, 0:512])
        yield
        p.cp(nat[:, 0, 512:768], PU[0:16, 0:256])
        yield
        p.dma(dout["conv_s"][l, :, 2, :], nat[:, 0, :])
    c = cv[:, :, 0:T]
    for j in range(6):
        cj = cv[:, j, 0:T]
        p.ts(cj, taps[0][:, j, :], convw[:, j, 0:1], OP.mult)
        yield
        for i in range(1, 4):
            p.stt(cj, taps[i][:, j, :], convw[:, j, i:i + 1], cj, OP.mult, OP.add)
            yield
    p.act(c, c, AF.Silu)
    yield
    qT, kT, vT, aT, bT, ldT, oT = (W[n] for n in ("qT", "kT", "vT", "aT", "bT", "ldT", "oT"))
    t0, t1 = W["t0"], W["t1"]
    for src, dst, sc in ((cv[:, 0:2, 0:T], qT, 0.125), (cv[:, 2:4, 0:T], aT, 1.0)):
        p.act(t0[:, :, 0:T], src, AF.Square)
        yield
        headsum(t1, t0, T)
        yield
        rstd_inplace(t1[:, :, 0:T], T, 1.0, 1e-6)
        yield
        p.stt(dst[:, :, 0:T], src, sc, t1[:, :, 0:T], OP.mult, OP.mult)
        yield
    p.cp(vT[:, :, 0:T], cv[:, 4:6, 0:T])
    yield
    abT, gf, bf = W["abT"], W["gf"], W["bf"]
    proj(abT[0:8, 0:T], OFF["gdn_ab"], 8, T)
    yield
    p.act(gf[0:8, 0:T], abT[0:8, 0:T], AF.Exp, bias=gab[0:8, 0:1])
    yield
    p.act(gf[0:8, 0:T], gf[0:8, 0:T], AF.Ln, bias=1.0)
    yield
    p.ts(gf[0:8, 0:T], gf[0:8, 0:T], gab[0:8, 1:2], OP.mult)
    yield
    p.act(bf[0:8, 0:T], abT[0:8, 0:T], AF.Sigmoid)
    yield
    for tl in range(2):
        pj = PJ[cnt[0] % 2]
        cnt[0] += 1
        p.mm(pj[:, 0:T], Eg[0:8, tl, :], gf[0:8, 0:T])
        p.cp(ldT[:, tl, 0:T], pj[:, 0:T], eng="act")
        yield
        pj = PJ[cnt[0] % 2]
        cnt[0] += 1
        p.mm(pj[:, 0:T], Eb[0:8, tl, :], bf[0:8, 0:T])
        p.tt(kT[:, tl, 0:T], aT[:, tl, 0:T], pj[:, 0:T], OP.mult)
        yield
    p.act(t0[:, :, 0:T], ldT[:, :, 0:T], AF.Exp)
    yield
    p.tt(bT[:, :, 0:T], kT[:, :, 0:T], t0[:, :, 0:T], OP.mult)
    yield
    for tl in range(2):
        proj(gate[:, tl, 0:T], OFF["gdn_gate"] + 128 * tl, 128, T)
        yield
    yield from mixer_core(p, "gdn", kind, T, W, K, H["gdn"], Hs["gdn"], dict(ab=True, scalar=True), PS, ident, identP, mI, mS, nmI,
                          mI4, mS4, dmask, rows, v3, din["state_gdn"], dout["gdn_s"], l, transposed_state=False)
    yield from out_norm_rms(p, oT, gate, gdn_ng, T, W, headsum, rstd_inplace, mixTs[2])
    if kind == "p" and st == NST - 1:
        p.dma(dout["gdn_p"][l].re("(t hp) d v -> (hp d) t v", hp=2), H["gdn"].ap())


def rwkv_block(kind, st, T, W, K, PS, LW, *, p, proj, OFF, PJ, cnt, ident, identP, mI, mS, nmI, mI4, mS4, dmask, rows, v3,
               headsum, rstd_inplace, din, dout, l, NST, H, Hs, mixTs):
    PA, PB, PU, PO, PH, PT = PS
    mu, w0, a0, k_k, k_a, r_k, ln_g, ln_b, wlo, hist = (LW[n] for n in (
        "mu", "w0", "a0", "k_k", "k_a", "r_k", "ln_g", "ln_b", "wlo", "hist_rwkv"))
    rin = W["rin"]
    xs = W["xs7"]
    gate = W["gate"]
    if kind == "p":
        p.cp(rin[:, :, 0:1], hist.ap())
        yield
        for j in range(7):
            proj(rin[:, j, 1:1 + T], OFF["rwkv_in"] + 128 * j, 128, T)
            yield
        p.cp(hist.ap(), rin[:, :, T:T + 1])
        yield
        prev, cur = rin[:, :, 0:T], rin[:, :, 1:1 + T]
        if st == NST - 1:
            p.dma(dout["shift_p"][l].re("(j q o) -> q j o", q=128, o=1), rin[:, :, T:T + 1], allow_slow_non_contiguous=True)
    else:
        nat = W["snat"]
        prevS = W["prevS"]
        p.dma(nat.ap(), din["state_rwkv_shift"][l])
        for j in range(7):
            p.tr(PT[:, j * 16:j * 16 + 16], nat[:, j * 128:(j + 1) * 128], ident[0:16, 0:16])
        p.cp(prevS.ap(), PT[:, 0:112].re("p (a b) -> p a b", b=16))
        yield
        for j in range(7):
            proj(rin[:, j, 0:T], OFF["rwkv_in"] + 128 * j, 128, T)
            yield
        prev, cur = prevS.ap(), rin[:, :, 0:T]
        for j in range(7):
            p.tr(PB[0:16, j * 128:(j + 1) * 128] if j < 4 else PU[0:16, (j - 4) * 128:(j - 3) * 128], rin[:, j, 0:T],
                 ident.ap())
        p.cp(nat[:, 0:512], PB[0:16, 0:512])
        yield
        p.cp(nat[:, 512:896], PU[0:16, 0:384])
        yield
        p.dma(dout["shift_s"][l], nat.ap())
    x = xs[:, :, 0:T]
    p.tt(x, prev, cur, OP.subtract)
    yield
    p.tt(x, x, mu.ap()[:, :, None].bc([128, 7, T]), OP.mult)
    yield
    p.tt(x, x, cur, OP.add)
    yield
    qT, kT, vT, aT, bT, ldT, oT = (W[n] for n in ("qT", "kT", "vT", "aT", "bT", "ldT", "oT"))
    t0, t1, t2 = W["t0"], W["t1"], W["t2"]
    p.cp(qT[:, :, 0:T], xs[:, 0:2, 0:T])
    yield
    p.cp(vT[:, :, 0:T], xs[:, 4:6, 0:T])
    yield
    rk = xs[:, 2:4, 0:T]
    p.act(t0[0:64, 0, 0:T], xs[0:64, 6, 0:T], AF.Tanh)
    yield
    asg = t2
    for tl in range(2):
        pj = PJ[cnt[0] % 2]
        cnt[0] += 1
        p.mm(pj[:, 0:T], wlo[0:64, tl * 128:(tl + 1) * 128], t0[0:64, 0, 0:T])
        p.act(ldT[:, tl, 0:T], pj[:, 0:T], AF.Sigmoid, bias=w0[:, tl:tl + 1])
        yield
        pj = PJ[cnt[0] % 2]
        cnt[0] += 1
        p.mm(pj[:, 0:T], wlo[64:128, tl * 128:(tl + 1) * 128], xs[64:128, 6, 0:T])
        p.act(asg[:, tl, 0:T], pj[:, 0:T], AF.Sigmoid, bias=a0[:, tl:tl + 1])
        yield
    p.ts(ldT[:, :, 0:T], ldT[:, :, 0:T], -math.exp(-0.5), OP.mult)
    yield
    p.tt(aT[:, :, 0:T], rk, k_k.ap()[:, :, None].bc([128, 2, T]), OP.mult)
    yield
    p.act(t0[:, :, 0:T], aT[:, :, 0:T], AF.Square)
    yield
    headsum(t1, t0, T)
    yield
    rstd_inplace(t1[:, :, 0:T], T, 1.0, 1e-6)
    yield
    p.tt(aT[:, :, 0:T], aT[:, :, 0:T], t1[:, :, 0:T], OP.mult)
    yield
    p.tt(bT[:, :, 0:T], aT[:, :, 0:T], asg[:, :, 0:T], OP.mult)
    yield
    p.ts(t0[:, :, 0:T], asg[:, :, 0:T], -1.0, OP.add)
    yield
    p.tt(t0[:, :, 0:T], t0[:, :, 0:T], k_a.ap()[:, :, None].bc([128, 2, T]), OP.mult)
    yield
    p.ts(t0[:, :, 0:T], t0[:, :, 0:T], 1.0, OP.add)
    yield
    p.tt(kT[:, :, 0:T], rk, t0[:, :, 0:T], OP.mult)
    yield
    bon = W["bon"]
    p.tt(t0[:, :, 0:T], qT[:, :, 0:T], kT[:, :, 0:T], OP.mult)
    yield
    p.tt(t0[:, :, 0:T], t0[:, :, 0:T], r_k.ap()[:, :, None].bc([128, 2, T]), OP.mult)
    yield
    headsum(t1, t0, T)
    yield
    p.tt(bon[:, :, 0:T], t1[:, :, 0:T], vT[:, :, 0:T], OP.mult)
    yield
    for tl in range(2):
        proj(gate[:, tl, 0:T], OFF["rwkv_gate"] + 128 * tl, 128, T)
        yield
    yield from mixer_core(p, "rwkv", kind, T, W, K, H["rwkv"], Hs["rwkv"], dict(ab=True, scalar=False), PS, ident, identP, mI, mS,
                          nmI, mI4, mS4, dmask, rows, v3, din["state_rwkv"], dout["rwkv_s"], l, transposed_state=True)
    o = oT[:, :, 0:T]
    headsum(t1, oT, T)
    yield
    p.stt(t0[:, :, 0:T], t1[:, :, 0:T], -1.0 / 64, o, OP.mult, OP.add)
    yield
    p.act(t1[:, :, 0:T], t0[:, :, 0:T], AF.Square)
    yield
    headsum(t2, t1, T)
    yield
    rstd_inplace(t2[:, :, 0:T], T, 1.0 / 64, 64e-5)
    yield
    p.tt(t0[:, :, 0:T], t0[:, :, 0:T], t2[:, :, 0:T], OP.mult)
    yield
    p.tt(t0[:, :, 0:T], t0[:, :, 0:T], ln_g.ap()[:, :, None].bc([128, 2, T]), OP.mult)
    yield
    p.tt(t0[:, :, 0:T], t0[:, :, 0:T], ln_b.ap()[:, :, None].bc([128, 2, T]), OP.add)
    yield
    p.tt(t0[:, :, 0:T], t0[:, :, 0:T], bon[:, :, 0:T], OP.add)
    yield
    p.act(t1[:, :, 0:T], gate[:, :, 0:T], AF.Silu)
    yield
    p.tt(mixTs[3][:, :, 0:T], t0[:, :, 0:T], t1[:, :, 0:T], OP.mult)
    yield
    if kind == "p" and st == NST - 1:
        ptv = v3(PT, 2)
        for tl in range(2):
            for hp in range(2):
                p.tr(ptv[rows(hp), tl, :], H["rwkv"][rows(hp), tl, :], ident[rows(hp), rows(hp)])
        p.cp(K["Ut"].ap(), ptv)
        yield
        p.dma(dout["rwkv_p"][l].re("(t hp) v d -> (hp v) t d", hp=2), K["Ut"].ap())


from concourse.bass_utils import run_bass_kernel_spmd

_CACHE = {}


def kernel(**inputs):
    if "nc" not in _CACHE:
        _CACHE["nc"] = build(DEPTH=4, NST=2048 // ST, SAMPLE=True)[0]
    nc = _CACHE["nc"]
    in_maps = []
    for c in range(8):
        m = {}
        for k, shp in SHAPES.items():
            a = np.asarray(inputs[k])
            if k == "x_prompt":
                a = a[c]
            elif k == "x_sample":
                a = a[16 * c:16 * c + 16, 0]
            elif k.startswith("state_"):
                a = a[:, 16 * c:16 * c + 16]
            m[k] = np.ascontiguousarray(a, dtype=np.float32)
        in_maps.append(m)
    res = run_bass_kernel_spmd(nc, in_maps, core_ids=list(range(8)))
    rs = res.results
    outs = []
    for k in OUT_ORDER:
        if k == "y_p":
            o = np.stack([r[k] for r in rs], axis=0)
        elif k == "y_s":
            o = np.concatenate([r[k] for r in rs], axis=0)[:, None, :]
        elif k.endswith("_p"):
            o = np.stack([r[k] for r in rs], axis=1)
        else:
            o = np.concatenate([r[k] for r in rs], axis=1)
        outs.append(np.ascontiguousarray(o, dtype=np.float32))
    return tuple(outs)
```

```python
import contextlib
import numpy as np
import concourse.bass as bass
import concourse.mybir as mybir

F32 = mybir.dt.float32
BF16 = mybir.dt.bfloat16
I32 = mybir.dt.int32
AF = mybir.ActivationFunctionType
OP = mybir.AluOpType
AX = mybir.AxisListType

ENGS = ("pe", "act", "dve", "pool", "sp")
SKIP_SAME_ENGINE = False
SERIALIZE_PSUM_READERS = True


class T:
    def __init__(self, name, handle):
        self.name = name
        self.h = handle
        self.is_psum = False
        self.w = None
        self.r = []

    def __getitem__(self, idx):
        return V(self, self.h[idx])

    def ap(self):
        return V(self, self.h[:])


class V:
    def __init__(self, t, ap):
        self.t = t
        self.a = ap

    def __getitem__(self, idx):
        return V(self.t, self.a[idx])

    def re(self, pat, **kw):
        return V(self.t, self.a.rearrange(pat, **kw))

    def bc(self, shape):
        return V(self.t, self.a.broadcast_to(shape))

    def bitcast(self, dt):
        return V(self.t, self.a.bitcast(dt))

    def ap(self):
        return self


class Prog:
    def __init__(self, nc, n_dma_sems=24):
        self.nc = nc
        self.st = contextlib.ExitStack()
        self.q = {e: [] for e in ENGS}
        self.cnt = {e: 0 for e in ENGS}
        self.sem = {e: self.st.enter_context(nc.semaphore("pg_" + e)) for e in ENGS}
        self.known = {e: {} for e in ENGS}
        self.dsem = [self.st.enter_context(nc.semaphore(f"dma{i}")) for i in range(n_dma_sems)]
        self.dcnt = [0] * n_dma_sems
        self.dnext = 0
        self.pool_sems = []
        self.pool_used = 0
        self.tiles = {}
        self.n_inst = 0

    def sb(self, name, shape, dt=F32):
        h = self.st.enter_context(self.nc.sbuf_tensor(name, list(shape), dt))
        t = T(name, h)
        self.tiles[name] = t
        return t

    def sbc(self, name, shape, dt=F32):
        if name in self.tiles:
            return self.tiles[name]
        return self.sb(name, shape, dt)

    def ps(self, name, shape, dt=F32):
        h = self.st.enter_context(self.nc.psum_tensor(name, list(shape), dt))
        t = T(name, h)
        t.is_psum = True
        self.tiles[name] = t
        return t

    def dram(self, name, shape, dt=F32, kind="Internal"):
        h = self.nc.dram_tensor(name, list(shape), dt, kind=kind)
        t = T(name, h.ap())
        self.tiles[name] = t
        return t

    def _need(self, eng, dep):
        if dep is None:
            return
        kind, i, c = dep
        if kind == "e" and i == eng and (SKIP_SAME_ENGINE or eng == "pe"):
            return
        key = (kind, i)
        if self.known[eng].get(key, 0) >= c:
            return
        self.known[eng][key] = c
        sem = self.sem[i] if kind == "e" else self.dsem[i]
        self.q[eng].append(lambda e, sem=sem, c=c: e.wait_ge(sem, c))

    def _deps(self, eng, reads, writes):
        for v in reads:
            if v is None or not isinstance(v, V):
                continue
            self._need(eng, v.t.w)
        for v in writes:
            self._need(eng, v.t.w)
            for r in v.t.r:
                self._need(eng, r)

    def _mark(self, token, reads, writes):
        for v in reads:
            if v is None or not isinstance(v, V):
                continue
            v.t.r.append(token)
            if len(v.t.r) > 64:
                best = {}
                for k, i, c in v.t.r:
                    best[(k, i)] = max(best.get((k, i), 0), c)
                v.t.r = [(k, i, c) for (k, i), c in best.items()]
        for v in writes:
            v.t.w = token
            v.t.r = []

    def op(self, eng, fn, reads, writes):
        if eng != "pe" and SERIALIZE_PSUM_READERS:
            writes = list(writes) + [v for v in reads if isinstance(v, V) and v.t.is_psum]
        self._deps(eng, reads, writes)
        self.cnt[eng] += 1
        c = self.cnt[eng]
        sem = self.sem[eng]
        self.q[eng].append(lambda e, fn=fn, sem=sem: fn(e).then_inc(sem, 1))
        self.known[eng][("e", eng)] = max(self.known[eng].get(("e", eng), 0), 0)
        self._mark(("e", eng, c), reads, writes)
        self.n_inst += 1

    def dma(self, out, in_, eng="sp", **kw):
        self._deps(eng, [in_], [out])
        if eng == "pool" and self.pool_used < 40:
            self.dsem.append(self.st.enter_context(self.nc.semaphore(f"pdma{self.pool_used}")))
            self.dcnt.append(0)
            self.pool_used += 1
            i = len(self.dsem) - 1
        else:
            i = self.dnext
            self.dnext = (self.dnext + 1) % 24
        self.dcnt[i] += 16
        c = self.dcnt[i]
        sem = self.dsem[i]
        oa, ia = out.a, in_.a
        self.q[eng].append(lambda e, oa=oa, ia=ia, sem=sem, kw=kw: e.dma_start(out=oa, in_=ia, **kw).then_inc(sem, 16))
        self._mark(("d", i, c), [in_], [out])
        self.n_inst += 1

    def barrier(self):
        for e in ENGS:
            for o in ENGS:
                if o != e and self.cnt[o]:
                    self._need(e, ("e", o, self.cnt[o]))
            for i, c in enumerate(self.dcnt):
                if c:
                    self._need(e, ("d", i, c))

    def wait_all_dma(self, eng="sp"):
        for i, c in enumerate(self.dcnt):
            if c:
                self._need(eng, ("d", i, c))

    def mm(self, out, lhsT, rhs, start=True, stop=True, **kw):
        reads = [lhsT, rhs] + ([] if start else [out])
        self.op("pe", lambda e: e.matmul(out.a, lhsT.a, rhs.a, start=start, stop=stop, **kw), reads, [out])

    def tr(self, out, in_, ident):
        if out.a.start_partition != 0 or in_.a.dtype != F32:
            return self.mm(out, in_, ident)
        self.op("pe", lambda e: e.transpose(out.a, in_.a, ident.a), [in_, ident], [out])

    def act(self, out, in_, func, bias=None, scale=None, accum=None, eng="act"):
        kw = {}
        reads = [in_]
        if bias is not None:
            kw["bias"] = bias.a if isinstance(bias, V) else bias
            reads.append(bias)
        if scale is not None:
            kw["scale"] = scale.a if isinstance(scale, V) else scale
            reads.append(scale)
        writes = [out]
        if accum is not None:
            kw["accum_out"] = accum.a
            writes.append(accum)
        self.op(eng, lambda e: e.activation(out.a, in_.a, func, **kw), reads, writes)

    def tt(self, out, a, b, op, eng="dve"):
        self.op(eng, lambda e: e.tensor_tensor(out.a, a.a, b.a, op), [a, b], [out])

    def ts(self, out, a, s1, op0, s2=None, op1=None, eng="dve", accum=None):
        reads = [a, s1, s2]
        x1 = s1.a if isinstance(s1, V) else s1
        x2 = s2.a if isinstance(s2, V) else s2
        kw = {}
        if op1 is not None:
            kw["op1"] = op1
        writes = [out]
        if accum is not None:
            kw["accum_out"] = accum.a
            writes.append(accum)
        self.op(eng, lambda e: e.tensor_scalar(out.a, a.a, x1, x2, op0, **kw), reads, writes)

    def stt(self, out, a, s, b, op0, op1, eng="dve"):
        x = s.a if isinstance(s, V) else s
        self.op(eng, lambda e: e.scalar_tensor_tensor(out.a, a.a, x, b.a, op0, op1), [a, s, b], [out])

    def cp(self, out, in_, eng="dve"):
        if eng == "act":
            self.op(eng, lambda e: e.copy(out.a, in_.a), [in_], [out])
        else:
            self.op(eng, lambda e: e.tensor_copy(out.a, in_.a), [in_], [out])

    def memset(self, out, val, eng="dve"):
        self.op(eng, lambda e: e.memset(out.a, val), [], [out])

    def scan(self, out, d0, d1, init, op0, op1):
        x = init.a if isinstance(init, V) else init
        self.op("dve", lambda e: e.tensor_tensor_scan(out.a, d0.a, d1.a, x, op0, op1), [d0, d1, init], [out])

    def reduce(self, out, in_, op, axis=AX.X):
        self.op("dve", lambda e: e.tensor_reduce(out.a, in_.a, axis, op), [in_], [out])

    def recip(self, out, in_):
        self.op("dve", lambda e: e.reciprocal(out.a, in_.a), [in_], [out])

    def iota(self, out, pattern, base=0, cm=0, **kw):
        self.op("pool", lambda e: e.iota(out.a, pattern, base=base, channel_multiplier=cm, **kw), [], [out])

    def affsel(self, out, in_, pattern, cmp, fill, base=0, cm=0):
        self.op("pool", lambda e: e.affine_select(out.a, in_.a, pattern, cmp, fill, base=base, channel_multiplier=cm),
                [in_], [out])

    def finish(self):
        self.wait_all_dma("sp")
        for e in ENGS:
            if e != "sp" and self.cnt[e]:
                self._need("sp", ("e", e, self.cnt[e]))
        nc = self.nc
        q = self.q
        with nc.Block() as block:
            @block.tensor
            def _(e):
                for f in q["pe"]:
                    f(e)

            @block.scalar
            def _(e):
                for f in q["act"]:
                    f(e)

            @block.vector
            def _(e):
                for f in q["dve"]:
                    f(e)

            @block.gpsimd
            def _(e):
                for f in q["pool"]:
                    f(e)

            @block.sync
            def _(e):
                for f in q["sp"]:
                    f(e)
        self.st.close()


import math
ST = 128
C = 64
NCH = ST // C
OFF = dict(gla_q=0, gla_k=256, gla_v=512, glr=768, gla_gate=784, s5_u=1040, s5_gate=1296,
           gdn_qkv=1552, gdn_ab=2320, gdn_gate=2328, rwkv_in=2584, rwkv_gate=3480)
D_IN = 3736
SHAPES = dict(
    x_prompt=[2048, 1024], x_sample=[16, 1024],
    state_gla=[4, 16, 4, 64, 64], state_s5_re=[4, 16, 16, 64], state_s5_im=[4, 16, 16, 64],
    state_gdn=[4, 16, 4, 64, 64], state_gdn_conv=[4, 16, 3, 768], state_rwkv=[4, 16, 4, 64, 64],
    state_rwkv_shift=[4, 16, 896],
    norm_g=[4, 1024], w_in=[4, 1024, 3736], gla_wg2=[4, 16, 256], gla_bg=[4, 256], gla_norm_g=[4, 64],
    s5_lam_re=[4, 16, 64], s5_lam_im=[4, 16, 64], s5_log_step=[4, 16], s5_b_re=[4, 16, 64, 16],
    s5_b_im=[4, 16, 64, 16], s5_c_re=[4, 16, 16, 64], s5_c_im=[4, 16, 16, 64], s5_d=[4, 256],
    s5_w_glu=[4, 256, 256], s5_b_glu=[4, 256], gdn_conv_w=[4, 4, 768], gdn_a_log=[4, 4], gdn_dt_bias=[4, 4],
    gdn_norm_g=[4, 64], rwkv_mu=[4, 896], rwkv_w0=[4, 256], rwkv_ww2=[4, 64, 256], rwkv_a0=[4, 256],
    rwkv_wa2=[4, 64, 256], rwkv_k_k=[4, 256], rwkv_k_a=[4, 256], rwkv_r_k=[4, 4, 64], rwkv_ln_g=[4, 256],
    rwkv_ln_b=[4, 256], w_out=[4, 1024, 1024], final_g=[1024])
OUT_SHAPES = dict(
    y_p=[2048, 1024], y_s=[16, 1024],
    gla_p=[4, 4, 64, 64], s5re_p=[4, 16, 64], s5im_p=[4, 16, 64], gdn_p=[4, 4, 64, 64], conv_p=[4, 3, 768],
    rwkv_p=[4, 4, 64, 64], shift_p=[4, 896],
    gla_s=[4, 16, 4, 64, 64], s5re_s=[4, 16, 16, 64], s5im_s=[4, 16, 16, 64], gdn_s=[4, 16, 4, 64, 64],
    conv_s=[4, 16, 3, 768], rwkv_s=[4, 16, 4, 64, 64], shift_s=[4, 16, 896])
OUT_ORDER = ["y_p", "y_s", "gla_p", "s5re_p", "s5im_p", "gdn_p", "conv_p", "rwkv_p", "shift_p",
             "gla_s", "s5re_s", "s5im_s", "gdn_s", "conv_s", "rwkv_s", "shift_s"]


def build(DEPTH=4, NST=8, SAMPLE=True, MIX=("gla", "s5", "gdn", "rwkv"), STREAMS=2, PSMODE=0):
    nc = bass.Bass("TRN2", target_bir_lowering=False, dynamic_dma_scratch_size=4096)
    p = Prog(nc)
    din = {k: p.dram(k, v, F32, kind="ExternalInput") for k, v in SHAPES.items()}
    dout = {k: p.dram(k, v, F32, kind="ExternalOutput") for k, v in OUT_SHAPES.items()}
    xbuf = p.dram("xbuf", [2048, 1024], F32)
    NS = True

    def rows(hp):
        return slice(64 * hp, 64 * hp + 64)

    ident = p.sb("ident", [128, 128])
    p.memset(ident.ap(), 1.0, eng="pool")
    p.affsel(ident.ap(), ident.ap(), [[-1, 128]], OP.is_equal, 0.0, base=0, cm=1)
    identb = p.sb("identb", [128, 128], BF16)
    p.cp(identb.ap(), ident.ap())
    identP = p.sb("identP", [128, 2, 64])
    for tl in range(2):
        p.cp(identP[0:64, tl, :], ident[0:64, 0:64])
        p.cp(identP[64:128, tl, :], ident[64:128, 64:128])
    mI = p.sb("mI", [128, 64])
    mS = p.sb("mS", [128, 64])
    for hp in range(2):
        p.memset(mI[rows(hp), :], 1.0, eng="pool")
        p.affsel(mI[rows(hp), :], mI[rows(hp), :], [[1, 64]], OP.is_ge, 0.0, base=0, cm=-1)
        p.memset(mS[rows(hp), :], 1.0, eng="pool")
        p.affsel(mS[rows(hp), :], mS[rows(hp), :], [[1, 64]], OP.is_ge, 0.0, base=-1, cm=-1)
    nmI = p.sb("nmI", [128, 64])
    p.ts(nmI.ap(), mI.ap(), -1.0, OP.mult)
    mS4 = p.sb("mS4", [128, 4, 64])
    mI4 = p.sb("mI4", [128, 4, 64])
    for i in range(4):
        p.ts(mS4[:, i, :], mS.ap(), -1.0 if i < 2 else 1.0, OP.mult)
        p.ts(mI4[:, i, :], mI.ap(), 1.0 if i < 2 else -1.0, OP.mult)
    bones = p.sb("bones", [128, 128])
    p.memset(bones.ap(), 0.0)
    p.memset(bones[0:64, 0:64], 1.0)
    p.memset(bones[64:128, 64:128], 1.0)
    Eg = p.sb("Eg", [8, 2, 128])
    Eb = p.sb("Eb", [8, 2, 128])
    for E, sh in ((Eg, 0), (Eb, 4)):
        p.memset(E.ap(), 1.0, eng="pool")
        p.affsel(E.ap(), E.ap(), [[128, 2], [1, 128]], OP.is_ge, 0.0, base=64 * sh, cm=-64)
        p.affsel(E.ap(), E.ap(), [[-128, 2], [-1, 128]], OP.is_ge, 0.0, base=63 - 64 * sh, cm=64)
    dmask = p.sb("dmask", [16, 16, 64])
    p.memset(dmask.ap(), 1.0, eng="pool")
    p.affsel(dmask.ap(), dmask.ap(), [[-1, 16], [0, 64]], OP.is_equal, 0.0, base=0, cm=1)
    tidx = p.sb("tidx", [128, ST])
    p.iota(tidx.ap(), [[1, ST]], base=0, cm=0, allow_small_or_imprecise_dtypes=True)
    ones = p.sb("ones", [128, ST])
    p.memset(ones.ap(), 1.0)
    fg = p.sb("fg", [128, 1024])
    p.dma(fg.ap(), din["final_g"].ap().re("(o n) -> o n", o=1).bc([128, 1024]))

    PJ = [p.ps("PJ0", [128, 512]), p.ps("PJ1", [128, 512])]
    BK = {nm: p.ps(nm, [128, 512]) for nm in ("PT_A", "PA_A", "PX_A", "PT_B", "PA_B", "PX_B")}

    def psset(sfx):
        b1, b2, b3 = BK["PT_" + sfx], BK["PA_" + sfx], BK["PX_" + sfx]
        return (b2, b1, b3, b2, b1, b1)
    PS_A, PS_B = psset("A"), psset("B")
    if PSMODE == 1:
        PS_A = PS_B = (BK["PA_A"], BK["PX_A"], BK["PT_B"], BK["PA_B"], BK["PX_B"], BK["PT_A"])
    PS_FULL = (BK["PA_A"], BK["PX_A"], BK["PT_B"], BK["PA_B"], BK["PX_B"], BK["PT_A"])
    PT = BK["PT_A"]
    PB = BK["PX_A"]

    def v3(ps, n):
        return ps[:, 0:n * 64].re("p (a b) -> p a b", b=64)

    WGRP = [(1552, 2584), (2584, 3736), (0, 1040), (1040, 1552)]
    Wins = [p.sb(f"Win{i}", [128, 8, c1 - c0], BF16) for i, (c0, c1) in enumerate(WGRP)]
    Wout = p.sb("Wout", [128, 8, 1024], BF16)
    xts = [p.sb(f"xt{b}", [128, 1, 1024]) for b in range(2)]
    xn = p.sb("xn", [128, 1024])
    xn2 = p.sb("xn2", [128, 1024])
    hTs = [p.sb(f"hT{b}", [128, 8, ST], BF16) for b in range(2)]
    mixT2 = [[p.sb(f"mixT{b}_{i}", [128, 2, ST], BF16) for i in range(4)] for b in range(2)]
    ss = p.sb("ss", [128, 1])
    ss2 = p.sb("ss2", [128, 1])
    cur = {"hT": hTs[0]}
    ng = p.sb("ng", [128, 8])
    cnt = [0]

    RAW = p.sb("RAW", [128, 8704])
    carve_off = [0]

    def carve(name, shape, dt=F32):
        n = 1
        for d in shape[1:]:
            n *= d
        n32 = n if dt != BF16 else (n + 1) // 2
        a = RAW.h[0:shape[0], carve_off[0]:carve_off[0] + n32]
        carve_off[0] += n32
        assert carve_off[0] <= 8704, carve_off[0]
        if dt != F32:
            a = a.bitcast(dt)
        if len(shape) > 2:
            names = " ".join(f"d{i}" for i in range(1, len(shape)))
            a = a.rearrange(f"p ({names}) -> p {names}", **{f"d{i}": shape[i] for i in range(1, len(shape) - 1)})
        t = type(ones)(name, a)
        p.tiles[name] = t
        return t

    def make_set(sfx, alloc):
        W = {}
        for nm in ["qT", "kT", "vT", "aT", "bT", "ldT", "gate", "oT", "t0", "t1", "t2", "t3", "bon"]:
            W[nm] = alloc(f"w{sfx}_" + nm, [128, 2, ST])
        W["ones"] = ones
        W["g0"] = alloc(f"w{sfx}_g0", [128, 4])
        W["g0a"] = alloc(f"w{sfx}_g0a", [128, 4, 8])
        for nm in ("ub", "hbr", "hbi"):
            W[nm] = alloc(f"w{sfx}_" + nm, [128, 2, ST], BF16)
        K = {}
        for nm in ["cum", "cumx", "E", "qs", "as", "ks", "bs", "kd", "bd", "X", "R2", "Ut", "WtT", "U", "araw", "qraw", "vb", "Hb"]:
            K[nm] = alloc(f"k{sfx}_" + nm, [128, 2, 64], F32 if nm in ("cum", "cumx", "E", "Ut") else BF16)
        K["identb"] = identb
        K["gam"] = alloc(f"k{sfx}_gam4", [128, 4, 64])
        K["tok"] = alloc(f"k{sfx}_tok", [128, 8, 64], BF16)
        K["gtok"] = alloc(f"k{sfx}_gtok", [128, 2, 64])
        K["amat"] = alloc(f"k{sfx}_amat", [128, 8, 64], BF16)
        K["BB"] = [alloc(f"k{sfx}_BB0", [128, 4, 64], BF16), alloc(f"k{sfx}_BB1", [128, 4, 64], BF16)]
        K["PC"] = alloc(f"k{sfx}_PC", [128, 2])
        return W, K
    W, K = make_set("A", p.sb)
    carve_off[0] = 0
    W2, K2 = make_set("B", carve)
    W2["rin"] = carve("wB_rin", [128, 7, ST + 1])
    W2["xs7"] = carve("wB_xs7", [128, 7, ST])
    W2["PH"] = BK["PA_B"]
    W["PH"] = BK["PA_B"]
    set_b_end = carve_off[0]
    W["xp"] = p.sb("wA_xp", [128, 6, ST + 3])
    W["cv"] = p.sb("wA_cv", [128, 6, ST])
    W["abT"] = p.sb("w_abT", [8, ST])
    W["gf"] = p.sb("w_gf", [8, ST])
    W["bf"] = p.sb("w_bf", [8, ST])
    W["rin"] = W["xp"]
    carve_off[0] = 0
    xs_s = p.sb("xs_s", [16, 1024])
    big1 = carve("big1", [128, 2304])
    W["Snat"] = big1[:, 0:2048].re("p (a b c) -> p a b c", a=2, b=16)
    W["s5nat"] = big1[0:16, 0:2048].re("p (a b) -> p a b", a=2)
    W["cnat"] = big1[0:16, 0:2304].re("p (a b) -> p a b", a=3)
    W["snat"] = big1[0:16, 0:896]
    Hs1 = carve("Hs1", [128, 2, 16, 64])
    W["Dd"] = carve("w_Dd", [128, 2, 16])
    W["tokS"] = carve("w_tokS", [16, 3, 256])
    W["Ud"] = carve("w_Ud", [16, 16, 64])
    W["Vd"] = carve("w_Vd", [16, 16, 64])
    W["tmpd"] = W["Ud"]
    W["oTok"] = carve("w_oTok", [16, 256])
    W["hS"] = carve("w_hS", [128, 2, 8, 16])
    W["xsS"] = carve("w_xsS", [128, 6, 4, 16])
    W["prevS"] = carve("w_prevS", [128, 7, 16])
    W["rinS"] = carve("w_rinS", [128, 7, 16])
    W["xs7S"] = carve("w_xs7S", [128, 7, 16])
    H = {m: p.sb("H_" + m, [128, 2, 64]) for m in ("gla", "gdn", "rwkv")}
    Hs = {m: Hs1 for m in ("gla", "gdn", "rwkv")}

    def colvec(name, src, n):
        t = p.sb(name, [128, n // 128])
        p.dma(t.ap(), src.re("(k p) -> p k", p=128), allow_slow_non_contiguous=NS)
        return t

    def proj(dst, col0, n, T, scale=1.0):
        pj = PJ[cnt[0] % 2]
        cnt[0] += 1
        gi = [i for i, (c0, c1) in enumerate(WGRP) if c0 <= col0 < c1][0]
        Wg, cb = Wins[gi], col0 - WGRP[gi][0]
        for k in range(8):
            p.mm(pj[0:n, 0:T], Wg[:, k, cb:cb + n], cur["hT"][:, k, 0:T], start=(k == 0), stop=(k == 7))
        p.act(dst, pj[0:n, 0:T], AF.Identity, scale=scale)

    def rstd_inplace(t, T, mult, eps):
        p.ts(t, t, mult, OP.mult, eps, OP.add)
        p.act(t, t, AF.Sqrt)
        p.recip(t, t)

    def headsum(dst, src, T):
        for tl in range(2):
            pj = PJ[cnt[0] % 2]
            cnt[0] += 1
            p.mm(pj[:, 0:T], bones.ap(), src[:, tl, 0:T])
            p.cp(dst[:, tl, 0:T], pj[:, 0:T], eng="act")

    def silu_(dst, src):
        p.act(dst, src, AF.Silu)

    for l in range(DEPTH):
        for i, (c0, c1) in enumerate(WGRP):
            p.dma(Wins[i].ap(), din["w_in"][l, :, c0:c1].re("(k q) c -> q k c", q=128), eng="pool")
        p.dma(Wout.ap(), din["w_out"][l].re("(k q) c -> q k c", q=128), eng="pool")
        p.dma(ng.ap(), din["norm_g"][l].re("(k p) -> p k", p=128), allow_slow_non_contiguous=NS)
        L = {}
        wg2 = p.sbc(f"wg2", [32, 256])
        p.memset(wg2.ap(), 0.0)
        p.dma(wg2[0:16, :], din["gla_wg2"][l])
        p.dma(wg2[16:17, :], din["gla_bg"][l].re("(o n) -> o n", o=1))
        gla_ng = p.sbc(f"gla_ng", [128, 1])
        gdn_ng = p.sbc(f"gdn_ng", [128, 1])
        for hp in range(2):
            p.dma(gla_ng[rows(hp), :], din["gla_norm_g"][l].re("(n o) -> n o", o=1), allow_slow_non_contiguous=NS)
            p.dma(gdn_ng[rows(hp), :], din["gdn_norm_g"][l].re("(n o) -> n o", o=1), allow_slow_non_contiguous=NS)
        s5d = colvec(f"s5d_{l}", din["s5_d"][l], 256)
        s5bg = colvec(f"s5bg_{l}", din["s5_b_glu"][l], 256)
        wglu = p.sbc(f"wglu", [128, 2, 256])
        p.dma(wglu.ap(), din["s5_w_glu"][l].re("(k p) n -> p k n", p=128))
        convw = p.sbc(f"convw", [128, 6, 4])
        for i in range(4):
            p.dma(convw[:, :, i], din["gdn_conv_w"][l, i].re("(j p) -> p j", p=128), allow_slow_non_contiguous=NS)
        gab = p.sbc(f"gab", [8, 2])
        p.memset(gab.ap(), 0.0)
        p.dma(gab[0:4, 0:1], din["gdn_dt_bias"][l].re("(n o) -> n o", o=1), allow_slow_non_contiguous=NS)
        p.dma(gab[0:4, 1:2], din["gdn_a_log"][l].re("(n o) -> n o", o=1), allow_slow_non_contiguous=NS)
        p.act(gab[:, 1:2], gab[:, 1:2], AF.Exp)
        p.ts(gab[:, 1:2], gab[:, 1:2], -1.0, OP.mult)
        mu = colvec(f"mu_{l}", din["rwkv_mu"][l], 896)
        w0 = colvec(f"w0_{l}", din["rwkv_w0"][l], 256)
        a0 = colvec(f"a0_{l}", din["rwkv_a0"][l], 256)
        k_k = colvec(f"kk_{l}", din["rwkv_k_k"][l], 256)
        k_a = colvec(f"ka_{l}", din["rwkv_k_a"][l], 256)
        r_k = colvec(f"rk_{l}", din["rwkv_r_k"][l].re("h n -> (h n)"), 256)
        ln_g = colvec(f"lng_{l}", din["rwkv_ln_g"][l], 256)
        ln_b = colvec(f"lnb_{l}", din["rwkv_ln_b"][l], 256)
        wlo = p.sbc(f"wlo", [128, 256])
        p.dma(wlo[0:64, :], din["rwkv_ww2"][l])
        p.dma(wlo[64:128, :], din["rwkv_wa2"][l])

        S5 = {}

        for m in H:
            p.memset(H[m].ap(), 0.0)
        hist_gdn = p.sbc(f"hist_gdn", [128, 6, 3])
        hist_rwkv = p.sbc(f"hist_rwkv", [128, 7, 1])
        p.memset(hist_gdn.ap(), 0.0)
        p.memset(hist_rwkv.ap(), 0.0)

        common = dict(p=p, proj=proj, OFF=OFF, PJ=PJ, cnt=cnt, ident=ident, identP=identP, mI=mI, mS=mS, nmI=nmI,
                      mI4=mI4, mS4=mS4, dmask=dmask, rows=rows, v3=v3, headsum=headsum, rstd_inplace=rstd_inplace,
                      din=din, dout=dout, l=l, NST=NST, H=H, Hs=Hs)
        LW = dict(wg2=wg2, gla_ng=gla_ng, gdn_ng=gdn_ng, s5d=s5d, s5bg=s5bg, wglu=wglu, convw=convw, gab=gab, Eg=Eg,
                  Eb=Eb, mu=mu, w0=w0, a0=a0, k_k=k_k, k_a=k_a, r_k=r_k, ln_g=ln_g, ln_b=ln_b, wlo=wlo,
                  hist_gdn=hist_gdn, hist_rwkv=hist_rwkv, S5=S5)
        last_layer = (l == DEPTH - 1)

        def run_streams(gens):
            gens = [g for g in gens if g is not None]
            while gens:
                for g in list(gens):
                    try:
                        next(g)
                    except StopIteration:
                        gens.remove(g)

        def chain(*gs):
            for g in gs:
                if g is not None:
                    yield from g

        def head_gen(kind, st, bi):
            if kind == "p":
                r0 = st * ST
                src = din["x_prompt"] if l == 0 else xbuf
                xv = xts[bi][:, 0, :]
                p.dma(xv, src[r0:r0 + 128, :])
                np_ = 128
            else:
                xv = xs_s[0:16, :]
                if l == 0:
                    p.dma(xv, din["x_sample"].ap())
                np_ = 16
            p.act(xn[0:np_, :], xv, AF.Square, accum=ss[0:np_, :])
            yield
            rstd_inplace(ss[0:np_, :], 1, 1.0 / 1024, 1e-6)
            yield
            p.ts(xn[0:np_, :], xv, ss[0:np_, :], OP.mult)
            yield
            for kk in range(2):
                pj = PJ[cnt[0] % 2]
                cnt[0] += 1
                for j in range(4):
                    k = kk * 4 + j
                    p.tr(pj[:, j * 128:j * 128 + np_], xn[0:np_, k * 128:(k + 1) * 128], ident[0:np_, 0:np_])
                p.tt(hTs[bi][:, kk * 4:kk * 4 + 4, 0:np_],
                     pj[:, 0:512].re("p (a b) -> p a b", b=128)[:, :, 0:np_],
                     ng[:, kk * 4:kk * 4 + 4, None].bc([128, 4, np_]), OP.mult)
                yield

        def tail_gen(kind, st, bi):
            np_ = 128 if kind == "p" else 16
            xv = xts[bi][:, 0, :] if kind == "p" else xs_s[0:16, :]
            r0 = st * ST
            mt = mixT2[bi]
            for half in range(2):
                pj = PJ[cnt[0] % 2]
                cnt[0] += 1
                for k in range(8):
                    p.mm(pj[0:np_, :], mt[k // 2][:, k % 2, 0:np_], Wout[:, k, half * 512:(half + 1) * 512],
                         start=(k == 0), stop=(k == 7))
                p.tt(xv[:, half * 512:(half + 1) * 512], xv[:, half * 512:(half + 1) * 512], pj[0:np_, :], OP.add)
                yield
            if not last_layer:
                if kind == "p":
                    p.dma(xbuf[r0:r0 + 128, :], xv)
            else:
                p.act(xn2[0:np_, :], xv, AF.Square, accum=ss2[0:np_, :])
                yield
                rstd_inplace(ss2[0:np_, :], 1, 1.0 / 1024, 1e-6)
                yield
                p.stt(xn2[0:np_, :], xv, ss2[0:np_, :], fg[0:np_, :], OP.mult, OP.mult)
                yield
                if kind == "p":
                    p.dma(dout["y_p"][r0:r0 + 128, :], xn2[0:np_, :])
                else:
                    p.dma(dout["y_s"].ap(), xn2[0:np_, :])

        def mixers(kind, st, bi):
            T = ST if kind == "p" else 16
            cur["hT"] = hTs[bi]
            cm = dict(common, mixTs=mixT2[bi])
            for i, m in enumerate(("gla", "s5", "gdn", "rwkv")):
                if m not in MIX:
                    p.memset(mixT2[bi][i][:, :, 0:T], 0.0)
            if kind == "p":
                ga = chain(gdn_block(kind, st, T, W, K, PS_A, LW, **cm) if "gdn" in MIX else None,
                           gla_block(kind, st, T, W, K, PS_A, LW, **cm) if "gla" in MIX else None)
                gb = chain(rwkv_block(kind, st, T, W2, K2, PS_B, LW, **cm) if "rwkv" in MIX else None,
                           s5_setup(S5, p, nc, din, l, ident, tidx, BK["PT_B"], BK["PX_B"], rows)
                           if ("s5" in MIX and st == 0) else None,
                           s5_block(kind, st, T, W2, PS_B, LW, **cm) if "s5" in MIX else None)
                return [ga, gb] if STREAMS == 2 else [chain(ga, gb)]
            W["rin"], W["xs7"] = W["rinS"], W["xs7S"]
            return [chain(gla_block(kind, st, T, W, K, PS_FULL, LW, **cm) if "gla" in MIX else None,
                          s5_block(kind, st, T, W, PS_FULL, LW, **cm) if "s5" in MIX else None,
                          gdn_block(kind, st, T, W, K, PS_FULL, LW, **cm) if "gdn" in MIX else None,
                          rwkv_block(kind, st, T, W, K, PS_FULL, LW, **cm) if "rwkv" in MIX else None)]

        run_streams([head_gen("p", 0, 0)])
        for step in range(NST + 1):
            gens = []
            if step < NST:
                gens += mixers("p", step, step % 2)
            gens.append(chain(tail_gen("p", step - 1, (step - 1) % 2) if step >= 1 else None,
                              head_gen("p", step + 1, (step + 1) % 2) if step + 1 < NST else None))
            run_streams(gens)
        if SAMPLE:
            p.barrier()
            run_streams([head_gen("s", 0, 0)])
            run_streams(mixers("s", 0, 0))
            run_streams([tail_gen("s", 0, 0)])
            p.barrier()
    p.finish()
    return nc, p


def gla_block(kind, st, T, W, K, PS, LW, *, p, proj, OFF, PJ, cnt, ident, identP, mI, mS, nmI, mI4, mS4, dmask, rows, v3,
              headsum, rstd_inplace, din, dout, l, NST, H, Hs, mixTs):
    qT, kT, vT, ldT, gate, oT = (W[n] for n in ("qT", "kT", "vT", "ldT", "gate", "oT"))
    wg2, gla_ng = LW["wg2"], LW["gla_ng"]
    for tl in range(2):
        proj(qT[:, tl, 0:T], OFF["gla_q"] + 128 * tl, 128, T, scale=0.125)
        yield
        proj(kT[:, tl, 0:T], OFF["gla_k"] + 128 * tl, 128, T)
        yield
        proj(vT[:, tl, 0:T], OFF["gla_v"] + 128 * tl, 128, T)
        yield
        proj(gate[:, tl, 0:T], OFF["gla_gate"] + 128 * tl, 128, T)
        yield
    glr = W["t0"]
    p.memset(glr[0:32, 0, 0:T], 1.0)
    proj(glr[0:16, 0, 0:T], OFF["glr"], 16, T)
    yield
    for tl in range(2):
        pj = PJ[cnt[0] % 2]
        cnt[0] += 1
        p.mm(pj[:, 0:T], wg2[0:17, tl * 128:(tl + 1) * 128], glr[0:17, 0, 0:T])
        p.act(ldT[:, tl, 0:T], pj[:, 0:T], AF.Exp, scale=-1.0)
        p.act(ldT[:, tl, 0:T], ldT[:, tl, 0:T], AF.Ln, bias=1.0)
        yield
    p.ts(ldT[:, :, 0:T], ldT[:, :, 0:T], -1.0 / 16, OP.mult)
    yield from mixer_core(p, "gla", kind, T, W, K, H["gla"], Hs["gla"], dict(ab=False, scalar=False),
                          PS, ident, identP, mI, mS, nmI, mI4, mS4, dmask, rows, v3,
                          din["state_gla"], dout["gla_s"], l, transposed_state=False)
    yield from out_norm_rms(p, oT, gate, gla_ng, T, W, headsum, rstd_inplace, mixTs[0])
    if kind == "p" and st == NST - 1:
        p.dma(dout["gla_p"][l].re("(t hp) d v -> (hp d) t v", hp=2), H["gla"].ap())


def mixer_core(p, name, kind, T, W, K, Hst, Hsamp, fl, PS, ident, identP, mI, mS, nmI, mI4, mS4, dmask, rows, v3,
               state_in, state_out, l, transposed_state):
    PA, PB, PU, PO, PH, PT = PS
    qT, aT, kT, bT, vT, ldT, oT = (W[n] for n in ("qT", "aT", "kT", "bT", "vT", "ldT", "oT"))
    ab, scalar = fl["ab"], fl["scalar"]
    if kind == "s":
        sample_core(p, name, W, Hsamp, fl, PS, ident, dmask, rows, v3, state_in, state_out, l, transposed_state)
        yield
        return
    cum, cumx, E, qs, as_, ks, bs, kd, bd, X, R2, Ut, WtT, U = (K[n] for n in (
        "cum", "cumx", "E", "qs", "as", "ks", "bs", "kd", "bd", "X", "R2", "Ut", "WtT", "U"))
    tok, gtok, amat, BB, PC, gam = K["tok"], K["gtok"], K["amat"], K["BB"], K["PC"], K["gam"]
    araw, qraw, vb, Hb, identb = K["araw"], K["qraw"], K["vb"], K["Hb"], K["identb"]
    p.cp(Hb.ap(), Hst.ap(), eng="act")
    yield
    ones64 = None
    for c in range(T // C):
        sl = slice(c * C, (c + 1) * C)
        for tl in range(2):
            p.scan(cum[:, tl, :], W["ones"][:, 0:64], ldT[:, tl, sl], 0.0, OP.mult, OP.add)
            yield
        p.act(E.ap(), cum.ap(), AF.Exp)
        yield
        p.tt(qs.ap(), qT[:, :, sl], E.ap(), OP.mult)
        yield
        for tl in range(2):
            p.cp(PC[:, tl:tl + 1], E[:, tl, 63:64])
            yield
        if ab:
            p.tt(cumx.ap(), cum.ap(), ldT[:, :, sl], OP.subtract)
            yield
            p.act(E.ap(), cumx.ap(), AF.Exp)
            yield
            p.tt(as_.ap(), aT[:, :, sl], E.ap(), OP.mult)
            yield
        for tl in range(2):
            p.act(E[:, tl, :], cum[:, tl, :], AF.Exp, scale=-1.0, bias=cum[:, tl, 63:64])
            yield
        p.tt(kd.ap(), kT[:, :, sl], E.ap(), OP.mult)
        yield
        if ab:
            p.stt(bd.ap(), bT[:, :, sl], -1.0, E.ap(), OP.mult, OP.mult)
            yield
        if not scalar:
            p.act(E.ap(), cum.ap(), AF.Exp, scale=-1.0)
            yield
            p.tt(ks.ap(), kT[:, :, sl], E.ap(), OP.mult)
            yield
            if ab:
                p.tt(bs.ap(), bT[:, :, sl], E.ap(), OP.mult)
                yield
            Yk, Yb, Xa, Xq = ks, bs, as_, qs
        else:
            p.cp(ks.ap(), kT[:, :, sl], eng="act")
            yield
            p.cp(bs.ap(), bT[:, :, sl], eng="act")
            yield
            p.cp(araw.ap(), aT[:, :, sl], eng="act")
            yield
            p.cp(qraw.ap(), qT[:, :, sl], eng="act")
            yield
            Yk, Yb, Xa, Xq = ks, bs, araw, qraw
        p.cp(vb.ap(), vT[:, :, sl], eng="act")
        yield
        tq = [("as", as_), ("kd", kd), ("bd", bd), ("v", None)]
        ptv = v3(PT, 8)
        for qi, (nm, src) in enumerate(tq):
            if nm in ("as", "bd") and not ab:
                continue
            for tl in range(2):
                for hp in range(2):
                    s_ = vb[rows(hp), tl, :] if nm == "v" else src[rows(hp), tl, :]
                    p.tr(ptv[rows(hp), qi * 2 + tl, :], s_, identb[rows(hp), rows(hp)])
        if ab:
            p.cp(tok.ap(), ptv, eng="act")
            yield
        else:
            p.cp(tok[:, 2:4, :], ptv[:, 2:4, :], eng="act")
            yield
            p.cp(tok[:, 6:8, :], ptv[:, 6:8, :], eng="act")
            yield
        aTok, kdTok, bdTok, vTok = tok[:, 0:2, :], tok[:, 2:4, :], tok[:, 4:6, :], tok[:, 6:8, :]
        pav = v3(PA, 8)
        pairs = [(0, Yb, Xa), (1, Yk, Xa), (2, Yk, Xq), (3, Yb, Xq)] if ab else [(2, Yk, Xq)]
        for ty, Y, Xx in pairs:
            for tl in range(2):
                for hp in range(2):
                    ysl = Y[rows(hp), tl, :]
                    xsl = Xx[rows(hp), tl, :]
                    p.mm(pav[rows(hp), ty * 2 + tl, :], ysl, xsl)
        if scalar:
            puv = v3(PU, 2)
            for tl in range(2):
                for hp in range(2):
                    p.tr(puv[rows(hp), tl, :], cum[rows(hp), tl, :], ident[rows(hp), rows(hp)])
            p.cp(gtok.ap(), puv)
            yield
            for tl in range(2):
                p.ts(gam[:, 0 * 2 + tl, :], cumx[:, tl, :], gtok[:, tl, 0:1], OP.subtract, 0.0, OP.min)
                p.ts(gam[:, 1 * 2 + tl, :], cum[:, tl, :], gtok[:, tl, 0:1], OP.subtract, 0.0, OP.min)
            yield
            p.act(gam.ap(), gam.ap(), AF.Exp)
            yield
            for ty in range(4):
                gsel = gam[:, 0:2, :] if ty < 2 else gam[:, 2:4, :]
                p.tt(amat[:, 2 * ty:2 * ty + 2, :], pav[:, 2 * ty:2 * ty + 2, :], gsel, OP.mult)
                yield
            p.tt(amat[:, 0:4, :], amat[:, 0:4, :], mS4.ap(), OP.mult)
            yield
            p.tt(amat[:, 4:8, :], amat[:, 4:8, :], mI4.ap(), OP.mult)
            yield
        elif ab:
            p.tt(amat[:, 0:4, :], pav[:, 0:4, :], mS4.ap(), OP.mult)
            yield
            p.tt(amat[:, 4:8, :], pav[:, 4:8, :], mI4.ap(), OP.mult)
            yield
        else:
            p.tt(amat[:, 4:6, :], pav[:, 4:6, :], mI4[:, 0:2, :], OP.mult)
            yield
        nLt, Akt, Qkt, nQbt = amat[:, 0:2, :], amat[:, 2:4, :], amat[:, 4:6, :], amat[:, 6:8, :]
        if ab:
            b0 = BB[0]
            p.cp(b0[:, 0:2, :], nLt)
            yield
            pbv = v3(PB, 4)
            for tl in range(2):
                for hp in range(2):
                    p.tr(pbv[rows(hp), tl, :], nLt[rows(hp), tl, :], identb[rows(hp), rows(hp)])
            p.cp(b0[:, 2:4, :], pbv[:, 0:2, :], eng="act")
            yield
            p.tt(X.ap(), identP.ap(), nLt, OP.add)
            yield
            for k in range(1, 6):
                prev, cur = BB[(k - 1) % 2], BB[k % 2]
                for tl in range(2):
                    for hp in range(2):
                        r = rows(hp)
                        p.mm(pbv[r, tl, :], prev[r, 2 + tl, :], prev[r, tl, :])
                        p.mm(pbv[r, 2 + tl, :], prev[r, tl, :], prev[r, 2 + tl, :])
                p.cp(cur.ap(), pbv, eng="act")
                yield
                puv = v3(PU, 2)
                for tl in range(2):
                    for hp in range(2):
                        r = rows(hp)
                        p.mm(puv[r, tl, :], cur[r, 2 + tl, :], X[r, tl, :])
                p.tt(X.ap(), X.ap(), puv, OP.add)
                yield
            pov = v3(PO, 2)
            for tl in range(2):
                for hp in range(2):
                    r = rows(hp)
                    p.mm(pov[r, tl, :], Akt[r, tl, :], vTok[r, tl, :])
            p.cp(R2.ap(), pov, eng="act")
            yield
            puv = v3(PU, 2)
            phv = v3(PH, 2)
            for tl in range(2):
                for hp in range(2):
                    r = rows(hp)
                    p.mm(puv[r, tl, :], X[r, tl, :], R2[r, tl, :])
                    p.mm(phv[r, tl, :], aTok[r, tl, :], X[r, tl, :])
            p.cp(Ut.ap(), puv)
            yield
            p.cp(WtT.ap(), phv, eng="act")
            yield
            for tl in range(2):
                for hp in range(2):
                    r = rows(hp)
                    p.mm(puv[r, tl, :], WtT[r, tl, :], Hb[r, tl, :])
            p.tt(U.ap(), puv, Ut.ap(), OP.add)
            yield
        pov = v3(PO, 2)
        for tl in range(2):
            for hp in range(2):
                r = rows(hp)
                p.mm(pov[r, tl, :], Hb[r, tl, :], qs[r, tl, :], start=True, stop=False)
                p.mm(pov[r, tl, :], vTok[r, tl, :], Qkt[r, tl, :], start=False, stop=not ab)
                if ab:
                    p.mm(pov[r, tl, :], U[r, tl, :], nQbt[r, tl, :], start=False, stop=True)
        p.cp(oT[:, :, sl], pov, eng="act")
        yield
        phv = v3(PH, 2)
        for tl in range(2):
            for hp in range(2):
                r = rows(hp)
                p.mm(phv[r, tl, :], kdTok[r, tl, :], vTok[r, tl, :], start=True, stop=not ab)
                if ab:
                    p.mm(phv[r, tl, :], bdTok[r, tl, :], U[r, tl, :], start=False, stop=True)
        for tl in range(2):
            p.stt(Hst[:, tl, :], Hst[:, tl, :], PC[:, tl:tl + 1], phv[:, tl, :], OP.mult, OP.add)
            yield
        p.cp(Hb.ap(), Hst.ap(), eng="act")
        yield


def out_norm_rms(p, oT, gate, gcol, T, W, headsum, rstd_inplace, mixTm):
    t0, t1 = W["t0"], W["t1"]
    p.act(t0[:, :, 0:T], oT[:, :, 0:T], AF.Square)
    yield
    headsum(t1, t0, T)
    yield
    rstd_inplace(t1[:, :, 0:T], T, 1.0 / 64, 1e-6)
    yield
    p.tt(t0[:, :, 0:T], oT[:, :, 0:T], t1[:, :, 0:T], OP.mult)
    yield
    p.act(t1[:, :, 0:T], gate[:, :, 0:T], AF.Silu)
    yield
    p.stt(mixTm[:, :, 0:T], t0[:, :, 0:T], gcol[:, 0:1], t1[:, :, 0:T], OP.mult, OP.mult)
    yield


def sample_core(p, name, W, Hs, fl, PS, ident, dmask, rows, v3, state_in, state_out, l, transposed_state):
    PA, PB, PU, PO, PH, PT = PS
    ab = fl["ab"]
    qT, aT, kT, bT, vT, ldT, oT = (W[n] for n in ("qT", "aT", "kT", "bT", "vT", "ldT", "oT"))
    Snat = W["Snat"]
    Dd, tokS, Ud, Vd, oTok, tmpd = (W[n] for n in ("Dd", "tokS", "Ud", "Vd", "oTok", "tmpd"))
    ptv = v3(PT, 8)
    for tl in range(2):
        for hp in range(2):
            h = 2 * tl + hp
            if not transposed_state:
                p.dma(Hs[rows(hp), tl, :, :], state_in[l, :, h].re("b d v -> d b v"))
            else:
                p.dma(Snat[rows(hp), tl, :, :], state_in[l, :, h].re("b v d -> v b d"))
    if transposed_state:
        for tl in range(2):
            for g in range(2):
                for j in range(8):
                    for hp in range(2):
                        p.tr(ptv[rows(hp), j, :], Snat[rows(hp), tl, 8 * g + j, :], ident[rows(hp), rows(hp)])
                p.cp(Hs[:, tl, 8 * g:8 * g + 8, :], ptv)
    p.act(Dd.ap(), ldT[:, :, 0:16], AF.Exp)
    pt2 = PT[0:16, 0:512].re("p (a b) -> p a b", b=128)
    srcs = [kT, bT, vT] if ab else [kT, vT]
    for qi, src in enumerate(srcs):
        for tl in range(2):
            p.tr(pt2[:, tl, :], src[:, tl, 0:16], ident.ap())
        if ab and qi == 1:
            p.ts(tokS[:, 1, :], pt2[:, 0:2, :].re("p a b -> p (a b)"), -1.0, OP.mult)
        else:
            p.cp(tokS[:, (qi if ab else 2 * qi), :], pt2[:, 0:2, :].re("p a b -> p (a b)"))
    for tl in range(2):
        for hp in range(2):
            h = 2 * tl + hp
            r = rows(hp)
            hc = slice(64 * h, 64 * h + 64)
            p.tt(Vd.ap(), tokS[:, 2, hc][:, None, :].bc([16, 16, 64]), dmask.ap(), OP.mult)
            if ab:
                for g, ps in enumerate((PU, PB)):
                    p.mm(ps[0:16, :], aT[r, tl, 0:16], Hs[r, tl, 8 * g:8 * g + 8, :].re("p b v -> p (b v)"))
                    p.tt(Ud[:, 8 * g:8 * g + 8, :], ps[0:16, :].re("p (b v) -> p b v", v=64),
                         dmask[:, 8 * g:8 * g + 8, :], OP.mult)
            for g, ps in enumerate((PH, PO)):
                p.mm(ps[r, :], tokS[:, 0, hc], Vd[:, 8 * g:8 * g + 8, :].re("p b v -> p (b v)"), start=True, stop=not ab)
                if ab:
                    p.mm(ps[r, :], tokS[:, 1, hc], Ud[:, 8 * g:8 * g + 8, :].re("p b v -> p (b v)"), start=False, stop=True)
        p.tt(Hs[:, tl, :, :], Hs[:, tl, :, :], Dd[:, tl, :][:, :, None].bc([128, 16, 64]), OP.mult)
        for g, ps in enumerate((PH, PO)):
            p.tt(Hs[:, tl, 8 * g:8 * g + 8, :], Hs[:, tl, 8 * g:8 * g + 8, :], ps[:, :].re("p (b v) -> p b v", v=64), OP.add)
        for hp in range(2):
            h = 2 * tl + hp
            r = rows(hp)
            hc = slice(64 * h, 64 * h + 64)
            for g, ps in enumerate((PU, PB)):
                p.mm(ps[0:16, :], qT[r, tl, 0:16], Hs[r, tl, 8 * g:8 * g + 8, :].re("p b v -> p (b v)"))
                p.tt(tmpd[:, 8 * g:8 * g + 8, :], ps[0:16, :].re("p (b v) -> p b v", v=64),
                     dmask[:, 8 * g:8 * g + 8, :], OP.mult)
            p.reduce(oTok[:, hc], tmpd.ap().re("p b v -> p v b"), OP.add)
    for tl in range(2):
        p.tr(PT[:, tl * 16:tl * 16 + 16], oTok[:, tl * 128:(tl + 1) * 128], ident[0:16, 0:16])
    p.cp(oT[:, :, 0:16], PT[:, 0:32].re("p (a b) -> p a b", b=16))
    if transposed_state:
        for tl in range(2):
            for g in range(2):
                for j in range(8):
                    for hp in range(2):
                        p.tr(ptv[rows(hp), j, :], Hs[rows(hp), tl, 8 * g + j, :], ident[rows(hp), rows(hp)])
                p.cp(Snat[:, tl, 8 * g:8 * g + 8, :], ptv)
    for tl in range(2):
        for hp in range(2):
            h = 2 * tl + hp
            if not transposed_state:
                p.dma(state_out[l, :, h].re("b d v -> d b v"), Hs[rows(hp), tl, :, :])
            else:
                p.dma(state_out[l, :, h].re("b v d -> v b d"), Snat[rows(hp), tl, :, :])


TWO_PI = 2.0 * math.pi


def sincos(p, dst_s, dst_c, ang, fr, ii):
    for dst, sh in ((dst_s, 0.0), (dst_c, 0.25)):
        p.ts(dst, ang, sh, OP.add)
        p.cp(ii, dst)
        p.cp(fr, ii)
        p.tt(fr, dst, fr, OP.subtract)
        p.ts(fr, fr, 0.4999995, OP.min, -0.4999995, OP.max)
        p.act(dst, fr, AF.Sin, scale=TWO_PI)


def s5_setup(S, p, nc, din, l, ident, tidx, PT, PB, rows):
    NS = True

    def ld(name, src):
        t = p.sbc(f"s5{name}", [128, 8])
        for gp in range(2):
            p.dma(t[rows(gp), :], src.re("(pr gp) q -> gp q pr", gp=2)[gp], allow_slow_non_contiguous=NS)
        return t
    lre = ld("lre", din["s5_lam_re"][l])
    lim = ld("lim", din["s5_lam_im"][l])
    stp = p.sbc(f"s5stp", [128, 8])
    for gp in range(2):
        p.dma(stp[rows(gp), :], din["s5_log_step"][l].re("(pr gp) -> gp pr", gp=2)[gp][None, :].bc([64, 8]),
              allow_slow_non_contiguous=NS)
    p.act(stp.ap(), stp.ap(), AF.Exp)
    names = ["lr", "li", "mag", "cs", "sn", "abre", "abim", "nabim", "den", "am1", "zre", "zim", "t", "fr", "ang"]
    c = {n: p.sbc(f"s5{n}", [128, 8]) for n in names}
    ii = p.sbc(f"s5ii", [128, 8], I32)
    p.tt(c["lr"].ap(), lre.ap(), stp.ap(), OP.mult)
    p.tt(c["li"].ap(), lim.ap(), stp.ap(), OP.mult)
    p.act(c["mag"].ap(), c["lr"].ap(), AF.Exp)
    p.ts(c["ang"].ap(), c["li"].ap(), 1.0 / TWO_PI, OP.mult)
    sincos(p, c["sn"].ap(), c["cs"].ap(), c["ang"].ap(), c["fr"].ap(), ii.ap())
    p.tt(c["abre"].ap(), c["mag"].ap(), c["cs"].ap(), OP.mult)
    p.tt(c["abim"].ap(), c["mag"].ap(), c["sn"].ap(), OP.mult)
    p.ts(c["nabim"].ap(), c["abim"].ap(), -1.0, OP.mult)
    p.tt(c["den"].ap(), lre.ap(), lre.ap(), OP.mult)
    p.tt(c["t"].ap(), lim.ap(), lim.ap(), OP.mult)
    p.tt(c["den"].ap(), c["den"].ap(), c["t"].ap(), OP.add)
    p.recip(c["den"].ap(), c["den"].ap())
    p.ts(c["am1"].ap(), c["abre"].ap(), -1.0, OP.add)
    p.tt(c["zre"].ap(), c["am1"].ap(), lre.ap(), OP.mult)
    p.tt(c["t"].ap(), c["abim"].ap(), lim.ap(), OP.mult)
    p.tt(c["zre"].ap(), c["zre"].ap(), c["t"].ap(), OP.add)
    p.tt(c["zre"].ap(), c["zre"].ap(), c["den"].ap(), OP.mult)
    p.tt(c["zim"].ap(), c["abim"].ap(), lre.ap(), OP.mult)
    p.tt(c["t"].ap(), c["am1"].ap(), lim.ap(), OP.mult)
    p.tt(c["zim"].ap(), c["zim"].ap(), c["t"].ap(), OP.subtract)
    p.tt(c["zim"].ap(), c["zim"].ap(), c["den"].ap(), OP.mult)
    yield
    bre = p.sbc(f"s5bre", [128, 8, 16])
    bim = p.sbc(f"s5bim", [128, 8, 16])
    for gp in range(2):
        p.dma(bre[rows(gp)], din["s5_b_re"][l].re("(pr gp) q c -> gp q pr c", gp=2)[gp], allow_slow_non_contiguous=NS)
        p.dma(bim[rows(gp)], din["s5_b_im"][l].re("(pr gp) q c -> gp q pr c", gp=2)[gp], allow_slow_non_contiguous=NS)
    BD = [p.sbc(f"s5BD{r}", [128, 8, 64]) for r in range(2)]
    tmp = p.sbc(f"s5tmp", [128, 8, 16])
    tmp2 = p.sbc(f"s5tmp2", [128, 8, 16])
    zre_b = c["zre"].ap()[:, :, None].bc([128, 8, 16])
    zim_b = c["zim"].ap()[:, :, None].bc([128, 8, 16])
    for r in range(2):
        p.memset(BD[r].ap(), 0.0)

    def scatter(r):
        for par in range(2):
            for q in range(4):
                pr = 2 * q + par
                p.cp(BD[r][0:64, pr, par * 32:par * 32 + 16], tmp[0:64, pr, :])
                p.cp(BD[r][64:128, pr, par * 32 + 16:par * 32 + 32], tmp[64:128, pr, :])
    p.tt(tmp.ap(), bre.ap(), zre_b, OP.mult)
    p.tt(tmp2.ap(), bim.ap(), zim_b, OP.mult)
    p.tt(tmp.ap(), tmp.ap(), tmp2.ap(), OP.subtract)
    scatter(0)
    p.tt(tmp.ap(), bim.ap(), zre_b, OP.mult)
    p.tt(tmp2.ap(), bre.ap(), zim_b, OP.mult)
    p.tt(tmp.ap(), tmp.ap(), tmp2.ap(), OP.add)
    scatter(1)
    BT = p.sbc(f"s5BT", [128, 8, 128])
    ptv = PT[:, 0:512].re("p (a b) -> p a b", b=128)
    for r in range(2):
        for pr in range(8):
            hf = (pr % 4) // 2
            p.tr(ptv[64 * hf:64 * hf + 64, (pr // 4) * 2 + pr % 2, :], BD[r][:, pr, :], ident.ap())
        p.cp(BT[:, r * 4:r * 4 + 4, :], ptv)
    yield
    par = p.sbc(f"s5par", [128, 2])
    pii = p.sbc(f"s5pii", [128, 1], I32)
    p.iota(pii.ap(), [[0, 1]], base=0, cm=1)
    p.op("dve", lambda e: e.tensor_scalar(pii.h[:], pii.h[:], 4, 1, OP.arith_shift_right, op1=OP.bitwise_and),
         [pii.ap()], [pii.ap()])
    p.cp(par[:, 1:2], pii.ap())
    p.ts(par[:, 0:1], par[:, 1:2], -1.0, OP.mult, 1.0, OP.add)
    Cn = p.sbc(f"s5Cn", [128, 2, 64])
    Cexp = p.sbc(f"s5Cexp", [128, 2, 128])
    pbv = PB[:, 0:512].re("p (a b) -> p a b", b=128)
    for r, nm in enumerate(("s5_c_re", "s5_c_im")):
        p.dma(Cn.ap(), din[nm][l].re("(s g) c q -> (g c) s q", s=2))
        sgn = 1.0 if r == 0 else -1.0
        p.ts(Cexp[:, :, 0:64], Cn.ap(), par[:, 0:1], OP.mult, sgn, OP.mult)
        p.ts(Cexp[:, :, 64:128], Cn.ap(), par[:, 1:2], OP.mult, sgn, OP.mult)
        for s in range(2):
            p.tr(pbv[:, r * 2 + s, :], Cexp[:, s, :], ident.ap())
    CTe = p.sbc("s5CTe", [128, 4, 128])
    CTo = p.sbc("s5CTo", [128, 4, 128])
    p.memset(CTe.ap(), 0.0)
    p.memset(CTo.ap(), 0.0)
    vw = "p s (q pp c) -> p s q pp c"
    p.cp(CTe.ap().re(vw, pp=2, c=32)[:, :, :, 0, :], pbv.re(vw, pp=2, c=32)[:, :, :, 0, :])
    p.cp(CTo.ap().re(vw, pp=2, c=32)[:, :, :, 1, :], pbv.re(vw, pp=2, c=32)[:, :, :, 1, :])
    yield
    cosT = p.sbc(f"s5cosT", [128, 8, ST])
    sinT = p.sbc(f"s5sinT", [128, 8, ST])
    frT = p.sbc(f"s5frT", [128, ST])
    angT = p.sbc(f"s5angT", [128, ST])
    iiT = p.sbc(f"s5iiT", [128, ST], I32)
    for pr in range(8):
        p.ts(angT.ap(), tidx.ap(), c["ang"][:, pr:pr + 1], OP.mult)
        sincos(p, sinT[:, pr, :], cosT[:, pr, :], angT.ap(), frT.ap(), iiT.ap())
        yield
    S.update(c)
    BTb = p.sbc("s5BTb", [128, 8, 128], BF16)
    CTeb = BD[0][:, 0:4, :].bitcast(BF16)
    CTob = BD[1][:, 0:4, :].bitcast(BF16)
    p.cp(BTb.ap(), BT.ap())
    p.cp(CTeb.ap(), CTe.ap())
    p.cp(CTob.ap(), CTo.ap())
    S.update(BT=BT, CTe=CTe, CTo=CTo, cosT=cosT, sinT=sinT, BTb=BTb, CTeb=CTeb, CTob=CTob)
    S["hre"] = p.sbc(f"s5hre", [128, 8])
    S["him"] = p.sbc(f"s5him", [128, 8])
    p.memset(S["hre"].ap(), 0.0)
    p.memset(S["him"].ap(), 0.0)
    yield


def s5_block(kind, st, T, W, PS, LW, *, p, proj, OFF, PJ, cnt, ident, identP, mI, mS, nmI, mI4, mS4, dmask, rows, v3,
             headsum, rstd_inplace, din, dout, l, NST, H, Hs, mixTs):
    PA_, PB, PU, PO_, PH_, PT = PS
    S, s5d, s5bg, wglu = LW["S5"], LW["s5d"], LW["s5bg"], LW["wglu"]
    uT, gate, yT = W["qT"], W["gate"], W["oT"]
    PH = W["PH"] if kind == "p" else PO_
    for tl in range(2):
        proj(uT[:, tl, 0:T], OFF["s5_u"] + 128 * tl, 128, T)
        yield
        proj(gate[:, tl, 0:T], OFF["s5_gate"] + 128 * tl, 128, T)
        yield
    BT, cosT, sinT = S["BT"], S["cosT"], S["sinT"]
    xre, xim, gre, gim, hre_t, him_t, ta, tb = (W[n][:, 0, 0:T] for n in ("t0", "t1", "t2", "t3", "kT", "vT", "aT", "bT"))
    g0 = W["g0"]
    if kind == "s":
        hS = W["hS"]
        nat = W["s5nat"]
        for r, nm in enumerate(("state_s5_re", "state_s5_im")):
            p.dma(nat[:, r, :], din[nm][l].re("b g q -> b (g q)"))
            for pr in range(8):
                p.tr(PT[:, pr * 16:pr * 16 + 16], nat[:, r, pr * 128:(pr + 1) * 128], ident[0:16, 0:16])
            p.cp(hS[:, r, :, :], PT[:, 0:128].re("p (a b) -> p a b", b=16))
            yield
    if kind == "p":
        g0a = W["g0a"]
        cs8, sn8, hr8, hi8 = S["cs"].ap(), S["sn"].ap(), S["hre"].ap(), S["him"].ap()
        p.tt(g0a[:, 0, :], cs8, hr8, OP.mult)
        p.tt(g0a[:, 1, :], sn8, hi8, OP.mult)
        p.tt(g0a[:, 2, :], g0a[:, 0, :], g0a[:, 1, :], OP.subtract)
        yield
        p.tt(g0a[:, 0, :], cs8, hi8, OP.mult)
        p.tt(g0a[:, 1, :], sn8, hr8, OP.mult)
        p.tt(g0a[:, 3, :], g0a[:, 0, :], g0a[:, 1, :], OP.add)
        yield
        Xre, Xim, Gre, Gim, Hre, Him, Ta, Tb = (W[n][:, :, 0:T] for n in ("t0", "t1", "t2", "t3", "kT", "vT", "aT", "bT"))
        ub, hbr, hbi = W["ub"], W["hbr"], W["hbi"]
        BTb = S["BTb"]
        p.cp(ub[:, :, 0:T], uT[:, :, 0:T], eng="act")
        yield
        PUv = PU[:, 0:256].re("p (a b) -> p a b", b=128)[:, :, 0:T]
        PBv = PB[:, 0:256].re("p (a b) -> p a b", b=128)[:, :, 0:T]
        for gi in range(4):
            prs = (2 * gi, 2 * gi + 1)
            for j, pr in enumerate(prs):
                q4, sl = pr % 4, pr // 4
                hf = q4 // 2
                rr = slice(64 * hf, 64 * hf + 64)
                bslot = sl * 2 + pr % 2
                p.mm(PUv[:, j, :], BTb[rr, 0 + bslot, :], ub[rr, sl, 0:T])
                p.mm(PBv[:, j, :], BTb[rr, 4 + bslot, :], ub[rr, sl, 0:T])
            cs, sn = cosT[:, 2 * gi:2 * gi + 2, 0:T], sinT[:, 2 * gi:2 * gi + 2, 0:T]
            p.tt(Ta, PUv, cs, OP.mult)
            yield
            p.tt(Tb, PBv, sn, OP.mult)
            yield
            p.tt(Xre, Ta, Tb, OP.add, eng="pool")
            yield
            p.tt(Ta, PBv, cs, OP.mult)
            yield
            p.tt(Tb, PUv, sn, OP.mult)
            yield
            p.tt(Xim, Ta, Tb, OP.subtract, eng="pool")
            yield
            for j, pr in enumerate(prs):
                magb = S["mag"][:, pr:pr + 1].bc([128, T])
                p.scan(Gre[:, j, :], magb, Xre[:, j, :], g0a[:, 2, pr:pr + 1], OP.mult, OP.add)
                yield
                p.scan(Gim[:, j, :], magb, Xim[:, j, :], g0a[:, 3, pr:pr + 1], OP.mult, OP.add)
                yield
            p.tt(Ta, Gre, cs, OP.mult)
            yield
            p.tt(Tb, Gim, sn, OP.mult, eng="pool")
            yield
            p.tt(Hre, Ta, Tb, OP.subtract)
            yield
            p.tt(Ta, Gim, cs, OP.mult, eng="pool")
            yield
            p.tt(Tb, Gre, sn, OP.mult)
            yield
            p.tt(Him, Ta, Tb, OP.add)
            yield
            p.cp(S["hre"][:, 2 * gi:2 * gi + 2], Hre[:, :, T - 1])
            p.cp(S["him"][:, 2 * gi:2 * gi + 2], Him[:, :, T - 1])
            p.cp(hbr[:, :, 0:T], Hre, eng="act")
            p.cp(hbi[:, :, 0:T], Him, eng="act")
            yield
            for j, pr in enumerate(prs):
                q4, sl = pr % 4, pr // 4
                hf = q4 // 2
                rr = slice(64 * hf, 64 * hf + 64)
                CTx = S["CTeb"] if pr % 2 == 0 else S["CTob"]
                p.mm(PH[rr, sl * ST:sl * ST + T], CTx[:, 0 + sl, rr], hbr[:, j, 0:T], start=(pr % 2 == 0), stop=False)
                p.mm(PH[rr, sl * ST:sl * ST + T], CTx[:, 2 + sl, rr], hbi[:, j, 0:T], start=False, stop=(pr % 2 == 1))
    for pr in (range(8) if kind == "s" else ()):
        q4, sl = pr % 4, pr // 4
        hf = q4 // 2
        rr = slice(64 * hf, 64 * hf + 64)
        bslot = sl * 2 + pr % 2
        p.mm(PU[:, 0:T], BT[rr, 0 + bslot, :], uT[rr, sl, 0:T])
        p.mm(PB[:, 0:T], BT[rr, 4 + bslot, :], uT[rr, sl, 0:T])
        if kind == "p":
            cs, sn = cosT[:, pr, 0:T], sinT[:, pr, 0:T]
            p.tt(ta, PU[:, 0:T], cs, OP.mult)
            yield
            p.tt(tb, PB[:, 0:T], sn, OP.mult, eng="dve")
            yield
            p.tt(xre, ta, tb, OP.add, eng="pool")
            yield
            p.tt(ta, PB[:, 0:T], cs, OP.mult)
            yield
            p.tt(tb, PU[:, 0:T], sn, OP.mult)
            yield
            p.tt(xim, ta, tb, OP.subtract, eng="pool")
            yield
            c1, s1 = S["cs"][:, pr:pr + 1], S["sn"][:, pr:pr + 1]
            hr, hi = S["hre"][:, pr:pr + 1], S["him"][:, pr:pr + 1]
            p.tt(g0[:, 0:1], c1, hr, OP.mult)
            yield
            p.tt(g0[:, 1:2], s1, hi, OP.mult)
            yield
            p.tt(g0[:, 2:3], g0[:, 0:1], g0[:, 1:2], OP.subtract)
            yield
            p.tt(g0[:, 0:1], c1, hi, OP.mult)
            yield
            p.tt(g0[:, 1:2], s1, hr, OP.mult)
            yield
            p.tt(g0[:, 3:4], g0[:, 0:1], g0[:, 1:2], OP.add)
            yield
            magb = S["mag"][:, pr:pr + 1].bc([128, T])
            p.scan(gre, magb, xre, g0[:, 2:3], OP.mult, OP.add)
            yield
            p.scan(gim, magb, xim, g0[:, 3:4], OP.mult, OP.add)
            yield
            p.tt(ta, gre, cs, OP.mult)
            yield
            p.tt(tb, gim, sn, OP.mult, eng="pool")
            yield
            p.tt(hre_t, ta, tb, OP.subtract)
            yield
            p.tt(ta, gim, cs, OP.mult, eng="pool")
            yield
            p.tt(tb, gre, sn, OP.mult)
            yield
            p.tt(him_t, ta, tb, OP.add)
            yield
            p.cp(S["hre"][:, pr:pr + 1], hre_t[:, T - 1:T])
            yield
            p.cp(S["him"][:, pr:pr + 1], him_t[:, T - 1:T])
            yield
        else:
            hS = W["hS"]
            hr, hi = hS[:, 0, pr, :], hS[:, 1, pr, :]
            p.ts(ta, hr, S["abre"][:, pr:pr + 1], OP.mult)
            yield
            p.stt(ta, hi, S["nabim"][:, pr:pr + 1], ta, OP.mult, OP.add)
            yield
            p.tt(hre_t, ta, PU[:, 0:T], OP.add)
            yield
            p.ts(tb, hr, S["abim"][:, pr:pr + 1], OP.mult)
            yield
            p.stt(tb, hi, S["abre"][:, pr:pr + 1], tb, OP.mult, OP.add)
            yield
            p.tt(him_t, tb, PB[:, 0:T], OP.add)
            yield
            p.cp(hr, hre_t)
            yield
            p.cp(hi, him_t)
            yield
        CTx = S["CTe"] if pr % 2 == 0 else S["CTo"]
        p.mm(PH[rr, sl * ST:sl * ST + T], CTx[:, 0 + sl, rr], hre_t, start=(pr % 2 == 0), stop=False)
        p.mm(PH[rr, sl * ST:sl * ST + T], CTx[:, 2 + sl, rr], him_t, start=False, stop=(pr % 2 == 1))
    for tl in range(2):
        p.stt(yT[:, tl, 0:T], uT[:, tl, 0:T], s5d[:, tl:tl + 1], PH[:, tl * ST:tl * ST + T], OP.mult, OP.add)
        yield
    if kind == "p" and st == NST - 1:
        for nm, src in (("s5re_p", S["hre"]), ("s5im_p", S["him"])):
            for gp in range(2):
                p.dma(dout[nm][l].re("(pr gp) q -> gp q pr", gp=2)[gp], src[rows(gp), :], allow_slow_non_contiguous=True)
    if kind == "s":
        hS, nat = W["hS"], W["s5nat"]
        for r, nm in enumerate(("s5re_s", "s5im_s")):
            for pr in range(8):
                p.tr(PT[0:16, pr * 128:(pr + 1) * 128] if pr < 4 else PB[0:16, (pr - 4) * 128:(pr - 3) * 128],
                     hS[:, r, pr, :], ident.ap())
            p.cp(nat[:, r, 0:512], PT[0:16, 0:512])
            yield
            p.cp(nat[:, r, 512:1024], PB[0:16, 0:512])
            yield
            p.dma(dout[nm][l].re("b g q -> b (g q)"), nat[:, r, :])
    a, b, gsb = W["t0"][:, :, 0:T], W["t1"][:, :, 0:T], W["t2"][:, :, 0:T]
    y = yT[:, :, 0:T]
    p.tt(a, y, y, OP.mult)
    yield
    p.ts(a, a, 0.044715, OP.mult, 1.0, OP.add)
    yield
    p.tt(a, a, y, OP.mult)
    yield
    p.act(a, a, AF.Tanh, scale=math.sqrt(2.0 / math.pi))
    yield
    p.ts(a, a, 1.0, OP.add, 0.5, OP.mult)
    yield
    p.tt(b, a, y, OP.mult)
    yield
    for tl in range(2):
        pj = PJ[cnt[0] % 2]
        cnt[0] += 1
        for k in range(2):
            p.mm(pj[:, 0:T], wglu[:, k, tl * 128:(tl + 1) * 128], W["t1"][:, k, 0:T], start=(k == 0), stop=(k == 1))
        p.act(W["t2"][:, tl, 0:T], pj[:, 0:T], AF.Sigmoid, bias=s5bg[:, tl:tl + 1])
        yield
    p.tt(gsb, gsb, b, OP.mult)
    yield
    p.act(a, gate[:, :, 0:T], AF.Silu)
    yield
    p.tt(mixTs[1][:, :, 0:T], gsb, a, OP.mult)
    yield


def gdn_block(kind, st, T, W, K, PS, LW, *, p, proj, OFF, PJ, cnt, ident, identP, mI, mS, nmI, mI4, mS4, dmask, rows, v3,
              headsum, rstd_inplace, din, dout, l, NST, H, Hs, mixTs):
    PA, PB, PU, PO, PH, PT = PS
    convw, gab, Eg, Eb, gdn_ng, hist = (LW[n] for n in ("convw", "gab", "Eg", "Eb", "gdn_ng", "hist_gdn"))
    xp = W["xp"]
    cv = W["cv"]
    gate = W["gate"]
    if kind == "p":
        p.cp(xp[:, :, 0:3], hist.ap())
        yield
        for j in range(6):
            proj(xp[:, j, 3:3 + T], OFF["gdn_qkv"] + 128 * j, 128, T)
            yield
        p.cp(hist.ap(), xp[:, :, T:T + 3])
        yield
        taps = [xp[:, :, i:i + T] for i in range(4)]
        if st == NST - 1:
            for r in range(3):
                p.dma(dout["conv_p"][l, r].re("(j q) -> q j", q=128), xp[:, :, T + r], allow_slow_non_contiguous=True)
    else:
        xsS = W["xsS"]
        nat = W["cnat"]
        p.dma(nat.ap(), din["state_gdn_conv"][l])
        for r in range(3):
            for j in range(6):
                p.tr(PT[:, j * 16:j * 16 + 16], nat[:, r, j * 128:(j + 1) * 128], ident[0:16, 0:16])
            p.cp(xsS[:, :, r, :], PT[:, 0:96].re("p (a b) -> p a b", b=16))
            yield
        for j in range(6):
            proj(xsS[:, j, 3, :], OFF["gdn_qkv"] + 128 * j, 128, T)
            yield
        taps = [xsS[:, :, i, :] for i in range(4)]
        p.dma(dout["conv_s"][l, :, 0:2, :], din["state_gdn_conv"][l, :, 1:3, :])
        for j in range(6):
            p.tr(PB[0:16, j * 128:(j + 1) * 128] if j < 4 else PU[0:16, (j - 4) * 128:(j - 3) * 128], xsS[:, j, 3, :],
                 ident.ap())
        p.cp(nat[:, 0, 0:512], PB[0:16, 0:512])
        yield
        p.cp(nat[:, 0, 512:768], PU[0:16, 0:256])
        yield
        p.dma(dout["conv_s"][l, :, 2, :], nat[:, 0, :])
    c = cv[:, :, 0:T]
    for j in range(6):
        cj = cv[:, j, 0:T]
        p.ts(cj, taps[0][:, j, :], convw[:, j, 0:1], OP.mult)
        yield
        for i in range(1, 4):
            p.stt(cj, taps[i][:, j, :], convw[:, j, i:i + 1], cj, OP.mult, OP.add)
            yield
    p.act(c, c, AF.Silu)
    yield
    qT, kT, vT, aT, bT, ldT, oT = (W[n] for n in ("qT", "kT", "vT", "aT", "bT", "ldT", "oT"))
    t0, t1 = W["t0"], W["t1"]
    for src, dst, sc in ((cv[:, 0:2, 0:T], qT, 0.125), (cv[:, 2:4, 0:T], aT, 1.0)):
        p.act(t0[:, :, 0:T], src, AF.Square)
        yield
        headsum(t1, t0, T)
        yield
        rstd_inplace(t1[:, :, 0:T], T, 1.0, 1e-6)
        yield
        p.stt(dst[:, :, 0:T], src, sc, t1[:, :, 0:T], OP.mult, OP.mult)
        yield
    p.cp(vT[:, :, 0:T], cv[:, 4:6, 0:T])
    yield
    abT, gf, bf = W["abT"], W["gf"], W["bf"]
    proj(abT[0:8, 0:T], OFF["gdn_ab"], 8, T)
    yield
    p.act(gf[0:8, 0:T], abT[0:8, 0:T], AF.Exp, bias=gab[0:8, 0:1])
    yield
    p.act(gf[0:8, 0:T], gf[0:8, 0:T], AF.Ln, bias=1.0)
    yield
    p.ts(gf[0:8, 0:T], gf[0:8, 0:T], gab[0:8, 1:2], OP.mult)
    yield
    p.act(bf[0:8, 0:T], abT[0:8, 0:T], AF.Sigmoid)
    yield
    for tl in range(2):
        pj = PJ[cnt[0] % 2]
        cnt[0] += 1
        p.mm(pj[:, 0:T], Eg[0:8, tl, :], gf[0:8, 0:T])
        p.cp(ldT[:, tl, 0:T], pj[:, 0:T], eng="act")
        yield
        pj = PJ[cnt[0] % 2]
        cnt[0] += 1
        p.mm(pj[:, 0:T], Eb[0:8, tl, :], bf[0:8, 0:T])
        p.tt(kT[:, tl, 0:T], aT[:, tl, 0:T], pj[:, 0:T], OP.mult)
        yield
    p.act(t0[:, :, 0:T], ldT[:, :, 0:T], AF.Exp)
    yield
    p.tt(bT[:, :, 0:T], kT[:, :, 0:T], t0[:, :, 0:T], OP.mult)
    yield
    for tl in range(2):
        proj(gate[:, tl, 0:T], OFF["gdn_gate"] + 128 * tl, 128, T)
        yield
    yield from mixer_core(p, "gdn", kind, T, W, K, H["gdn"], Hs["gdn"], dict(ab=True, scalar=True), PS, ident, identP, mI, mS, nmI,
                          mI4, mS4, dmask, rows, v3, din["state_gdn"], dout["gdn_s"], l, transposed_state=False)
    yield from out_norm_rms(p, oT, gate, gdn_ng, T, W, headsum, rstd_inplace, mixTs[2])
    if kind == "p" and st == NST - 1:
        p.dma(dout["gdn_p"][l].re("(t hp) d v -> (hp d) t v", hp=2), H["gdn"].ap())


def rwkv_block(kind, st, T, W, K, PS, LW, *, p, proj, OFF, PJ, cnt, ident, identP, mI, mS, nmI, mI4, mS4, dmask, rows, v3,
               headsum, rstd_inplace, din, dout, l, NST, H, Hs, mixTs):
    PA, PB, PU, PO, PH, PT = PS
    mu, w0, a0, k_k, k_a, r_k, ln_g, ln_b, wlo, hist = (LW[n] for n in (
        "mu", "w0", "a0", "k_k", "k_a", "r_k", "ln_g", "ln_b", "wlo", "hist_rwkv"))
    rin = W["rin"]
    xs = W["xs7"]
    gate = W["gate"]
    if kind == "p":
        p.cp(rin[:, :, 0:1], hist.ap())
        yield
        for j in range(7):
            proj(rin[:, j, 1:1 + T], OFF["rwkv_in"] + 128 * j, 128, T)
            yield
        p.cp(hist.ap(), rin[:, :, T:T + 1])
        yield
        prev, cur = rin[:, :, 0:T], rin[:, :, 1:1 + T]
        if st == NST - 1:
            p.dma(dout["shift_p"][l].re("(j q o) -> q j o", q=128, o=1), rin[:, :, T:T + 1], allow_slow_non_contiguous=True)
    else:
        nat = W["snat"]
        prevS = W["prevS"]
        p.dma(nat.ap(), din["state_rwkv_shift"][l])
        for j in range(7):
            p.tr(PT[:, j * 16:j * 16 + 16], nat[:, j * 128:(j + 1) * 128], ident[0:16, 0:16])
        p.cp(prevS.ap(), PT[:, 0:112].re("p (a b) -> p a b", b=16))
        yield
        for j in range(7):
            proj(rin[:, j, 0:T], OFF["rwkv_in"] + 128 * j, 128, T)
            yield
        prev, cur = prevS.ap(), rin[:, :, 0:T]
        for j in range(7):
            p.tr(PB[0:16, j * 128:(j + 1) * 128] if j < 4 else PU[0:16, (j - 4) * 128:(j - 3) * 128], rin[:, j, 0:T],
                 ident.ap())
        p.cp(nat[:, 0:512], PB[0:16, 0:512])
        yield
        p.cp(nat[:, 512:896], PU[0:16, 0:384])
        yield
        p.dma(dout["shift_s"][l], nat.ap())
    x = xs[:, :, 0:T]
    p.tt(x, prev, cur, OP.subtract)
    yield
    p.tt(x, x, mu.ap()[:, :, None].bc([128, 7, T]), OP.mult)
    yield
    p.tt(x, x, cur, OP.add)
    yield
    qT, kT, vT, aT, bT, ldT, oT = (W[n] for n in ("qT", "kT", "vT", "aT", "bT", "ldT", "oT"))
    t0, t1, t2 = W["t0"], W["t1"], W["t2"]
    p.cp(qT[:, :, 0:T], xs[:, 0:2, 0:T])
    yield
    p.cp(vT[:, :, 0:T], xs[:, 4:6, 0:T])
    yield
    rk = xs[:, 2:4, 0:T]
    p.act(t0[0:64, 0, 0:T], xs[0:64, 6, 0:T], AF.Tanh)
    yield
    asg = t2
    for tl in range(2):
        pj = PJ[cnt[0] % 2]
        cnt[0] += 1
        p.mm(pj[:, 0:T], wlo[0:64, tl * 128:(tl + 1) * 128], t0[0:64, 0, 0:T])
        p.act(ldT[:, tl, 0:T], pj[:, 0:T], AF.Sigmoid, bias=w0[:, tl:tl + 1])
        yield
        pj = PJ[cnt[0] % 2]
        cnt[0] += 1
        p.mm(pj[:, 0:T], wlo[64:128, tl * 128:(tl + 1) * 128], xs[64:128, 6, 0:T])
        p.act(asg[:, tl, 0:T], pj[:, 0:T], AF.Sigmoid, bias=a0[:, tl:tl + 1])
        yield
    p.ts(ldT[:, :, 0:T], ldT[:, :, 0:T], -math.exp(-0.5), OP.mult)
    yield
    p.tt(aT[:, :, 0:T], rk, k_k.ap()[:, :, None].bc([128, 2, T]), OP.mult)
    yield
    p.act(t0[:, :, 0:T], aT[:, :, 0:T], AF.Square)
    yield
    headsum(t1, t0, T)
    yield
    rstd_inplace(t1[:, :, 0:T], T, 1.0, 1e-6)
    yield
    p.tt(aT[:, :, 0:T], aT[:, :, 0:T], t1[:, :, 0:T], OP.mult)
    yield
    p.tt(bT[:, :, 0:T], aT[:, :, 0:T], asg[:, :, 0:T], OP.mult)
    yield
    p.ts(t0[:, :, 0:T], asg[:, :, 0:T], -1.0, OP.add)
    yield
    p.tt(t0[:, :, 0:T], t0[:, :, 0:T], k_a.ap()[:, :, None].bc([128, 2, T]), OP.mult)
    yield
    p.ts(t0[:, :, 0:T], t0[:, :, 0:T], 1.0, OP.add)
    yield
    p.tt(kT[:, :, 0:T], rk, t0[:, :, 0:T], OP.mult)
    yield
    bon = W["bon"]
    p.tt(t0[:, :, 0:T], qT[:, :, 0:T], kT[:, :, 0:T], OP.mult)
    yield
    p.tt(t0[:, :, 0:T], t0[:, :, 0:T], r_k.ap()[:, :, None].bc([128, 2, T]), OP.mult)
    yield
    headsum(t1, t0, T)
    yield
    p.tt(bon[:, :, 0:T], t1[:, :, 0:T], vT[:, :, 0:T], OP.mult)
    yield
    for tl in range(2):
        proj(gate[:, tl, 0:T], OFF["rwkv_gate"] + 128 * tl, 128, T)
        yield
    yield from mixer_core(p, "rwkv", kind, T, W, K, H["rwkv"], Hs["rwkv"], dict(ab=True, scalar=False), PS, ident, identP, mI, mS,
                          nmI, mI4, mS4, dmask, rows, v3, din["state_rwkv"], dout["rwkv_s"], l, transposed_state=True)
    o = oT[:, :, 0:T]
    headsum(t1, oT, T)
    yield
    p.stt(t0[:, :, 0:T], t1[:, :, 0:T], -1.0 / 64, o, OP.mult, OP.add)
    yield
    p.act(t1[:, :, 0:T], t0[:, :, 0:T], AF.Square)
    yield
    headsum(t2, t1, T)
    yield
    rstd_inplace(t2[:, :, 0:T], T, 1.0 / 64, 64e-5)
    yield
    p.tt(t0[:, :, 0:T], t0[:, :, 0:T], t2[:, :, 0:T], OP.mult)
    yield
    p.tt(t0[:, :, 0:T], t0[:, :, 0:T], ln_g.ap()[:, :, None].bc([128, 2, T]), OP.mult)
    yield
    p.tt(t0[:, :, 0:T], t0[:, :, 0:T], ln_b.ap()[:, :, None].bc([128, 2, T]), OP.add)
    yield
    p.tt(t0[:, :, 0:T], t0[:, :, 0:T], bon[:, :, 0:T], OP.add)
    yield
    p.act(t1[:, :, 0:T], gate[:, :, 0:T], AF.Silu)
    yield
    p.tt(mixTs[3][:, :, 0:T], t0[:, :, 0:T], t1[:, :, 0:T], OP.mult)
    yield
    if kind == "p" and st == NST - 1:
        ptv = v3(PT, 2)
        for tl in range(2):
            for hp in range(2):
                p.tr(ptv[rows(hp), tl, :], H["rwkv"][rows(hp), tl, :], ident[rows(hp), rows(hp)])
        p.cp(K["Ut"].ap(), ptv)
        yield
        p.dma(dout["rwkv_p"][l].re("(t hp) v d -> (hp v) t d", hp=2), K["Ut"].ap())


from concourse.bass_utils import run_bass_kernel_spmd

_CACHE = {}


def kernel(**inputs):
    if "nc" not in _CACHE:
        _CACHE["nc"] = build(DEPTH=4, NST=2048 // ST, SAMPLE=True)[0]
    nc = _CACHE["nc"]
    in_maps = []
    for c in range(8):
        m = {}
        for k, shp in SHAPES.items():
            a = np.asarray(inputs[k])
            if k == "x_prompt":
                a = a[c]
            elif k == "x_sample":
                a = a[16 * c:16 * c + 16, 0]
            elif k.startswith("state_"):
                a = a[:, 16 * c:16 * c + 16]
            m[k] = np.ascontiguousarray(a, dtype=np.float32)
        in_maps.append(m)
    res = run_bass_kernel_spmd(nc, in_maps, core_ids=list(range(8)))
    rs = res.results
    outs = []
    for k in OUT_ORDER:
        if k == "y_p":
            o = np.stack([r[k] for r in rs], axis=0)
        elif k == "y_s":
            o = np.concatenate([r[k] for r in rs], axis=0)[:, None, :]
        elif k.endswith("_p"):
            o = np.stack([r[k] for r in rs], axis=1)
        else:
            o = np.concatenate([r[k] for r in rs], axis=1)
        outs.append(np.ascontiguousarray(o, dtype=np.float32))
    return tuple(outs)
```

```python
import contextlib
import numpy as np
import concourse.bass as bass
import concourse.mybir as mybir

F32 = mybir.dt.float32
BF16 = mybir.dt.bfloat16
I32 = mybir.dt.int32
AF = mybir.ActivationFunctionType
OP = mybir.AluOpType
AX = mybir.AxisListType

ENGS = ("pe", "act", "dve", "pool", "sp")
SKIP_SAME_ENGINE = False
SERIALIZE_PSUM_READERS = True


class T:
    def __init__(self, name, handle):
        self.name = name
        self.h = handle
        self.is_psum = False
        self.w = None
        self.r = []

    def __getitem__(self, idx):
        return V(self, self.h[idx])

    def ap(self):
        return V(self, self.h[:])


class V:
    def __init__(self, t, ap):
        self.t = t
        self.a = ap

    def __getitem__(self, idx):
        return V(self.t, self.a[idx])

    def re(self, pat, **kw):
        return V(self.t, self.a.rearrange(pat, **kw))

    def bc(self, shape):
        return V(self.t, self.a.broadcast_to(shape))

    def bitcast(self, dt):
        return V(self.t, self.a.bitcast(dt))

    def ap(self):
        return self


class Prog:
    def __init__(self, nc, n_dma_sems=24):
        self.nc = nc
        self.st = contextlib.ExitStack()
        self.q = {e: [] for e in ENGS}
        self.cnt = {e: 0 for e in ENGS}
        self.sem = {e: self.st.enter_context(nc.semaphore("pg_" + e)) for e in ENGS}
        self.known = {e: {} for e in ENGS}
        self.dsem = [self.st.enter_context(nc.semaphore(f"dma{i}")) for i in range(n_dma_sems)]
        self.dcnt = [0] * n_dma_sems
        self.dnext = 0
        self.pool_sems = []
        self.pool_used = 0
        self.tiles = {}
        self.n_inst = 0

    def sb(self, name, shape, dt=F32):
        h = self.st.enter_context(self.nc.sbuf_tensor(name, list(shape), dt))
        t = T(name, h)
        self.tiles[name] = t
        return t

    def sbc(self, name, shape, dt=F32):
        if name in self.tiles:
            return self.tiles[name]
        return self.sb(name, shape, dt)

    def ps(self, name, shape, dt=F32):
        h = self.st.enter_context(self.nc.psum_tensor(name, list(shape), dt))
        t = T(name, h)
        t.is_psum = True
        self.tiles[name] = t
        return t

    def dram(self, name, shape, dt=F32, kind="Internal"):
        h = self.nc.dram_tensor(name, list(shape), dt, kind=kind)
        t = T(name, h.ap())
        self.tiles[name] = t
        return t

    def _need(self, eng, dep):
        if dep is None:
            return
        kind, i, c = dep
        if kind == "e" and i == eng and (SKIP_SAME_ENGINE or eng == "pe"):
            return
        key = (kind, i)
        if self.known[eng].get(key, 0) >= c:
            return
        self.known[eng][key] = c
        sem = self.sem[i] if kind == "e" else self.dsem[i]
        self.q[eng].append(lambda e, sem=sem, c=c: e.wait_ge(sem, c))

    def _deps(self, eng, reads, writes):
        for v in reads:
            if v is None or not isinstance(v, V):
                continue
            self._need(eng, v.t.w)
        for v in writes:
            self._need(eng, v.t.w)
            for r in v.t.r:
                self._need(eng, r)

    def _mark(self, token, reads, writes):
        for v in reads:
            if v is None or not isinstance(v, V):
                continue
            v.t.r.append(token)
            if len(v.t.r) > 64:
                best = {}
                for k, i, c in v.t.r:
                    best[(k, i)] = max(best.get((k, i), 0), c)
                v.t.r = [(k, i, c) for (k, i), c in best.items()]
        for v in writes:
            v.t.w = token
            v.t.r = []

    def op(self, eng, fn, reads, writes):
        if eng != "pe" and SERIALIZE_PSUM_READERS:
            writes = list(writes) + [v for v in reads if isinstance(v, V) and v.t.is_psum]
        self._deps(eng, reads, writes)
        self.cnt[eng] += 1
        c = self.cnt[eng]
        sem = self.sem[eng]
        self.q[eng].append(lambda e, fn=fn, sem=sem: fn(e).then_inc(sem, 1))
        self.known[eng][("e", eng)] = max(self.known[eng].get(("e", eng), 0), 0)
        self._mark(("e", eng, c), reads, writes)
        self.n_inst += 1

    def dma(self, out, in_, eng="sp", **kw):
        self._deps(eng, [in_], [out])
        if eng == "pool" and self.pool_used < 40:
            self.dsem.append(self.st.enter_context(self.nc.semaphore(f"pdma{self.pool_used}")))
            self.dcnt.append(0)
            self.pool_used += 1
            i = len(self.dsem) - 1
        else:
            i = self.dnext
            self.dnext = (self.dnext + 1) % 24
        self.dcnt[i] += 16
        c = self.dcnt[i]
        sem = self.dsem[i]
        oa, ia = out.a, in_.a
        self.q[eng].append(lambda e, oa=oa, ia=ia, sem=sem, kw=kw: e.dma_start(out=oa, in_=ia, **kw).then_inc(sem, 16))
        self._mark(("d", i, c), [in_], [out])
        self.n_inst += 1

    def barrier(self):
        for e in ENGS:
            for o in ENGS:
                if o != e and self.cnt[o]:
                    self._need(e, ("e", o, self.cnt[o]))
            for i, c in enumerate(self.dcnt):
                if c:
                    self._need(e, ("d", i, c))

    def wait_all_dma(self, eng="sp"):
        for i, c in enumerate(self.dcnt):
            if c:
                self._need(eng, ("d", i, c))

    def mm(self, out, lhsT, rhs, start=True, stop=True, **kw):
        reads = [lhsT, rhs] + ([] if start else [out])
        self.op("pe", lambda e: e.matmul(out.a, lhsT.a, rhs.a, start=start, stop=stop, **kw), reads, [out])

    def tr(self, out, in_, ident):
        if out.a.start_partition != 0 or in_.a.dtype != F32:
            return self.mm(out, in_, ident)
        self.op("pe", lambda e: e.transpose(out.a, in_.a, ident.a), [in_, ident], [out])

    def act(self, out, in_, func, bias=None, scale=None, accum=None, eng="act"):
        kw = {}
        reads = [in_]
        if bias is not None:
            kw["bias"] = bias.a if isinstance(bias, V) else bias
            reads.append(bias)
        if scale is not None:
            kw["scale"] = scale.a if isinstance(scale, V) else scale
            reads.append(scale)
        writes = [out]
        if accum is not None:
            kw["accum_out"] = accum.a
            writes.append(accum)
        self.op(eng, lambda e: e.activation(out.a, in_.a, func, **kw), reads, writes)

    def tt(self, out, a, b, op, eng="dve"):
        self.op(eng, lambda e: e.tensor_tensor(out.a, a.a, b.a, op), [a, b], [out])

    def ts(self, out, a, s1, op0, s2=None, op1=None, eng="dve", accum=None):
        reads = [a, s1, s2]
        x1 = s1.a if isinstance(s1, V) else s1
        x2 = s2.a if isinstance(s2, V) else s2
        kw = {}
        if op1 is not None:
            kw["op1"] = op1
        writes = [out]
        if accum is not None:
            kw["accum_out"] = accum.a
            writes.append(accum)
        self.op(eng, lambda e: e.tensor_scalar(out.a, a.a, x1, x2, op0, **kw), reads, writes)

    def stt(self, out, a, s, b, op0, op1, eng="dve"):
        x = s.a if isinstance(s, V) else s
        self.op(eng, lambda e: e.scalar_tensor_tensor(out.a, a.a, x, b.a, op0, op1), [a, s, b], [out])

    def cp(self, out, in_, eng="dve"):
        if eng == "act":
            self.op(eng, lambda e: e.copy(out.a, in_.a), [in_], [out])
        else:
            self.op(eng, lambda e: e.tensor_copy(out.a, in_.a), [in_], [out])

    def memset(self, out, val, eng="dve"):
        self.op(eng, lambda e: e.memset(out.a, val), [], [out])

    def scan(self, out, d0, d1, init, op0, op1):
        x = init.a if isinstance(init, V) else init
        self.op("dve", lambda e: e.tensor_tensor_scan(out.a, d0.a, d1.a, x, op0, op1), [d0, d1, init], [out])

    def reduce(self, out, in_, op, axis=AX.X):
        self.op("dve", lambda e: e.tensor_reduce(out.a, in_.a, axis, op), [in_], [out])

    def recip(self, out, in_):
        self.op("dve", lambda e: e.reciprocal(out.a, in_.a), [in_], [out])

    def iota(self, out, pattern, base=0, cm=0, **kw):
        self.op("pool", lambda e: e.iota(out.a, pattern, base=base, channel_multiplier=cm, **kw), [], [out])

    def affsel(self, out, in_, pattern, cmp, fill, base=0, cm=0):
        self.op("pool", lambda e: e.affine_select(out.a, in_.a, pattern, cmp, fill, base=base, channel_multiplier=cm),
                [in_], [out])

    def finish(self):
        self.wait_all_dma("sp")
        for e in ENGS:
            if e != "sp" and self.cnt[e]:
                self._need("sp", ("e", e, self.cnt[e]))
        nc = self.nc
        q = self.q
        with nc.Block() as block:
            @block.tensor
            def _(e):
                for f in q["pe"]:
                    f(e)

            @block.scalar
            def _(e):
                for f in q["act"]:
                    f(e)

            @block.vector
            def _(e):
                for f in q["dve"]:
                    f(e)

            @block.gpsimd
            def _(e):
                for f in q["pool"]:
                    f(e)

            @block.sync
            def _(e):
                for f in q["sp"]:
                    f(e)
        self.st.close()


import math
ST = 128
C = 64
NCH = ST // C
OFF = dict(gla_q=0, gla_k=256, gla_v=512, glr=768, gla_gate=784, s5_u=1040, s5_gate=1296,
           gdn_qkv=1552, gdn_ab=2320, gdn_gate=2328, rwkv_in=2584, rwkv_gate=3480)
D_IN = 3736
SHAPES = dict(
    x_prompt=[2048, 1024], x_sample=[16, 1024],
    state_gla=[4, 16, 4, 64, 64], state_s5_re=[4, 16, 16, 64], state_s5_im=[4, 16, 16, 64],
    state_gdn=[4, 16, 4, 64, 64], state_gdn_conv=[4, 16, 3, 768], state_rwkv=[4, 16, 4, 64, 64],
    state_rwkv_shift=[4, 16, 896],
    norm_g=[4, 1024], w_in=[4, 1024, 3736], gla_wg2=[4, 16, 256], gla_bg=[4, 256], gla_norm_g=[4, 64],
    s5_lam_re=[4, 16, 64], s5_lam_im=[4, 16, 64], s5_log_step=[4, 16], s5_b_re=[4, 16, 64, 16],
    s5_b_im=[4, 16, 64, 16], s5_c_re=[4, 16, 16, 64], s5_c_im=[4, 16, 16, 64], s5_d=[4, 256],
    s5_w_glu=[4, 256, 256], s5_b_glu=[4, 256], gdn_conv_w=[4, 4, 768], gdn_a_log=[4, 4], gdn_dt_bias=[4, 4],
    gdn_norm_g=[4, 64], rwkv_mu=[4, 896], rwkv_w0=[4, 256], rwkv_ww2=[4, 64, 256], rwkv_a0=[4, 256],
    rwkv_wa2=[4, 64, 256], rwkv_k_k=[4, 256], rwkv_k_a=[4, 256], rwkv_r_k=[4, 4, 64], rwkv_ln_g=[4, 256],
    rwkv_ln_b=[4, 256], w_out=[4, 1024, 1024], final_g=[1024])
OUT_SHAPES = dict(
    y_p=[2048, 1024], y_s=[16, 1024],
    gla_p=[4, 4, 64, 64], s5re_p=[4, 16, 64], s5im_p=[4, 16, 64], gdn_p=[4, 4, 64, 64], conv_p=[4, 3, 768],
    rwkv_p=[4, 4, 64, 64], shift_p=[4, 896],
    gla_s=[4, 16, 4, 64, 64], s5re_s=[4, 16, 16, 64], s5im_s=[4, 16, 16, 64], gdn_s=[4, 16, 4, 64, 64],
    conv_s=[4, 16, 3, 768], rwkv_s=[4, 16, 4, 64, 64], shift_s=[4, 16, 896])
OUT_ORDER = ["y_p", "y_s", "gla_p", "s5re_p", "s5im_p", "gdn_p", "conv_p", "rwkv_p", "shift_p",
             "gla_s", "s5re_s", "s5im_s", "gdn_s", "conv_s", "rwkv_s", "shift_s"]


def sig_(p, out, x, scale=1.0):
    p.act(out, x, AF.Exp, scale=-scale)
    p.act(out, out, AF.Ln, bias=1.0)
    p.act(out, out, AF.Exp, scale=-1.0)


def silu_(p, out, x):
    sig_(p, out, x)
    p.tt(out, out, x, OP.mult)


def build(DEPTH=4, NST=8, SAMPLE=True, MIX=("gla", "s5", "gdn", "rwkv"), STREAMS=2, PSMODE=0):
    nc = bass.Bass("TRN2", target_bir_lowering=False, dynamic_dma_scratch_size=4096)
    p = Prog(nc)
    din = {k: p.dram(k, v, F32, kind="ExternalInput") for k, v in SHAPES.items()}
    dout = {k: p.dram(k, v, F32, kind="ExternalOutput") for k, v in OUT_SHAPES.items()}
    xbuf = p.dram("xbuf", [2048, 1024], F32)
    NS = True

    def rows(hp):
        return slice(64 * hp, 64 * hp + 64)

    ident = p.sb("ident", [128, 128])
    p.memset(ident.ap(), 1.0, eng="pool")
    p.affsel(ident.ap(), ident.ap(), [[-1, 128]], OP.is_equal, 0.0, base=0, cm=1)
    identb = p.sb("identb", [128, 128], BF16)
    p.cp(identb.ap(), ident.ap())
    identP = p.sb("identP", [128, 2, 64])
    for tl in range(2):
        p.cp(identP[0:64, tl, :], ident[0:64, 0:64])
        p.cp(identP[64:128, tl, :], ident[64:128, 64:128])
    mI = p.sb("mI", [128, 64])
    mS = p.sb("mS", [128, 64])
    for hp in range(2):
        p.memset(mI[rows(hp), :], 1.0, eng="pool")
        p.affsel(mI[rows(hp), :], mI[rows(hp), :], [[1, 64]], OP.is_ge, 0.0, base=0, cm=-1)
        p.memset(mS[rows(hp), :], 1.0, eng="pool")
        p.affsel(mS[rows(hp), :], mS[rows(hp), :], [[1, 64]], OP.is_ge, 0.0, base=-1, cm=-1)
    nmI = p.sb("nmI", [128, 64])
    p.ts(nmI.ap(), mI.ap(), -1.0, OP.mult)
    mS4 = p.sb("mS4", [128, 4, 64])
    mI4 = p.sb("mI4", [128, 4, 64])
    for i in range(4):
        p.ts(mS4[:, i, :], mS.ap(), -1.0 if i < 2 else 1.0, OP.mult)
        p.ts(mI4[:, i, :], mI.ap(), 1.0 if i < 2 else -1.0, OP.mult)
    bones = p.sb("bones", [128, 128])
    p.memset(bones.ap(), 0.0)
    p.memset(bones[0:64, 0:64], 1.0)
    p.memset(bones[64:128, 64:128], 1.0)
    Eg = p.sb("Eg", [8, 2, 128])
    Eb = p.sb("Eb", [8, 2, 128])
    for E, sh in ((Eg, 0), (Eb, 4)):
        p.memset(E.ap(), 1.0, eng="pool")
        p.affsel(E.ap(), E.ap(), [[128, 2], [1, 128]], OP.is_ge, 0.0, base=64 * sh, cm=-64)
        p.affsel(E.ap(), E.ap(), [[-128, 2], [-1, 128]], OP.is_ge, 0.0, base=63 - 64 * sh, cm=64)
    dmask = p.sb("dmask", [16, 16, 64])
    p.memset(dmask.ap(), 1.0, eng="pool")
    p.affsel(dmask.ap(), dmask.ap(), [[-1, 16], [0, 64]], OP.is_equal, 0.0, base=0, cm=1)
    tidx = p.sb("tidx", [128, ST])
    p.iota(tidx.ap(), [[1, ST]], base=0, cm=0, allow_small_or_imprecise_dtypes=True)
    ones = p.sb("ones", [128, ST])
    p.memset(ones.ap(), 1.0)
    fg = p.sb("fg", [128, 1024])
    p.dma(fg.ap(), din["final_g"].ap().re("(o n) -> o n", o=1).bc([128, 1024]))

    PJ = [p.ps("PJ0", [128, 512]), p.ps("PJ1", [128, 512])]
    BK = {nm: p.ps(nm, [128, 512]) for nm in ("PT_A", "PA_A", "PX_A", "PT_B", "PA_B", "PX_B")}

    def psset(sfx):
        b1, b2, b3 = BK["PT_" + sfx], BK["PA_" + sfx], BK["PX_" + sfx]
        return (b2, b1, b3, b2, b1, b1)
    PS_A, PS_B = psset("A"), psset("B")
    if PSMODE == 1:
        PS_A = PS_B = (BK["PA_A"], BK["PX_A"], BK["PT_B"], BK["PA_B"], BK["PX_B"], BK["PT_A"])
    PS_FULL = (BK["PA_A"], BK["PX_A"], BK["PT_B"], BK["PA_B"], BK["PX_B"], BK["PT_A"])
    PT = BK["PT_A"]
    PB = BK["PX_A"]

    def v3(ps, n):
        return ps[:, 0:n * 64].re("p (a b) -> p a b", b=64)

    WGRP = [(1552, 2584), (2584, 3736), (0, 1040), (1040, 1552)]
    Wins = [p.sb(f"Win{i}", [128, 8, c1 - c0], BF16) for i, (c0, c1) in enumerate(WGRP)]
    Wout = p.sb("Wout", [128, 8, 1024], BF16)
    xts = [p.sb(f"xt{b}", [128, 1, 1024]) for b in range(2)]
    xn = p.sb("xn", [128, 1024])
    xn2 = p.sb("xn2", [128, 1024])
    hTs = [p.sb(f"hT{b}", [128, 8, ST], BF16) for b in range(2)]
    mixT2 = [[p.sb(f"mixT{b}_{i}", [128, 2, ST], BF16) for i in range(4)] for b in range(2)]
    ss = p.sb("ss", [128, 1])
    ss2 = p.sb("ss2", [128, 1])
    cur = {"hT": hTs[0]}
    ng = p.sb("ng", [128, 8])
    cnt = [0]

    RAW = p.sb("RAW", [128, 8704])
    carve_off = [0]

    def carve(name, shape, dt=F32):
        n = 1
        for d in shape[1:]:
            n *= d
        n32 = n if dt != BF16 else (n + 1) // 2
        a = RAW.h[0:shape[0], carve_off[0]:carve_off[0] + n32]
        carve_off[0] += n32
        assert carve_off[0] <= 8704, carve_off[0]
        if dt != F32:
            a = a.bitcast(dt)
        if len(shape) > 2:
            names = " ".join(f"d{i}" for i in range(1, len(shape)))
            a = a.rearrange(f"p ({names}) -> p {names}", **{f"d{i}": shape[i] for i in range(1, len(shape) - 1)})
        t = type(ones)(name, a)
        p.tiles[name] = t
        return t

    def make_set(sfx, alloc):
        W = {}
        for nm in ["qT", "kT", "vT", "aT", "bT", "ldT", "gate", "oT", "t0", "t1", "t2", "t3", "bon"]:
            W[nm] = alloc(f"w{sfx}_" + nm, [128, 2, ST])
        W["ones"] = ones
        W["g0"] = alloc(f"w{sfx}_g0", [128, 4])
        W["g0a"] = alloc(f"w{sfx}_g0a", [128, 4, 8])
        for nm in ("ub", "hbr", "hbi"):
            W[nm] = alloc(f"w{sfx}_" + nm, [128, 2, ST], BF16)
        K = {}
        for nm in ["cum", "cumx", "E", "qs", "as", "ks", "bs", "kd", "bd", "X", "R2", "Ut", "WtT", "U", "araw", "qraw", "vb", "Hb"]:
            K[nm] = alloc(f"k{sfx}_" + nm, [128, 2, 64], F32 if nm in ("cum", "cumx", "E", "Ut") else BF16)
        K["identb"] = identb
        K["gam"] = alloc(f"k{sfx}_gam4", [128, 4, 64])
        K["tok"] = alloc(f"k{sfx}_tok", [128, 8, 64], BF16)
        K["gtok"] = alloc(f"k{sfx}_gtok", [128, 2, 64])
        K["amat"] = alloc(f"k{sfx}_amat", [128, 8, 64], BF16)
        K["BB"] = [alloc(f"k{sfx}_BB0", [128, 4, 64], BF16), alloc(f"k{sfx}_BB1", [128, 4, 64], BF16)]
        K["PC"] = alloc(f"k{sfx}_PC", [128, 2])
        return W, K
    W, K = make_set("A", p.sb)
    carve_off[0] = 0
    W2, K2 = make_set("B", carve)
    W2["rin"] = carve("wB_rin", [128, 7, ST + 1])
    W2["xs7"] = carve("wB_xs7", [128, 7, ST])
    W2["PH"] = BK["PA_B"]
    W["PH"] = BK["PA_B"]
    set_b_end = carve_off[0]
    W["xp"] = p.sb("wA_xp", [128, 6, ST + 3])
    W["cv"] = p.sb("wA_cv", [128, 6, ST])
    W["abT"] = p.sb("w_abT", [8, ST])
    W["gf"] = p.sb("w_gf", [8, ST])
    W["bf"] = p.sb("w_bf", [8, ST])
    W["rin"] = W["xp"]
    carve_off[0] = 0
    xs_s = p.sb("xs_s", [16, 1024])
    big1 = carve("big1", [128, 2304])
    W["Snat"] = big1[:, 0:2048].re("p (a b c) -> p a b c", a=2, b=16)
    W["s5nat"] = big1[0:16, 0:2048].re("p (a b) -> p a b", a=2)
    W["cnat"] = big1[0:16, 0:2304].re("p (a b) -> p a b", a=3)
    W["snat"] = big1[0:16, 0:896]
    Hs1 = carve("Hs1", [128, 2, 16, 64])
    W["Dd"] = carve("w_Dd", [128, 2, 16])
    W["tokS"] = carve("w_tokS", [16, 3, 256])
    W["Ud"] = carve("w_Ud", [16, 16, 64])
    W["Vd"] = carve("w_Vd", [16, 16, 64])
    W["tmpd"] = W["Ud"]
    W["oTok"] = carve("w_oTok", [16, 256])
    W["hS"] = carve("w_hS", [128, 2, 8, 16])
    W["xsS"] = carve("w_xsS", [128, 6, 4, 16])
    W["prevS"] = carve("w_prevS", [128, 7, 16])
    W["rinS"] = carve("w_rinS", [128, 7, 16])
    W["xs7S"] = carve("w_xs7S", [128, 7, 16])
    H = {m: p.sb("H_" + m, [128, 2, 64]) for m in ("gla", "gdn", "rwkv")}
    Hs = {m: Hs1 for m in ("gla", "gdn", "rwkv")}

    def colvec(name, src, n):
        t = p.sb(name, [128, n // 128])
        p.dma(t.ap(), src.re("(k p) -> p k", p=128), allow_slow_non_contiguous=NS)
        return t

    def proj(dst, col0, n, T, scale=1.0):
        pj = PJ[cnt[0] % 2]
        cnt[0] += 1
        gi = [i for i, (c0, c1) in enumerate(WGRP) if c0 <= col0 < c1][0]
        Wg, cb = Wins[gi], col0 - WGRP[gi][0]
        for k in range(8):
            p.mm(pj[0:n, 0:T], Wg[:, k, cb:cb + n], cur["hT"][:, k, 0:T], start=(k == 0), stop=(k == 7))
        p.act(dst, pj[0:n, 0:T], AF.Identity, scale=scale)

    def rstd_inplace(t, T, mult, eps):
        p.ts(t, t, mult, OP.mult, eps, OP.add)
        p.act(t, t, AF.Ln)
        p.act(t, t, AF.Exp, scale=-0.5)

    def headsum(dst, src, T):
        for tl in range(2):
            pj = PJ[cnt[0] % 2]
            cnt[0] += 1
            p.mm(pj[:, 0:T], bones.ap(), src[:, tl, 0:T])
            p.cp(dst[:, tl, 0:T], pj[:, 0:T], eng="act")


    for l in range(DEPTH):
        for i, (c0, c1) in enumerate(WGRP):
            p.dma(Wins[i].ap(), din["w_in"][l, :, c0:c1].re("(k q) c -> q k c", q=128), eng="pool")
        p.dma(Wout.ap(), din["w_out"][l].re("(k q) c -> q k c", q=128), eng="pool")
        p.dma(ng.ap(), din["norm_g"][l].re("(k p) -> p k", p=128), allow_slow_non_contiguous=NS)
        L = {}
        wg2 = p.sbc(f"wg2", [32, 256])
        p.memset(wg2.ap(), 0.0)
        p.dma(wg2[0:16, :], din["gla_wg2"][l])
        p.dma(wg2[16:17, :], din["gla_bg"][l].re("(o n) -> o n", o=1))
        gla_ng = p.sbc(f"gla_ng", [128, 1])
        gdn_ng = p.sbc(f"gdn_ng", [128, 1])
        for hp in range(2):
            p.dma(gla_ng[rows(hp), :], din["gla_norm_g"][l].re("(n o) -> n o", o=1), allow_slow_non_contiguous=NS)
            p.dma(gdn_ng[rows(hp), :], din["gdn_norm_g"][l].re("(n o) -> n o", o=1), allow_slow_non_contiguous=NS)
        s5d = colvec(f"s5d_{l}", din["s5_d"][l], 256)
        s5bg = colvec(f"s5bg_{l}", din["s5_b_glu"][l], 256)
        wglu = p.sbc(f"wglu", [128, 2, 256])
        p.dma(wglu.ap(), din["s5_w_glu"][l].re("(k p) n -> p k n", p=128))
        convw = p.sbc(f"convw", [128, 6, 4])
        for i in range(4):
            p.dma(convw[:, :, i], din["gdn_conv_w"][l, i].re("(j p) -> p j", p=128), allow_slow_non_contiguous=NS)
        gab = p.sbc(f"gab", [8, 2])
        p.memset(gab.ap(), 0.0)
        p.dma(gab[0:4, 0:1], din["gdn_dt_bias"][l].re("(n o) -> n o", o=1), allow_slow_non_contiguous=NS)
        p.dma(gab[0:4, 1:2], din["gdn_a_log"][l].re("(n o) -> n o", o=1), allow_slow_non_contiguous=NS)
        p.act(gab[:, 1:2], gab[:, 1:2], AF.Exp)
        p.ts(gab[:, 1:2], gab[:, 1:2], -1.0, OP.mult)
        mu = colvec(f"mu_{l}", din["rwkv_mu"][l], 896)
        w0 = colvec(f"w0_{l}", din["rwkv_w0"][l], 256)
        a0 = colvec(f"a0_{l}", din["rwkv_a0"][l], 256)
        k_k = colvec(f"kk_{l}", din["rwkv_k_k"][l], 256)
        k_a = colvec(f"ka_{l}", din["rwkv_k_a"][l], 256)
        r_k = colvec(f"rk_{l}", din["rwkv_r_k"][l].re("h n -> (h n)"), 256)
        ln_g = colvec(f"lng_{l}", din["rwkv_ln_g"][l], 256)
        ln_b = colvec(f"lnb_{l}", din["rwkv_ln_b"][l], 256)
        wlo = p.sbc(f"wlo", [128, 256])
        p.dma(wlo[0:64, :], din["rwkv_ww2"][l])
        p.dma(wlo[64:128, :], din["rwkv_wa2"][l])

        S5 = {}

        for m in H:
            p.memset(H[m].ap(), 0.0)
        hist_gdn = p.sbc(f"hist_gdn", [128, 6, 3])
        hist_rwkv = p.sbc(f"hist_rwkv", [128, 7, 1])
        p.memset(hist_gdn.ap(), 0.0)
        p.memset(hist_rwkv.ap(), 0.0)

        common = dict(p=p, proj=proj, OFF=OFF, PJ=PJ, cnt=cnt, ident=ident, identP=identP, mI=mI, mS=mS, nmI=nmI,
                      mI4=mI4, mS4=mS4, dmask=dmask, rows=rows, v3=v3, headsum=headsum, rstd_inplace=rstd_inplace,
                      din=din, dout=dout, l=l, NST=NST, H=H, Hs=Hs)
        LW = dict(wg2=wg2, gla_ng=gla_ng, gdn_ng=gdn_ng, s5d=s5d, s5bg=s5bg, wglu=wglu, convw=convw, gab=gab, Eg=Eg,
                  Eb=Eb, mu=mu, w0=w0, a0=a0, k_k=k_k, k_a=k_a, r_k=r_k, ln_g=ln_g, ln_b=ln_b, wlo=wlo,
                  hist_gdn=hist_gdn, hist_rwkv=hist_rwkv, S5=S5)
        last_layer = (l == DEPTH - 1)

        def run_streams(gens):
            gens = [g for g in gens if g is not None]
            while gens:
                for g in list(gens):
                    try:
                        next(g)
                    except StopIteration:
                        gens.remove(g)

        def chain(*gs):
            for g in gs:
                if g is not None:
                    yield from g

        def head_gen(kind, st, bi):
            if kind == "p":
                r0 = st * ST
                src = din["x_prompt"] if l == 0 else xbuf
                xv = xts[bi][:, 0, :]
                p.dma(xv, src[r0:r0 + 128, :])
                np_ = 128
            else:
                xv = xs_s[0:16, :]
                if l == 0:
                    p.dma(xv, din["x_sample"].ap())
                np_ = 16
            p.act(xn[0:np_, :], xv, AF.Square, accum=ss[0:np_, :])
            yield
            rstd_inplace(ss[0:np_, :], 1, 1.0 / 1024, 1e-6)
            yield
            p.ts(xn[0:np_, :], xv, ss[0:np_, :], OP.mult)
            yield
            for kk in range(2):
                pj = PJ[cnt[0] % 2]
                cnt[0] += 1
                for j in range(4):
                    k = kk * 4 + j
                    p.tr(pj[:, j * 128:j * 128 + np_], xn[0:np_, k * 128:(k + 1) * 128], ident[0:np_, 0:np_])
                p.tt(hTs[bi][:, kk * 4:kk * 4 + 4, 0:np_],
                     pj[:, 0:512].re("p (a b) -> p a b", b=128)[:, :, 0:np_],
                     ng[:, kk * 4:kk * 4 + 4, None].bc([128, 4, np_]), OP.mult)
                yield

        def tail_gen(kind, st, bi):
            np_ = 128 if kind == "p" else 16
            xv = xts[bi][:, 0, :] if kind == "p" else xs_s[0:16, :]
            r0 = st * ST
            mt = mixT2[bi]
            for half in range(2):
                pj = PJ[cnt[0] % 2]
                cnt[0] += 1
                for k in range(8):
                    p.mm(pj[0:np_, :], mt[k // 2][:, k % 2, 0:np_], Wout[:, k, half * 512:(half + 1) * 512],
                         start=(k == 0), stop=(k == 7))
                p.tt(xv[:, half * 512:(half + 1) * 512], xv[:, half * 512:(half + 1) * 512], pj[0:np_, :], OP.add)
                yield
            if not last_layer:
                if kind == "p":
                    p.dma(xbuf[r0:r0 + 128, :], xv)
            else:
                p.act(xn2[0:np_, :], xv, AF.Square, accum=ss2[0:np_, :])
                yield
                rstd_inplace(ss2[0:np_, :], 1, 1.0 / 1024, 1e-6)
                yield
                p.stt(xn2[0:np_, :], xv, ss2[0:np_, :], fg[0:np_, :], OP.mult, OP.mult)
                yield
                if kind == "p":
                    p.dma(dout["y_p"][r0:r0 + 128, :], xn2[0:np_, :])
                else:
                    p.dma(dout["y_s"].ap(), xn2[0:np_, :])

        def mixers(kind, st, bi):
            T = ST if kind == "p" else 16
            cur["hT"] = hTs[bi]
            cm = dict(common, mixTs=mixT2[bi])
            for i, m in enumerate(("gla", "s5", "gdn", "rwkv")):
                if m not in MIX:
                    p.memset(mixT2[bi][i][:, :, 0:T], 0.0)
            if kind == "p":
                ga = chain(gdn_block(kind, st, T, W, K, PS_A, LW, **cm) if "gdn" in MIX else None,
                           gla_block(kind, st, T, W, K, PS_A, LW, **cm) if "gla" in MIX else None)
                gb = chain(rwkv_block(kind, st, T, W2, K2, PS_B, LW, **cm) if "rwkv" in MIX else None,
                           s5_setup(S5, p, nc, din, l, ident, tidx, BK["PT_B"], BK["PX_B"], rows)
                           if ("s5" in MIX and st == 0) else None,
                           s5_block(kind, st, T, W2, PS_B, LW, **cm) if "s5" in MIX else None)
                return [ga, gb] if STREAMS == 2 else [chain(ga, gb)]
            W["rin"], W["xs7"] = W["rinS"], W["xs7S"]
            return [chain(gla_block(kind, st, T, W, K, PS_FULL, LW, **cm) if "gla" in MIX else None,
                          s5_block(kind, st, T, W, PS_FULL, LW, **cm) if "s5" in MIX else None,
                          gdn_block(kind, st, T, W, K, PS_FULL, LW, **cm) if "gdn" in MIX else None,
                          rwkv_block(kind, st, T, W, K, PS_FULL, LW, **cm) if "rwkv" in MIX else None)]

        run_streams([head_gen("p", 0, 0)])
        for step in range(NST + 1):
            gens = []
            if step < NST:
                gens += mixers("p", step, step % 2)
            gens.append(chain(tail_gen("p", step - 1, (step - 1) % 2) if step >= 1 else None,
                              head_gen("p", step + 1, (step + 1) % 2) if step + 1 < NST else None))
            run_streams(gens)
        if SAMPLE:
            p.barrier()
            run_streams([head_gen("s", 0, 0)])
            run_streams(mixers("s", 0, 0))
            run_streams([tail_gen("s", 0, 0)])
            p.barrier()
    p.finish()
    return nc, p


def gla_block(kind, st, T, W, K, PS, LW, *, p, proj, OFF, PJ, cnt, ident, identP, mI, mS, nmI, mI4, mS4, dmask, rows, v3,
              headsum, rstd_inplace, din, dout, l, NST, H, Hs, mixTs):
    qT, kT, vT, ldT, gate, oT = (W[n] for n in ("qT", "kT", "vT", "ldT", "gate", "oT"))
    wg2, gla_ng = LW["wg2"], LW["gla_ng"]
    for tl in range(2):
        proj(qT[:, tl, 0:T], OFF["gla_q"] + 128 * tl, 128, T, scale=0.125)
        yield
        proj(kT[:, tl, 0:T], OFF["gla_k"] + 128 * tl, 128, T)
        yield
        proj(vT[:, tl, 0:T], OFF["gla_v"] + 128 * tl, 128, T)
        yield
        proj(gate[:, tl, 0:T], OFF["gla_gate"] + 128 * tl, 128, T)
        yield
    glr = W["t0"]
    p.memset(glr[0:32, 0, 0:T], 1.0)
    proj(glr[0:16, 0, 0:T], OFF["glr"], 16, T)
    yield
    for tl in range(2):
        pj = PJ[cnt[0] % 2]
        cnt[0] += 1
        p.mm(pj[:, 0:T], wg2[0:17, tl * 128:(tl + 1) * 128], glr[0:17, 0, 0:T])
        p.act(ldT[:, tl, 0:T], pj[:, 0:T], AF.Exp, scale=-1.0)
        p.act(ldT[:, tl, 0:T], ldT[:, tl, 0:T], AF.Ln, bias=1.0)
        yield
    p.ts(ldT[:, :, 0:T], ldT[:, :, 0:T], -1.0 / 16, OP.mult)
    yield from mixer_core(p, "gla", kind, T, W, K, H["gla"], Hs["gla"], dict(ab=False, scalar=False),
                          PS, ident, identP, mI, mS, nmI, mI4, mS4, dmask, rows, v3,
                          din["state_gla"], dout["gla_s"], l, transposed_state=False)
    yield from out_norm_rms(p, oT, gate, gla_ng, T, W, headsum, rstd_inplace, mixTs[0])
    if kind == "p" and st == NST - 1:
        p.dma(dout["gla_p"][l].re("(t hp) d v -> (hp d) t v", hp=2), H["gla"].ap())


def mixer_core(p, name, kind, T, W, K, Hst, Hsamp, fl, PS, ident, identP, mI, mS, nmI, mI4, mS4, dmask, rows, v3,
               state_in, state_out, l, transposed_state):
    PA, PB, PU, PO, PH, PT = PS
    qT, aT, kT, bT, vT, ldT, oT = (W[n] for n in ("qT", "aT", "kT", "bT", "vT", "ldT", "oT"))
    ab, scalar = fl["ab"], fl["scalar"]
    if kind == "s":
        sample_core(p, name, W, Hsamp, fl, PS, ident, dmask, rows, v3, state_in, state_out, l, transposed_state)
        yield
        return
    cum, cumx, E, qs, as_, ks, bs, kd, bd, X, R2, Ut, WtT, U = (K[n] for n in (
        "cum", "cumx", "E", "qs", "as", "ks", "bs", "kd", "bd", "X", "R2", "Ut", "WtT", "U"))
    tok, gtok, amat, BB, PC, gam = K["tok"], K["gtok"], K["amat"], K["BB"], K["PC"], K["gam"]
    araw, qraw, vb, Hb, identb = K["araw"], K["qraw"], K["vb"], K["Hb"], K["identb"]
    p.cp(Hb.ap(), Hst.ap(), eng="act")
    yield
    ones64 = None
    for c in range(T // C):
        sl = slice(c * C, (c + 1) * C)
        for tl in range(2):
            p.scan(cum[:, tl, :], W["ones"][:, 0:64], ldT[:, tl, sl], 0.0, OP.mult, OP.add)
            yield
        p.act(E.ap(), cum.ap(), AF.Exp)
        yield
        p.tt(qs.ap(), qT[:, :, sl], E.ap(), OP.mult)
        yield
        for tl in range(2):
            p.cp(PC[:, tl:tl + 1], E[:, tl, 63:64])
            yield
        if ab:
            p.tt(cumx.ap(), cum.ap(), ldT[:, :, sl], OP.subtract)
            yield
            p.act(E.ap(), cumx.ap(), AF.Exp)
            yield
            p.tt(as_.ap(), aT[:, :, sl], E.ap(), OP.mult)
            yield
        for tl in range(2):
            p.act(E[:, tl, :], cum[:, tl, :], AF.Exp, scale=-1.0, bias=cum[:, tl, 63:64])
            yield
        p.tt(kd.ap(), kT[:, :, sl], E.ap(), OP.mult)
        yield
        if ab:
            p.stt(bd.ap(), bT[:, :, sl], -1.0, E.ap(), OP.mult, OP.mult)
            yield
        if not scalar:
            p.act(E.ap(), cum.ap(), AF.Exp, scale=-1.0)
            yield
            p.tt(ks.ap(), kT[:, :, sl], E.ap(), OP.mult)
            yield
            if ab:
                p.tt(bs.ap(), bT[:, :, sl], E.ap(), OP.mult)
                yield
            Yk, Yb, Xa, Xq = ks, bs, as_, qs
        else:
            p.cp(ks.ap(), kT[:, :, sl], eng="act")
            yield
            p.cp(bs.ap(), bT[:, :, sl], eng="act")
            yield
            p.cp(araw.ap(), aT[:, :, sl], eng="act")
            yield
            p.cp(qraw.ap(), qT[:, :, sl], eng="act")
            yield
            Yk, Yb, Xa, Xq = ks, bs, araw, qraw
        p.cp(vb.ap(), vT[:, :, sl], eng="act")
        yield
        tq = [("as", as_), ("kd", kd), ("bd", bd), ("v", None)]
        ptv = v3(PT, 8)
        for qi, (nm, src) in enumerate(tq):
            if nm in ("as", "bd") and not ab:
                continue
            for tl in range(2):
                for hp in range(2):
                    s_ = vb[rows(hp), tl, :] if nm == "v" else src[rows(hp), tl, :]
                    p.tr(ptv[rows(hp), qi * 2 + tl, :], s_, identb[rows(hp), rows(hp)])
        if ab:
            p.cp(tok.ap(), ptv, eng="act")
            yield
        else:
            p.cp(tok[:, 2:4, :], ptv[:, 2:4, :], eng="act")
            yield
            p.cp(tok[:, 6:8, :], ptv[:, 6:8, :], eng="act")
            yield
        aTok, kdTok, bdTok, vTok = tok[:, 0:2, :], tok[:, 2:4, :], tok[:, 4:6, :], tok[:, 6:8, :]
        pav = v3(PA, 8)
        pairs = [(0, Yb, Xa), (1, Yk, Xa), (2, Yk, Xq), (3, Yb, Xq)] if ab else [(2, Yk, Xq)]
        for ty, Y, Xx in pairs:
            for tl in range(2):
                for hp in range(2):
                    ysl = Y[rows(hp), tl, :]
                    xsl = Xx[rows(hp), tl, :]
                    p.mm(pav[rows(hp), ty * 2 + tl, :], ysl, xsl)
        if scalar:
            puv = v3(PU, 2)
            for tl in range(2):
                for hp in range(2):
                    p.tr(puv[rows(hp), tl, :], cum[rows(hp), tl, :], ident[rows(hp), rows(hp)])
            p.cp(gtok.ap(), puv)
            yield
            for tl in range(2):
                p.ts(gam[:, 0 * 2 + tl, :], cumx[:, tl, :], gtok[:, tl, 0:1], OP.subtract, 0.0, OP.min)
                p.ts(gam[:, 1 * 2 + tl, :], cum[:, tl, :], gtok[:, tl, 0:1], OP.subtract, 0.0, OP.min)
            yield
            p.act(gam.ap(), gam.ap(), AF.Exp)
            yield
            for ty in range(4):
                gsel = gam[:, 0:2, :] if ty < 2 else gam[:, 2:4, :]
                p.tt(amat[:, 2 * ty:2 * ty + 2, :], pav[:, 2 * ty:2 * ty + 2, :], gsel, OP.mult)
                yield
            p.tt(amat[:, 0:4, :], amat[:, 0:4, :], mS4.ap(), OP.mult)
            yield
            p.tt(amat[:, 4:8, :], amat[:, 4:8, :], mI4.ap(), OP.mult)
            yield
        elif ab:
            p.tt(amat[:, 0:4, :], pav[:, 0:4, :], mS4.ap(), OP.mult)
            yield
            p.tt(amat[:, 4:8, :], pav[:, 4:8, :], mI4.ap(), OP.mult)
            yield
        else:
            p.tt(amat[:, 4:6, :], pav[:, 4:6, :], mI4[:, 0:2, :], OP.mult)
            yield
        nLt, Akt, Qkt, nQbt = amat[:, 0:2, :], amat[:, 2:4, :], amat[:, 4:6, :], amat[:, 6:8, :]
        if ab:
            b0 = BB[0]
            p.cp(b0[:, 0:2, :], nLt)
            yield
            pbv = v3(PB, 4)
            for tl in range(2):
                for hp in range(2):
                    p.tr(pbv[rows(hp), tl, :], nLt[rows(hp), tl, :], identb[rows(hp), rows(hp)])
            p.cp(b0[:, 2:4, :], pbv[:, 0:2, :], eng="act")
            yield
            p.tt(X.ap(), identP.ap(), nLt, OP.add)
            yield
            for k in range(1, 6):
                prev, cur = BB[(k - 1) % 2], BB[k % 2]
                for tl in range(2):
                    for hp in range(2):
                        r = rows(hp)
                        p.mm(pbv[r, tl, :], prev[r, 2 + tl, :], prev[r, tl, :])
                        p.mm(pbv[r, 2 + tl, :], prev[r, tl, :], prev[r, 2 + tl, :])
                p.cp(cur.ap(), pbv, eng="act")
                yield
                puv = v3(PU, 2)
                for tl in range(2):
                    for hp in range(2):
                        r = rows(hp)
                        p.mm(puv[r, tl, :], cur[r, 2 + tl, :], X[r, tl, :])
                p.tt(X.ap(), X.ap(), puv, OP.add)
                yield
            pov = v3(PO, 2)
            for tl in range(2):
                for hp in range(2):
                    r = rows(hp)
                    p.mm(pov[r, tl, :], Akt[r, tl, :], vTok[r, tl, :])
            p.cp(R2.ap(), pov, eng="act")
            yield
            puv = v3(PU, 2)
            phv = v3(PH, 2)
            for tl in range(2):
                for hp in range(2):
                    r = rows(hp)
                    p.mm(puv[r, tl, :], X[r, tl, :], R2[r, tl, :])
                    p.mm(phv[r, tl, :], aTok[r, tl, :], X[r, tl, :])
            p.cp(Ut.ap(), puv)
            yield
            p.cp(WtT.ap(), phv, eng="act")
            yield
            for tl in range(2):
                for hp in range(2):
                    r = rows(hp)
                    p.mm(puv[r, tl, :], WtT[r, tl, :], Hb[r, tl, :])
            p.tt(U.ap(), puv, Ut.ap(), OP.add)
            yield
        pov = v3(PO, 2)
        for tl in range(2):
            for hp in range(2):
                r = rows(hp)
                p.mm(pov[r, tl, :], Hb[r, tl, :], qs[r, tl, :], start=True, stop=False)
                p.mm(pov[r, tl, :], vTok[r, tl, :], Qkt[r, tl, :], start=False, stop=not ab)
                if ab:
                    p.mm(pov[r, tl, :], U[r, tl, :], nQbt[r, tl, :], start=False, stop=True)
        p.cp(oT[:, :, sl], pov, eng="act")
        yield
        phv = v3(PH, 2)
        for tl in range(2):
            for hp in range(2):
                r = rows(hp)
                p.mm(phv[r, tl, :], kdTok[r, tl, :], vTok[r, tl, :], start=True, stop=not ab)
                if ab:
                    p.mm(phv[r, tl, :], bdTok[r, tl, :], U[r, tl, :], start=False, stop=True)
        for tl in range(2):
            p.stt(Hst[:, tl, :], Hst[:, tl, :], PC[:, tl:tl + 1], phv[:, tl, :], OP.mult, OP.add)
            yield
        p.cp(Hb.ap(), Hst.ap(), eng="act")
        yield


def out_norm_rms(p, oT, gate, gcol, T, W, headsum, rstd_inplace, mixTm):
    t0, t1 = W["t0"], W["t1"]
    p.act(t0[:, :, 0:T], oT[:, :, 0:T], AF.Square)
    yield
    headsum(t1, t0, T)
    yield
    rstd_inplace(t1[:, :, 0:T], T, 1.0 / 64, 1e-6)
    yield
    p.tt(t0[:, :, 0:T], oT[:, :, 0:T], t1[:, :, 0:T], OP.mult)
    yield
    silu_(p, t1[:, :, 0:T], gate[:, :, 0:T])
    yield
    p.stt(mixTm[:, :, 0:T], t0[:, :, 0:T], gcol[:, 0:1], t1[:, :, 0:T], OP.mult, OP.mult)
    yield


def sample_core(p, name, W, Hs, fl, PS, ident, dmask, rows, v3, state_in, state_out, l, transposed_state):
    PA, PB, PU, PO, PH, PT = PS
    ab = fl["ab"]
    qT, aT, kT, bT, vT, ldT, oT = (W[n] for n in ("qT", "aT", "kT", "bT", "vT", "ldT", "oT"))
    Snat = W["Snat"]
    Dd, tokS, Ud, Vd, oTok, tmpd = (W[n] for n in ("Dd", "tokS", "Ud", "Vd", "oTok", "tmpd"))
    ptv = v3(PT, 8)
    for tl in range(2):
        for hp in range(2):
            h = 2 * tl + hp
            if not transposed_state:
                p.dma(Hs[rows(hp), tl, :, :], state_in[l, :, h].re("b d v -> d b v"))
            else:
                p.dma(Snat[rows(hp), tl, :, :], state_in[l, :, h].re("b v d -> v b d"))
    if transposed_state:
        for tl in range(2):
            for g in range(2):
                for j in range(8):
                    for hp in range(2):
                        p.tr(ptv[rows(hp), j, :], Snat[rows(hp), tl, 8 * g + j, :], ident[rows(hp), rows(hp)])
                p.cp(Hs[:, tl, 8 * g:8 * g + 8, :], ptv)
    p.act(Dd.ap(), ldT[:, :, 0:16], AF.Exp)
    pt2 = PT[0:16, 0:512].re("p (a b) -> p a b", b=128)
    srcs = [kT, bT, vT] if ab else [kT, vT]
    for qi, src in enumerate(srcs):
        for tl in range(2):
            p.tr(pt2[:, tl, :], src[:, tl, 0:16], ident.ap())
        if ab and qi == 1:
            p.ts(tokS[:, 1, :], pt2[:, 0:2, :].re("p a b -> p (a b)"), -1.0, OP.mult)
        else:
            p.cp(tokS[:, (qi if ab else 2 * qi), :], pt2[:, 0:2, :].re("p a b -> p (a b)"))
    for tl in range(2):
        for hp in range(2):
            h = 2 * tl + hp
            r = rows(hp)
            hc = slice(64 * h, 64 * h + 64)
            p.tt(Vd.ap(), tokS[:, 2, hc][:, None, :].bc([16, 16, 64]), dmask.ap(), OP.mult)
            if ab:
                for g, ps in enumerate((PU, PB)):
                    p.mm(ps[0:16, :], aT[r, tl, 0:16], Hs[r, tl, 8 * g:8 * g + 8, :].re("p b v -> p (b v)"))
                    p.tt(Ud[:, 8 * g:8 * g + 8, :], ps[0:16, :].re("p (b v) -> p b v", v=64),
                         dmask[:, 8 * g:8 * g + 8, :], OP.mult)
            for g, ps in enumerate((PH, PO)):
                p.mm(ps[r, :], tokS[:, 0, hc], Vd[:, 8 * g:8 * g + 8, :].re("p b v -> p (b v)"), start=True, stop=not ab)
                if ab:
                    p.mm(ps[r, :], tokS[:, 1, hc], Ud[:, 8 * g:8 * g + 8, :].re("p b v -> p (b v)"), start=False, stop=True)
        p.tt(Hs[:, tl, :, :], Hs[:, tl, :, :], Dd[:, tl, :][:, :, None].bc([128, 16, 64]), OP.mult)
        for g, ps in enumerate((PH, PO)):
            p.tt(Hs[:, tl, 8 * g:8 * g + 8, :], Hs[:, tl, 8 * g:8 * g + 8, :], ps[:, :].re("p (b v) -> p b v", v=64), OP.add)
        for hp in range(2):
            h = 2 * tl + hp
            r = rows(hp)
            hc = slice(64 * h, 64 * h + 64)
            for g, ps in enumerate((PU, PB)):
                p.mm(ps[0:16, :], qT[r, tl, 0:16], Hs[r, tl, 8 * g:8 * g + 8, :].re("p b v -> p (b v)"))
                p.tt(tmpd[:, 8 * g:8 * g + 8, :], ps[0:16, :].re("p (b v) -> p b v", v=64),
                     dmask[:, 8 * g:8 * g + 8, :], OP.mult)
            p.reduce(oTok[:, hc], tmpd.ap().re("p b v -> p v b"), OP.add)
    for tl in range(2):
        p.tr(PT[:, tl * 16:tl * 16 + 16], oTok[:, tl * 128:(tl + 1) * 128], ident[0:16, 0:16])
    p.cp(oT[:, :, 0:16], PT[:, 0:32].re("p (a b) -> p a b", b=16))
    if transposed_state:
        for tl in range(2):
            for g in range(2):
                for j in range(8):
                    for hp in range(2):
                        p.tr(ptv[rows(hp), j, :], Hs[rows(hp), tl, 8 * g + j, :], ident[rows(hp), rows(hp)])
                p.cp(Snat[:, tl, 8 * g:8 * g + 8, :], ptv)
    for tl in range(2):
        for hp in range(2):
            h = 2 * tl + hp
            if not transposed_state:
                p.dma(state_out[l, :, h].re("b d v -> d b v"), Hs[rows(hp), tl, :, :])
            else:
                p.dma(state_out[l, :, h].re("b v d -> v b d"), Snat[rows(hp), tl, :, :])


TWO_PI = 2.0 * math.pi


def sincos(p, dst_s, dst_c, ang, fr, ii):
    for dst, sh in ((dst_s, 0.0), (dst_c, 0.25)):
        p.ts(dst, ang, sh, OP.add)
        p.cp(ii, dst)
        p.cp(fr, ii)
        p.tt(fr, dst, fr, OP.subtract)
        p.ts(fr, fr, 0.4999995, OP.min, -0.4999995, OP.max)
        p.act(dst, fr, AF.Sin, scale=TWO_PI)


def s5_setup(S, p, nc, din, l, ident, tidx, PT, PB, rows):
    NS = True

    def ld(name, src):
        t = p.sbc(f"s5{name}", [128, 8])
        for gp in range(2):
            p.dma(t[rows(gp), :], src.re("(pr gp) q -> gp q pr", gp=2)[gp], allow_slow_non_contiguous=NS)
        return t
    lre = ld("lre", din["s5_lam_re"][l])
    lim = ld("lim", din["s5_lam_im"][l])
    stp = p.sbc(f"s5stp", [128, 8])
    for gp in range(2):
        p.dma(stp[rows(gp), :], din["s5_log_step"][l].re("(pr gp) -> gp pr", gp=2)[gp][None, :].bc([64, 8]),
              allow_slow_non_contiguous=NS)
    p.act(stp.ap(), stp.ap(), AF.Exp)
    names = ["lr", "li", "mag", "cs", "sn", "abre", "abim", "nabim", "den", "am1", "zre", "zim", "t", "fr", "ang"]
    c = {n: p.sbc(f"s5{n}", [128, 8]) for n in names}
    ii = p.sbc(f"s5ii", [128, 8], I32)
    p.tt(c["lr"].ap(), lre.ap(), stp.ap(), OP.mult)
    p.tt(c["li"].ap(), lim.ap(), stp.ap(), OP.mult)
    p.act(c["mag"].ap(), c["lr"].ap(), AF.Exp)
    p.ts(c["ang"].ap(), c["li"].ap(), 1.0 / TWO_PI, OP.mult)
    sincos(p, c["sn"].ap(), c["cs"].ap(), c["ang"].ap(), c["fr"].ap(), ii.ap())
    p.tt(c["abre"].ap(), c["mag"].ap(), c["cs"].ap(), OP.mult)
    p.tt(c["abim"].ap(), c["mag"].ap(), c["sn"].ap(), OP.mult)
    p.ts(c["nabim"].ap(), c["abim"].ap(), -1.0, OP.mult)
    p.tt(c["den"].ap(), lre.ap(), lre.ap(), OP.mult)
    p.tt(c["t"].ap(), lim.ap(), lim.ap(), OP.mult)
    p.tt(c["den"].ap(), c["den"].ap(), c["t"].ap(), OP.add)
    p.recip(c["den"].ap(), c["den"].ap())
    p.ts(c["am1"].ap(), c["abre"].ap(), -1.0, OP.add)
    p.tt(c["zre"].ap(), c["am1"].ap(), lre.ap(), OP.mult)
    p.tt(c["t"].ap(), c["abim"].ap(), lim.ap(), OP.mult)
    p.tt(c["zre"].ap(), c["zre"].ap(), c["t"].ap(), OP.add)
    p.tt(c["zre"].ap(), c["zre"].ap(), c["den"].ap(), OP.mult)
    p.tt(c["zim"].ap(), c["abim"].ap(), lre.ap(), OP.mult)
    p.tt(c["t"].ap(), c["am1"].ap(), lim.ap(), OP.mult)
    p.tt(c["zim"].ap(), c["zim"].ap(), c["t"].ap(), OP.subtract)
    p.tt(c["zim"].ap(), c["zim"].ap(), c["den"].ap(), OP.mult)
    yield
    bre = p.sbc(f"s5bre", [128, 8, 16])
    bim = p.sbc(f"s5bim", [128, 8, 16])
    for gp in range(2):
        p.dma(bre[rows(gp)], din["s5_b_re"][l].re("(pr gp) q c -> gp q pr c", gp=2)[gp], allow_slow_non_contiguous=NS)
        p.dma(bim[rows(gp)], din["s5_b_im"][l].re("(pr gp) q c -> gp q pr c", gp=2)[gp], allow_slow_non_contiguous=NS)
    BD = [p.sbc(f"s5BD{r}", [128, 8, 64]) for r in range(2)]
    tmp = p.sbc(f"s5tmp", [128, 8, 16])
    tmp2 = p.sbc(f"s5tmp2", [128, 8, 16])
    zre_b = c["zre"].ap()[:, :, None].bc([128, 8, 16])
    zim_b = c["zim"].ap()[:, :, None].bc([128, 8, 16])
    for r in range(2):
        p.memset(BD[r].ap(), 0.0)

    def scatter(r):
        for par in range(2):
            for q in range(4):
                pr = 2 * q + par
                p.cp(BD[r][0:64, pr, par * 32:par * 32 + 16], tmp[0:64, pr, :])
                p.cp(BD[r][64:128, pr, par * 32 + 16:par * 32 + 32], tmp[64:128, pr, :])
    p.tt(tmp.ap(), bre.ap(), zre_b, OP.mult)
    p.tt(tmp2.ap(), bim.ap(), zim_b, OP.mult)
    p.tt(tmp.ap(), tmp.ap(), tmp2.ap(), OP.subtract)
    scatter(0)
    p.tt(tmp.ap(), bim.ap(), zre_b, OP.mult)
    p.tt(tmp2.ap(), bre.ap(), zim_b, OP.mult)
    p.tt(tmp.ap(), tmp.ap(), tmp2.ap(), OP.add)
    scatter(1)
    BT = p.sbc(f"s5BT", [128, 8, 128])
    ptv = PT[:, 0:512].re("p (a b) -> p a b", b=128)
    for r in range(2):
        for pr in range(8):
            hf = (pr % 4) // 2
            p.tr(ptv[64 * hf:64 * hf + 64, (pr // 4) * 2 + pr % 2, :], BD[r][:, pr, :], ident.ap())
        p.cp(BT[:, r * 4:r * 4 + 4, :], ptv)
    yield
    par = p.sbc(f"s5par", [128, 2])
    pii = p.sbc(f"s5pii", [128, 1], I32)
    p.iota(pii.ap(), [[0, 1]], base=0, cm=1)
    p.op("dve", lambda e: e.tensor_scalar(pii.h[:], pii.h[:], 4, 1, OP.arith_shift_right, op1=OP.bitwise_and),
         [pii.ap()], [pii.ap()])
    p.cp(par[:, 1:2], pii.ap())
    p.ts(par[:, 0:1], par[:, 1:2], -1.0, OP.mult, 1.0, OP.add)
    Cn = p.sbc(f"s5Cn", [128, 2, 64])
    Cexp = p.sbc(f"s5Cexp", [128, 2, 128])
    pbv = PB[:, 0:512].re("p (a b) -> p a b", b=128)
    for r, nm in enumerate(("s5_c_re", "s5_c_im")):
        p.dma(Cn.ap(), din[nm][l].re("(s g) c q -> (g c) s q", s=2))
        sgn = 1.0 if r == 0 else -1.0
        p.ts(Cexp[:, :, 0:64], Cn.ap(), par[:, 0:1], OP.mult, sgn, OP.mult)
        p.ts(Cexp[:, :, 64:128], Cn.ap(), par[:, 1:2], OP.mult, sgn, OP.mult)
        for s in range(2):
            p.tr(pbv[:, r * 2 + s, :], Cexp[:, s, :], ident.ap())
    CTe = p.sbc("s5CTe", [128, 4, 128])
    CTo = p.sbc("s5CTo", [128, 4, 128])
    p.memset(CTe.ap(), 0.0)
    p.memset(CTo.ap(), 0.0)
    vw = "p s (q pp c) -> p s q pp c"
    p.cp(CTe.ap().re(vw, pp=2, c=32)[:, :, :, 0, :], pbv.re(vw, pp=2, c=32)[:, :, :, 0, :])
    p.cp(CTo.ap().re(vw, pp=2, c=32)[:, :, :, 1, :], pbv.re(vw, pp=2, c=32)[:, :, :, 1, :])
    yield
    cosT = p.sbc(f"s5cosT", [128, 8, ST])
    sinT = p.sbc(f"s5sinT", [128, 8, ST])
    frT = p.sbc(f"s5frT", [128, ST])
    angT = p.sbc(f"s5angT", [128, ST])
    iiT = p.sbc(f"s5iiT", [128, ST], I32)
    for pr in range(8):
        p.ts(angT.ap(), tidx.ap(), c["ang"][:, pr:pr + 1], OP.mult)
        sincos(p, sinT[:, pr, :], cosT[:, pr, :], angT.ap(), frT.ap(), iiT.ap())
        yield
    S.update(c)
    BTb = p.sbc("s5BTb", [128, 8, 128], BF16)
    CTeb = BD[0][:, 0:4, :].bitcast(BF16)
    CTob = BD[1][:, 0:4, :].bitcast(BF16)
    p.cp(BTb.ap(), BT.ap())
    p.cp(CTeb.ap(), CTe.ap())
    p.cp(CTob.ap(), CTo.ap())
    S.update(BT=BT, CTe=CTe, CTo=CTo, cosT=cosT, sinT=sinT, BTb=BTb, CTeb=CTeb, CTob=CTob)
    S["hre"] = p.sbc(f"s5hre", [128, 8])
    S["him"] = p.sbc(f"s5him", [128, 8])
    p.memset(S["hre"].ap(), 0.0)
    p.memset(S["him"].ap(), 0.0)
    yield


def s5_block(kind, st, T, W, PS, LW, *, p, proj, OFF, PJ, cnt, ident, identP, mI, mS, nmI, mI4, mS4, dmask, rows, v3,
             headsum, rstd_inplace, din, dout, l, NST, H, Hs, mixTs):
    PA_, PB, PU, PO_, PH_, PT = PS
    S, s5d, s5bg, wglu = LW["S5"], LW["s5d"], LW["s5bg"], LW["wglu"]
    uT, gate, yT = W["qT"], W["gate"], W["oT"]
    PH = W["PH"] if kind == "p" else PO_
    for tl in range(2):
        proj(uT[:, tl, 0:T], OFF["s5_u"] + 128 * tl, 128, T)
        yield
        proj(gate[:, tl, 0:T], OFF["s5_gate"] + 128 * tl, 128, T)
        yield
    BT, cosT, sinT = S["BT"], S["cosT"], S["sinT"]
    xre, xim, gre, gim, hre_t, him_t, ta, tb = (W[n][:, 0, 0:T] for n in ("t0", "t1", "t2", "t3", "kT", "vT", "aT", "bT"))
    g0 = W["g0"]
    if kind == "s":
        hS = W["hS"]
        nat = W["s5nat"]
        for r, nm in enumerate(("state_s5_re", "state_s5_im")):
            p.dma(nat[:, r, :], din[nm][l].re("b g q -> b (g q)"))
            for pr in range(8):
                p.tr(PT[:, pr * 16:pr * 16 + 16], nat[:, r, pr * 128:(pr + 1) * 128], ident[0:16, 0:16])
            p.cp(hS[:, r, :, :], PT[:, 0:128].re("p (a b) -> p a b", b=16))
            yield
    if kind == "p":
        g0a = W["g0a"]
        cs8, sn8, hr8, hi8 = S["cs"].ap(), S["sn"].ap(), S["hre"].ap(), S["him"].ap()
        p.tt(g0a[:, 0, :], cs8, hr8, OP.mult)
        p.tt(g0a[:, 1, :], sn8, hi8, OP.mult)
        p.tt(g0a[:, 2, :], g0a[:, 0, :], g0a[:, 1, :], OP.subtract)
        yield
        p.tt(g0a[:, 0, :], cs8, hi8, OP.mult)
        p.tt(g0a[:, 1, :], sn8, hr8, OP.mult)
        p.tt(g0a[:, 3, :], g0a[:, 0, :], g0a[:, 1, :], OP.add)
        yield
        Xre, Xim, Gre, Gim, Hre, Him, Ta, Tb = (W[n][:, :, 0:T] for n in ("t0", "t1", "t2", "t3", "kT", "vT", "aT", "bT"))
        ub, hbr, hbi = W["ub"], W["hbr"], W["hbi"]
        BTb = S["BTb"]
        p.cp(ub[:, :, 0:T], uT[:, :, 0:T], eng="act")
        yield
        PUv = PU[:, 0:256].re("p (a b) -> p a b", b=128)[:, :, 0:T]
        PBv = PB[:, 0:256].re("p (a b) -> p a b", b=128)[:, :, 0:T]
        for gi in range(4):
            prs = (2 * gi, 2 * gi + 1)
            for j, pr in enumerate(prs):
                q4, sl = pr % 4, pr // 4
                hf = q4 // 2
                rr = slice(64 * hf, 64 * hf + 64)
                bslot = sl * 2 + pr % 2
                p.mm(PUv[:, j, :], BTb[rr, 0 + bslot, :], ub[rr, sl, 0:T])
                p.mm(PBv[:, j, :], BTb[rr, 4 + bslot, :], ub[rr, sl, 0:T])
            cs, sn = cosT[:, 2 * gi:2 * gi + 2, 0:T], sinT[:, 2 * gi:2 * gi + 2, 0:T]
            p.tt(Ta, PUv, cs, OP.mult)
            yield
            p.tt(Tb, PBv, sn, OP.mult)
            yield
            p.tt(Xre, Ta, Tb, OP.add, eng="pool")
            yield
            p.tt(Ta, PBv, cs, OP.mult)
            yield
            p.tt(Tb, PUv, sn, OP.mult)
            yield
            p.tt(Xim, Ta, Tb, OP.subtract, eng="pool")
            yield
            for j, pr in enumerate(prs):
                magb = S["mag"][:, pr:pr + 1].bc([128, T])
                p.scan(Gre[:, j, :], magb, Xre[:, j, :], g0a[:, 2, pr:pr + 1], OP.mult, OP.add)
                yield
                p.scan(Gim[:, j, :], magb, Xim[:, j, :], g0a[:, 3, pr:pr + 1], OP.mult, OP.add)
                yield
            p.tt(Ta, Gre, cs, OP.mult)
            yield
            p.tt(Tb, Gim, sn, OP.mult, eng="pool")
            yield
            p.tt(Hre, Ta, Tb, OP.subtract)
            yield
            p.tt(Ta, Gim, cs, OP.mult, eng="pool")
            yield
            p.tt(Tb, Gre, sn, OP.mult)
            yield
            p.tt(Him, Ta, Tb, OP.add)
            yield
            p.cp(S["hre"][:, 2 * gi:2 * gi + 2], Hre[:, :, T - 1])
            p.cp(S["him"][:, 2 * gi:2 * gi + 2], Him[:, :, T - 1])
            p.cp(hbr[:, :, 0:T], Hre, eng="act")
            p.cp(hbi[:, :, 0:T], Him, eng="act")
            yield
            for j, pr in enumerate(prs):
                q4, sl = pr % 4, pr // 4
                hf = q4 // 2
                rr = slice(64 * hf, 64 * hf + 64)
                CTx = S["CTeb"] if pr % 2 == 0 else S["CTob"]
                p.mm(PH[rr, sl * ST:sl * ST + T], CTx[:, 0 + sl, rr], hbr[:, j, 0:T], start=(pr % 2 == 0), stop=False)
                p.mm(PH[rr, sl * ST:sl * ST + T], CTx[:, 2 + sl, rr], hbi[:, j, 0:T], start=False, stop=(pr % 2 == 1))
    for pr in (range(8) if kind == "s" else ()):
        q4, sl = pr % 4, pr // 4
        hf = q4 // 2
        rr = slice(64 * hf, 64 * hf + 64)
        bslot = sl * 2 + pr % 2
        p.mm(PU[:, 0:T], BT[rr, 0 + bslot, :], uT[rr, sl, 0:T])
        p.mm(PB[:, 0:T], BT[rr, 4 + bslot, :], uT[rr, sl, 0:T])
        if kind == "p":
            cs, sn = cosT[:, pr, 0:T], sinT[:, pr, 0:T]
            p.tt(ta, PU[:, 0:T], cs, OP.mult)
            yield
            p.tt(tb, PB[:, 0:T], sn, OP.mult, eng="dve")
            yield
            p.tt(xre, ta, tb, OP.add, eng="pool")
            yield
            p.tt(ta, PB[:, 0:T], cs, OP.mult)
            yield
            p.tt(tb, PU[:, 0:T], sn, OP.mult)
            yield
            p.tt(xim, ta, tb, OP.subtract, eng="pool")
            yield
            c1, s1 = S["cs"][:, pr:pr + 1], S["sn"][:, pr:pr + 1]
            hr, hi = S["hre"][:, pr:pr + 1], S["him"][:, pr:pr + 1]
            p.tt(g0[:, 0:1], c1, hr, OP.mult)
            yield
            p.tt(g0[:, 1:2], s1, hi, OP.mult)
            yield
            p.tt(g0[:, 2:3], g0[:, 0:1], g0[:, 1:2], OP.subtract)
            yield
            p.tt(g0[:, 0:1], c1, hi, OP.mult)
            yield
            p.tt(g0[:, 1:2], s1, hr, OP.mult)
            yield
            p.tt(g0[:, 3:4], g0[:, 0:1], g0[:, 1:2], OP.add)
            yield
            magb = S["mag"][:, pr:pr + 1].bc([128, T])
            p.scan(gre, magb, xre, g0[:, 2:3], OP.mult, OP.add)
            yield
            p.scan(gim, magb, xim, g0[:, 3:4], OP.mult, OP.add)
            yield
            p.tt(ta, gre, cs, OP.mult)
            yield
            p.tt(tb, gim, sn, OP.mult, eng="pool")
            yield
            p.tt(hre_t, ta, tb, OP.subtract)
            yield
            p.tt(ta, gim, cs, OP.mult, eng="pool")
            yield
            p.tt(tb, gre, sn, OP.mult)
            yield
            p.tt(him_t, ta, tb, OP.add)
            yield
            p.cp(S["hre"][:, pr:pr + 1], hre_t[:, T - 1:T])
            yield
            p.cp(S["him"][:, pr:pr + 1], him_t[:, T - 1:T])
            yield
        else:
            hS = W["hS"]
            hr, hi = hS[:, 0, pr, :], hS[:, 1, pr, :]
            p.ts(ta, hr, S["abre"][:, pr:pr + 1], OP.mult)
            yield
            p.stt(ta, hi, S["nabim"][:, pr:pr + 1], ta, OP.mult, OP.add)
            yield
            p.tt(hre_t, ta, PU[:, 0:T], OP.add)
            yield
            p.ts(tb, hr, S["abim"][:, pr:pr + 1], OP.mult)
            yield
            p.stt(tb, hi, S["abre"][:, pr:pr + 1], tb, OP.mult, OP.add)
            yield
            p.tt(him_t, tb, PB[:, 0:T], OP.add)
            yield
            p.cp(hr, hre_t)
            yield
            p.cp(hi, him_t)
            yield
        CTx = S["CTe"] if pr % 2 == 0 else S["CTo"]
        p.mm(PH[rr, sl * ST:sl * ST + T], CTx[:, 0 + sl, rr], hre_t, start=(pr % 2 == 0), stop=False)
        p.mm(PH[rr, sl * ST:sl * ST + T], CTx[:, 2 + sl, rr], him_t, start=False, stop=(pr % 2 == 1))
    for tl in range(2):
        p.stt(yT[:, tl, 0:T], uT[:, tl, 0:T], s5d[:, tl:tl + 1], PH[:, tl * ST:tl * ST + T], OP.mult, OP.add)
        yield
    if kind == "p" and st == NST - 1:
        for nm, src in (("s5re_p", S["hre"]), ("s5im_p", S["him"])):
            for gp in range(2):
                p.dma(dout[nm][l].re("(pr gp) q -> gp q pr", gp=2)[gp], src[rows(gp), :], allow_slow_non_contiguous=True)
    if kind == "s":
        hS, nat = W["hS"], W["s5nat"]
        for r, nm in enumerate(("s5re_s", "s5im_s")):
            for pr in range(8):
                p.tr(PT[0:16, pr * 128:(pr + 1) * 128] if pr < 4 else PB[0:16, (pr - 4) * 128:(pr - 3) * 128],
                     hS[:, r, pr, :], ident.ap())
            p.cp(nat[:, r, 0:512], PT[0:16, 0:512])
            yield
            p.cp(nat[:, r, 512:1024], PB[0:16, 0:512])
            yield
            p.dma(dout[nm][l].re("b g q -> b (g q)"), nat[:, r, :])
    a, b, gsb = W["t0"][:, :, 0:T], W["t1"][:, :, 0:T], W["t2"][:, :, 0:T]
    y = yT[:, :, 0:T]
    p.tt(a, y, y, OP.mult)
    yield
    p.ts(a, a, 0.044715, OP.mult, 1.0, OP.add)
    yield
    p.tt(a, a, y, OP.mult)
    yield
    p.ts(a, a, -15.0, OP.max)
    sig_(p, a, a, scale=2.0 * math.sqrt(2.0 / math.pi))
    yield
    p.tt(b, a, y, OP.mult)
    yield
    for tl in range(2):
        pj = PJ[cnt[0] % 2]
        cnt[0] += 1
        for k in range(2):
            p.mm(pj[:, 0:T], wglu[:, k, tl * 128:(tl + 1) * 128], W["t1"][:, k, 0:T], start=(k == 0), stop=(k == 1))
        p.ts(W["t2"][:, tl, 0:T], pj[:, 0:T], s5bg[:, tl:tl + 1], OP.add, -30.0, OP.max)
        sig_(p, W["t2"][:, tl, 0:T], W["t2"][:, tl, 0:T])
        yield
    p.tt(gsb, gsb, b, OP.mult)
    yield
    silu_(p, a, gate[:, :, 0:T])
    yield
    p.tt(mixTs[1][:, :, 0:T], gsb, a, OP.mult)
    yield


def gdn_block(kind, st, T, W, K, PS, LW, *, p, proj, OFF, PJ, cnt, ident, identP, mI, mS, nmI, mI4, mS4, dmask, rows, v3,
              headsum, rstd_inplace, din, dout, l, NST, H, Hs, mixTs):
    PA, PB, PU, PO, PH, PT = PS
    convw, gab, Eg, Eb, gdn_ng, hist = (LW[n] for n in ("convw", "gab", "Eg", "Eb", "gdn_ng", "hist_gdn"))
    xp = W["xp"]
    cv = W["cv"]
    gate = W["gate"]
    if kind == "p":
        p.cp(xp[:, :, 0:3], hist.ap())
        yield
        for j in range(6):
            proj(xp[:, j, 3:3 + T], OFF["gdn_qkv"] + 128 * j, 128, T)
            yield
        p.cp(hist.ap(), xp[:, :, T:T + 3])
        yield
        taps = [xp[:, :, i:i + T] for i in range(4)]
        if st == NST - 1:
            for r in range(3):
                p.dma(dout["conv_p"][l, r].re("(j q) -> q j", q=128), xp[:, :, T + r], allow_slow_non_contiguous=True)
    else:
        xsS = W["xsS"]
        nat = W["cnat"]
        p.dma(nat.ap(), din["state_gdn_conv"][l])
        for r in range(3):
            for j in range(6):
                p.tr(PT[:, j * 16:j * 16 + 16], nat[:, r, j * 128:(j + 1) * 128], ident[0:16, 0:16])
            p.cp(xsS[:, :, r, :], PT[:, 0:96].re("p (a b) -> p a b", b=16))
            yield
        for j in range(6):
            proj(xsS[:, j, 3, :], OFF["gdn_qkv"] + 128 * j, 128, T)
            yield
        taps = [xsS[:, :, i, :] for i in range(4)]
        p.dma(dout["conv_s"][l, :, 0:2, :], din["state_gdn_conv"][l, :, 1:3, :])
        for j in range(6):
            p.tr(PB[0:16, j * 128:(j + 1) * 128] if j < 4 else PU[0:16, (j - 4) * 128:(j - 3) * 128], xsS[:, j, 3, :],
                 ident.ap())
        p.cp(nat[:, 0, 0:512], PB[0:16, 0:512])
        yield
        p.cp(nat[:, 0, 512:768], PU[0:16, 0:256])
        yield
        p.dma(dout["conv_s"][l, :, 2, :], nat[:, 0, :])
    c = cv[:, :, 0:T]
    for j in range(6):
        cj = cv[:, j, 0:T]
        p.ts(cj, taps[0][:, j, :], convw[:, j, 0:1], OP.mult)
        yield
        for i in range(1, 4):
            p.stt(cj, taps[i][:, j, :], convw[:, j, i:i + 1], cj, OP.mult, OP.add)
            yield
    ctmp = taps[0]
    sig_(p, ctmp, c)
    yield
    p.tt(c, c, ctmp, OP.mult)
    yield
    qT, kT, vT, aT, bT, ldT, oT = (W[n] for n in ("qT", "kT", "vT", "aT", "bT", "ldT", "oT"))
    t0, t1 = W["t0"], W["t1"]
    for src, dst, sc in ((cv[:, 0:2, 0:T], qT, 0.125), (cv[:, 2:4, 0:T], aT, 1.0)):
        p.act(t0[:, :, 0:T], src, AF.Square)
        yield
        headsum(t1, t0, T)
        yield
        rstd_inplace(t1[:, :, 0:T], T, 1.0, 1e-6)
        yield
        p.stt(dst[:, :, 0:T], src, sc, t1[:, :, 0:T], OP.mult, OP.mult)
        yield
    p.cp(vT[:, :, 0:T], cv[:, 4:6, 0:T])
    yield
    abT, gf, bf = W["abT"], W["gf"], W["bf"]
    proj(abT[0:8, 0:T], OFF["gdn_ab"], 8, T)
    yield
    p.act(gf[0:8, 0:T], abT[0:8, 0:T], AF.Exp, bias=gab[0:8, 0:1])
    yield
    p.act(gf[0:8, 0:T], gf[0:8, 0:T], AF.Ln, bias=1.0)
    yield
    p.ts(gf[0:8, 0:T], gf[0:8, 0:T], gab[0:8, 1:2], OP.mult)
    yield
    sig_(p, bf[0:8, 0:T], abT[0:8, 0:T])
    yield
    for tl in range(2):
        pj = PJ[cnt[0] % 2]
        cnt[0] += 1
        p.mm(pj[:, 0:T], Eg[0:8, tl, :], gf[0:8, 0:T])
        p.cp(ldT[:, tl, 0:T], pj[:, 0:T], eng="act")
        yield
        pj = PJ[cnt[0] % 2]
        cnt[0] += 1
        p.mm(pj[:, 0:T], Eb[0:8, tl, :], bf[0:8, 0:T])
        p.tt(kT[:, tl, 0:T], aT[:, tl, 0:T], pj[:, 0:T], OP.mult)
        yield
    p.act(t0[:, :, 0:T], ldT[:, :, 0:T], AF.Exp)
    yield
    p.tt(bT[:, :, 0:T], kT[:, :, 0:T], t0[:, :, 0:T], OP.mult)
    yield
    for tl in range(2):
        proj(gate[:, tl, 0:T], OFF["gdn_gate"] + 128 * tl, 128, T)
        yield
    yield from mixer_core(p, "gdn", kind, T, W, K, H["gdn"], Hs["gdn"], dict(ab=True, scalar=True), PS, ident, identP, mI, mS, nmI,
                          mI4, mS4, dmask, rows, v3, din["state_gdn"], dout["gdn_s"], l, transposed_state=False)
    yield from out_norm_rms(p, oT, gate, gdn_ng, T, W, headsum, rstd_inplace, mixTs[2])
    if kind == "p" and st == NST - 1:
        p.dma(dout["gdn_p"][l].re("(t hp) d v -> (hp d) t v", hp=2), H["gdn"].ap())


def rwkv_block(kind, st, T, W, K, PS, LW, *, p, proj, OFF, PJ, cnt, ident, identP, mI, mS, nmI, mI4, mS4, dmask, rows, v3,
               headsum, rstd_inplace, din, dout, l, NST, H, Hs, mixTs):
    PA, PB, PU, PO, PH, PT = PS
    mu, w0, a0, k_k, k_a, r_k, ln_g, ln_b, wlo, hist = (LW[n] for n in (
        "mu", "w0", "a0", "k_k", "k_a", "r_k", "ln_g", "ln_b", "wlo", "hist_rwkv"))
    rin = W["rin"]
    xs = W["xs7"]
    gate = W["gate"]
    if kind == "p":
        p.cp(rin[:, :, 0:1], hist.ap())
        yield
        for j in range(7):
            proj(rin[:, j, 1:1 + T], OFF["rwkv_in"] + 128 * j, 128, T)
            yield
        p.cp(hist.ap(), rin[:, :, T:T + 1])
        yield
        prev, cur = rin[:, :, 0:T], rin[:, :, 1:1 + T]
        if st == NST - 1:
            p.dma(dout["shift_p"][l].re("(j q o) -> q j o", q=128, o=1), rin[:, :, T:T + 1], allow_slow_non_contiguous=True)
    else:
        nat = W["snat"]
        prevS = W["prevS"]
        p.dma(nat.ap(), din["state_rwkv_shift"][l])
        for j in range(7):
            p.tr(PT[:, j * 16:j * 16 + 16], nat[:, j * 128:(j + 1) * 128], ident[0:16, 0:16])
        p.cp(prevS.ap(), PT[:, 0:112].re("p (a b) -> p a b", b=16))
        yield
        for j in range(7):
            proj(rin[:, j, 0:T], OFF["rwkv_in"] + 128 * j, 128, T)
            yield
        prev, cur = prevS.ap(), rin[:, :, 0:T]
        for j in range(7):
            p.tr(PB[0:16, j * 128:(j + 1) * 128] if j < 4 else PU[0:16, (j - 4) * 128:(j - 3) * 128], rin[:, j, 0:T],
                 ident.ap())
        p.cp(nat[:, 0:512], PB[0:16, 0:512])
        yield
        p.cp(nat[:, 512:896], PU[0:16, 0:384])
        yield
        p.dma(dout["shift_s"][l], nat.ap())
    x = xs[:, :, 0:T]
    p.tt(x, prev, cur, OP.subtract)
    yield
    p.tt(x, x, mu.ap()[:, :, None].bc([128, 7, T]), OP.mult)
    yield
    p.tt(x, x, cur, OP.add)
    yield
    qT, kT, vT, aT, bT, ldT, oT = (W[n] for n in ("qT", "kT", "vT", "aT", "bT", "ldT", "oT"))
    t0, t1, t2 = W["t0"], W["t1"], W["t2"]
    p.cp(qT[:, :, 0:T], xs[:, 0:2, 0:T])
    yield
    p.cp(vT[:, :, 0:T], xs[:, 4:6, 0:T])
    yield
    rk = xs[:, 2:4, 0:T]
    sig_(p, t0[0:64, 0, 0:T], xs[0:64, 6, 0:T], scale=2.0)
    p.ts(t0[0:64, 0, 0:T], t0[0:64, 0, 0:T], 2.0, OP.mult, -1.0, OP.add)
    yield
    asg = t2
    for tl in range(2):
        pj = PJ[cnt[0] % 2]
        cnt[0] += 1
        p.mm(pj[:, 0:T], wlo[0:64, tl * 128:(tl + 1) * 128], t0[0:64, 0, 0:T])
        p.ts(ldT[:, tl, 0:T], pj[:, 0:T], w0[:, tl:tl + 1], OP.add)
        sig_(p, ldT[:, tl, 0:T], ldT[:, tl, 0:T])
        yield
        pj = PJ[cnt[0] % 2]
        cnt[0] += 1
        p.mm(pj[:, 0:T], wlo[64:128, tl * 128:(tl + 1) * 128], xs[64:128, 6, 0:T])
        p.ts(asg[:, tl, 0:T], pj[:, 0:T], a0[:, tl:tl + 1], OP.add)
        sig_(p, asg[:, tl, 0:T], asg[:, tl, 0:T])
        yield
    p.ts(ldT[:, :, 0:T], ldT[:, :, 0:T], -math.exp(-0.5), OP.mult)
    yield
    p.tt(aT[:, :, 0:T], rk, k_k.ap()[:, :, None].bc([128, 2, T]), OP.mult)
    yield
    p.act(t0[:, :, 0:T], aT[:, :, 0:T], AF.Square)
    yield
    headsum(t1, t0, T)
    yield
    rstd_inplace(t1[:, :, 0:T], T, 1.0, 1e-6)
    yield
    p.tt(aT[:, :, 0:T], aT[:, :, 0:T], t1[:, :, 0:T], OP.mult)
    yield
    p.tt(bT[:, :, 0:T], aT[:, :, 0:T], asg[:, :, 0:T], OP.mult)
    yield
    p.ts(t0[:, :, 0:T], asg[:, :, 0:T], -1.0, OP.add)
    yield
    p.tt(t0[:, :, 0:T], t0[:, :, 0:T], k_a.ap()[:, :, None].bc([128, 2, T]), OP.mult)
    yield
    p.ts(t0[:, :, 0:T], t0[:, :, 0:T], 1.0, OP.add)
    yield
    p.tt(kT[:, :, 0:T], rk, t0[:, :, 0:T], OP.mult)
    yield
    bon = W["bon"]
    p.tt(t0[:, :, 0:T], qT[:, :, 0:T], kT[:, :, 0:T], OP.mult)
    yield
    p.tt(t0[:, :, 0:T], t0[:, :, 0:T], r_k.ap()[:, :, None].bc([128, 2, T]), OP.mult)
    yield
    headsum(t1, t0, T)
    yield
    p.tt(bon[:, :, 0:T], t1[:, :, 0:T], vT[:, :, 0:T], OP.mult)
    yield
    for tl in range(2):
        proj(gate[:, tl, 0:T], OFF["rwkv_gate"] + 128 * tl, 128, T)
        yield
    yield from mixer_core(p, "rwkv", kind, T, W, K, H["rwkv"], Hs["rwkv"], dict(ab=True, scalar=False), PS, ident, identP, mI, mS,
                          nmI, mI4, mS4, dmask, rows, v3, din["state_rwkv"], dout["rwkv_s"], l, transposed_state=True)
    o = oT[:, :, 0:T]
    headsum(t1, oT, T)
    yield
    p.stt(t0[:, :, 0:T], t1[:, :, 0:T], -1.0 / 64, o, OP.mult, OP.add)
    yield
    p.act(t1[:, :, 0:T], t0[:, :, 0:T], AF.Square)
    yield
    headsum(t2, t1, T)
    yield
    rstd_inplace(t2[:, :, 0:T], T, 1.0 / 64, 64e-5)
    yield
    p.tt(t0[:, :, 0:T], t0[:, :, 0:T], t2[:, :, 0:T], OP.mult)
    yield
    p.tt(t0[:, :, 0:T], t0[:, :, 0:T], ln_g.ap()[:, :, None].bc([128, 2, T]), OP.mult)
    yield
    p.tt(t0[:, :, 0:T], t0[:, :, 0:T], ln_b.ap()[:, :, None].bc([128, 2, T]), OP.add)
    yield
    p.tt(t0[:, :, 0:T], t0[:, :, 0:T], bon[:, :, 0:T], OP.add)
    yield
    silu_(p, t1[:, :, 0:T], gate[:, :, 0:T])
    yield
    p.tt(mixTs[3][:, :, 0:T], t0[:, :, 0:T], t1[:, :, 0:T], OP.mult)
    yield
    if kind == "p" and st == NST - 1:
        ptv = v3(PT, 2)
        for tl in range(2):
            for hp in range(2):
                p.tr(ptv[rows(hp), tl, :], H["rwkv"][rows(hp), tl, :], ident[rows(hp), rows(hp)])
        p.cp(K["Ut"].ap(), ptv)
        yield
        p.dma(dout["rwkv_p"][l].re("(t hp) v d -> (hp v) t d", hp=2), K["Ut"].ap())


from concourse.bass_utils import run_bass_kernel_spmd

_CACHE = {}


def kernel(**inputs):
    if "nc" not in _CACHE:
        _CACHE["nc"] = build(DEPTH=4, NST=2048 // ST, SAMPLE=True)[0]
    nc = _CACHE["nc"]
    in_maps = []
    for c in range(8):
        m = {}
        for k, shp in SHAPES.items():
            a = np.asarray(inputs[k])
            if k == "x_prompt":
                a = a[c]
            elif k == "x_sample":
                a = a[16 * c:16 * c + 16, 0]
            elif k.startswith("state_"):
                a = a[:, 16 * c:16 * c + 16]
            m[k] = np.ascontiguousarray(a, dtype=np.float32)
        in_maps.append(m)
    res = run_bass_kernel_spmd(nc, in_maps, core_ids=list(range(8)))
    rs = res.results
    outs = []
    for k in OUT_ORDER:
        if k == "y_p":
            o = np.stack([r[k] for r in rs], axis=0)
        elif k == "y_s":
            o = np.concatenate([r[k] for r in rs], axis=0)[:, None, :]
        elif k.endswith("_p"):
            o = np.stack([r[k] for r in rs], axis=1)
        else:
            o = np.concatenate([r[k] for r in rs], axis=1)
        outs.append(np.ascontiguousarray(o, dtype=np.float32))
    return tuple(outs)
```
